# Optimizing a Trainium2 kernel written in Bass

```python
import math
import jax
import jax.numpy as jnp
from jax import lax
import numpy as np

D_MODEL = 2048
BATCH = 4
SEQ = 4096
DEPTH = 4

GRID_W = 64
CTX_LEN = 256
RMS_EPS = 1e-6
ROPE_THETA = 10000.0
Q_BLOCK = 128

A_HEADS = 8
A_KV_HEADS = 2
A_HEAD_DIM = 128
B_HEADS = 16
B_HEAD_DIM = 64
B_D_INNER = B_HEADS * B_HEAD_DIM
B_GROUPS = 2
B_D_STATE = 128
B_CONV = 5
B_CHUNK = 128
B_XBC = B_D_INNER + 2 * B_GROUPS * B_D_STATE
C_HEADS = 8
C_Q_RANK = 512
C_KV_RANK = 256
C_NOPE_DIM = 128
C_ROPE_DIM = 64
C_V_DIM = 128
D_HEAD_DIM = 64
D_WIDTH = 1024
D_HEADS = D_WIDTH // D_HEAD_DIM
D_DECAY_LORA = 96
D_AAA_LORA = 96
D_GATE_LORA = 256
D_LN_EPS = 64e-5
D_FF = 5632
FFN_CONV = 3

EVEN_SPLITS = (A_HEADS * A_HEAD_DIM, A_KV_HEADS * A_HEAD_DIM, A_KV_HEADS * A_HEAD_DIM, B_D_INNER, B_XBC, 2 * B_HEADS)
EVEN_IN = sum(EVEN_SPLITS)
EVEN_OUT = A_HEADS * A_HEAD_DIM + B_D_INNER
MLA_SPLITS = (C_Q_RANK, C_KV_RANK, C_ROPE_DIM)
MLA_IN = sum(MLA_SPLITS)
RWKV_SPLITS = (D_WIDTH, D_WIDTH, D_WIDTH, D_DECAY_LORA, D_DECAY_LORA, D_AAA_LORA, D_AAA_LORA, D_GATE_LORA)
RWKV_IN = sum(RWKV_SPLITS)
ODD_IN = MLA_IN + RWKV_IN
ODD_OUT = C_HEADS * C_V_DIM + D_WIDTH

kernel_name = "hybrid_diffusion_gqa_ssd_mla_rwkv7"


def split_cols(u, sizes):
    return jnp.split(u, [int(s) for s in np.cumsum(sizes)[:-1]], axis=-1)


def flip(t):
    return jnp.flip(t, axis=1)


def rms_norm(x, g):
    xf = x.astype(jnp.float32)
    y = xf * lax.rsqrt(jnp.mean(xf * xf, axis=-1, keepdims=True) + RMS_EPS)
    return (y * g.astype(jnp.float32)).astype(x.dtype)


def modulate(x, g, shift, scale):
    return rms_norm(x, g) * (1.0 + scale[:, None, :]) + shift[:, None, :]


def dwconv(u, w, b):
    y = lax.conv_general_dilated(u, w[:, None, :].astype(u.dtype), window_strides=(1,), padding='SAME',
                                 dimension_numbers=('NWC', 'WIO', 'NWC'), feature_group_count=u.shape[-1])
    return y + b


def centred_shift(u):
    up = jnp.pad(u, ((0, 0), (1, 1), (0, 0)))
    return 0.5 * (up[:, :-2] + up[:, 2:])


def rope_1d(x, pos):
    half = x.shape[-1] // 2
    inv = ROPE_THETA ** (-jnp.arange(half, dtype=jnp.float32) / half)
    ang = pos.astype(jnp.float32)[:, None] * inv[None, :]
    cos = jnp.cos(ang)[None, :, None, :].astype(x.dtype)
    sin = jnp.sin(ang)[None, :, None, :].astype(x.dtype)
    x1, x2 = x[..., :half], x[..., half:]
    return jnp.concatenate([x1 * cos - x2 * sin, x2 * cos + x1 * sin], axis=-1)


def rope_2d(x, row, col):
    d = x.shape[-1] // 2
    return jnp.concatenate([rope_1d(x[..., :d], row), rope_1d(x[..., d:], col)], axis=-1)


def block_attention(q, k, v):
    b, lq, hkv, g, dk = q.shape
    nb = lq // Q_BLOCK
    scale = dk ** -0.5
    qb = q.reshape(b, nb, Q_BLOCK, hkv, g, dk).swapaxes(0, 1)

    def one_block(qi):
        s = jnp.einsum('bqkgd,bskd->bkgqs', qi, k).astype(jnp.float32) * scale
        p = jax.nn.softmax(s, axis=-1).astype(v.dtype)
        return jnp.einsum('bkgqs,bskd->bqkgd', p, v)

    o = lax.map(one_block, qb)
    return o.swapaxes(0, 1).reshape(b, lq, hkv * g * v.shape[-1])


def ssd_chunked(xs, dt, a, bm, cm, h0, emit):
    f32 = jnp.float32
    b, l, h, p = xs.shape
    nc = l // B_CHUNK
    rep = h // bm.shape[2]

    def chunks(t):
        return t.astype(f32).reshape(b, nc, B_CHUNK, *t.shape[2:])

    xdt = chunks(xs * dt[..., None])
    bh = chunks(jnp.repeat(bm, rep, axis=2))
    ch = chunks(jnp.repeat(cm, rep, axis=2))
    a_cs = jnp.cumsum(chunks(dt * a), axis=2)
    states = jnp.einsum('bclhn,bclh,bclhp->bchpn', bh, jnp.exp(a_cs[:, :, -1:] - a_cs), xdt)

    def carry_step(hc, inp):
        dec, st = inp
        return hc * dec[:, :, None, None] + st, hc

    h_final, h_start = lax.scan(carry_step, h0, (jnp.exp(a_cs[:, :, -1]).swapaxes(0, 1), states.swapaxes(0, 1)))
    if not emit:
        return None, h_final
    tri = jnp.tril(jnp.ones((B_CHUNK, B_CHUNK), dtype=bool))[None, None, :, :, None]
    seg = jnp.exp(jnp.where(tri, a_cs[:, :, :, None, :] - a_cs[:, :, None, :, :], -jnp.inf))
    y_diag = jnp.einsum('bclsh,bcshp->bclhp', jnp.einsum('bclhn,bcshn->bclsh', ch, bh) * seg, xdt)
    y_off = jnp.einsum('bclhn,bchpn->bclhp', ch, h_start.swapaxes(0, 1)) * jnp.exp(a_cs)[..., None]
    return (y_diag + y_off).reshape(b, l, h, p), h_final


def wkv7_scan(w, k, v, kk, a, s0, r=None):
    emit = r is not None

    def step(s, inp):
        w_t, k_t, v_t, kk_t, a_t = inp[:5]
        sa = jnp.einsum('bhij,bhj->bhi', s, kk_t)
        s = (s * w_t[:, :, None, :] - sa[..., None] * (kk_t * a_t)[:, :, None, :]
             + v_t[..., None] * k_t[:, :, None, :])
        y = jnp.einsum('bhij,bhj->bhi', s, inp[5]) if emit else None
        return s, y

    xs = tuple(t.swapaxes(0, 1) for t in ((w, k, v, kk, a) + ((r,) if emit else ())))
    s_final, ys = lax.scan(step, s0, xs)
    return (ys.swapaxes(0, 1) if emit else None), s_final


def even_mixer(u_l, u_c, row, col, q_norm, k_norm, conv_w, conv_b, dt_bias, a_log, d_skip, ssm_norm, ctx_out):
    f32 = jnp.float32
    qa_l, ka_l, va_l, z_l, xbc_l, dt_l = split_cols(u_l, EVEN_SPLITS)
    qa_c, ka_c, va_c, z_c, xbc_c, dt_c = split_cols(u_c, EVEN_SPLITS)

    def gqa_q(qa, rotary):
        b, l = qa.shape[:2]
        q = rms_norm(qa.reshape(b, l, A_HEADS, A_HEAD_DIM), q_norm)
        q = rope_2d(q, row, col) if rotary else q
        return q.reshape(b, l, A_KV_HEADS, A_HEADS // A_KV_HEADS, A_HEAD_DIM)

    def gqa_kv(ka, va, rotary):
        b, l = ka.shape[:2]
        k = rms_norm(ka.reshape(b, l, A_KV_HEADS, A_HEAD_DIM), k_norm)
        k = rope_2d(k, row, col) if rotary else k
        return k, va.reshape(b, l, A_KV_HEADS, A_HEAD_DIM)

    k_c, v_c = gqa_kv(ka_c, va_c, False)
    k_l, v_l = gqa_kv(ka_l, va_l, True)
    ya_l = block_attention(gqa_q(qa_l, True), jnp.concatenate([k_c, k_l], axis=1), jnp.concatenate([v_c, v_l], axis=1))

    a_neg = -jnp.exp(a_log.astype(f32))

    def ssd_prep(xbc, dt_raw):
        b, l = xbc.shape[:2]
        xbc = jax.nn.silu(dwconv(xbc, conv_w, conv_b))
        xs, bm, cm = split_cols(xbc, (B_D_INNER, B_GROUPS * B_D_STATE, B_GROUPS * B_D_STATE))
        dt = jax.nn.softplus(dt_raw.astype(f32).reshape(b, l, 2, B_HEADS) + dt_bias.astype(f32))
        return (xs.reshape(b, l, B_HEADS, B_HEAD_DIM), bm.reshape(b, l, B_GROUPS, B_D_STATE),
                cm.reshape(b, l, B_GROUPS, B_D_STATE), dt)

    def ssd_bidir(prep, h_f, h_b, emit):
        xs, bm, cm, dt = prep
        y_f, h_f = ssd_chunked(xs, dt[:, :, 0], a_neg[0], bm, cm, h_f, emit)
        y_b, h_b = ssd_chunked(flip(xs), flip(dt[:, :, 1]), a_neg[1], flip(bm), flip(cm), h_b, emit)
        return (y_f + flip(y_b)) if emit else None, h_f, h_b

    def ssd_out(y, xs, z):
        b, l = z.shape[:2]
        y = (y + d_skip.astype(f32)[:, None] * xs.astype(f32)).reshape(b, l, B_D_INNER).astype(z.dtype)
        y = y * jax.nn.silu(z)
        return rms_norm(y.reshape(b, l, B_GROUPS, -1), ssm_norm.reshape(B_GROUPS, -1)).reshape(b, l, B_D_INNER)

    h0 = jnp.zeros((u_l.shape[0], B_HEADS, B_HEAD_DIM, B_D_STATE), f32)
    prep_c = ssd_prep(xbc_c, dt_c)
    ys_c, h_f, h_b = ssd_bidir(prep_c, h0, h0, ctx_out)
    prep_l = ssd_prep(xbc_l, dt_l)
    ys_l, _, _ = ssd_bidir(prep_l, h_f, h_b, True)
    y_l = jnp.concatenate([ya_l, ssd_out(ys_l, prep_l[0], z_l)], axis=-1)
    if not ctx_out:
        return y_l, None
    ya_c = block_attention(gqa_q(qa_c, False), k_c, v_c)
    y_c = jnp.concatenate([ya_c, ssd_out(ys_c, prep_c[0], z_c)], axis=-1)
    return y_l, y_c


def odd_mixer(u_l, u_c, row, col, q_a_norm, q_b, kv_a_norm, kv_b, mu, w0, w2, a0, a2, g2, k_k, k_a, r_k,
              ln_w, ln_b, ctx_out):
    f32 = jnp.float32

    def mla_q(u, rotary):
        b, l = u.shape[:2]
        q = (rms_norm(u[..., :C_Q_RANK], q_a_norm) @ q_b).reshape(b, l, C_HEADS, C_NOPE_DIM + C_ROPE_DIM)
        q_pe = q[..., C_NOPE_DIM:]
        q_pe = rope_2d(q_pe, row, col) if rotary else q_pe
        return jnp.concatenate([q[..., :C_NOPE_DIM], q_pe], axis=-1)[:, :, :, None, :]

    def mla_kv(u, rotary):
        b, l = u.shape[:2]
        _, kva, k_pe = split_cols(u[..., :MLA_IN], MLA_SPLITS)
        kv = (rms_norm(kva, kv_a_norm) @ kv_b).reshape(b, l, C_HEADS, C_NOPE_DIM + C_V_DIM)
        k_pe = k_pe[:, :, None, :]
        k_pe = rope_2d(k_pe, row, col) if rotary else k_pe
        k = jnp.concatenate([kv[..., :C_NOPE_DIM], jnp.broadcast_to(k_pe, (b, l, C_HEADS, C_ROPE_DIM))], axis=-1)
        return k, kv[..., C_NOPE_DIM:]

    def rwkv_prep(u):
        b, l = u.shape[:2]
        ud = u[..., MLA_IN:]
        ud = (ud + mu * (centred_shift(ud) - ud)).astype(f32)
        r, k, v, wd_f, wd_b, ad_f, ad_b, gd = split_cols(ud, RWKV_SPLITS)

        def hd(t):
            return t.reshape(b, l, D_HEADS, D_HEAD_DIM)

        kk = hd(k * k_k)
        kk = kk * lax.rsqrt(jnp.sum(kk * kk, axis=-1, keepdims=True) + 1e-12)

        def direction(wd, ad, i):
            w_log = -jax.nn.softplus(-(w0[i] + jnp.tanh(wd) @ w2[i])) - 0.5
            a = jax.nn.sigmoid(a0[i] + ad @ a2[i])
            return hd(jnp.exp(-jnp.exp(w_log))), hd(a), hd(k * (1.0 + (a - 1.0) * k_a))

        return hd(r), hd(v), kk, gd, direction(wd_f, ad_f, 0), direction(wd_b, ad_b, 1)

    def rwkv_bidir(prep, s_f, s_b, emit):
        r, v, kk, _, (w_f, a_f, k_f), (w_b, a_b, k_b) = prep
        y_f, s_f = wkv7_scan(w_f, k_f, v, kk, a_f, s_f, r if emit else None)
        y_b, s_b = wkv7_scan(flip(w_b), flip(k_b), flip(v), flip(kk), flip(a_b), s_b, flip(r) if emit else None)
        return (y_f + flip(y_b)) if emit else None, s_f, s_b

    def rwkv_out(y, prep):
        r, v, _, gd, (_, _, k_f), (_, _, k_b) = prep
        b, l = y.shape[:2]
        mean = jnp.mean(y, axis=-1, keepdims=True)
        var = jnp.mean(jnp.square(y - mean), axis=-1, keepdims=True)
        y = ((y - mean) * lax.rsqrt(var + D_LN_EPS)).reshape(b, l, D_WIDTH) * ln_w + ln_b
        bonus = (jnp.sum(r * (k_f + k_b) * r_k, axis=-1, keepdims=True) * v).reshape(b, l, D_WIDTH)
        return ((y + bonus) * (jax.nn.sigmoid(gd) @ g2)).astype(u_l.dtype)

    k_c, v_c = mla_kv(u_c, False)
    k_l, v_l = mla_kv(u_l, True)
    ym_l = block_attention(mla_q(u_l, True), jnp.concatenate([k_c, k_l], axis=1), jnp.concatenate([v_c, v_l], axis=1))

    s0 = jnp.zeros((u_l.shape[0], D_HEADS, D_HEAD_DIM, D_HEAD_DIM), f32)
    prep_c = rwkv_prep(u_c)
    yr_c, s_f, s_b = rwkv_bidir(prep_c, s0, s0, ctx_out)
    prep_l = rwkv_prep(u_l)
    yr_l, _, _ = rwkv_bidir(prep_l, s_f, s_b, True)
    y_l = jnp.concatenate([ym_l, rwkv_out(yr_l, prep_l)], axis=-1)
    if not ctx_out:
        return y_l, None
    y_c = jnp.concatenate([block_attention(mla_q(u_c, False), k_c, v_c), rwkv_out(yr_c, prep_c)], axis=-1)
    return y_l, y_c


def conv_ffn(h, w_up, conv_w, conv_b, w_down):
    gate, val = jnp.split(dwconv(h @ w_up, conv_w, conv_b), 2, axis=-1)
    return (jax.nn.silu(gate) * val) @ w_down


def setup_inputs(seed: int = 0) -> dict:
    key = jax.random.key(seed)
    ks = iter(jax.random.split(key, 64))
    n_even, n_odd = (DEPTH + 1) // 2, DEPTH // 2
    f32 = jnp.float32

    def normal(shape, scale):
        return scale * jax.random.normal(next(ks), shape, f32)

    def lin(shape):
        return normal(shape, shape[-2] ** -0.5)

    def gain(shape):
        return 1.0 + normal(shape, 0.02)

    def unif(shape, lo, hi):
        return jax.random.uniform(next(ks), shape, f32, lo, hi)

    dt0 = jnp.exp(unif((n_even, 2, B_HEADS), math.log(1e-3), math.log(1e-1)))
    return {
        "x": normal((BATCH, SEQ, D_MODEL), 1.0),
        "c": normal((BATCH, D_MODEL), 1.0),
        "ctx": normal((BATCH, CTX_LEN, D_MODEL), 1.0),
        "c_ctx": normal((D_MODEL,), 1.0),
        "ada_w": normal((DEPTH, D_MODEL, 6 * D_MODEL), 0.5 * D_MODEL ** -0.5),
        "ada_b": normal((DEPTH, 6 * D_MODEL), 0.01),
        "norm_mix": gain((DEPTH, D_MODEL)),
        "norm_ffn": gain((DEPTH, D_MODEL)),
        "ffn_w_up": lin((DEPTH, D_MODEL, 2 * D_FF)),
        "ffn_conv_w": normal((DEPTH, FFN_CONV, 2 * D_FF), FFN_CONV ** -0.5),
        "ffn_conv_b": normal((DEPTH, 2 * D_FF), 0.01),
        "ffn_w_down": lin((DEPTH, D_FF, D_MODEL)),
        "ev_w_in": lin((n_even, D_MODEL, EVEN_IN)),
        "ev_w_out": lin((n_even, EVEN_OUT, D_MODEL)),
        "attn_q_norm": gain((n_even, A_HEAD_DIM)),
        "attn_k_norm": gain((n_even, A_HEAD_DIM)),
        "ssm_conv_w": normal((n_even, B_CONV, B_XBC), B_CONV ** -0.5),
        "ssm_conv_b": normal((n_even, B_XBC), 0.01),
        "ssm_dt_bias": dt0 + jnp.log(-jnp.expm1(-dt0)),
        "ssm_a_log": jnp.log(unif((n_even, 2, B_HEADS), 1.0, 16.0)),
        "ssm_d": 1.0 + normal((n_even, B_HEADS), 0.1),
        "ssm_norm": gain((n_even, B_D_INNER)),
        "od_w_in": lin((n_odd, D_MODEL, ODD_IN)),
        "od_w_out": lin((n_odd, ODD_OUT, D_MODEL)),
        "mla_q_a_norm": gain((n_odd, C_Q_RANK)),
        "mla_q_b": lin((n_odd, C_Q_RANK, C_HEADS * (C_NOPE_DIM + C_ROPE_DIM))),
        "mla_kv_a_norm": gain((n_odd, C_KV_RANK)),
        "mla_kv_b": lin((n_odd, C_KV_RANK, C_HEADS * (C_NOPE_DIM + C_V_DIM))),
        "rwkv_mu": unif((n_odd, RWKV_IN), 0.0, 1.0),
        "rwkv_w0": unif((n_odd, 2, D_WIDTH), -6.0, -1.0),
        "rwkv_w2": normal((n_odd, 2, D_DECAY_LORA, D_WIDTH), 0.1 * D_DECAY_LORA ** -0.5),
        "rwkv_a0": normal((n_odd, 2, D_WIDTH), 0.5),
        "rwkv_a2": normal((n_odd, 2, D_AAA_LORA, D_WIDTH), 0.1 * D_AAA_LORA ** -0.5),
        "rwkv_g2": lin((n_odd, D_GATE_LORA, D_WIDTH)),
        "rwkv_k_k": 0.85 + normal((n_odd, D_WIDTH), 0.05),
        "rwkv_k_a": 1.0 + normal((n_odd, D_WIDTH), 0.05),
        "rwkv_r_k": normal((n_odd, D_HEADS, D_HEAD_DIM), 0.1),
        "rwkv_ln_w": gain((n_odd, D_WIDTH)),
        "rwkv_ln_b": normal((n_odd, D_WIDTH), 0.01),
        "final_norm": gain((D_MODEL,)),
    }


def reference(x, c, ctx, c_ctx, ada_w, ada_b, norm_mix, norm_ffn, ffn_w_up, ffn_conv_w, ffn_conv_b, ffn_w_down,
              ev_w_in, ev_w_out, attn_q_norm, attn_k_norm, ssm_conv_w, ssm_conv_b, ssm_dt_bias, ssm_a_log, ssm_d,
              ssm_norm, od_w_in, od_w_out, mla_q_a_norm, mla_q_b, mla_kv_a_norm, mla_kv_b, rwkv_mu, rwkv_w0,
              rwkv_w2, rwkv_a0, rwkv_a2, rwkv_g2, rwkv_k_k, rwkv_k_a, rwkv_r_k, rwkv_ln_w, rwkv_ln_b, final_norm):
    b, n_lat = x.shape[:2]
    n_rows = n_lat // GRID_W
    row = jnp.repeat(jnp.arange(n_rows), GRID_W)
    col = jnp.tile(jnp.arange(GRID_W), n_rows)
    c_act = jax.nn.silu(c)
    cc_act = jax.nn.silu(c_ctx)[None]

    for l in range(DEPTH):
        ctx_out = l < DEPTH - 1
        mod_l = (c_act @ ada_w[l] + ada_b[l]).reshape(b, 6, D_MODEL)
        mod_c = (cc_act @ ada_w[l] + ada_b[l]).reshape(1, 6, D_MODEL)
        h_l = modulate(x, norm_mix[l], mod_l[:, 0], mod_l[:, 1])
        h_c = modulate(ctx, norm_mix[l], mod_c[:, 0], mod_c[:, 1])
        if l % 2 == 0:
            e = l // 2
            y_l, y_c = even_mixer(h_l @ ev_w_in[e], h_c @ ev_w_in[e], row, col, attn_q_norm[e], attn_k_norm[e],
                                  ssm_conv_w[e], ssm_conv_b[e], ssm_dt_bias[e], ssm_a_log[e], ssm_d[e], ssm_norm[e],
                                  ctx_out)
            w_out = ev_w_out[e]
        else:
            o = l // 2
            y_l, y_c = odd_mixer(h_l @ od_w_in[o], h_c @ od_w_in[o], row, col, mla_q_a_norm[o], mla_q_b[o],
                                 mla_kv_a_norm[o], mla_kv_b[o], rwkv_mu[o], rwkv_w0[o], rwkv_w2[o], rwkv_a0[o],
                                 rwkv_a2[o], rwkv_g2[o], rwkv_k_k[o], rwkv_k_a[o], rwkv_r_k[o], rwkv_ln_w[o],
                                 rwkv_ln_b[o], ctx_out)
            w_out = od_w_out[o]
        x = x + mod_l[:, 2, None] * (y_l @ w_out)
        x = x + mod_l[:, 5, None] * conv_ffn(modulate(x, norm_ffn[l], mod_l[:, 3], mod_l[:, 4]),
                                             ffn_w_up[l], ffn_conv_w[l], ffn_conv_b[l], ffn_w_down[l])
        if ctx_out:
            ctx = ctx + mod_c[:, 2, None] * (y_c @ w_out)
            ctx = ctx + mod_c[:, 5, None] * conv_ffn(modulate(ctx, norm_ffn[l], mod_c[:, 3], mod_c[:, 4]),
                                                     ffn_w_up[l], ffn_conv_w[l], ffn_conv_b[l], ffn_w_down[l])
    return rms_norm(x, final_norm)
```

```python
from concourse.bass_utils import run_bass_kernel_spmd
import numpy as np
import concourse.bass as bass
import concourse.mybir as mybir
from contextlib import ExitStack

F32 = mybir.dt.float32
BF16 = mybir.dt.bfloat16
AF = mybir.ActivationFunctionType
ALU = mybir.AluOpType
AX = mybir.AxisListType

ENGS = ("pe", "act", "dve", "pool", "sp")


class Buf:
    _n = 0

    def __init__(self, t, name):
        self.t = t
        self.name = name
        self.st = {}
        Buf._n += 1

    def __getitem__(self, idx):
        return self.t[idx]


class Prog:
    def __init__(self, nc, es):
        self.nc = nc
        self.es = es
        self.q = {e: [] for e in ENGS}
        self.cnt = {e: 0 for e in ENGS if e != "sp"}
        self.esem = {e: es.enter_context(nc.semaphore("s_" + e)) for e in ENGS if e != "sp"}
        self.NS = 12
        self.dsem = {qn: [es.enter_context(nc.semaphore("d_%s%d" % (qn, i))) for i in range(self.NS)]
                     for qn in ("sp", "pool")}
        self.dcnt = {"sp": 0, "pool": 0}
        self.dval = {qn: [0] * self.NS for qn in ("sp", "pool")}
        self.bar = es.enter_context(nc.semaphore("bar"))
        self.nbar = 0
        self.known = {e: {} for e in ENGS}
        self.semobj = {}
        for e in self.esem:
            self.semobj[("c", e)] = self.esem[e]
        for qn in self.dsem:
            for i, s in enumerate(self.dsem[qn]):
                self.semobj[("d", qn, i)] = s
        self.bufs = []
        self.n_ins = 0
        self.stage_es = None

    def sb(self, name, shape, dt=F32):
        self.n_sb = getattr(self, "n_sb", 0) + 1
        name = "%s_%d" % (name, self.n_sb)
        t = (self.stage_es or self.es).enter_context(self.nc.sbuf_tensor(name, list(shape), dt))
        b = Buf(t, name)
        self.bufs.append(b)
        return b

    def ps(self, name, shape, dt=F32):
        t = self.es.enter_context(self.nc.psum_tensor(name, list(shape), dt))
        b = Buf(t, name)
        self.bufs.append(b)
        return b

    def _need(self, eng, tok):
        sk, val = tok
        if self.known[eng].get(sk, 0) >= val:
            return None
        self.known[eng][sk] = val
        return (sk, val)

    def _deps(self, eng, reads, writes):
        waits = {}

        def add(tok):
            r = self._need(eng, tok)
            if r is not None:
                waits[r[0]] = max(waits.get(r[0], 0), r[1])

        def keys(b, k):
            if k is None:
                return list(b.st.keys())
            return [k, None]

        for (b, k) in reads:
            for kk in keys(b, k):
                st = b.st.get(kk)
                if st:
                    for tok in st[0]:
                        add(tok)
        for (b, k) in writes:
            for kk in keys(b, k):
                st = b.st.get(kk)
                if st:
                    for tok in st[0]:
                        add(tok)
                    for tok in st[1]:
                        add(tok)
        return list(waits.items())

    def _commit(self, tok, reads, writes):
        for (b, k) in writes:
            if k is None:
                b.st = {None: ([tok], [])}
            else:
                b.st[k] = ([tok], [])
        for (b, k) in reads:
            st = b.st.setdefault(k, ([], []))
            rl = st[1]
            rl[:] = [t for t in rl if t[0] != tok[0]]
            rl.append(tok)

    @staticmethod
    def _norm(lst):
        out = []
        for x in lst or []:
            if isinstance(x, Buf):
                out.append((x, None))
            else:
                out.append(x)
        return out

    def op(self, eng, fn, reads=None, writes=None):
        reads = self._norm(reads)
        writes = self._norm(writes)
        waits = self._deps(eng, reads, writes)
        self.cnt[eng] += 1
        tok = (("c", eng), self.cnt[eng])
        sem = self.esem[eng]
        semobj = self.semobj

        def emit(e, waits=waits, fn=fn, sem=sem):
            for sk, v in waits:
                e.wait_ge(semobj[sk], v)
            fn(e).then_inc(sem, 1)

        self.q[eng].append(emit)
        self._commit(tok, reads, writes)
        self.n_ins += 1
        return tok

    def dma(self, out_ap, in_ap, reads=None, writes=None, qn="sp"):
        reads = self._norm(reads)
        writes = self._norm(writes)
        waits = self._deps(qn, reads, writes)
        i = self.dcnt[qn] % self.NS
        self.dcnt[qn] += 1
        prev = self.dval[qn][i]
        sk = ("d", qn, i)
        r = self._need(qn, (sk, prev)) if prev > 0 else None
        if r is not None:
            waits.append(r)
        self.dval[qn][i] = prev + 16
        tok = (sk, prev + 16)
        sem = self.semobj[sk]
        semobj = self.semobj

        def emit(e, waits=waits, sem=sem, out_ap=out_ap, in_ap=in_ap):
            for k, v in waits:
                e.wait_ge(semobj[k], v)
            e.dma_start(out=out_ap, in_=in_ap).then_inc(sem, 16)

        self.q[qn].append(emit)
        self._commit(tok, reads, writes)
        self.n_ins += 1
        return tok

    def barrier(self):
        self.nbar += 1
        nb = self.nbar
        semobj = self.semobj
        bar = self.bar
        for e in ENGS:
            waits = []
            if e in self.cnt:
                if self.cnt[e] > 0:
                    waits.append((("c", e), self.cnt[e]))
            if e in self.dsem:
                for i in range(self.NS):
                    if self.dval[e][i] > 0:
                        waits.append((("d", e, i), self.dval[e][i]))

            def emit(en, waits=waits, nb=nb):
                for k, v in waits:
                    en.wait_ge(semobj[k], v)
                en.sem_inc(bar, 1)
                en.wait_ge(bar, len(ENGS) * nb)

            self.q[e].append(emit)
        allk = {}
        for e in self.cnt:
            allk[("c", e)] = self.cnt[e]
        for qn in self.dsem:
            for i in range(self.NS):
                allk[("d", qn, i)] = self.dval[qn][i]
        for e in ENGS:
            self.known[e] = dict(allk)
        for b in self.bufs:
            b.st = {}

    def emit_all(self):
        nc = self.nc
        q = self.q
        with nc.Block() as block:
            @block.sync
            def _(e):
                for f in q["sp"]:
                    f(e)

            @block.tensor
            def _(e):
                for f in q["pe"]:
                    f(e)

            @block.scalar
            def _(e):
                for f in q["act"]:
                    f(e)

            @block.vector
            def _(e):
                for f in q["dve"]:
                    f(e)

            @block.gpsimd
            def _(e):
                for f in q["pool"]:
                    f(e)

import numpy as np

D = 2048
NKD = 16
NCTX = 256
NLAT = 4096
NT = NCTX + NLAT
DFF = 5632
EVEN_IN = 4128
ODD_IN = 4544
EPS = 1e-6


def tiles_main():
    t = [(0, NCTX, True)]
    for i in range(NLAT // 512):
        t.append((NCTX + 512 * i, 512, False))
    return t


def tiles_ffn():
    t = [(0, NCTX, 0, 0)]
    nt = 9
    base, rem = divmod(NLAT, nt)
    o = NCTX
    for i in range(nt):
        n = base + (1 if i < rem else 0)
        t.append((o, n, 0 if i == 0 else 1, 0 if i == nt - 1 else 1))
        o += n
    return t


class K:
    def __init__(self, nc, es):
        self.nc = nc
        self.P = Prog(nc, es)
        P = self.P
        self.ins = {}
        self.psb = [P.ps("ps%d" % i, [128, 512]) for i in range(8)]
        self.wb = [P.sb("wb%d" % i, [128, 16, 256]) for i in range(2)]
        self.wbi = 0
        self.wbh = [P.sb("wbh%d" % i, [128, 16, 256], BF16) for i in range(3)]
        self.wbhi = 0
        self.ones = P.sb("ones", [128, 128])
        self.ident = P.sb("ident", [128, 128])
        self.consts_loaded = False

    def din(self, name, shape):
        t = self.nc.dram_tensor(name, list(shape), F32, kind="ExternalInput").ap()
        self.ins[name] = t
        return t

    def dscr(self, name, shape, out=False):
        return self.nc.dram_tensor(name, list(shape), F32,
                                   kind="ExternalOutput" if out else "Internal").ap()


def _cast_w(Kk, wb, nfull, bw):
    P = Kk.P
    wh = Kk.wbh[Kk.wbhi % len(Kk.wbh)]
    eng = "pool" if Kk.wbhi % 2 == 0 else "act"
    Kk.wbhi += 1
    if eng == "pool":
        P.op("pool", lambda e: e.tensor_copy(wh[:, 0:nfull, 0:bw], wb[:, 0:nfull, 0:bw]), reads=[wb], writes=[wh])
    else:
        P.op("act", lambda e: e.activation(wh[:, 0:nfull, 0:bw], wb[:, 0:nfull, 0:bw], AF.Identity), reads=[wb], writes=[wh])
    return wh


def gemm_fm(Kk, W, krows, chunks, rhs_fn, ntok, evac, ps_ids=(0, 1), rhs_reads=(), lowp=False):
    P = Kk.P
    nk = (krows + 127) // 128
    nfull = krows // 128
    blocks = []
    cur = []
    for i, (c0, cw) in enumerate(chunks):
        if cur and (cur[0][1] + sum(c[2] for c in cur) == c0) and (sum(c[2] for c in cur) + cw <= 256):
            cur.append((i, c0, cw))
        else:
            if cur:
                blocks.append(cur)
            cur = [(i, c0, cw)]
    if cur:
        blocks.append(cur)
    pi = 0
    for blk in blocks:
        b0 = blk[0][1]
        bw = sum(c[2] for c in blk)
        wb = Kk.wb[Kk.wbi % len(Kk.wb)]
        Kk.wbi += 1
        if nfull > 0:
            src = W[0:nfull * 128, b0:b0 + bw].rearrange("(kc p) c -> p kc c", p=128)
            P.dma(wb[:, 0:nfull, 0:bw], src, writes=[wb])
        if nfull < nk:
            kp = krows - nfull * 128
            P.dma(wb[0:kp, nfull, 0:bw], W[nfull * 128:krows, b0:b0 + bw], writes=[wb])
        if lowp:
            assert nfull == nk
            wb = _cast_w(Kk, wb, nfull, bw)
        for (i, c0, cw) in blk:
            ps = Kk.psb[ps_ids[pi % len(ps_ids)]]
            pi += 1
            off = c0 - b0
            for kc in range(nk):
                kp = min(128, krows - kc * 128)
                rhs = rhs_fn(kc, kp)
                P.op("pe", lambda e, ps=ps, wb=wb, kc=kc, kp=kp, off=off, cw=cw, rhs=rhs:
                     e.matmul(ps[0:cw, 0:ntok], wb[0:kp, kc, off:off + cw], rhs,
                              start=(kc == 0), stop=(kc == nk - 1)),
                     reads=[wb] + list(rhs_reads), writes=[ps])
            evac(i, ps, cw)


def gemm_tm(Kk, W, krows, c0, ncols, lhs_fn, ntok, evac, ps_ids=(0, 1), lhs_reads=(), lowp=False):
    P = Kk.P
    nk = (krows + 127) // 128
    nfull = krows // 128
    pi = 0
    for cb0 in range(0, ncols, 256):
        cbw = min(256, ncols - cb0)
        wb = Kk.wb[Kk.wbi % len(Kk.wb)]
        Kk.wbi += 1
        if nfull > 0:
            src = W[0:nfull * 128, c0 + cb0:c0 + cb0 + cbw].rearrange("(kc p) c -> p kc c", p=128)
            P.dma(wb[:, 0:nfull, 0:cbw], src, writes=[wb])
        if nfull < nk:
            kp = krows - nfull * 128
            P.dma(wb[0:kp, nfull, 0:cbw], W[nfull * 128:krows, c0 + cb0:c0 + cb0 + cbw], writes=[wb])
        if lowp:
            assert nfull == nk
            wb = _cast_w(Kk, wb, nfull, cbw)
        for tb in range((ntok + 127) // 128):
            t0 = tb * 128
            tn = min(128, ntok - t0)
            ps = Kk.psb[ps_ids[pi % len(ps_ids)]]
            pi += 1
            for kc in range(nk):
                kp = min(128, krows - kc * 128)
                lhs = lhs_fn(kc, kp, t0, tn)
                P.op("pe", lambda e, ps=ps, wb=wb, kc=kc, kp=kp, t0=t0, tn=tn, cbw=cbw, lhs=lhs:
                     e.matmul(ps[0:tn, 0:cbw], lhs, wb[0:kp, kc, 0:cbw],
                              start=(kc == 0), stop=(kc == nk - 1)),
                     reads=[wb] + list(lhs_reads), writes=[ps])
            evac(tb, ps, tn, cb0, cbw)


def colsum_bcast(Kk, src_fn, nchunks, ntok, ps, sq, scale_ones, src_reads, kp_fn=None):
    P = Kk.P
    for c in range(nchunks):
        kp = 128 if kp_fn is None else kp_fn(c)
        s = c % 4
        src = src_fn(c, kp)
        P.op("act", lambda e, c=c, s=s, kp=kp, src=src: e.activation(sq[0:kp, s, 0:ntok], src, AF.Square),
             reads=list(src_reads), writes=[(sq, s)])
        P.op("pe", lambda e, c=c, s=s, kp=kp: e.matmul(ps[:, 0:ntok], scale_ones[0:kp, :], sq[0:kp, s, 0:ntok],
                                                        start=(c == 0), stop=(c == nchunks - 1)),
             reads=[(sq, s), scale_ones], writes=[ps])


def rstd_from(Kk, ps, ntok, rstd, eps):
    P = Kk.P
    P.op("act", lambda e: e.activation(rstd[:, 0:ntok], ps[:, 0:ntok], AF.Sqrt, bias=Kk.epsb[:, 0:1] if eps == EPS else Kk.eps2b[:, 0:1], scale=1.0),
         reads=[ps, Kk.epsb], writes=[rstd])
    P.op("dve", lambda e: e.reciprocal(rstd[:, 0:ntok], rstd[:, 0:ntok]), reads=[rstd], writes=[rstd])


def load_consts(Kk):
    P = Kk.P
    P.dma(Kk.ones[:, :], Kk.ins["c_ones"][:, :], writes=[Kk.ones])
    P.dma(Kk.ident[:, :], Kk.ins["c_ident"][:, :], writes=[Kk.ident])
    Kk.onesD = P.sb("onesD", [128, 128])
    P.dma(Kk.onesD[:, :], Kk.ins["c_onesD"][:, :], writes=[Kk.onesD])
    Kk.epsb = P.sb("epsb", [128, 4])
    P.dma(Kk.epsb[:, :], Kk.ins["c_eps"][:, :], writes=[Kk.epsb])
    Kk.modT = P.sb("modT", [128, 96, 2])
    Kk.Amix = P.sb("Amix", [128, 16, 2])
    Kk.Affn = P.sb("Affn", [128, 16, 2])
    Kk.actT = P.sb("actT", [128, 16, 2])
    P.dma(Kk.actT[:, :, :], Kk.ins["cT"][:, :, :], writes=[Kk.actT])
    P.op("act", lambda e: e.activation(Kk.actT[:, :, :], Kk.actT[:, :, :], AF.Silu), reads=[Kk.actT], writes=[Kk.actT])


def stage_adaln(Kk, l):
    P = Kk.P
    with ExitStack() as ses:
        P.stage_es = ses
        adab = P.sb("adab", [128, 96])
        nrm = P.sb("nrm", [128, 2, 16, 2])
        P.dma(adab[:, :], Kk.ins["ada_b"][l], writes=[adab])
        P.dma(nrm[:, 0], Kk.ins["norm_mix"][l], writes=[nrm])
        P.dma(nrm[:, 1], Kk.ins["norm_ffn"][l], writes=[nrm])
        W = Kk.ins["ada_w"][l]
        chunks = [(i * 128, 128) for i in range(96)]

        def evac(i, ps, cw):
            P.op("act", lambda e, i=i, ps=ps: e.activation(Kk.modT[:, i, :], ps[:, 0:2], AF.Identity,
                                                            bias=adab[:, i:i + 1], scale=1.0),
                 reads=[ps, adab], writes=[(Kk.modT, i)])

        gemm_fm(Kk, W, D, chunks, lambda kc, kp: Kk.actT[0:kp, kc, :], 2, evac, rhs_reads=[Kk.actT])
        allm = [(Kk.modT, i) for i in range(96)]
        P.op("dve", lambda e: e.scalar_tensor_tensor(Kk.Amix[:, :, :], Kk.modT[:, 16:32, :], 1.0, nrm[:, 0], ALU.add, ALU.mult),
             reads=allm + [nrm], writes=[Kk.Amix])
        P.op("dve", lambda e: e.scalar_tensor_tensor(Kk.Affn[:, :, :], Kk.modT[:, 64:80, :], 1.0, nrm[:, 1], ALU.add, ALU.mult),
             reads=allm + [nrm], writes=[Kk.Affn])
        P.barrier()
        P.stage_es = None


def modulate(Kk, xt, ht, ntok, A, Bidx, m, sq, rstd, ps, tmpf=None):
    P = Kk.P
    colsum_bcast(Kk, lambda c, kp: xt[:, c, 0:ntok], 16, ntok, ps, sq, Kk.onesD, [xt])
    rstd_from(Kk, ps, ntok, rstd, EPS)
    allm = [(Kk.modT, i) for i in range(96)]
    for c in range(16):
        if tmpf is None:
            P.op("dve", lambda e, c=c: e.scalar_tensor_tensor(ht[:, c, 0:ntok], xt[:, c, 0:ntok], A[:, c, m:m + 1],
                                                               rstd[:, 0:ntok], ALU.mult, ALU.mult),
                 reads=[xt, A, rstd], writes=[(ht, c)])
            P.op("pool", lambda e, c=c: e.tensor_scalar(ht[:, c, 0:ntok], ht[:, c, 0:ntok], Kk.modT[:, Bidx + c, m:m + 1], None, ALU.add),
                 reads=[(ht, c), (Kk.modT, Bidx + c)], writes=[(ht, c)])
        else:
            s_ = c % 2
            P.op("dve", lambda e, c=c, s_=s_: e.scalar_tensor_tensor(tmpf[:, s_, 0:ntok], xt[:, c, 0:ntok], A[:, c, m:m + 1],
                                                                      rstd[:, 0:ntok], ALU.mult, ALU.mult),
                 reads=[xt, A, rstd], writes=[(tmpf, s_)])
            P.op("pool", lambda e, c=c, s_=s_: e.tensor_scalar(ht[:, c, 0:ntok], tmpf[:, s_, 0:ntok], Kk.modT[:, Bidx + c, m:m + 1], None, ALU.add),
                 reads=[(tmpf, s_), (Kk.modT, Bidx + c)], writes=[(ht, c)])


def stage_A(Kk, l, xT, W, plan, nsteps_extra=None):
    P = Kk.P
    with ExitStack() as ses:
        P.stage_es = ses
        xt = P.sb("xt", [128, 16, 512])
        ht = P.sb("ht", [128, 16, 512], BF16)
        htf = P.sb("htf", [128, 2, 512])
        sq = P.sb("sq", [128, 4, 512])
        rstd = P.sb("rstd", [128, 512])
        Kk.ev = P.sb("ev", [128, 4, 512])
        Kk.evi = 0
        Kk.stA = dict(xt=xt, ht=ht, sq=sq, rstd=rstd)
        if nsteps_extra:
            nsteps_extra("alloc")
        for (t0, n, isctx) in tiles_main():
            m = 1 if isctx else 0
            P.dma(xt[:, :, 0:n], xT[:, t0:t0 + n].rearrange("(c p) t -> p c t", p=128), writes=[xt])
            modulate(Kk, xt, ht, n, Kk.Amix, 0, m, sq, rstd, Kk.psb[7], tmpf=htf)
            hreads = [(ht, c) for c in range(16)]
            for g in plan:
                if g[0] == "fm":
                    gemm_fm(Kk, W, D, g[1], lambda kc, kp: ht[0:kp, kc, 0:n], n, g[2](t0, n, isctx), rhs_reads=hreads, lowp=True)
                elif g[0] == "tm":
                    gemm_tm(Kk, W, D, g[1], g[2], lambda kc, kp, a, tn: ht[0:kp, kc, a:a + tn], n, g[3](t0, n, isctx),
                            lhs_reads=hreads, lowp=True)
                else:
                    g[1](t0, n, isctx)
        P.barrier()
        P.stage_es = None


def ev_store(Kk, ps, rows, n, dst_ap, func=None, eng="act"):
    P = Kk.P
    s = Kk.evi % 4
    Kk.evi += 1
    ev = Kk.ev
    if eng == "act":
        P.op("act", lambda e: e.activation(ev[0:rows, s, 0:n], ps[0:rows, 0:n], func or AF.Identity),
             reads=[ps], writes=[(ev, s)])
    else:
        P.op("dve", lambda e: e.tensor_copy(ev[0:rows, s, 0:n], ps[0:rows, 0:n]), reads=[ps], writes=[(ev, s)])
    P.dma(dst_ap, ev[0:rows, s, 0:n], reads=[(ev, s)], qn="pool")


def stage_C1(Kk, l, xT, yT, Wout, do_ctx):
    P = Kk.P
    with ExitStack() as ses:
        P.stage_es = ses
        xt = P.sb("xt", [128, 16, 512])
        yt = P.sb("yt", [128, 16, 512])
        ytb = P.sb("ytb", [128, 16, 512], BF16)
        for (t0, n, isctx) in tiles_main():
            if isctx and not do_ctx:
                continue
            m = 1 if isctx else 0
            P.dma(xt[:, :, 0:n], xT[:, t0:t0 + n].rearrange("(c p) t -> p c t", p=128), writes=[(xt, c) for c in range(16)])
            P.dma(yt[:, :, 0:n], yT[:, t0:t0 + n].rearrange("(c p) t -> p c t", p=128), writes=[yt])
            for c4 in range(4):
                if c4 % 2 == 0:
                    P.op("pool", lambda e, c4=c4, n=n: e.tensor_copy(ytb[:, c4 * 4:(c4 + 1) * 4, 0:n], yt[:, c4 * 4:(c4 + 1) * 4, 0:n]),
                         reads=[yt], writes=[(ytb, c4)])
                else:
                    P.op("act", lambda e, c4=c4, n=n: e.activation(ytb[:, c4 * 4:(c4 + 1) * 4, 0:n], yt[:, c4 * 4:(c4 + 1) * 4, 0:n], AF.Identity),
                         reads=[yt], writes=[(ytb, c4)])

            def evac(i, ps, cw, n=n, m=m):
                P.op("dve", lambda e: e.scalar_tensor_tensor(xt[:, i, 0:n], ps[:, 0:n], Kk.modT[:, 32 + i, m:m + 1],
                                                              xt[:, i, 0:n], ALU.mult, ALU.add),
                     reads=[ps, (xt, i), (Kk.modT, 32 + i)], writes=[(xt, i)])

            gemm_fm(Kk, Wout, D, [(i * 128, 128) for i in range(16)], lambda kc, kp: ytb[0:kp, kc, 0:n], n, evac,
                    rhs_reads=[(ytb, c4) for c4 in range(4)], lowp=True)
            P.dma(xT[:, t0:t0 + n].rearrange("(c p) t -> p c t", p=128), xt[:, :, 0:n],
                  reads=[(xt, c) for c in range(16)], qn="pool")
        P.barrier()
        P.stage_es = None


def stage_C2(Kk, l, xTa, xTb, do_ctx, final_norm=None, outT=None):
    P = Kk.P
    Wup = Kk.ins["ffn_w_up"][l]
    Wdn = Kk.ins["ffn_w_down"][l]
    with ExitStack() as ses:
        P.stage_es = ses
        xt = P.sb("xt", [128, 16, 512])
        ht = P.sb("ht", [128, 16, 512], BF16)
        htf = P.sb("htf", [128, 2, 512])
        gt = P.sb("gt", [128, 11, 512], BF16)
        sq = P.sb("sq", [128, 4, 512])
        rstd = P.sb("rstd", [128, 512])
        acc = P.sb("acc", [128, 2, 2, 512])
        cw_ = P.sb("convw", [128, 3, 88])
        cb_ = P.sb("convb", [128, 88])
        P.dma(cw_[:, :, :], Kk.ins["ffn_conv_w"][l], writes=[cw_])
        P.dma(cb_[:, :], Kk.ins["ffn_conv_b"][l], writes=[cb_])
        if final_norm is not None:
            ot_ = P.sb("otf", [128, 16, 512])
            fnw = P.sb("fnw", [128, 16])
            P.dma(fnw[:, :], final_norm, writes=[fnw])
        for (o0, n, lh, rh) in tiles_ffn():
            isctx = o0 < NCTX
            if isctx and not do_ctx:
                continue
            m = 1 if isctx else 0
            nn = n + lh + rh
            a0 = o0 - lh
            xk = [(xt, c) for c in range(16)]
            P.dma(xt[:, :, 0:nn], xTa[:, a0:a0 + nn].rearrange("(c p) t -> p c t", p=128), writes=xk)
            modulate(Kk, xt, ht, nn, Kk.Affn, 48, m, sq, rstd, Kk.psb[7], tmpf=htf)
            hreads = [(ht, c) for c in range(16)]
            for grp in range(4):
                chunks = []
                for j in range(11):
                    chunks.append(((grp * 11 + j) * 128, 128))
                    chunks.append((DFF + (grp * 11 + j) * 128, 128))

                def evac(i, ps, cw, grp=grp, n=n, lh=lh, rh=rh):
                    j = i // 2
                    isval = i % 2
                    fc = (grp * 11 + j) + (44 if isval else 0)
                    s = j % 2
                    a = acc[:, isval, s, :]
                    key = (acc, (isval, s))
                    P.op("act", lambda e: e.activation(a[:, 0:n], ps[:, lh:lh + n], AF.Identity,
                                                        bias=cb_[:, fc:fc + 1], scale=cw_[:, 1, fc:fc + 1]),
                         reads=[ps, cw_, cb_], writes=[key])
                    lo = 0 if lh else 1
                    P.op("dve", lambda e: e.scalar_tensor_tensor(a[:, lo:n], ps[:, lh - 1 + lo:lh - 1 + n], cw_[:, 0, fc:fc + 1],
                                                                  a[:, lo:n], ALU.mult, ALU.add),
                         reads=[ps, cw_, key], writes=[key])
                    hi = 0 if rh else 1
                    P.op("dve", lambda e: e.scalar_tensor_tensor(a[:, 0:n - hi], ps[:, lh + 1:lh + 1 + n - hi], cw_[:, 2, fc:fc + 1],
                                                                  a[:, 0:n - hi], ALU.mult, ALU.add),
                         reads=[ps, cw_, key], writes=[key])
                    if isval:
                        kg = (acc, (0, s))
                        P.op("act", lambda e: e.activation(acc[:, 0, s, 0:n], acc[:, 0, s, 0:n], AF.Silu),
                             reads=[kg], writes=[kg])
                        P.op("pool", lambda e: e.tensor_tensor(gt[:, j, 0:n], acc[:, 0, s, 0:n], acc[:, 1, s, 0:n], ALU.mult),
                             reads=[kg, key], writes=[(gt, j)])

                gemm_fm(Kk, Wup, D, chunks, lambda kc, kp: ht[0:kp, kc, 0:nn], nn, evac, ps_ids=(0, 1, 2, 3), rhs_reads=hreads, lowp=True)
                Wd = Wdn[grp * 11 * 128:(grp + 1) * 11 * 128, :]

                def evac2(i, ps, cw, n=n, lh=lh, m=m):
                    P.op("dve", lambda e: e.scalar_tensor_tensor(xt[:, i, lh:lh + n], ps[:, 0:n], Kk.modT[:, 80 + i, m:m + 1],
                                                                  xt[:, i, lh:lh + n], ALU.mult, ALU.add),
                         reads=[ps, (xt, i), (Kk.modT, 80 + i)], writes=[(xt, i)])

                gemm_fm(Kk, Wd, 11 * 128, [(i * 128, 128) for i in range(16)], lambda kc, kp: gt[0:kp, kc, 0:n], n, evac2,
                        ps_ids=(4, 5), rhs_reads=[(gt, j) for j in range(11)], lowp=True)
            if final_norm is None:
                P.dma(xTb[:, o0:o0 + n].rearrange("(c p) t -> p c t", p=128), xt[:, :, lh:lh + n], reads=xk, qn="pool")
            else:
                if not isctx:
                    colsum_bcast(Kk, lambda c, kp: xt[:, c, lh:lh + n], 16, n, Kk.psb[7], sq, Kk.onesD, xk)
                    rstd_from(Kk, Kk.psb[7], n, rstd, EPS)
                    for c in range(16):
                        P.op("dve", lambda e, c=c, n=n, lh=lh: e.scalar_tensor_tensor(ot_[:, c, 0:n], xt[:, c, lh:lh + n], fnw[:, c:c + 1],
                                                                           rstd[:, 0:n], ALU.mult, ALU.mult),
                             reads=[(xt, c), fnw, rstd], writes=[(ot_, c)])
                    P.dma(outT[:, o0 - NCTX:o0 - NCTX + n].rearrange("(c p) t -> p c t", p=128), ot_[:, :, 0:n],
                          reads=[(ot_, c) for c in range(16)], qn="pool")
        P.barrier()
        P.stage_es = None


A_SCALE = 128 ** -0.5


def rope_apply(Kk, src, rows, n, cosb, sinb, RT, ps, dst, tmp):
    P = Kk.P
    sbuf, skey, sap = src
    dbuf, dkey, dap = dst
    tbuf, tkey, tap = tmp
    P.op("pe", lambda e: e.matmul(ps[0:rows, 0:n], RT[0:rows, 0:rows], sap, start=True, stop=True),
         reads=[(sbuf, skey), RT], writes=[ps])
    P.op("dve", lambda e: e.tensor_tensor(tap, ps[0:rows, 0:n], sinb[0:rows, 0:n], ALU.mult),
         reads=[ps, sinb], writes=[(tbuf, tkey)])
    P.op("pool", lambda e: e.tensor_tensor(dap, sap, cosb[0:rows, 0:n], ALU.mult),
         reads=[(sbuf, skey), cosb], writes=[(dbuf, dkey)])
    P.op("pool", lambda e: e.tensor_tensor(dap, dap, tap, ALU.add),
         reads=[(dbuf, dkey), (tbuf, tkey)], writes=[(dbuf, dkey)])


def stage_A_even(Kk, l, xT, S):
    P = Kk.P
    e_ = l // 2
    W = Kk.ins["ev_w_in"][e_]
    st = {}

    def extra(_):
        st["qs"] = P.sb("qs", [128, 2, 512])
        st["qn"] = P.sb("qn", [128, 2, 512])
        st["qo"] = P.sb("qo", [128, 2, 512])
        st["tmp"] = P.sb("tmpr", [128, 2, 512])
        st["cos"] = P.sb("cosb", [128, 512])
        st["sin"] = P.sb("sinb", [128, 512])
        st["gain"] = P.sb("qkgain", [128, 2])
        st["RT"] = P.sb("RT", [128, 128])
        st["onesH"] = P.sb("onesH", [128, 128])
        st["rs2"] = P.sb("rs2", [128, 2, 512])
        st["dtb"] = P.sb("dtb", [32, 2])
        P.dma(st["gain"][:, :], Kk.ins["qk_gain"][e_], writes=[st["gain"]])
        P.dma(st["RT"][:, :], Kk.ins["c_RT128"][:, :], writes=[st["RT"]])
        P.dma(st["onesH"][:, :], Kk.ins["c_onesH"][:, :], writes=[st["onesH"]])
        P.dma(st["dtb"][:, :], Kk.ins["dt_ba"][e_], writes=[st["dtb"]])
        st["i"] = 0

    def tile_pre(t0, n, isctx):
        if not isctx:
            P.dma(st["cos"][:, 0:n], Kk.ins["c_cos128"][:, t0 - NCTX:t0 - NCTX + n], writes=[st["cos"]])
            P.dma(st["sin"][:, 0:n], Kk.ins["c_sin128"][:, t0 - NCTX:t0 - NCTX + n], writes=[st["sin"]])

    def mk_qk(t0, n, isctx):
        def evac(i, ps, cw):
            s = st["i"] % 2
            st["i"] += 1
            qs, qn, qo, tmp, rs2 = st["qs"], st["qn"], st["qo"], st["tmp"], st["rs2"]
            g = 0 if i < 8 else 1
            ps2 = Kk.psb[4 + s]
            P.op("act", lambda e: e.activation(qs[:, s, 0:n], ps[:, 0:n], AF.Identity), reads=[ps], writes=[(qs, s)])
            P.op("act", lambda e: e.activation(qn[:, s, 0:n], ps[:, 0:n], AF.Square), reads=[ps], writes=[(qn, s)])
            P.op("pe", lambda e: e.matmul(ps2[:, 0:n], st["onesH"][:, :], qn[:, s, 0:n], start=True, stop=True),
                 reads=[(qn, s), st["onesH"]], writes=[ps2])
            P.op("act", lambda e: e.activation(rs2[:, s, 0:n], ps2[:, 0:n], AF.Sqrt, bias=Kk.epsb[:, 0:1], scale=1.0),
                 reads=[ps2, Kk.epsb], writes=[(rs2, s)])
            P.op("dve", lambda e: e.reciprocal(rs2[:, s, 0:n], rs2[:, s, 0:n]), reads=[(rs2, s)], writes=[(rs2, s)])
            P.op("dve", lambda e: e.scalar_tensor_tensor(qn[:, s, 0:n], qs[:, s, 0:n], st["gain"][:, g:g + 1], rs2[:, s, 0:n],
                                                          ALU.mult, ALU.mult),
                 reads=[(qs, s), st["gain"], (rs2, s)], writes=[(qn, s)])
            dst = S["qT"][i] if i < 8 else S["kT"][i - 8]
            if isctx:
                P.dma(dst[:, t0:t0 + n], qn[:, s, 0:n], reads=[(qn, s)], qn="pool")
            else:
                rope_apply(Kk, (qn, s, qn[:, s, 0:n]), 128, n, st["cos"], st["sin"], st["RT"], ps2,
                           (qo, s, qo[:, s, 0:n]), (tmp, s, tmp[:, s, 0:n]))
                P.dma(dst[:, t0:t0 + n], qo[:, s, 0:n], reads=[(qo, s)], qn="pool")
        return evac

    def mk_v(t0, n, isctx):
        def evac(tb, ps, tn, cb0, cbw):
            ev_store(Kk, ps, tn, cbw, S["vM"][t0 + tb * 128:t0 + tb * 128 + tn, cb0:cb0 + cbw])
        return evac

    def mk_z(t0, n, isctx):
        def evac(i, ps, cw):
            ev_store(Kk, ps, cw, n, S["zT"][i * 128:i * 128 + cw, t0:t0 + n], func=AF.Silu)
        return evac

    def mk_xbc(t0, n, isctx):
        def evac(i, ps, cw):
            ev_store(Kk, ps, cw, n, S["xbcT"][i * 128:i * 128 + cw, t0:t0 + n])
        return evac

    def mk_dt(t0, n, isctx):
        def evac(i, ps, cw):
            s = Kk.evi % 4
            Kk.evi += 1
            ev = Kk.ev
            P.op("act", lambda e: e.activation(ev[0:32, s, 0:n], ps[0:32, 0:n], AF.Exp, bias=st["dtb"][:, 0:1], scale=1.0),
                 reads=[ps, st["dtb"]], writes=[(ev, s)])
            P.op("act", lambda e: e.activation(ev[0:32, s, 0:n], ev[0:32, s, 0:n], AF.Ln, bias=Kk.epsb[0:32, 3:4], scale=1.0),
                 reads=[(ev, s), Kk.epsb], writes=[(ev, s)])
            P.dma(S["dtT"][0:32, t0:t0 + n], ev[0:32, s, 0:n], reads=[(ev, s)], qn="pool")
        return evac

    qk_chunks = [(i * 128, 128) for i in range(10)]
    z_chunks = [(1536 + i * 128, 128) for i in range(8)]
    xbc_chunks = [(2560 + i * 128, 128) for i in range(12)]
    dt_chunks = [(4096, 32)]
    plan = [("post", tile_pre), ("fm", qk_chunks, mk_qk), ("tm", 1280, 256, mk_v),
            ("fm", z_chunks, mk_z), ("fm", xbc_chunks, mk_xbc), ("fm", dt_chunks, mk_dt)]
    stage_A(Kk, l, xT, W, plan, nsteps_extra=extra)


def attention(Kk, nheads, kv_of, load_k, load_v, load_q, kparts, scale, yT, row0, do_ctx):
    P = Kk.P
    pt = Kk.att["pt"]
    ot = Kk.att["ot"]
    rd = Kk.att["rd"]
    cur_kv = None
    it = 0
    pti = 0
    for h in range(nheads):
        kvh = kv_of(h)
        if kvh != cur_kv:
            kbufs = load_k(kvh)
            vbuf = load_v(kvh)
            cur_kv = kvh
        for (t0, n, isctx) in tiles_main():
            if isctx and not do_ctx:
                continue
            pti = _attn_tile(Kk, h, t0, n, isctx, it, pti, kbufs, vbuf, load_q, kparts, scale, yT, row0)
            it += 1


def _attn_tile(Kk, h, t0, n, isctx, it, pti, kbufs, vbuf, load_q, kparts, scale, yT, row0):
    P = Kk.P
    pt = Kk.att["pt"]
    ot = Kk.att["ot"]
    rd = Kk.att["rd"]
    qaps, qreads = load_q(h, t0, n, it % 2)
    ps_o = Kk.psb[2 + it % 2]
    ps_d = Kk.psb[4 + it % 2]
    jt = list(range(2)) if isctx else list(range(NT // 128))
    for ji, j in enumerate(jt):
        ps_s = Kk.psb[pti % 2]
        sl = pti % 3
        pti += 1
        for pi_, rows in enumerate(kparts):
            kb = kbufs[pi_]
            P.op("pe", lambda e, ps_s=ps_s, kb=kb, rows=rows, j=j, qa=qaps[pi_], pi_=pi_:
                 e.matmul(ps_s[:, 0:n], kb[0:rows, j * 128:(j + 1) * 128], qa,
                          start=(pi_ == 0), stop=(pi_ == len(kparts) - 1)),
                 reads=[kb] + qreads, writes=[ps_s])
        P.op("act", lambda e, ps_s=ps_s, sl=sl: e.activation(pt[:, sl, 0:n], ps_s[:, 0:n], AF.Exp, scale=scale),
             reads=[ps_s], writes=[(pt, sl)])
        P.op("pe", lambda e, sl=sl, j=j, ji=ji: e.matmul(ps_o[:, 0:n], vbuf[:, j, :], pt[:, sl, 0:n],
                                                         start=(ji == 0), stop=(ji == len(jt) - 1)),
             reads=[vbuf, (pt, sl)], writes=[ps_o])
        P.op("pe", lambda e, sl=sl, ji=ji: e.matmul(ps_d[:, 0:n], Kk.ones[:, :], pt[:, sl, 0:n],
                                                    start=(ji == 0), stop=(ji == len(jt) - 1)),
             reads=[Kk.ones, (pt, sl)], writes=[ps_d])
    s = it % 2
    P.op("dve", lambda e: e.reciprocal(rd[:, s, 0:n], ps_d[:, 0:n]), reads=[ps_d], writes=[(rd, s)])
    P.op("dve", lambda e: e.tensor_tensor(ot[:, s, 0:n], ps_o[:, 0:n], rd[:, s, 0:n], ALU.mult),
         reads=[ps_o, (rd, s)], writes=[(ot, s)])
    P.dma(yT[row0 + h * 128:row0 + (h + 1) * 128, t0:t0 + n], ot[:, s, 0:n], reads=[(ot, s)], qn="pool")
    return pti


def stage_attn_even(Kk, S, yT, do_ctx):
    P = Kk.P
    with ExitStack() as ses:
        P.stage_es = ses
        kt = P.sb("kt", [128, NT])
        vt = P.sb("vt", [128, NT // 128, 128])
        qt = P.sb("qt", [128, 2, 512])
        Kk.att = dict(pt=P.sb("pt", [128, 3, 512]), ot=P.sb("ot", [128, 2, 512]), rd=P.sb("rd", [128, 2, 512]))

        def load_k(kvh):
            P.dma(kt[:, :], S["kT"][kvh], writes=[kt])
            return [kt]

        def load_v(kvh):
            for j0 in range(0, NT // 128, 4):
                j1 = min(NT // 128, j0 + 4)
                P.dma(vt[:, j0:j1, :], S["vM"][j0 * 128:j1 * 128, kvh * 128:(kvh + 1) * 128].rearrange("(j p) d -> p j d", p=128),
                      writes=[vt])
            return vt

        def load_q(h, t0, n, slot):
            P.dma(qt[:, slot, 0:n], S["qT"][h][:, t0:t0 + n], writes=[(qt, slot)])
            return [qt[:, slot, 0:n]], [(qt, slot)]

        attention(Kk, 8, lambda h: h // 4, load_k, load_v, load_q, [128], A_SCALE, yT, 0, do_ctx)
        P.barrier()
        P.stage_es = None


def stage_ssd_conv(Kk, l, S):
    P = Kk.P
    e_ = l // 2
    with ExitStack() as ses:
        P.stage_es = ses
        cw = P.sb("scw", [128, 12, 5])
        cb = P.sb("scb", [128, 12])
        ub = P.sb("ub", [128, 3, 516])
        ac = P.sb("sac", [128, 3, 512])
        P.dma(cw[:, :, :], Kk.ins["ssm_conv_w"][e_], writes=[cw])
        P.dma(cb[:, :], Kk.ins["ssm_conv_b"][e_], writes=[cb])
        it = 0
        for (t0, n, isctx) in tiles_main():
            seg0, seg1 = (0, NCTX) if isctx else (NCTX, NT)
            lh = min(2, t0 - seg0)
            rh = min(2, seg1 - (t0 + n))
            for c in range(12):
                s = it % 3
                it += 1
                _ssd_conv_tile(Kk, S, cw, cb, ub, ac, t0, n, lh, rh, c, s)
        P.barrier()
        P.stage_es = None


def _ssd_conv_tile(Kk, S, cw, cb, ub, ac, t0, n, lh, rh, c, s):
    P = Kk.P
    k = (ub, s)
    if lh < 2 or rh < 2:
        P.op("pool", lambda e: e.memset(ub[:, s, :], 0.0), writes=[k])
    P.dma(ub[:, s, 2 - lh:2 + n + rh], S["xbcT"][c * 128:(c + 1) * 128, t0 - lh:t0 + n + rh], writes=[k])
    ka = (ac, s)
    P.op("act", lambda e: e.activation(ac[:, s, 0:n], ub[:, s, 2:2 + n], AF.Identity, bias=cb[:, c:c + 1], scale=cw[:, c, 2:3]),
         reads=[k, cw, cb], writes=[ka])
    for kk in (0, 1, 3, 4):
        P.op("dve", lambda e, kk=kk: e.scalar_tensor_tensor(ac[:, s, 0:n], ub[:, s, kk:kk + n], cw[:, c, kk:kk + 1], ac[:, s, 0:n],
                                                             ALU.mult, ALU.add),
             reads=[k, cw, ka], writes=[ka])
    P.op("act", lambda e: e.activation(ac[:, s, 0:n], ac[:, s, 0:n], AF.Silu), reads=[ka], writes=[ka])
    P.dma(S["xcT"][c * 128:(c + 1) * 128, t0:t0 + n], ac[:, s, 0:n], reads=[ka], qn="pool")


def stage_ssd(Kk, l, S, yT, do_ctx):
    for d in range(2):
        _ssd_dir(Kk, l, S, yT, do_ctx, d)


def _ssd_dir(Kk, l, S, yT, do_ctx, d):
    P = Kk.P
    e_ = l // 2
    NCH = NT // 128
    if True:
        with ExitStack() as ses:
            P.stage_es = ses
            R = {}
            R["tri"] = P.sb("tri", [128, 4, 128])
            P.dma(R["tri"][:, :, :], Kk.ins["c_tri"][:, :, :], writes=[R["tri"]])
            R["alog"] = P.sb("alog", [32, 2])
            P.dma(R["alog"][:, :], Kk.ins["dt_ba"][e_], writes=[R["alog"]])
            P.op("act", lambda e: e.activation(R["alog"][:, 1:2], R["alog"][:, 1:2], AF.Exp), reads=[R["alog"]], writes=[R["alog"]])
            P.op("dve", lambda e: e.tensor_scalar(R["alog"][:, 1:2], R["alog"][:, 1:2], -1.0, None, ALU.mult),
                 reads=[R["alog"]], writes=[R["alog"]])
            R["fm"] = P.sb("sfm", [128, 2, 12, 128])
            R["dtf"] = P.sb("dtf", [128, 2, 2, 128])
            P.op("pool", lambda e: e.memset(R["dtf"][:, :, :, :], 0.0), writes=[R["dtf"]])
            R["xs"] = P.sb("xstm", [128, 2, 1024])
            R["btm"] = P.sb("btm", [128, 2, 256])
            R["dtm"] = P.sb("dtm", [128, 2, 64])
            R["bc"] = P.sb("bc", [128, 16, 128])
            R["xdtp"] = P.sb("xdtp", [128, 16, 128])
            R["xdtw"] = P.sb("xdtw", [128, 1024])
            R["H"] = P.sb("Hs", [128, 16, 128])
            R["gm"] = P.sb("gm", [128, 2, 2, 128])
            R["acs"] = P.sb("acs", [128, 2, 48])
            R["rot"] = P.sb("rot", [128, 5, 4, 128])
            R["yacc"] = P.sb("yacc", [128, 2, 8, 128])
            R["fin"] = P.sb("fin", [128, 2, 8, 128])
            R["zt"] = P.sb("zt", [128, 2, 8, 128])
            R["sq"] = P.sb("ssq", [128, 4, 128])
            R["rs"] = P.sb("srs", [128, 2, 128])
            R["dsk"] = P.sb("dsk", [128, 8])
            R["gn"] = P.sb("gn", [128, 8])
            R["ones512"] = P.sb("ones512", [128, 128])
            P.dma(R["dsk"][:, :], Kk.ins["ssm_d"][e_], writes=[R["dsk"]])
            P.dma(R["gn"][:, :], Kk.ins["ssm_norm"][e_], writes=[R["gn"]])
            P.dma(R["ones512"][:, :], Kk.ins["c_ones512"][:, :], writes=[R["ones512"]])
            P.op("pool", lambda e: e.memset(R["H"][:, :, :], 0.0), writes=[R["H"]])
            P.op("pool", lambda e: e.memset(R["xdtp"][:, :, :], 0.0), writes=[R["xdtp"]])
            order = [0, 1] + list(range(2, NCH)) if d == 0 else [1, 0] + list(range(NCH - 1, 1, -1))
            R["hi"] = 0
            import os
            nlim = int(os.environ.get("SSD_NCH", "999"))
            for ci, ch in enumerate(order[:nlim]):
                _ssd_chunk(Kk, S, yT, R, d, ch, ci, do_ctx)
            P.barrier()
            P.stage_es = None


def _ssd_chunk(Kk, S, yT, R, d, ch, ci, do_ctx):
    P = Kk.P
    s = ci % 2
    t0 = ch * 128
    tri = R["tri"]
    TD = tri[:, 0 if d == 0 else 2, :]
    TDx = tri[:, 1 if d == 0 else 3, :]
    last = 127 if d == 0 else 0
    fm, dtf, xs, btm, dtm = R["fm"], R["dtf"], R["xs"], R["btm"], R["dtm"]
    import os
    LV = int(os.environ.get("SSD_LEVEL", "99"))
    if LV <= 0:
        return
    kfm = (fm, s)
    P.dma(fm[:, s, :, :], S["xcT"][:, t0:t0 + 128].rearrange("(c p) t -> p c t", p=128), writes=[kfm])
    kdf = (dtf, s)
    P.dma(dtf[0:32, s, 0, :], S["dtT"][:, t0:t0 + 128], writes=[kdf])
    P.op("dve", lambda e: e.tensor_scalar(dtf[0:32, s, 1, :], dtf[0:32, s, 0, :], R["alog"][:, 1:2], None, ALU.mult),
         reads=[kdf, R["alog"]], writes=[kdf])
    if LV <= 1:
        return
    pst = Kk.psb[4]
    for g4 in range(2):
        for j in range(4):
            c = g4 * 4 + j
            P.op("pe", lambda e, c=c, j=j: e.matmul(pst[:, j * 128:(j + 1) * 128], fm[:, s, c, :], Kk.ident[:, :], start=True, stop=True),
                 reads=[kfm, Kk.ident], writes=[pst])
        if not os.environ.get("SKIP_EV"):
            P.op("dve", lambda e, g4=g4: e.tensor_copy(xs[:, s, g4 * 512:(g4 + 1) * 512], pst[:, :]),
                 reads=[pst], writes=[(xs, (s, c)) for c in range(g4 * 4, g4 * 4 + 4)])
    for j in range(2):
        P.op("pe", lambda e, j=j: e.matmul(pst[:, j * 128:(j + 1) * 128], fm[:, s, 8 + j, :], Kk.ident[:, :], start=True, stop=True),
             reads=[kfm, Kk.ident], writes=[pst])
    for q in range(0 if os.environ.get("SKIP_DT") else 2):
        P.op("pe", lambda e, q=q: e.matmul(pst[:, 256 + q * 32:256 + (q + 1) * 32], dtf[:, s, q, :], Kk.ident[:, 0:32], start=True, stop=True),
             reads=[kdf, Kk.ident], writes=[pst])
    if not os.environ.get("SKIP_EV"):
        P.op("dve", lambda e: e.tensor_copy(btm[:, s, :], pst[:, 0:256]), reads=[pst], writes=[(btm, s)])
    if not os.environ.get("SKIP_DT") and not os.environ.get("SKIP_DTEV"):
        P.op("dve", lambda e: e.tensor_copy(dtm[:, s, :], pst[:, 256:320]), reads=[pst], writes=[(dtm, s)])
    LV = int(os.environ.get("SSD_LEVEL", "99"))
    if LV <= 2:
        return
    dt_d = dtm[:, s, d * 16:(d + 1) * 16]
    dta_d = dtm[:, s, 32 + d * 16:32 + (d + 1) * 16]
    psG = Kk.psb[5]
    gm = R["gm"]
    acs = R["acs"]
    ka = (acs, s)
    for g in range(2):
        P.op("pe", lambda e, g=g: e.matmul(psG[:, g * 128:(g + 1) * 128], fm[:, s, 8 + g, :], fm[:, s, 10 + g, :],
                                           start=True, stop=True),
             reads=[kfm], writes=[psG])
    P.op("pe", lambda e: e.matmul(psG[:, 256:272], TD, dta_d, start=True, stop=True), reads=[tri, (dtm, s)], writes=[psG])
    P.op("pe", lambda e: e.matmul(psG[:, 272:288], TDx, dta_d, start=True, stop=True), reads=[tri, (dtm, s)], writes=[psG])
    for g in range(2):
        P.op("dve", lambda e, g=g: e.tensor_tensor(gm[:, s, g, :], psG[:, g * 128:(g + 1) * 128], TD, ALU.mult),
             reads=[psG, tri], writes=[(gm, (s, g))])
    P.op("dve", lambda e: e.tensor_copy(acs[:, s, 0:32], psG[:, 256:288]), reads=[psG], writes=[ka])
    P.op("act", lambda e: e.activation(acs[:, s, 16:32], acs[:, s, 16:32], AF.Exp), reads=[ka], writes=[ka])
    P.op("dve", lambda e: e.tensor_tensor(acs[:, s, 32:48], acs[:, s, 16:32], dt_d, ALU.mult), reads=[ka, (dtm, s)], writes=[ka])
    if LV <= 4:
        return
    bc, xdtp, xdtw = R["bc"], R["xdtp"], R["xdtw"]
    P.op("pool", lambda e: e.tensor_copy(bc[:, :, :], dta_d.rearrange("p (h o) -> p h o", o=1).broadcast_to([128, 16, 128])),
         reads=[(dtm, s)], writes=[bc])
    xs3 = xs[:, s, :].rearrange("p (h q) -> p h q", q=64)
    for par in range(2):
        P.op("dve", lambda e, par=par: e.tensor_tensor(
            xdtp[:, par::2, par * 64:(par + 1) * 64], xs3[:, par::2, :],
            dtm[:, s, d * 16 + par:(d + 1) * 16:2].rearrange("p (h o) -> p h o", o=1).broadcast_to([128, 8, 64]), ALU.mult),
            reads=[(xs, (s, c)) for c in range(8)] + [(dtm, s)], writes=[xdtp])
    P.op("pool", lambda e: e.tensor_tensor(xdtw[:, :].rearrange("p (h q) -> p h q", q=64), xs3,
                                           acs[:, s, 32:48].rearrange("p (h o) -> p h o", o=1).broadcast_to([128, 16, 64]), ALU.mult),
         reads=[(xs, (s, c)) for c in range(8)] + [ka], writes=[xdtw])
    if LV <= 5:
        return
    pss = [Kk.psb[6], Kk.psb[7]]
    for g in range(2):
        P.op("pe", lambda e, g=g: e.matmul(pss[g][:, :], btm[:, s, g * 128:(g + 1) * 128], xdtw[:, g * 512:(g + 1) * 512],
                                           start=True, stop=True),
             reads=[(btm, s), xdtw], writes=[pss[g]])
    if LV <= 6:
        return
    rot = R["rot"]
    H = R["H"]
    yacc = R["yacc"]
    for h in range(16):
        g = h // 8
        pair = h // 2
        par = h % 2
        hi = R["hi"]
        R["hi"] += 1
        r = hi % 4
        psa = Kk.psb[hi % 2]
        psy = Kk.psb[2 + (hi // 2) % 2]
        kr = lambda i, r=r: (rot, (i, r))
        P.op("pe", lambda e, h=h, psa=psa: e.matmul(psa[:, 0:128], bc[:, h, :], TD, start=True, stop=True),
             reads=[bc, tri], writes=[psa])
        P.op("dve", lambda e, h=h, psa=psa, r=r: e.tensor_scalar(rot[:, 0, r, :], psa[:, 0:128], acs[:, s, h:h + 1], 0.0,
                                                                 ALU.subtract, ALU.min),
             reads=[psa, ka], writes=[kr(0)])
        P.op("act", lambda e, r=r: e.activation(rot[:, 1, r, :], rot[:, 0, r, :], AF.Exp), reads=[kr(0)], writes=[kr(1)])
        P.op("pool", lambda e, r=r, g=g: e.tensor_tensor(rot[:, 2, r, :], rot[:, 1, r, :], gm[:, s, g, :], ALU.mult),
             reads=[kr(1), (gm, (s, g))], writes=[kr(2)])
        P.op("act", lambda e, psa=psa, r=r: e.activation(rot[:, 3, r, :], psa[:, 0:128], AF.Exp), reads=[psa], writes=[kr(3)])
        P.op("dve", lambda e, r=r, g=g: e.tensor_tensor(rot[:, 4, r, :], rot[:, 3, r, :], fm[:, s, 10 + g, :], ALU.mult),
             reads=[kr(3), kfm], writes=[kr(4)])
        P.op("pe", lambda e, h=h, psy=psy, r=r, par=par: e.matmul(psy[:, 0:128], xdtp[:, h, :], rot[:, 2, r, :],
                                                                  start=(par == 0), stop=False),
             reads=[xdtp, kr(2)], writes=[psy])
        P.op("pe", lambda e, h=h, psy=psy, r=r, par=par: e.matmul(psy[:, 0:128], H[:, h, :], rot[:, 4, r, :],
                                                                  start=False, stop=(par == 1)),
             reads=[(H, h), kr(4)], writes=[psy])
        P.op("dve", lambda e, h=h, r=r, g=g, par=par: e.scalar_tensor_tensor(
            H[:, h, par * 64:(par + 1) * 64], H[:, h, par * 64:(par + 1) * 64], rot[:, 3, r, last:last + 1],
            pss[g][:, (h % 8) * 64:(h % 8 + 1) * 64], ALU.mult, ALU.add),
            reads=[(H, h), kr(3), pss[g]], writes=[(H, h)])
        if par == 1:
            ky = (yacc, (s, pair))
            P.op("act", lambda e, psy=psy, pair=pair: e.activation(yacc[:, s, pair, :], psy[:, 0:128], AF.Identity),
                 reads=[psy], writes=[ky])
    if LV <= 7:
        return
    isctx = ch < 2
    allk = [(yacc, (s, p)) for p in range(8)]
    if d == 0:
        P.dma(S["ysT"][:, t0:t0 + 128].rearrange("(c p) t -> p c t", p=128), yacc[:, s, :, :], reads=allk, qn="pool")
        return
    if isctx and not do_ctx:
        return
    fin, zt, sq, rs = R["fin"], R["zt"], R["sq"], R["rs"]
    kf = (fin, s)
    P.dma(fin[:, s, :, :], S["ysT"][:, t0:t0 + 128].rearrange("(c p) t -> p c t", p=128), writes=[kf])
    kz = (zt, s)
    P.dma(zt[:, s, :, :], S["zT"][:, t0:t0 + 128].rearrange("(c p) t -> p c t", p=128), writes=[kz])
    P.op("dve", lambda e: e.tensor_tensor(fin[:, s, :, :], fin[:, s, :, :], yacc[:, s, :, :], ALU.add), reads=[kf] + allk, writes=[kf])
    for c in range(8):
        P.op("dve", lambda e, c=c: e.scalar_tensor_tensor(fin[:, s, c, :], fm[:, s, c, :], R["dsk"][:, c:c + 1], fin[:, s, c, :],
                                                           ALU.mult, ALU.add),
             reads=[kf, kfm, R["dsk"]], writes=[kf])
    P.op("pool", lambda e: e.tensor_tensor(fin[:, s, :, :], fin[:, s, :, :], zt[:, s, :, :], ALU.mult), reads=[kf, kz], writes=[kf])
    psn = Kk.psb[4]
    for g in range(2):
        for c4 in range(4):
            c = g * 4 + c4
            q = c % 4
            P.op("act", lambda e, c=c, q=q: e.activation(sq[:, q, :], fin[:, s, c, :], AF.Square), reads=[kf], writes=[(sq, q)])
            P.op("pe", lambda e, c4=c4, q=q, g=g: e.matmul(psn[:, g * 128:(g + 1) * 128], R["ones512"][:, :], sq[:, q, :],
                                                           start=(c4 == 0), stop=(c4 == 3)),
                 reads=[(sq, q), R["ones512"]], writes=[psn])
        P.op("dve", lambda e, g=g: e.tensor_copy(rs[:, g, :], psn[:, g * 128:(g + 1) * 128]), reads=[psn], writes=[(rs, g)])
        P.op("act", lambda e, g=g: e.activation(rs[:, g, :], rs[:, g, :], AF.Sqrt, bias=Kk.epsb[:, 0:1], scale=1.0),
             reads=[(rs, g), Kk.epsb], writes=[(rs, g)])
        P.op("dve", lambda e, g=g: e.reciprocal(rs[:, g, :], rs[:, g, :]), reads=[(rs, g)], writes=[(rs, g)])
        for c4 in range(4):
            c = g * 4 + c4
            P.op("dve", lambda e, c=c, g=g: e.scalar_tensor_tensor(fin[:, s, c, :], fin[:, s, c, :], R["gn"][:, c:c + 1], rs[:, g, :],
                                                                    ALU.mult, ALU.mult),
                 reads=[kf, R["gn"], (rs, g)], writes=[kf])
    P.dma(yT[1024:2048, t0:t0 + 128].rearrange("(c p) t -> p c t", p=128), fin[:, s, :, :], reads=[kf], qn="pool")


C_SCALE = 192 ** -0.5
RW0 = 832


def stage_A_odd(Kk, l, xT, S):
    P = Kk.P
    o_ = l // 2
    W = Kk.ins["od_w_in"][o_]
    Wqb = Kk.ins["mla_q_b"][o_]
    Wkvb = Kk.ins["mla_kv_b"][o_]
    st = {}

    def extra(_):
        st["qa"] = P.sb("qa", [128, 6, 512])
        st["qn"] = P.sb("qan", [128, 6, 512])
        st["rs"] = P.sb("rsq", [128, 2, 512])
        st["gain"] = P.sb("mlagain", [128, 6])
        st["cos"] = P.sb("cosb", [64, 512])
        st["sin"] = P.sb("sinb", [64, 512])
        st["RT"] = P.sb("RT64", [64, 64])
        st["pe"] = P.sb("pe", [64, 3, 512])
        st["po"] = P.sb("po", [64, 3, 512])
        st["tmp"] = P.sb("ptmp", [64, 3, 512])
        st["o512"] = P.sb("o512", [128, 128])
        st["o256"] = P.sb("o256", [128, 128])
        P.dma(st["gain"][:, :], Kk.ins["mla_gain"][o_], writes=[st["gain"]])
        P.dma(st["RT"][:, :], Kk.ins["c_RT64"][:, :], writes=[st["RT"]])
        P.dma(st["o512"][:, :], Kk.ins["c_ones512"][:, :], writes=[st["o512"]])
        P.dma(st["o256"][:, :], Kk.ins["c_ones256"][:, :], writes=[st["o256"]])
        st["i"] = 0

    def tile_pre(t0, n, isctx):
        if not isctx:
            P.dma(st["cos"][:, 0:n], Kk.ins["c_cos64"][:, t0 - NCTX:t0 - NCTX + n], writes=[st["cos"]])
            P.dma(st["sin"][:, 0:n], Kk.ins["c_sin64"][:, t0 - NCTX:t0 - NCTX + n], writes=[st["sin"]])

    def rope64(ps, n, t0, isctx, dst):
        s = st["i"] % 3
        st["i"] += 1
        pe, po, tmp = st["pe"], st["po"], st["tmp"]
        P.op("act", lambda e: e.activation(pe[:, s, 0:n], ps[0:64, 0:n], AF.Identity), reads=[ps], writes=[(pe, s)])
        if isctx:
            P.dma(dst, pe[:, s, 0:n], reads=[(pe, s)], qn="pool")
        else:
            rope_apply(Kk, (pe, s, pe[:, s, 0:n]), 64, n, st["cos"], st["sin"], st["RT"], Kk.psb[6],
                       (po, s, po[:, s, 0:n]), (tmp, s, tmp[:, s, 0:n]))
            P.dma(dst, po[:, s, 0:n], reads=[(po, s)], qn="pool")

    def mk_a(t0, n, isctx):
        def evac(i, ps, cw):
            P.op("act", lambda e: e.activation(st["qa"][:, i, 0:n], ps[:, 0:n], AF.Identity), reads=[ps], writes=[(st["qa"], i)])
        return evac

    def mk_kpe(t0, n, isctx):
        def evac(i, ps, cw):
            rope64(ps, n, t0, isctx, S["kpT"][:, t0:t0 + n])
        return evac

    def post_a(t0, n, isctx):
        qa, qn, rs = st["qa"], st["qn"], st["rs"]
        for part, (c0, nch, ones_) in enumerate([(0, 4, st["o512"]), (4, 2, st["o256"])]):
            psr = Kk.psb[6]
            colsum_bcast(Kk, lambda c, kp, c0=c0: qa[:, c0 + c, 0:n], nch, n, psr, Kk.stA["sq"], ones_,
                         [(qa, c0 + c) for c in range(nch)])
            P.op("act", lambda e, part=part, psr=psr: e.activation(rs[:, part, 0:n], psr[:, 0:n], AF.Sqrt, bias=Kk.epsb[:, 0:1], scale=1.0),
                 reads=[psr, Kk.epsb], writes=[(rs, part)])
            P.op("dve", lambda e, part=part: e.reciprocal(rs[:, part, 0:n], rs[:, part, 0:n]), reads=[(rs, part)], writes=[(rs, part)])
            for c in range(c0, c0 + nch):
                P.op("dve", lambda e, c=c, part=part: e.scalar_tensor_tensor(qn[:, c, 0:n], qa[:, c, 0:n], st["gain"][:, c:c + 1],
                                                                              rs[:, part, 0:n], ALU.mult, ALU.mult),
                     reads=[(qa, c), st["gain"], (rs, part)], writes=[(qn, c)])
        chunks = []
        for h in range(8):
            chunks.append((h * 192, 128))
            chunks.append((h * 192 + 128, 64))

        def evq(i, ps, cw):
            h = i // 2
            if i % 2 == 0:
                ev_store(Kk, ps, 128, n, S["qnT"][h][:, t0:t0 + n])
            else:
                rope64(ps, n, t0, isctx, S["qpT"][h][:, t0:t0 + n])

        gemm_fm(Kk, Wqb, 512, chunks, lambda kc, kp: qn[0:kp, kc, 0:n], n, evq, ps_ids=(2, 3),
                rhs_reads=[(qn, c) for c in range(4)])
        kchunks = [(h * 256, 128) for h in range(8)]

        def evk(i, ps, cw):
            ev_store(Kk, ps, 128, n, S["knT"][i][:, t0:t0 + n])

        gemm_fm(Kk, Wkvb, 256, kchunks, lambda kc, kp: qn[0:kp, 4 + kc, 0:n], n, evk, ps_ids=(2, 3),
                rhs_reads=[(qn, 4), (qn, 5)])
        for h in range(8):
            def evv(tb, ps, tn, cb0, cbw, h=h):
                ev_store(Kk, ps, tn, cbw, S["vM"][t0 + tb * 128:t0 + tb * 128 + tn, h * 128:h * 128 + cbw])
            gemm_tm(Kk, Wkvb, 256, h * 256 + 128, 128, lambda kc, kp, a, tn: qn[0:kp, 4 + kc, a:a + tn], n, evv, ps_ids=(2, 3),
                    lhs_reads=[(qn, 4), (qn, 5)])

    def mk_ud(t0, n, isctx):
        def evac(i, ps, cw):
            c0 = ud_chunks[i][0] - RW0
            ev_store(Kk, ps, cw, n, S["udT"][c0:c0 + cw, t0:t0 + n])
        return evac

    a_chunks = [(i * 128, 128) for i in range(6)]
    kpe_chunks = [(768, 64)]
    ud_chunks = [(RW0 + i * 128, 128) for i in range(24)] + [(3904 + i * 96, 96) for i in range(4)] + [(4288, 128), (4416, 128)]
    plan = [("post", tile_pre), ("fm", a_chunks, mk_a), ("fm", kpe_chunks, mk_kpe), ("post", post_a), ("fm", ud_chunks, mk_ud)]
    stage_A(Kk, l, xT, W, plan, nsteps_extra=extra)


def stage_attn_odd(Kk, S, yT, do_ctx):
    P = Kk.P
    with ExitStack() as ses:
        P.stage_es = ses
        kt = P.sb("kt", [128, NT])
        kp = P.sb("kp", [64, NT])
        vt = P.sb("vt", [128, NT // 128, 128])
        qt = P.sb("qt", [128, 2, 512])
        qp = P.sb("qp", [64, 2, 512])
        Kk.att = dict(pt=P.sb("pt", [128, 3, 512]), ot=P.sb("ot", [128, 2, 512]), rd=P.sb("rd", [128, 2, 512]))
        P.dma(kp[:, :], S["kpT"][:, :], writes=[kp])

        def load_k(h):
            P.dma(kt[:, :], S["knT"][h], writes=[kt])
            return [kt, kp]

        def load_v(h):
            for j0 in range(0, NT // 128, 4):
                j1 = min(NT // 128, j0 + 4)
                P.dma(vt[:, j0:j1, :], S["vM"][j0 * 128:j1 * 128, h * 128:(h + 1) * 128].rearrange("(j p) d -> p j d", p=128),
                      writes=[vt])
            return vt

        def load_q(h, t0, n, slot):
            P.dma(qt[:, slot, 0:n], S["qnT"][h][:, t0:t0 + n], writes=[(qt, slot)])
            P.dma(qp[:, slot, 0:n], S["qpT"][h][:, t0:t0 + n], writes=[(qp, slot)])
            return [qt[:, slot, 0:n], qp[:, slot, 0:n]], [(qt, slot), (qp, slot)]

        attention(Kk, 8, lambda h: h, load_k, load_v, load_q, [128, 64], C_SCALE, yT, 0, do_ctx)
        P.barrier()
        P.stage_es = None

import math

NEG_EH = -math.exp(-0.5)
NCH64 = NT // 64


def stage_rwkv_prep(Kk, l, S):
    P = Kk.P
    o_ = l // 2
    with ExitStack() as ses:
        P.stage_es = ses
        R = {}
        R["udp"] = P.sb("udp", [128, 30, 512])
        R["ub"] = P.sb("rub", [128, 2, 516])
        R["T"] = P.sb("rT", [128, 12, 512])
        R["PK"] = P.sb("rPK", [128, 4, 512])
        R["stg"] = P.sb("rstg", [128, 6, 512])
        R["tm"] = P.sb("rtm", [128, 3, 512])
        R["mu"] = P.sb("rmu", [128, 30])
        R["vec"] = P.sb("rvec", [128, 7, 8])
        R["g2w"] = P.sb("g2w", [128, 2, 1024])
        R["w2w"] = P.sb("w2w", [128, 2, 1024])
        R["a2w"] = P.sb("a2w", [128, 2, 1024])
        R["blk"] = P.sb("blk", [128, 128])
        R["on64"] = P.sb("on64", [128, 64])
        R["wc"] = P.sb("wcst", [128, 4, 8])
        P.dma(R["mu"][:, :], Kk.ins["rw_mu"][o_], writes=[R["mu"]])
        P.dma(R["vec"][:, :, :], Kk.ins["rw_vec"][o_], writes=[R["vec"]])
        P.dma(R["g2w"][:, :, :], Kk.ins["rwkv_g2"][o_].rearrange("(kc p) c -> p kc c", p=128), writes=[R["g2w"]])
        for d in range(2):
            P.dma(R["w2w"][0:96, d, :], Kk.ins["rwkv_w2"][o_][d], writes=[R["w2w"]])
            P.dma(R["a2w"][0:96, d, :], Kk.ins["rwkv_a2"][o_][d], writes=[R["a2w"]])
        P.dma(R["blk"][:, :], Kk.ins["c_blk64"][:, :], writes=[R["blk"]])
        P.op("pool", lambda e: e.memset(R["on64"][:, :], 1.0), writes=[R["on64"]])
        R["si"] = 0
        R["ti"] = 0
        R["tmi"] = 0
        R["wi"] = 0
        for (t0, n, isctx) in tiles_main():
            _rw_prep_tile(Kk, S, R, t0, n, isctx)
        P.barrier()
        P.stage_es = None


def _rw_prep_tile(Kk, S, R, t0, n, isctx):
    P = Kk.P
    udp, ub, T, stg, tm, mu, vec = R["udp"], R["ub"], R["T"], R["stg"], R["tm"], R["mu"], R["vec"]
    seg0, seg1 = (0, NCTX) if isctx else (NCTX, NT)
    lh = 1 if t0 > seg0 else 0
    rh = 1 if t0 + n < seg1 else 0
    nch = n // 64
    ch0 = t0 // 64
    rows = [128] * 24 + [96] * 4 + [128] * 2
    roff = [i * 128 for i in range(24)] + [3072 + i * 96 for i in range(4)] + [3456, 3584]

    def tslot():
        s = R["ti"] % 12
        R["ti"] += 1
        return s

    def store_fm(src_ap, skey, dst):
        P.dma(dst, src_ap, reads=[skey], qn="pool")

    for c in range(30):
        rw = rows[c]
        s = c % 2
        kb = (ub, s)
        if not (lh and rh):
            P.op("pool", lambda e, s=s: e.memset(ub[:, s, :], 0.0), writes=[kb])
        P.dma(ub[0:rw, s, 1 - lh:1 + n + rh], S["udT"][roff[c]:roff[c] + rw, t0 - lh:t0 + n + rh], writes=[kb])
        ts = tslot()
        kt = (T, ts)
        P.op("dve", lambda e, s=s, ts=ts, rw=rw: e.tensor_tensor(T[0:rw, ts, 0:n], ub[0:rw, s, 0:n], ub[0:rw, s, 2:2 + n], ALU.add),
             reads=[kb], writes=[kt])
        P.op("dve", lambda e, s=s, ts=ts, rw=rw: e.scalar_tensor_tensor(T[0:rw, ts, 0:n], T[0:rw, ts, 0:n], 0.5, ub[0:rw, s, 1:1 + n],
                                                                         ALU.mult, ALU.subtract),
             reads=[kb, kt], writes=[kt])
        P.op("dve", lambda e, s=s, ts=ts, rw=rw, c=c: e.scalar_tensor_tensor(udp[0:rw, c, 0:n], T[0:rw, ts, 0:n], mu[0:rw, c:c + 1],
                                                                              ub[0:rw, s, 1:1 + n], ALU.mult, ALU.add),
             reads=[kb, kt, mu], writes=[(udp, c)])
    for c in (28, 29):
        P.op("act", lambda e, c=c: e.activation(udp[:, c, 0:n], udp[:, c, 0:n], AF.Sigmoid), reads=[(udp, c)], writes=[(udp, c)])
    for c in range(8):
        ps = Kk.psb[c % 2]
        for kc in range(2):
            P.op("pe", lambda e, c=c, kc=kc, ps=ps: e.matmul(ps[:, 0:n], R["g2w"][:, kc, c * 128:(c + 1) * 128], udp[:, 28 + kc, 0:n],
                                                            start=(kc == 0), stop=(kc == 1)),
                 reads=[R["g2w"], (udp, 28 + kc)], writes=[ps])
        s = R["si"] % 6
        R["si"] += 1
        P.op("act", lambda e, ps=ps, s=s: e.activation(stg[:, s, 0:n], ps[:, 0:n], AF.Identity), reads=[ps], writes=[(stg, s)])
        store_fm(stg[:, s, 0:n], (stg, s), S["gateT"][c * 128:(c + 1) * 128, t0:t0 + n])
    for d in range(2):
        P.op("act", lambda e, d=d: e.activation(udp[0:96, 24 + d, 0:n], udp[0:96, 24 + d, 0:n], AF.Tanh),
             reads=[(udp, 24 + d)], writes=[(udp, 24 + d)])
    for c in range(8):
        _rw_prep_chunk(Kk, S, R, t0, n, c, nch, ch0)


def _rw_prep_chunk(Kk, S, R, t0, n, c, nch, ch0):
    P = Kk.P
    udp, T, stg, tm, vec = R["udp"], R["T"], R["stg"], R["tm"], R["vec"]
    rC, kC, vC = (udp, c), (udp, 8 + c), (udp, 16 + c)
    r_ = udp[:, c, 0:n]
    k_ = udp[:, 8 + c, 0:n]
    v_ = udp[:, 16 + c, 0:n]

    def tslot():
        s = R["ti"] % 12
        R["ti"] += 1
        return s, (T, s)

    def sslot():
        s = R["si"] % 6
        R["si"] += 1
        return s, (stg, s)

    def transpose_store(src_ap, skey, dst_tm):
        pst = Kk.psb[4 + R["tmi"] % 2]
        s = R["tmi"] % 3
        R["tmi"] += 1
        nb = n // 128
        for b in range(nb):
            P.op("pe", lambda e, b=b: e.matmul(pst[:, b * 128:(b + 1) * 128], src_ap[:, b * 128:(b + 1) * 128], Kk.ident[:, :],
                                               start=True, stop=True),
                 reads=[skey, Kk.ident], writes=[pst])
        P.op("act", lambda e: e.activation(tm[:, s, 0:n], pst[:, 0:n], AF.Identity), reads=[pst], writes=[(tm, s)])
        P.dma(dst_tm[t0:t0 + n, c * 128:(c + 1) * 128].rearrange("(b p) f -> p b f", p=128),
              tm[:, s, 0:n].rearrange("p (b f) -> p b f", f=128), reads=[(tm, s)], qn="pool")

    def headsum(src_ap, skey, ps):
        P.op("pe", lambda e: e.matmul(ps[:, 0:n], R["blk"][:, :], src_ap, start=True, stop=True), reads=[skey, R["blk"]], writes=[ps])

    transpose_store(v_, vC, S["vTM"])
    PK = R["PK"]
    k_kkr, k_sq, k_kk, k_ks = (PK, 0), (PK, 1), (PK, 2), (PK, 3)
    P.op("dve", lambda e: e.tensor_scalar(PK[:, 0, 0:n], k_, vec[:, 0, c:c + 1], None, ALU.mult), reads=[kC, vec], writes=[k_kkr])
    P.op("act", lambda e: e.activation(PK[:, 1, 0:n], PK[:, 0, 0:n], AF.Square), reads=[k_kkr], writes=[k_sq])
    ps = Kk.psb[2]
    headsum(PK[:, 1, 0:n], k_sq, ps)
    P.op("act", lambda e: e.activation(PK[:, 1, 0:n], ps[:, 0:n], AF.Sqrt, bias=Kk.epsb[:, 2:3], scale=1.0),
         reads=[ps, Kk.epsb], writes=[k_sq])
    P.op("dve", lambda e: e.reciprocal(PK[:, 1, 0:n], PK[:, 1, 0:n]), reads=[k_sq], writes=[k_sq])
    P.op("dve", lambda e: e.tensor_tensor(PK[:, 2, 0:n], PK[:, 0, 0:n], PK[:, 1, 0:n], ALU.mult), reads=[k_kkr, k_sq], writes=[k_kk])
    kk = PK[:, 2, 0:n]
    for d in range(2):
        ps1 = Kk.psb[d]
        P.op("pe", lambda e, d=d, ps1=ps1: e.matmul(ps1[:, 0:n], R["w2w"][0:96, d, c * 128:(c + 1) * 128], udp[0:96, 24 + d, 0:n],
                                                    start=True, stop=True),
             reads=[R["w2w"], (udp, 24 + d)], writes=[ps1])
        s_lw, k_lw = tslot()
        P.op("act", lambda e, d=d, ps1=ps1, s_lw=s_lw: e.activation(T[:, s_lw, 0:n], ps1[:, 0:n], AF.Sigmoid, bias=vec[:, 3 + d, c:c + 1], scale=1.0),
             reads=[ps1, vec], writes=[k_lw])
        P.op("dve", lambda e, s_lw=s_lw: e.tensor_scalar(T[:, s_lw, 0:n], T[:, s_lw, 0:n], NEG_EH, None, ALU.mult), reads=[k_lw], writes=[k_lw])
        ps2 = Kk.psb[3]
        P.op("pe", lambda e, d=d: e.matmul(ps2[:, 0:n], R["a2w"][0:96, d, c * 128:(c + 1) * 128], udp[0:96, 26 + d, 0:n],
                                           start=True, stop=True),
             reads=[R["a2w"], (udp, 26 + d)], writes=[ps2])
        s_a, k_a = tslot()
        P.op("act", lambda e, d=d, s_a=s_a: e.activation(T[:, s_a, 0:n], ps2[:, 0:n], AF.Sigmoid, bias=vec[:, 5 + d, c:c + 1], scale=1.0),
             reads=[ps2, vec], writes=[k_a])
        s_kd, k_kd = tslot()
        P.op("dve", lambda e, s_a=s_a, s_kd=s_kd: e.tensor_scalar(T[:, s_kd, 0:n], T[:, s_a, 0:n], -1.0, vec[:, 1, c:c + 1], ALU.add, ALU.mult),
             reads=[k_a, vec], writes=[k_kd])
        P.op("dve", lambda e, s_kd=s_kd: e.scalar_tensor_tensor(T[:, s_kd, 0:n], T[:, s_kd, 0:n], 1.0, k_, ALU.add, ALU.mult),
             reads=[k_kd, kC], writes=[k_kd])
        if d == 0:
            P.op("pool", lambda e, s_kd=s_kd: e.tensor_copy(PK[:, 3, 0:n], T[:, s_kd, 0:n]), reads=[k_kd], writes=[k_ks])
        else:
            P.op("pool", lambda e, s_kd=s_kd: e.tensor_tensor(PK[:, 3, 0:n], PK[:, 3, 0:n], T[:, s_kd, 0:n], ALU.add),
                 reads=[k_kd, k_ks], writes=[k_ks])
        s_cl, k_cl = tslot()
        for q in range(nch):
            P.op("dve", lambda e, q=q, s_cl=s_cl, s_lw=s_lw: e.tensor_tensor_scan(T[:, s_cl, q * 64:(q + 1) * 64], R["on64"][:, :],
                                                                                  T[:, s_lw, q * 64:(q + 1) * 64], 0.0, ALU.mult, ALU.add),
                 reads=[k_lw, R["on64"]], writes=[k_cl])
        if d == 1:
            s_t2, k_t2 = tslot()
            P.op("dve", lambda e, s_t2=s_t2, s_cl=s_cl, s_lw=s_lw: e.tensor_tensor(T[:, s_t2, 0:n], T[:, s_lw, 0:n], T[:, s_cl, 0:n], ALU.subtract),
                 reads=[k_lw, k_cl], writes=[k_t2])
            cl3 = T[:, s_cl, 0:n].rearrange("p (q t) -> p q t", t=64)
            P.op("dve", lambda e, s_t2=s_t2, cl3=cl3: e.tensor_tensor(T[:, s_t2, 0:n].rearrange("p (q t) -> p q t", t=64),
                                                                      T[:, s_t2, 0:n].rearrange("p (q t) -> p q t", t=64),
                                                                      cl3[:, :, 63:64].broadcast_to([128, nch, 64]), ALU.add),
                 reads=[k_t2, k_cl], writes=[k_t2])
            s_cl, k_cl = s_t2, k_t2
        cl = T[:, s_cl, 0:n]
        s_ep, k_ep = tslot()
        P.op("act", lambda e, s_ep=s_ep, cl=cl: e.activation(T[:, s_ep, 0:n], cl, AF.Exp), reads=[k_cl], writes=[k_ep])
        s_em, k_em = tslot()
        P.op("act", lambda e, s_em=s_em, cl=cl: e.activation(T[:, s_em, 0:n], cl, AF.Exp, scale=-1.0), reads=[k_cl], writes=[k_em])
        P.op("dve", lambda e, s_lw=s_lw, cl=cl: e.tensor_tensor(T[:, s_lw, 0:n], cl, T[:, s_lw, 0:n], ALU.subtract), reads=[k_cl, k_lw], writes=[k_lw])
        P.op("act", lambda e, s_lw=s_lw: e.activation(T[:, s_lw, 0:n], T[:, s_lw, 0:n], AF.Exp), reads=[k_lw], writes=[k_lw])
        last = 63 if d == 0 else 0
        wi = R["wi"] % 4
        R["wi"] += 1
        P.op("pool", lambda e, s_ep=s_ep, wi=wi, last=last: e.tensor_copy(
            R["wc"][:, wi, 0:nch].rearrange("p (q o) -> p q o", o=1),
            T[:, s_ep, 0:n].rearrange("p (q t) -> p q t", t=64)[:, :, last:last + 1]), reads=[k_ep], writes=[(R["wc"], wi)])
        P.dma(S["wC"][d][c * 128:(c + 1) * 128, ch0:ch0 + nch], R["wc"][:, wi, 0:nch], reads=[(R["wc"], wi)], qn="pool")
        s1, ks1 = sslot()
        P.op("dve", lambda e, s1=s1, s_ep=s_ep: e.tensor_tensor(stg[:, s1, 0:n], r_, T[:, s_ep, 0:n], ALU.mult), reads=[rC, k_ep], writes=[ks1])
        P.dma(S["RW"][d][3][c * 128:(c + 1) * 128, t0:t0 + n], stg[:, s1, 0:n], reads=[ks1], qn="pool")
        s2, ks2 = sslot()
        P.op("pool", lambda e, s2=s2, s_lw=s_lw: e.tensor_tensor(stg[:, s2, 0:n], kk, T[:, s_lw, 0:n], ALU.mult), reads=[k_kk, k_lw], writes=[ks2])
        P.dma(S["RW"][d][2][c * 128:(c + 1) * 128, t0:t0 + n], stg[:, s2, 0:n], reads=[ks2], qn="pool")
        s3, ks3 = sslot()
        P.op("dve", lambda e, s3=s3, s_kd=s_kd, s_em=s_em: e.tensor_tensor(stg[:, s3, 0:n], T[:, s_kd, 0:n], T[:, s_em, 0:n], ALU.mult),
             reads=[k_kd, k_em], writes=[ks3])
        P.dma(S["RW"][d][0][c * 128:(c + 1) * 128, t0:t0 + n], stg[:, s3, 0:n], reads=[ks3], qn="pool")
        transpose_store(stg[:, s3, 0:n], ks3, S["kbTM"][d])
        s4, ks4 = sslot()
        P.op("pool", lambda e, s4=s4, s_a=s_a: e.tensor_tensor(stg[:, s4, 0:n], T[:, s_a, 0:n], kk, ALU.mult), reads=[k_a, k_kk], writes=[ks4])
        P.op("dve", lambda e, s4=s4, s_em=s_em: e.tensor_tensor(stg[:, s4, 0:n], stg[:, s4, 0:n], T[:, s_em, 0:n], ALU.mult),
             reads=[ks4, k_em], writes=[ks4])
        P.dma(S["RW"][d][1][c * 128:(c + 1) * 128, t0:t0 + n], stg[:, s4, 0:n], reads=[ks4], qn="pool")
        transpose_store(stg[:, s4, 0:n], ks4, S["bbTM"][d])
    P.op("dve", lambda e: e.scalar_tensor_tensor(PK[:, 3, 0:n], PK[:, 3, 0:n], vec[:, 2, c:c + 1], r_, ALU.mult, ALU.mult),
         reads=[k_ks, vec, rC], writes=[k_ks])
    ps3 = Kk.psb[2]
    headsum(PK[:, 3, 0:n], k_ks, ps3)
    s5, ks5 = sslot()
    P.op("dve", lambda e: e.tensor_tensor(stg[:, s5, 0:n], ps3[:, 0:n], v_, ALU.mult), reads=[ps3, vC], writes=[ks5])
    P.dma(S["bonT"][c * 128:(c + 1) * 128, t0:t0 + n], stg[:, s5, 0:n], reads=[ks5], qn="pool")


def stage_rwkv(Kk, l, S, yT, do_ctx):
    for d in range(2):
        _rwkv_dir(Kk, l, S, yT, do_ctx, d)


def _rwkv_dir(Kk, l, S, yT, do_ctx, d):
    P = Kk.P
    o_ = l // 2
    with ExitStack() as ses:
        P.stage_es = ses
        R = {}
        R["m64"] = P.sb("m64", [64, 4, 64])
        P.dma(R["m64"][:, :, :], Kk.ins["c_m64"][:, :, :], writes=[R["m64"]])
        R["wCt"] = P.sb("wCt", [64, 16, NCH64])
        P.dma(R["wCt"][:, :, :], S["wC"][d].rearrange("(h j) q -> j h q", j=64), writes=[R["wCt"]])
        R["ST"] = P.sb("ST", [64, 16, 64])
        P.op("pool", lambda e: e.memset(R["ST"][:, :, :], 0.0), writes=[R["ST"]])
        R["Xc"] = P.sb("Xc", [64, 2, 16, 4, 64])
        R["TMc"] = P.sb("TMc", [64, 1, 3, 1024])
        R["AM"] = P.sb("AM", [64, 16, 4, 64])
        R["Lm"] = P.sb("Lm", [64, 16, 64])
        R["Pb"] = P.sb("Pb", [64, 2, 8, 64])
        R["Qb"] = P.sb("Qb", [64, 2, 8, 64])
        R["Xb"] = P.sb("Xb", [64, 2, 8, 64])
        R["Qi"] = P.sb("Qi", [64, 8, 64])
        R["TT"] = P.sb("TT", [64, 16, 64])
        R["Zs"] = P.sb("Zs", [64, 16, 64])
        R["Us"] = P.sb("Us", [64, 16, 64])
        R["Yc"] = P.sb("Yc", [64, 2, 1024])
        R["tmpS"] = P.sb("tmpS", [64, 16, 64])
        if d == 1:
            R["Yf"] = P.sb("Yf", [64, 1, 1024])
            R["ln"] = P.sb("lnrow", [64, 2, 1024])
            P.dma(R["ln"][:, :, :], Kk.ins["rw_ln"][o_].rearrange("(o a) f -> o a f", o=1).broadcast_to([64, 2, 1024]), writes=[R["ln"]])
            R["st"] = P.sb("lnst", [64, 4, 16])
            R["cen"] = P.sb("cen", [64, 1024])
            R["sq"] = P.sb("lsq", [64, 1024])
            R["bg"] = P.sb("bg", [128, 1, 2, 8, 64])
            R["yo"] = P.sb("yo", [128, 2, 8, 64])
        order = list(range(4)) + list(range(4, NCH64)) if d == 0 else [3, 2, 1, 0] + list(range(NCH64 - 1, 3, -1))
        import os
        nlim = int(os.environ.get("RW_NCH", "999"))
        for ci, ch in enumerate(order[:nlim]):
            _rwkv_chunk(Kk, S, yT, R, d, ch, ci, do_ctx)
        P.barrier()
        P.stage_es = None


def _rwkv_chunk(Kk, S, yT, R, d, ch, ci, do_ctx):
    P = Kk.P
    s = ci % 2
    t0 = ch * 64
    Xc, TMc, AM, Lm, TT, ST, Zs, Us, Yc, m64 = R["Xc"], R["TMc"], R["AM"], R["Lm"], R["TT"], R["ST"], R["Zs"], R["Us"], R["Yc"], R["m64"]
    id64 = Kk.ident[0:64, 0:64]
    kx = (Xc, s)
    ktm = (TMc, 0)
    for q in range(4):
        P.dma(Xc[:, s, :, q, :], S["RW"][d][q][:, t0:t0 + 64].rearrange("(h j) t -> j h t", j=64), writes=[kx])
    P.dma(TMc[:, 0, 0, :], S["kbTM"][d][t0:t0 + 64, :], writes=[ktm])
    P.dma(TMc[:, 0, 1, :], S["bbTM"][d][t0:t0 + 64, :], writes=[ktm])
    P.dma(TMc[:, 0, 2, :], S["vTM"][t0:t0 + 64, :], writes=[ktm])
    import os
    LV = int(os.environ.get("RW_LEVEL", "99"))
    if LV <= 1:
        return
    mi = 0 if d == 0 else 2
    mL = m64[:, 2 if d == 0 else 0, :]
    psb = Kk.psb
    for hf in range(2):
        h0 = hf * 8
        for hh in range(8):
            h = h0 + hh
            bk = hh // 4
            col = (hh % 4) * 128
            P.op("pe", lambda e, h=h, bk=bk, col=col: e.matmul(psb[bk][0:64, col:col + 128], Xc[:, s, h, 0, :], Xc[:, s, h, 2:4, :],
                                                              start=True, stop=True), reads=[kx], writes=[psb[bk]])
            P.op("pe", lambda e, h=h, bk=bk, col=col: e.matmul(psb[2 + bk][0:64, col:col + 128], Xc[:, s, h, 1, :], Xc[:, s, h, 2:4, :],
                                                              start=True, stop=True), reads=[kx], writes=[psb[2 + bk]])
            P.op("pe", lambda e, h=h, hh=hh: e.matmul(psb[4][0:64, hh * 64:(hh + 1) * 64], Xc[:, s, h, 2, :], Xc[:, s, h, 1, :],
                                                      start=True, stop=True), reads=[kx], writes=[psb[4]])
        mask2 = m64[:, mi:mi + 2, :].rearrange("p (o a) t -> p o a t", o=1).broadcast_to([64, 4, 2, 64])
        for bk in range(2):
            hs = slice(h0 + bk * 4, h0 + bk * 4 + 4)
            P.op("dve", lambda e, bk=bk, hs=hs: e.tensor_tensor(AM[:, hs, 0:2, :], psb[bk][0:64, :].rearrange("p (h a t) -> p h a t", a=2, t=64),
                                                               mask2, ALU.mult), reads=[psb[bk], m64], writes=[(AM, (hf, bk, 0))])
            P.op("dve", lambda e, bk=bk, hs=hs: e.tensor_tensor(AM[:, hs, 2:4, :], psb[2 + bk][0:64, :].rearrange("p (h a t) -> p h a t", a=2, t=64),
                                                               mask2, ALU.mult), reads=[psb[2 + bk], m64], writes=[(AM, (hf, bk, 1))])
        kAM = [(AM, (hf, bk, a)) for bk in range(2) for a in range(2)]
        hs8 = slice(h0, h0 + 8)
        kL = (Lm, hf)
        P.op("dve", lambda e, hs8=hs8: e.tensor_tensor(Lm[:, hs8, :], psb[4][0:64, :].rearrange("p (h t) -> p h t", t=64),
                                              mL.rearrange("p (o t) -> p o t", o=1).broadcast_to([64, 8, 64]), ALU.mult),
             reads=[psb[4], m64], writes=[kL])
        if LV <= 2:
            continue
        Pb, Qb, Xb, Qi = R["Pb"], R["Qb"], R["Xb"], R["Qi"]
        idb = id64.rearrange("p (o t) -> p o t", o=1).broadcast_to([64, 8, 64])
        P.op("dve", lambda e, hs8=hs8, idb=idb: e.tensor_tensor(Xb[:, 0, :, :], idb, AM[:, hs8, 2, :], ALU.subtract), reads=kAM + [Kk.ident], writes=[(Xb, 0)])
        for lev in range(5):
            pi_, po_ = lev % 2, (lev + 1) % 2
            lastlev = (lev == 4)
            for hh in range(8):
                h = h0 + hh
                Pk = AM[:, h, 2, :] if lev == 0 else Pb[:, pi_, hh, :]
                Qk = Lm[:, h, :] if lev == 0 else Qb[:, pi_, hh, :]
                rd = kAM + [kL] if lev == 0 else [(Pb, pi_), (Qb, pi_)]
                if not lastlev:
                    P.op("pe", lambda e, hh=hh, Pk=Pk, Qk=Qk: e.matmul(psb[5][0:64, hh * 64:(hh + 1) * 64], Qk, Pk, start=True, stop=True),
                         reads=rd, writes=[psb[5]])
                P.op("pe", lambda e, hh=hh, Pk=Pk, Qk=Qk: e.matmul(psb[6][0:64, hh * 64:(hh + 1) * 64], Pk, Qk, start=True, stop=True),
                     reads=rd, writes=[psb[6]])
            if not lastlev:
                P.op("act", lambda e, po_=po_: e.activation(Pb[:, po_, :, :], psb[5][0:64, :].rearrange("p (h t) -> p h t", t=64), AF.Identity),
                     reads=[psb[5]], writes=[(Pb, po_)])
                P.op("dve", lambda e, po_=po_: e.tensor_copy(Qb[:, po_, :, :], psb[6][0:64, :].rearrange("p (h t) -> p h t", t=64)),
                     reads=[psb[6]], writes=[(Qb, po_)])
            P.op("dve", lambda e, idb=idb: e.tensor_tensor(Qi[:, :, :], psb[6][0:64, :].rearrange("p (h t) -> p h t", t=64), idb, ALU.add),
                 reads=[psb[6], Kk.ident], writes=[Qi])
            for hh in range(8):
                P.op("pe", lambda e, hh=hh, pi_=pi_: e.matmul(psb[7][0:64, hh * 64:(hh + 1) * 64], Qi[:, hh, :], Xb[:, pi_, hh, :],
                                                              start=True, stop=True), reads=[Qi, (Xb, pi_)], writes=[psb[7]])
            if lastlev:
                P.op("act", lambda e, hs8=hs8: e.activation(TT[:, hs8, :], psb[7][0:64, :].rearrange("p (h t) -> p h t", t=64), AF.Identity),
                     reads=[psb[7]], writes=[(TT, hf)])
            else:
                P.op("act", lambda e, po_=po_: e.activation(Xb[:, po_, :, :], psb[7][0:64, :].rearrange("p (h t) -> p h t", t=64), AF.Identity),
                     reads=[psb[7]], writes=[(Xb, po_)])
    allAM = [(AM, (hf, bk, a)) for hf in range(2) for bk in range(2) for a in range(2)]
    kTT = [(TT, 0), (TT, 1)]
    if LV <= 3:
        return
    V = lambda h: TMc[:, 0, 2, h * 64:(h + 1) * 64]
    for h in range(16):
        bk, col = h // 8, (h % 8) * 64
        P.op("pe", lambda e, h=h, bk=bk, col=col: e.matmul(psb[bk][0:64, col:col + 64], Xc[:, s, h, 2, :], ST[:, h, :], start=True, stop=False),
             reads=[kx, ST], writes=[psb[bk]])
        P.op("pe", lambda e, h=h, bk=bk, col=col: e.matmul(psb[bk][0:64, col:col + 64], AM[:, h, 0, :], V(h), start=False, stop=True),
             reads=allAM + [ktm], writes=[psb[bk]])
    for bk in range(2):
        P.op("act", lambda e, bk=bk: e.activation(Zs[:, bk * 8:(bk + 1) * 8, :], psb[bk][0:64, :].rearrange("p (h t) -> p h t", t=64),
                                                  AF.Identity, scale=-1.0), reads=[psb[bk]], writes=[(Zs, bk)])
    for h in range(16):
        bk, col = h // 8, (h % 8) * 64
        P.op("pe", lambda e, h=h, bk=bk, col=col: e.matmul(psb[2 + bk][0:64, col:col + 64], TT[:, h, :], Zs[:, h, :], start=True, stop=True),
             reads=kTT + [(Zs, bk)], writes=[psb[2 + bk]])
    for bk in range(2):
        P.op("dve", lambda e, bk=bk: e.tensor_copy(Us[:, bk * 8:(bk + 1) * 8, :], psb[2 + bk][0:64, :].rearrange("p (h t) -> p h t", t=64)),
             reads=[psb[2 + bk]], writes=[(Us, bk)])
    for h in range(16):
        bk, col = h // 8, (h % 8) * 64
        P.op("pe", lambda e, h=h, bk=bk, col=col: e.matmul(psb[4 + bk][0:64, col:col + 64], Xc[:, s, h, 3, :], ST[:, h, :], start=True, stop=False),
             reads=[kx, ST], writes=[psb[4 + bk]])
        P.op("pe", lambda e, h=h, bk=bk, col=col: e.matmul(psb[4 + bk][0:64, col:col + 64], AM[:, h, 1, :], V(h), start=False, stop=False),
             reads=allAM + [ktm], writes=[psb[4 + bk]])
        P.op("pe", lambda e, h=h, bk=bk, col=col: e.matmul(psb[4 + bk][0:64, col:col + 64], AM[:, h, 3, :], Us[:, h, :], start=False, stop=True),
             reads=allAM + [(Us, bk)], writes=[psb[4 + bk]])
        P.op("pe", lambda e, h=h, bk=bk, col=col: e.matmul(psb[6 + bk][0:64, col:col + 64], TMc[:, 0, 0, h * 64:(h + 1) * 64], V(h), start=True, stop=False),
             reads=[ktm], writes=[psb[6 + bk]])
        P.op("pe", lambda e, h=h, bk=bk, col=col: e.matmul(psb[6 + bk][0:64, col:col + 64], TMc[:, 0, 1, h * 64:(h + 1) * 64], Us[:, h, :], start=False, stop=True),
             reads=[ktm, (Us, bk)], writes=[psb[6 + bk]])
    ky = (Yc, s)
    tmpS = R["tmpS"]
    for bk in range(2):
        P.op("act", lambda e, bk=bk: e.activation(Yc[:, s, bk * 512:(bk + 1) * 512], psb[4 + bk][0:64, :], AF.Identity),
             reads=[psb[4 + bk]], writes=[(Yc, (s, bk))])
        hsb = slice(bk * 8, (bk + 1) * 8)
        P.op("dve", lambda e, bk=bk, hsb=hsb: e.tensor_tensor(tmpS[:, hsb, :], psb[6 + bk][0:64, :].rearrange("p (h t) -> p h t", t=64),
                                                             ST[:, hsb, :], ALU.add), reads=[psb[6 + bk], ST], writes=[(tmpS, bk)])
        P.op("pool", lambda e, bk=bk, hsb=hsb: e.tensor_tensor(ST[:, hsb, :], tmpS[:, hsb, :],
                                                              R["wCt"][:, hsb, ch:ch + 1].broadcast_to([64, 8, 64]), ALU.mult),
             reads=[(tmpS, bk), R["wCt"]], writes=[ST])
    kyall = [(Yc, (s, 0)), (Yc, (s, 1))]
    if os.environ.get("RW_DBG") and d == 0 and ci == 0:
        D = S["dbg"]
        P.dma(D["AM"], AM[:, :, :, :], reads=allAM, qn="pool")
        P.dma(D["Lm"], Lm[:, :, :], reads=[(Lm, 0), (Lm, 1)], qn="pool")
        P.dma(D["TT"], TT[:, :, :], reads=kTT, qn="pool")
        P.dma(D["Zs"], Zs[:, :, :], reads=[(Zs, 0), (Zs, 1)], qn="pool")
        P.dma(D["Us"], Us[:, :, :], reads=[(Us, 0), (Us, 1)], qn="pool")
        P.dma(D["Yc"], Yc[:, s, :], reads=kyall, qn="pool")
        P.dma(D["ST"], ST[:, :, :], reads=[ST], qn="pool")
    if LV <= 4:
        return
    if d == 0:
        P.dma(S["YfTM"][t0:t0 + 64, :], Yc[:, s, :], reads=kyall, qn="pool")
        return
    if ch < 4 and not do_ctx:
        return
    Yf, ln, stt, cen, sq, bg, yo = R["Yf"], R["ln"], R["st"], R["cen"], R["sq"], R["bg"], R["yo"]
    kf = (Yf, 0)
    P.dma(Yf[:, 0, :], S["YfTM"][t0:t0 + 64, :], writes=[kf])
    kb_ = (bg, 0)
    P.dma(bg[:, 0, 0, :, :], S["bonT"][:, t0:t0 + 64].rearrange("(c p) t -> p c t", p=128), writes=[kb_])
    P.dma(bg[:, 0, 1, :, :], S["gateT"][:, t0:t0 + 64].rearrange("(c p) t -> p c t", p=128), writes=[kb_])
    P.op("dve", lambda e: e.tensor_tensor(Yf[:, 0, :], Yf[:, 0, :], Yc[:, s, :], ALU.add), reads=[kf] + kyall, writes=[kf])
    y3 = Yf[:, 0, :].rearrange("p (h t) -> p h t", t=64)
    c3 = cen[:, :].rearrange("p (h t) -> p h t", t=64)
    q3 = sq[:, :].rearrange("p (h t) -> p h t", t=64)
    P.op("dve", lambda e: e.tensor_reduce(stt[:, 0, :], y3, AX.X, ALU.add), reads=[kf], writes=[(stt, 0)])
    P.op("dve", lambda e: e.tensor_scalar(stt[:, 0, :], stt[:, 0, :], 1.0 / 64, None, ALU.mult), reads=[(stt, 0)], writes=[(stt, 0)])
    P.op("dve", lambda e: e.tensor_tensor(c3, y3, stt[:, 0, :].rearrange("p (h o) -> p h o", o=1).broadcast_to([64, 16, 64]), ALU.subtract),
         reads=[kf, (stt, 0)], writes=[cen])
    P.op("act", lambda e: e.activation(sq[:, :], cen[:, :], AF.Square), reads=[cen], writes=[sq])
    P.op("dve", lambda e: e.tensor_reduce(stt[:, 1, :], q3, AX.X, ALU.add), reads=[sq], writes=[(stt, 1)])
    P.op("act", lambda e: e.activation(stt[:, 1, :], stt[:, 1, :], AF.Sqrt, bias=Kk.epsb[0:64, 1:2], scale=1.0 / 64),
         reads=[(stt, 1), Kk.epsb], writes=[(stt, 1)])
    P.op("dve", lambda e: e.reciprocal(stt[:, 1, :], stt[:, 1, :]), reads=[(stt, 1)], writes=[(stt, 1)])
    P.op("dve", lambda e: e.tensor_tensor(c3, c3, stt[:, 1, :].rearrange("p (h o) -> p h o", o=1).broadcast_to([64, 16, 64]), ALU.mult),
         reads=[cen, (stt, 1)], writes=[cen])
    P.op("pool", lambda e: e.tensor_tensor(cen[:, :], cen[:, :], ln[:, 0, :], ALU.mult), reads=[cen, ln], writes=[cen])
    P.op("pool", lambda e: e.tensor_tensor(cen[:, :], cen[:, :], ln[:, 1, :], ALU.add), reads=[cen, ln], writes=[cen])
    pst = psb[0]
    for c in range(8):
        P.op("pe", lambda e, c=c: e.matmul(pst[:, c * 64:(c + 1) * 64], cen[:, c * 128:(c + 1) * 128], id64, start=True, stop=True),
             reads=[cen, Kk.ident], writes=[pst])
    ko = (yo, s)
    P.op("dve", lambda e: e.tensor_tensor(yo[:, s, :, :], pst[:, :].rearrange("p (c t) -> p c t", t=64), bg[:, 0, 0, :, :], ALU.add),
         reads=[pst, kb_], writes=[ko])
    P.op("pool", lambda e: e.tensor_tensor(yo[:, s, :, :], yo[:, s, :, :], bg[:, 0, 1, :, :], ALU.mult), reads=[ko, kb_], writes=[ko])
    P.dma(yT[1024:2048, t0:t0 + 64].rearrange("(c p) t -> p c t", p=128), yo[:, s, :, :], reads=[ko], qn="pool")


def fm(v, n=None):
    v = np.asarray(v, np.float32)
    return np.ascontiguousarray(v.reshape(-1, 128).T)


def host_prep(inp, ncores=8):
    f32 = np.float32
    L = 4
    sh = {}
    sh["c_ones"] = np.ones((128, 128), f32)
    sh["c_ident"] = np.eye(128, dtype=f32)
    sh["c_onesD"] = np.full((128, 128), 1.0 / D, f32)
    eps = np.zeros((128, 4), f32)
    eps[:, 0] = EPS
    eps[:, 1] = 64e-5
    eps[:, 2] = 1e-12
    sh["c_eps"] = eps
    sh["ada_w"] = np.asarray(inp["ada_w"], f32)
    sh["ada_b"] = np.stack([fm(inp["ada_b"][l]) for l in range(L)])
    sh["norm_mix"] = np.stack([np.repeat(fm(inp["norm_mix"][l])[:, :, None], 2, 2) for l in range(L)])
    sh["norm_ffn"] = np.stack([np.repeat(fm(inp["norm_ffn"][l])[:, :, None], 2, 2) for l in range(L)])
    sh["ffn_w_up"] = np.asarray(inp["ffn_w_up"], f32)
    sh["ffn_w_down"] = np.asarray(inp["ffn_w_down"], f32)
    sh["ffn_conv_w"] = np.stack([np.stack([fm(inp["ffn_conv_w"][l][k]) for k in range(3)], 1) for l in range(L)])
    sh["ffn_conv_b"] = np.stack([fm(inp["ffn_conv_b"][l]) for l in range(L)])
    sh["ev_w_in"] = np.asarray(inp["ev_w_in"], f32)
    sh["ev_w_out"] = np.asarray(inp["ev_w_out"], f32)
    sh["od_w_in"] = np.asarray(inp["od_w_in"], f32)
    sh["od_w_out"] = np.asarray(inp["od_w_out"], f32)
    sh["final_norm"] = fm(inp["final_norm"])
    eps[:, 3] = 1.0
    host_prep_even(inp, sh)
    host_prep_odd(inp, sh)
    per = []
    x = np.asarray(inp["x"], f32)
    ctx = np.asarray(inp["ctx"], f32)
    c = np.asarray(inp["c"], f32)
    cc = np.asarray(inp["c_ctx"], f32)
    for i in range(ncores):
        b = i % 4
        d = {}
        d["xT0"] = np.ascontiguousarray(np.concatenate([ctx[b], x[b]], 0).T)
        d["cT"] = np.ascontiguousarray(np.stack([fm(c[b]), fm(cc)], 2))
        per.append(d)
    return sh, per


def rope_tables(hd):
    d = hd // 2
    half = d // 2
    t = np.arange(NLAT)
    row = (t // 64).astype(np.float32)
    col = (t % 64).astype(np.float32)
    inv = (10000.0 ** (-np.arange(half, dtype=np.float32) / half)).astype(np.float32)
    cos = np.zeros((hd, NLAT), np.float32)
    sin = np.zeros((hd, NLAT), np.float32)
    R = np.zeros((hd, hd), np.float32)
    for p in range(hd):
        pos = row if p < d else col
        i = (p % d) % half
        ang = pos * inv[i]
        cos[p] = np.cos(ang)
        sin[p] = np.sin(ang)
        if (p % d) < half:
            R[p, p + half] = -1.0
        else:
            R[p, p - half] = 1.0
    return cos, sin, np.ascontiguousarray(R.T)


def host_prep_even(inp, sh):
    f32 = np.float32
    sh["qk_gain"] = np.stack([np.stack([inp["attn_q_norm"][e], inp["attn_k_norm"][e]], 1) for e in range(2)]).astype(f32)
    cos, sin, RT = rope_tables(128)
    sh["c_cos128"], sh["c_sin128"], sh["c_RT128"] = cos, sin, RT
    sh["c_onesH"] = np.full((128, 128), 1.0 / 128, f32)
    sh["c_ones512"] = np.full((128, 128), 1.0 / 512, f32)
    sh["dt_ba"] = np.stack([np.stack([inp["ssm_dt_bias"][e].reshape(32), inp["ssm_a_log"][e].reshape(32)], 1) for e in range(2)]).astype(f32)
    sh["ssm_conv_w"] = np.stack([np.stack([fm(inp["ssm_conv_w"][e][k]) for k in range(5)], 2) for e in range(2)]).astype(f32)
    sh["ssm_conv_b"] = np.stack([fm(inp["ssm_conv_b"][e]) for e in range(2)]).astype(f32)
    s_ = np.arange(128)[:, None]
    l_ = np.arange(128)[None, :]
    sh["c_tri"] = np.ascontiguousarray(np.stack([s_ <= l_, s_ > l_, s_ >= l_, s_ < l_], 1).astype(f32))
    sh["ssm_d"] = np.stack([fm(np.repeat(inp["ssm_d"][e], 64)) for e in range(2)]).astype(f32)
    sh["ssm_norm"] = np.stack([fm(inp["ssm_norm"][e]) for e in range(2)]).astype(f32)


def host_prep_odd(inp, sh):
    f32 = np.float32
    sh["mla_q_b"] = np.asarray(inp["mla_q_b"], f32)
    sh["mla_kv_b"] = np.asarray(inp["mla_kv_b"], f32)
    sh["mla_gain"] = np.stack([np.concatenate([fm(inp["mla_q_a_norm"][o]), fm(inp["mla_kv_a_norm"][o])], 1) for o in range(2)]).astype(f32)
    cos, sin, RT = rope_tables(64)
    sh["c_cos64"], sh["c_sin64"], sh["c_RT64"] = cos, sin, RT
    sh["c_ones256"] = np.full((128, 128), 1.0 / 256, f32)
    def fm_ud(v):
        out = np.zeros((128, 30), f32)
        v = np.asarray(v, f32)
        for i in range(24):
            out[:, i] = v[i * 128:(i + 1) * 128]
        for i in range(4):
            out[:96, 24 + i] = v[3072 + i * 96:3072 + (i + 1) * 96]
        out[:, 28] = v[3456:3584]
        out[:, 29] = v[3584:3712]
        return out
    sh["rw_mu"] = np.stack([fm_ud(inp["rwkv_mu"][o]) for o in range(2)])
    sh["rw_vec"] = np.stack([np.stack([fm(inp["rwkv_k_k"][o]), fm(inp["rwkv_k_a"][o]), fm(inp["rwkv_r_k"][o].reshape(-1)),
                                       fm(inp["rwkv_w0"][o][0]), fm(inp["rwkv_w0"][o][1]), fm(inp["rwkv_a0"][o][0]), fm(inp["rwkv_a0"][o][1])], 1)
                             for o in range(2)]).astype(f32)
    sh["rwkv_g2"] = np.asarray(inp["rwkv_g2"], f32)
    sh["rwkv_w2"] = np.asarray(inp["rwkv_w2"], f32)
    sh["rwkv_a2"] = np.asarray(inp["rwkv_a2"], f32)
    sh["rw_ln"] = np.stack([np.stack([inp["rwkv_ln_w"][o], inp["rwkv_ln_b"][o]]) for o in range(2)]).astype(f32)
    blk = np.zeros((128, 128), f32); blk[:64, :64] = 1; blk[64:, 64:] = 1
    sh["c_blk64"] = blk
    s_ = np.arange(64)[:, None]; t_ = np.arange(64)[None, :]
    sh["c_m64"] = np.ascontiguousarray(np.stack([s_ < t_, s_ <= t_, s_ > t_, s_ >= t_], 1).astype(f32))


def declare_inputs(Kk, sh, per0):
    for k, v in list(sh.items()) + list(per0.items()):
        Kk.din(k, v.shape)


def run(nc, sh, per, trace=False):
    in_maps = []
    for d in per:
        m = dict(sh)
        m.update(d)
        in_maps.append(m)
    return run_bass_kernel_spmd(nc, in_maps, core_ids=list(range(len(per))), trace=trace)

def build_program(sh, per0):
    nc = bass.Bass("TRN2", target_bir_lowering=False)
    with ExitStack() as es:
        Kk = K(nc, es)
        declare_inputs(Kk, sh, per0)
        P = Kk.P
        Se = dict(qT=Kk.dscr("qT", [8, 128, NT]), kT=Kk.dscr("kT", [2, 128, NT]), vM=Kk.dscr("vMe", [NT, 256]),
                  zT=Kk.dscr("zT", [1024, NT]), xbcT=Kk.dscr("xbcT", [1536, NT]), dtT=Kk.dscr("dtT", [32, NT]),
                  xcT=Kk.dscr("xcT", [1536, NT]), ysT=Kk.dscr("ysT", [1024, NT]))
        So = dict(qnT=Kk.dscr("qnT", [8, 128, NT]), qpT=Kk.dscr("qpT", [8, 64, NT]), knT=Kk.dscr("knT", [8, 128, NT]),
                  kpT=Kk.dscr("kpT", [64, NT]), vM=Kk.dscr("vMo", [NT, 1024]), udT=Kk.dscr("udT", [3712, NT]),
                  RW=Kk.dscr("RW", [2, 4, 1024, NT]), kbTM=Kk.dscr("kbTM", [2, NT, 1024]), bbTM=Kk.dscr("bbTM", [2, NT, 1024]),
                  vTM=Kk.dscr("vTM", [NT, 1024]), wC=Kk.dscr("wC", [2, 1024, NCH64]), gateT=Kk.dscr("gateT", [1024, NT]),
                  bonT=Kk.dscr("bonT", [1024, NT]), YfTM=Kk.dscr("YfTM", [NT, 1024]))
        yT = Kk.dscr("yT", [2048, NT])
        xA = Kk.dscr("xA", [2048, NT])
        xB = Kk.dscr("xB", [2048, NT])
        outT = Kk.dscr("outT", [2048, NLAT], out=True)
        load_consts(Kk)
        with ExitStack() as ses:
            P.stage_es = ses
            cp = P.sb("cp", [128, 2, 16, 512])
            for i, (t0_, n, isctx) in enumerate(tiles_main()):
                P.dma(cp[:, i % 2, :, 0:n], Kk.ins["xT0"][:, t0_:t0_ + n].rearrange("(c p) t -> p c t", p=128), writes=[(cp, i % 2)])
                P.dma(xA[:, t0_:t0_ + n].rearrange("(c p) t -> p c t", p=128), cp[:, i % 2, :, 0:n], reads=[(cp, i % 2)], qn="pool")
            P.barrier()
            P.stage_es = None
        cur, oth = xA, xB
        for l in range(4):
            last = (l == 3)
            stage_adaln(Kk, l)
            if l % 2 == 0:
                stage_A_even(Kk, l, cur, Se)
                stage_attn_even(Kk, Se, yT, not last)
                stage_ssd_conv(Kk, l, Se)
                stage_ssd(Kk, l, Se, yT, not last)
                stage_C1(Kk, l, cur, yT, Kk.ins["ev_w_out"][l // 2], not last)
            else:
                stage_A_odd(Kk, l, cur, So)
                stage_attn_odd(Kk, So, yT, not last)
                stage_rwkv_prep(Kk, l, So)
                stage_rwkv(Kk, l, So, yT, not last)
                stage_C1(Kk, l, cur, yT, Kk.ins["od_w_out"][l // 2], not last)
            if last:
                stage_C2(Kk, l, cur, oth, False, final_norm=Kk.ins["final_norm"], outT=outT)
            else:
                stage_C2(Kk, l, cur, oth, True)
            cur, oth = oth, cur
        P.emit_all()
    return nc


def kernel(**inputs):
    sh, per = host_prep(inputs, ncores=8)
    nc = build_program(sh, per[0])
    res = run(nc, sh, per, trace=False)
    out = np.stack([np.ascontiguousarray(res.results[b]["outT"].T) for b in range(4)], 0)
    return out.astype(np.float32)
```

```python
from concourse.bass_utils import run_bass_kernel_spmd
import numpy as np
import concourse.bass as bass
import concourse.mybir as mybir
from contextlib import ExitStack

F32 = mybir.dt.float32
BF16 = mybir.dt.bfloat16
AF = mybir.ActivationFunctionType
ALU = mybir.AluOpType
AX = mybir.AxisListType

ENGS = ("pe", "act", "dve", "pool", "sp")


class Buf:
    _n = 0

    def __init__(self, t, name):
        self.t = t
        self.name = name
        self.st = {}
        Buf._n += 1

    def __getitem__(self, idx):
        return self.t[idx]


class Prog:
    def __init__(self, nc, es):
        self.nc = nc
        self.es = es
        self.q = {e: [] for e in ENGS}
        self.cnt = {e: 0 for e in ENGS if e != "sp"}
        self.esem = {e: es.enter_context(nc.semaphore("s_" + e)) for e in ENGS if e != "sp"}
        self.NS = 12
        self.dsem = {qn: [es.enter_context(nc.semaphore("d_%s%d" % (qn, i))) for i in range(self.NS)]
                     for qn in ("sp", "pool")}
        self.dcnt = {"sp": 0, "pool": 0}
        self.dval = {qn: [0] * self.NS for qn in ("sp", "pool")}
        self.bar = es.enter_context(nc.semaphore("bar"))
        self.nbar = 0
        self.known = {e: {} for e in ENGS}
        self.semobj = {}
        for e in self.esem:
            self.semobj[("c", e)] = self.esem[e]
        for qn in self.dsem:
            for i, s in enumerate(self.dsem[qn]):
                self.semobj[("d", qn, i)] = s
        self.bufs = []
        self.n_ins = 0
        self.stage_es = None

    def sb(self, name, shape, dt=F32):
        self.n_sb = getattr(self, "n_sb", 0) + 1
        name = "%s_%d" % (name, self.n_sb)
        t = (self.stage_es or self.es).enter_context(self.nc.sbuf_tensor(name, list(shape), dt))
        b = Buf(t, name)
        self.bufs.append(b)
        return b

    def ps(self, name, shape, dt=F32):
        t = self.es.enter_context(self.nc.psum_tensor(name, list(shape), dt))
        b = Buf(t, name)
        self.bufs.append(b)
        return b

    def _need(self, eng, tok):
        sk, val = tok
        if self.known[eng].get(sk, 0) >= val:
            return None
        self.known[eng][sk] = val
        return (sk, val)

    def _deps(self, eng, reads, writes):
        waits = {}

        def add(tok):
            r = self._need(eng, tok)
            if r is not None:
                waits[r[0]] = max(waits.get(r[0], 0), r[1])

        def keys(b, k):
            if k is None:
                return list(b.st.keys())
            return [k, None]

        for (b, k) in reads:
            for kk in keys(b, k):
                st = b.st.get(kk)
                if st:
                    for tok in st[0]:
                        add(tok)
        for (b, k) in writes:
            for kk in keys(b, k):
                st = b.st.get(kk)
                if st:
                    for tok in st[0]:
                        add(tok)
                    for tok in st[1]:
                        add(tok)
        return list(waits.items())

    def _commit(self, tok, reads, writes):
        for (b, k) in writes:
            if k is None:
                b.st = {None: ([tok], [])}
            else:
                b.st[k] = ([tok], [])
        for (b, k) in reads:
            st = b.st.setdefault(k, ([], []))
            rl = st[1]
            rl[:] = [t for t in rl if t[0] != tok[0]]
            rl.append(tok)

    @staticmethod
    def _norm(lst):
        out = []
        for x in lst or []:
            if isinstance(x, Buf):
                out.append((x, None))
            else:
                out.append(x)
        return out

    def op(self, eng, fn, reads=None, writes=None):
        reads = self._norm(reads)
        writes = self._norm(writes)
        waits = self._deps(eng, reads, writes)
        self.cnt[eng] += 1
        tok = (("c", eng), self.cnt[eng])
        sem = self.esem[eng]
        semobj = self.semobj

        def emit(e, waits=waits, fn=fn, sem=sem):
            for sk, v in waits:
                e.wait_ge(semobj[sk], v)
            fn(e).then_inc(sem, 1)

        self.q[eng].append(emit)
        self._commit(tok, reads, writes)
        self.n_ins += 1
        return tok

    def dma(self, out_ap, in_ap, reads=None, writes=None, qn="sp"):
        reads = self._norm(reads)
        writes = self._norm(writes)
        waits = self._deps(qn, reads, writes)
        i = self.dcnt[qn] % self.NS
        self.dcnt[qn] += 1
        prev = self.dval[qn][i]
        sk = ("d", qn, i)
        r = self._need(qn, (sk, prev)) if prev > 0 else None
        if r is not None:
            waits.append(r)
        self.dval[qn][i] = prev + 16
        tok = (sk, prev + 16)
        sem = self.semobj[sk]
        semobj = self.semobj

        def emit(e, waits=waits, sem=sem, out_ap=out_ap, in_ap=in_ap):
            for k, v in waits:
                e.wait_ge(semobj[k], v)
            e.dma_start(out=out_ap, in_=in_ap).then_inc(sem, 16)

        self.q[qn].append(emit)
        self._commit(tok, reads, writes)
        self.n_ins += 1
        return tok

    def barrier(self):
        self.nbar += 1
        nb = self.nbar
        semobj = self.semobj
        bar = self.bar
        for e in ENGS:
            waits = []
            if e in self.cnt:
                if self.cnt[e] > 0:
                    waits.append((("c", e), self.cnt[e]))
            if e in self.dsem:
                for i in range(self.NS):
                    if self.dval[e][i] > 0:
                        waits.append((("d", e, i), self.dval[e][i]))

            def emit(en, waits=waits, nb=nb):
                for k, v in waits:
                    en.wait_ge(semobj[k], v)
                en.sem_inc(bar, 1)
                en.wait_ge(bar, len(ENGS) * nb)

            self.q[e].append(emit)
        allk = {}
        for e in self.cnt:
            allk[("c", e)] = self.cnt[e]
        for qn in self.dsem:
            for i in range(self.NS):
                allk[("d", qn, i)] = self.dval[qn][i]
        for e in ENGS:
            self.known[e] = dict(allk)
        for b in self.bufs:
            b.st = {}

    def emit_all(self):
        nc = self.nc
        q = self.q
        with nc.Block() as block:
            @block.sync
            def _(e):
                for f in q["sp"]:
                    f(e)

            @block.tensor
            def _(e):
                for f in q["pe"]:
                    f(e)

            @block.scalar
            def _(e):
                for f in q["act"]:
                    f(e)

            @block.vector
            def _(e):
                for f in q["dve"]:
                    f(e)

            @block.gpsimd
            def _(e):
                for f in q["pool"]:
                    f(e)

import numpy as np

D = 2048
NKD = 16
NCTX = 256
NLAT = 4096
NT = NCTX + NLAT
DFF = 5632
EVEN_IN = 4128
ODD_IN = 4544
EPS = 1e-6


def tiles_main():
    t = [(0, NCTX, True)]
    for i in range(NLAT // 512):
        t.append((NCTX + 512 * i, 512, False))
    return t


def tiles_ffn():
    t = [(0, NCTX, 0, 0)]
    nt = 9
    base, rem = divmod(NLAT, nt)
    o = NCTX
    for i in range(nt):
        n = base + (1 if i < rem else 0)
        t.append((o, n, 0 if i == 0 else 1, 0 if i == nt - 1 else 1))
        o += n
    return t


class K:
    def __init__(self, nc, es):
        self.nc = nc
        self.P = Prog(nc, es)
        P = self.P
        self.ins = {}
        self.psb = [P.ps("ps%d" % i, [128, 512]) for i in range(8)]
        self.wb = []
        self.wbi = 0
        self.wbh = []
        self.wbhi = 0
        self.ones = P.sb("ones", [128, 128])
        self.ident = P.sb("ident", [128, 128])
        self.consts_loaded = False

    def din(self, name, shape):
        t = self.nc.dram_tensor(name, list(shape), F32, kind="ExternalInput").ap()
        self.ins[name] = t
        return t

    def dscr(self, name, shape, out=False):
        return self.nc.dram_tensor(name, list(shape), F32,
                                   kind="ExternalOutput" if out else "Internal").ap()


def alloc_wb(Kk, n_stage, n_bf=3):
    P = Kk.P
    Kk.wb = [P.sb("wb%d" % i, [128, 16, 256]) for i in range(n_stage)]
    Kk.wbh = [P.sb("wbh%d" % i, [128, 16, 256], BF16) for i in range(n_bf)] if n_bf else []


def _cast_w(Kk, wb, nfull, bw):
    P = Kk.P
    wh = Kk.wbh[Kk.wbhi % len(Kk.wbh)]
    eng = "act"
    Kk.wbhi += 1
    if eng == "pool":
        P.op("pool", lambda e: e.tensor_copy(wh[:, 0:nfull, 0:bw], wb[:, 0:nfull, 0:bw]), reads=[wb], writes=[wh])
    else:
        P.op("act", lambda e: e.activation(wh[:, 0:nfull, 0:bw], wb[:, 0:nfull, 0:bw], AF.Identity), reads=[wb], writes=[wh])
    return wh


def gemm_fm(Kk, W, krows, chunks, rhs_fn, ntok, evac, ps_ids=(0, 1), rhs_reads=(), lowp=False):
    P = Kk.P
    nk = (krows + 127) // 128
    nfull = krows // 128
    blocks = []
    cur = []
    for i, (c0, cw) in enumerate(chunks):
        if cur and (cur[0][1] + sum(c[2] for c in cur) == c0) and (sum(c[2] for c in cur) + cw <= 256):
            cur.append((i, c0, cw))
        else:
            if cur:
                blocks.append(cur)
            cur = [(i, c0, cw)]
    if cur:
        blocks.append(cur)
    pi = 0
    for blk in blocks:
        b0 = blk[0][1]
        bw = sum(c[2] for c in blk)
        wb = Kk.wb[Kk.wbi % len(Kk.wb)]
        Kk.wbi += 1
        if nfull > 0:
            src = W[0:nfull * 128, b0:b0 + bw].rearrange("(kc p) c -> p kc c", p=128)
            P.dma(wb[:, 0:nfull, 0:bw], src, writes=[wb])
        if nfull < nk:
            kp = krows - nfull * 128
            P.dma(wb[0:kp, nfull, 0:bw], W[nfull * 128:krows, b0:b0 + bw], writes=[wb])
        if lowp:
            assert nfull == nk
            wb = _cast_w(Kk, wb, nfull, bw)
        for (i, c0, cw) in blk:
            ps = Kk.psb[ps_ids[pi % len(ps_ids)]]
            pi += 1
            off = c0 - b0
            for kc in range(nk):
                kp = min(128, krows - kc * 128)
                rhs = rhs_fn(kc, kp)
                P.op("pe", lambda e, ps=ps, wb=wb, kc=kc, kp=kp, off=off, cw=cw, rhs=rhs:
                     e.matmul(ps[0:cw, 0:ntok], wb[0:kp, kc, off:off + cw], rhs,
                              start=(kc == 0), stop=(kc == nk - 1)),
                     reads=[wb] + list(rhs_reads), writes=[ps])
            evac(i, ps, cw)


def gemm_tm(Kk, W, krows, c0, ncols, lhs_fn, ntok, evac, ps_ids=(0, 1), lhs_reads=(), lowp=False):
    P = Kk.P
    nk = (krows + 127) // 128
    nfull = krows // 128
    pi = 0
    for cb0 in range(0, ncols, 256):
        cbw = min(256, ncols - cb0)
        wb = Kk.wb[Kk.wbi % len(Kk.wb)]
        Kk.wbi += 1
        if nfull > 0:
            src = W[0:nfull * 128, c0 + cb0:c0 + cb0 + cbw].rearrange("(kc p) c -> p kc c", p=128)
            P.dma(wb[:, 0:nfull, 0:cbw], src, writes=[wb])
        if nfull < nk:
            kp = krows - nfull * 128
            P.dma(wb[0:kp, nfull, 0:cbw], W[nfull * 128:krows, c0 + cb0:c0 + cb0 + cbw], writes=[wb])
        if lowp:
            assert nfull == nk
            wb = _cast_w(Kk, wb, nfull, cbw)
        for tb in range((ntok + 127) // 128):
            t0 = tb * 128
            tn = min(128, ntok - t0)
            ps = Kk.psb[ps_ids[pi % len(ps_ids)]]
            pi += 1
            for kc in range(nk):
                kp = min(128, krows - kc * 128)
                lhs = lhs_fn(kc, kp, t0, tn)
                P.op("pe", lambda e, ps=ps, wb=wb, kc=kc, kp=kp, t0=t0, tn=tn, cbw=cbw, lhs=lhs:
                     e.matmul(ps[0:tn, 0:cbw], lhs, wb[0:kp, kc, 0:cbw],
                              start=(kc == 0), stop=(kc == nk - 1)),
                     reads=[wb] + list(lhs_reads), writes=[ps])
            evac(tb, ps, tn, cb0, cbw)


def colsum_bcast(Kk, src_fn, nchunks, ntok, ps, sq, scale_ones, src_reads, kp_fn=None):
    P = Kk.P
    for c in range(nchunks):
        kp = 128 if kp_fn is None else kp_fn(c)
        s = c % 4
        src = src_fn(c, kp)
        P.op("act", lambda e, c=c, s=s, kp=kp, src=src: e.activation(sq[0:kp, s, 0:ntok], src, AF.Square),
             reads=list(src_reads), writes=[(sq, s)])
        P.op("pe", lambda e, c=c, s=s, kp=kp: e.matmul(ps[:, 0:ntok], scale_ones[0:kp, :], sq[0:kp, s, 0:ntok],
                                                        start=(c == 0), stop=(c == nchunks - 1)),
             reads=[(sq, s), scale_ones], writes=[ps])


def rstd_from(Kk, ps, ntok, rstd, eps):
    P = Kk.P
    P.op("act", lambda e: e.activation(rstd[:, 0:ntok], ps[:, 0:ntok], AF.Sqrt, bias=Kk.epsb[:, 0:1] if eps == EPS else Kk.eps2b[:, 0:1], scale=1.0),
         reads=[ps, Kk.epsb], writes=[rstd])
    P.op("dve", lambda e: e.reciprocal(rstd[:, 0:ntok], rstd[:, 0:ntok]), reads=[rstd], writes=[rstd])


def load_consts(Kk):
    P = Kk.P
    P.dma(Kk.ones[:, :], Kk.ins["c_ones"][:, :], writes=[Kk.ones])
    P.dma(Kk.ident[:, :], Kk.ins["c_ident"][:, :], writes=[Kk.ident])
    Kk.onesD = P.sb("onesD", [128, 128])
    P.dma(Kk.onesD[:, :], Kk.ins["c_onesD"][:, :], writes=[Kk.onesD])
    Kk.epsb = P.sb("epsb", [128, 4])
    P.dma(Kk.epsb[:, :], Kk.ins["c_eps"][:, :], writes=[Kk.epsb])
    Kk.modT = P.sb("modT", [128, 96, 2])
    Kk.Amix = P.sb("Amix", [128, 16, 2])
    Kk.Affn = P.sb("Affn", [128, 16, 2])
    Kk.actT = P.sb("actT", [128, 16, 2])
    P.dma(Kk.actT[:, :, :], Kk.ins["cT"][:, :, :], writes=[Kk.actT])
    P.op("act", lambda e: e.activation(Kk.actT[:, :, :], Kk.actT[:, :, :], AF.Silu), reads=[Kk.actT], writes=[Kk.actT])


def stage_adaln(Kk, l):
    P = Kk.P
    with ExitStack() as ses:
        P.stage_es = ses
        alloc_wb(Kk, 4, 0)
        adab = P.sb("adab", [128, 96])
        nrm = P.sb("nrm", [128, 2, 16, 2])
        P.dma(adab[:, :], Kk.ins["ada_b"][l], writes=[adab])
        P.dma(nrm[:, 0], Kk.ins["norm_mix"][l], writes=[nrm])
        P.dma(nrm[:, 1], Kk.ins["norm_ffn"][l], writes=[nrm])
        W = Kk.ins["ada_w"][l]
        chunks = [(i * 128, 128) for i in range(96)]

        def evac(i, ps, cw):
            P.op("act", lambda e, i=i, ps=ps: e.activation(Kk.modT[:, i, :], ps[:, 0:2], AF.Identity,
                                                            bias=adab[:, i:i + 1], scale=1.0),
                 reads=[ps, adab], writes=[(Kk.modT, i)])

        gemm_fm(Kk, W, D, chunks, lambda kc, kp: Kk.actT[0:kp, kc, :], 2, evac, rhs_reads=[Kk.actT])
        allm = [(Kk.modT, i) for i in range(96)]
        P.op("dve", lambda e: e.scalar_tensor_tensor(Kk.Amix[:, :, :], Kk.modT[:, 16:32, :], 1.0, nrm[:, 0], ALU.add, ALU.mult),
             reads=allm + [nrm], writes=[Kk.Amix])
        P.op("dve", lambda e: e.scalar_tensor_tensor(Kk.Affn[:, :, :], Kk.modT[:, 64:80, :], 1.0, nrm[:, 1], ALU.add, ALU.mult),
             reads=allm + [nrm], writes=[Kk.Affn])
        P.barrier()
        P.stage_es = None


def modulate(Kk, xt, ht, ntok, A, Bidx, m, sq, rstd, ps, tmpf=None):
    P = Kk.P
    colsum_bcast(Kk, lambda c, kp: xt[:, c, 0:ntok], 16, ntok, ps, sq, Kk.onesD, [xt])
    rstd_from(Kk, ps, ntok, rstd, EPS)
    allm = [(Kk.modT, i) for i in range(96)]
    for c in range(16):
        if tmpf is None:
            P.op("dve", lambda e, c=c: e.scalar_tensor_tensor(ht[:, c, 0:ntok], xt[:, c, 0:ntok], A[:, c, m:m + 1],
                                                               rstd[:, 0:ntok], ALU.mult, ALU.mult),
                 reads=[xt, A, rstd], writes=[(ht, c)])
            P.op("pool", lambda e, c=c: e.tensor_scalar(ht[:, c, 0:ntok], ht[:, c, 0:ntok], Kk.modT[:, Bidx + c, m:m + 1], None, ALU.add),
                 reads=[(ht, c), (Kk.modT, Bidx + c)], writes=[(ht, c)])
        else:
            s_ = c % 2
            P.op("dve", lambda e, c=c, s_=s_: e.scalar_tensor_tensor(tmpf[:, s_, 0:ntok], xt[:, c, 0:ntok], A[:, c, m:m + 1],
                                                                      rstd[:, 0:ntok], ALU.mult, ALU.mult),
                 reads=[xt, A, rstd], writes=[(tmpf, s_)])
            P.op("pool", lambda e, c=c, s_=s_: e.tensor_scalar(ht[:, c, 0:ntok], tmpf[:, s_, 0:ntok], Kk.modT[:, Bidx + c, m:m + 1], None, ALU.add),
                 reads=[(tmpf, s_), (Kk.modT, Bidx + c)], writes=[(ht, c)])


def stage_A(Kk, l, xT, W, plan, nsteps_extra=None):
    P = Kk.P
    with ExitStack() as ses:
        P.stage_es = ses
        alloc_wb(Kk, 3, 3)
        xt = P.sb("xt", [128, 16, 512])
        ht = P.sb("ht", [128, 16, 512], BF16)
        htf = P.sb("htf", [128, 2, 512])
        sq = P.sb("sq", [128, 4, 512])
        rstd = P.sb("rstd", [128, 512])
        Kk.ev = P.sb("ev", [128, 4, 512])
        Kk.evi = 0
        Kk.stA = dict(xt=xt, ht=ht, sq=sq, rstd=rstd)
        if nsteps_extra:
            nsteps_extra("alloc")
        for (t0, n, isctx) in tiles_main():
            m = 1 if isctx else 0
            P.dma(xt[:, :, 0:n], xT[:, t0:t0 + n].rearrange("(c p) t -> p c t", p=128), writes=[xt])
            modulate(Kk, xt, ht, n, Kk.Amix, 0, m, sq, rstd, Kk.psb[7], tmpf=htf)
            hreads = [(ht, c) for c in range(16)]
            for g in plan:
                if g[0] == "fm":
                    gemm_fm(Kk, W, D, g[1], lambda kc, kp: ht[0:kp, kc, 0:n], n, g[2](t0, n, isctx), rhs_reads=hreads, lowp=True)
                elif g[0] == "tm":
                    gemm_tm(Kk, W, D, g[1], g[2], lambda kc, kp, a, tn: ht[0:kp, kc, a:a + tn], n, g[3](t0, n, isctx),
                            lhs_reads=hreads, lowp=True)
                else:
                    g[1](t0, n, isctx)
        P.barrier()
        P.stage_es = None


def ev_store(Kk, ps, rows, n, dst_ap, func=None, eng="act"):
    P = Kk.P
    s = Kk.evi % 4
    Kk.evi += 1
    ev = Kk.ev
    if eng == "act":
        P.op("act", lambda e: e.activation(ev[0:rows, s, 0:n], ps[0:rows, 0:n], func or AF.Identity),
             reads=[ps], writes=[(ev, s)])
    else:
        P.op("dve", lambda e: e.tensor_copy(ev[0:rows, s, 0:n], ps[0:rows, 0:n]), reads=[ps], writes=[(ev, s)])
    P.dma(dst_ap, ev[0:rows, s, 0:n], reads=[(ev, s)], qn="pool")


def stage_C1(Kk, l, xT, yT, Wout, do_ctx):
    P = Kk.P
    with ExitStack() as ses:
        P.stage_es = ses
        alloc_wb(Kk, 4, 3)
        xt = P.sb("xt", [128, 16, 512])
        yt = P.sb("yt", [128, 16, 512])
        ytb = P.sb("ytb", [128, 16, 512], BF16)
        for (t0, n, isctx) in tiles_main():
            if isctx and not do_ctx:
                continue
            m = 1 if isctx else 0
            P.dma(xt[:, :, 0:n], xT[:, t0:t0 + n].rearrange("(c p) t -> p c t", p=128), writes=[(xt, c) for c in range(16)])
            P.dma(yt[:, :, 0:n], yT[:, t0:t0 + n].rearrange("(c p) t -> p c t", p=128), writes=[yt])
            for c4 in range(4):
                if c4 % 2 == 0:
                    P.op("pool", lambda e, c4=c4, n=n: e.tensor_copy(ytb[:, c4 * 4:(c4 + 1) * 4, 0:n], yt[:, c4 * 4:(c4 + 1) * 4, 0:n]),
                         reads=[yt], writes=[(ytb, c4)])
                else:
                    P.op("act", lambda e, c4=c4, n=n: e.activation(ytb[:, c4 * 4:(c4 + 1) * 4, 0:n], yt[:, c4 * 4:(c4 + 1) * 4, 0:n], AF.Identity),
                         reads=[yt], writes=[(ytb, c4)])

            def evac(i, ps, cw, n=n, m=m):
                P.op("dve", lambda e: e.scalar_tensor_tensor(xt[:, i, 0:n], ps[:, 0:n], Kk.modT[:, 32 + i, m:m + 1],
                                                              xt[:, i, 0:n], ALU.mult, ALU.add),
                     reads=[ps, (xt, i), (Kk.modT, 32 + i)], writes=[(xt, i)])

            gemm_fm(Kk, Wout, D, [(i * 128, 128) for i in range(16)], lambda kc, kp: ytb[0:kp, kc, 0:n], n, evac,
                    rhs_reads=[(ytb, c4) for c4 in range(4)], lowp=True)
            P.dma(xT[:, t0:t0 + n].rearrange("(c p) t -> p c t", p=128), xt[:, :, 0:n],
                  reads=[(xt, c) for c in range(16)], qn="pool")
        P.barrier()
        P.stage_es = None


def stage_C2(Kk, l, xTa, xTb, do_ctx, final_norm=None, outT=None):
    P = Kk.P
    Wup = Kk.ins["ffn_w_up"][l]
    Wdn = Kk.ins["ffn_w_down"][l]
    with ExitStack() as ses:
        P.stage_es = ses
        alloc_wb(Kk, 3 if final_norm is not None else 5, 3)
        xt = P.sb("xt", [128, 16, 512])
        ht = P.sb("ht", [128, 16, 512], BF16)
        htf = P.sb("htf", [128, 2, 512])
        gt = P.sb("gt", [128, 11, 512], BF16)
        sq = P.sb("sq", [128, 4, 512])
        rstd = P.sb("rstd", [128, 512])
        acc = P.sb("acc", [128, 2, 2, 512])
        cw_ = P.sb("convw", [128, 3, 88])
        cb_ = P.sb("convb", [128, 88])
        P.dma(cw_[:, :, :], Kk.ins["ffn_conv_w"][l], writes=[cw_])
        P.dma(cb_[:, :], Kk.ins["ffn_conv_b"][l], writes=[cb_])
        if final_norm is not None:
            ot_ = P.sb("otf", [128, 16, 512])
            fnw = P.sb("fnw", [128, 16])
            P.dma(fnw[:, :], final_norm, writes=[fnw])
        for (o0, n, lh, rh) in tiles_ffn():
            isctx = o0 < NCTX
            if isctx and not do_ctx:
                continue
            m = 1 if isctx else 0
            nn = n + lh + rh
            a0 = o0 - lh
            xk = [(xt, c) for c in range(16)]
            P.dma(xt[:, :, 0:nn], xTa[:, a0:a0 + nn].rearrange("(c p) t -> p c t", p=128), writes=xk)
            modulate(Kk, xt, ht, nn, Kk.Affn, 48, m, sq, rstd, Kk.psb[7], tmpf=htf)
            hreads = [(ht, c) for c in range(16)]
            for grp in range(4):
                chunks = []
                for j in range(11):
                    chunks.append(((grp * 11 + j) * 128, 128))
                    chunks.append((DFF + (grp * 11 + j) * 128, 128))

                def evac(i, ps, cw, grp=grp, n=n, lh=lh, rh=rh):
                    j = i // 2
                    isval = i % 2
                    fc = (grp * 11 + j) + (44 if isval else 0)
                    s = j % 2
                    a = acc[:, isval, s, :]
                    key = (acc, (isval, s))
                    P.op("act", lambda e: e.activation(a[:, 0:n], ps[:, lh:lh + n], AF.Identity,
                                                        bias=cb_[:, fc:fc + 1], scale=cw_[:, 1, fc:fc + 1]),
                         reads=[ps, cw_, cb_], writes=[key])
                    lo = 0 if lh else 1
                    P.op("dve", lambda e: e.scalar_tensor_tensor(a[:, lo:n], ps[:, lh - 1 + lo:lh - 1 + n], cw_[:, 0, fc:fc + 1],
                                                                  a[:, lo:n], ALU.mult, ALU.add),
                         reads=[ps, cw_, key], writes=[key])
                    hi = 0 if rh else 1
                    P.op("dve", lambda e: e.scalar_tensor_tensor(a[:, 0:n - hi], ps[:, lh + 1:lh + 1 + n - hi], cw_[:, 2, fc:fc + 1],
                                                                  a[:, 0:n - hi], ALU.mult, ALU.add),
                         reads=[ps, cw_, key], writes=[key])
                    if isval:
                        kg = (acc, (0, s))
                        P.op("act", lambda e: e.activation(acc[:, 0, s, 0:n], acc[:, 0, s, 0:n], AF.Silu),
                             reads=[kg], writes=[kg])
                        P.op("pool", lambda e: e.tensor_tensor(gt[:, j, 0:n], acc[:, 0, s, 0:n], acc[:, 1, s, 0:n], ALU.mult),
                             reads=[kg, key], writes=[(gt, j)])

                gemm_fm(Kk, Wup, D, chunks, lambda kc, kp: ht[0:kp, kc, 0:nn], nn, evac, ps_ids=(0, 1, 2, 3), rhs_reads=hreads, lowp=True)
                Wd = Wdn[grp * 11 * 128:(grp + 1) * 11 * 128, :]

                def evac2(i, ps, cw, n=n, lh=lh, m=m):
                    P.op("dve", lambda e: e.scalar_tensor_tensor(xt[:, i, lh:lh + n], ps[:, 0:n], Kk.modT[:, 80 + i, m:m + 1],
                                                                  xt[:, i, lh:lh + n], ALU.mult, ALU.add),
                         reads=[ps, (xt, i), (Kk.modT, 80 + i)], writes=[(xt, i)])

                gemm_fm(Kk, Wd, 11 * 128, [(i * 128, 128) for i in range(16)], lambda kc, kp: gt[0:kp, kc, 0:n], n, evac2,
                        ps_ids=(4, 5), rhs_reads=[(gt, j) for j in range(11)], lowp=True)
            if final_norm is None:
                P.dma(xTb[:, o0:o0 + n].rearrange("(c p) t -> p c t", p=128), xt[:, :, lh:lh + n], reads=xk, qn="pool")
            else:
                if not isctx:
                    colsum_bcast(Kk, lambda c, kp: xt[:, c, lh:lh + n], 16, n, Kk.psb[7], sq, Kk.onesD, xk)
                    rstd_from(Kk, Kk.psb[7], n, rstd, EPS)
                    for c in range(16):
                        P.op("dve", lambda e, c=c, n=n, lh=lh: e.scalar_tensor_tensor(ot_[:, c, 0:n], xt[:, c, lh:lh + n], fnw[:, c:c + 1],
                                                                           rstd[:, 0:n], ALU.mult, ALU.mult),
                             reads=[(xt, c), fnw, rstd], writes=[(ot_, c)])
                    P.dma(outT[:, o0 - NCTX:o0 - NCTX + n].rearrange("(c p) t -> p c t", p=128), ot_[:, :, 0:n],
                          reads=[(ot_, c) for c in range(16)], qn="pool")
        P.barrier()
        P.stage_es = None


A_SCALE = 128 ** -0.5


def rope_apply(Kk, src, rows, n, cosb, sinb, RT, ps, dst, tmp):
    P = Kk.P
    sbuf, skey, sap = src
    dbuf, dkey, dap = dst
    tbuf, tkey, tap = tmp
    P.op("pe", lambda e: e.matmul(ps[0:rows, 0:n], RT[0:rows, 0:rows], sap, start=True, stop=True),
         reads=[(sbuf, skey), RT], writes=[ps])
    P.op("dve", lambda e: e.tensor_tensor(tap, ps[0:rows, 0:n], sinb[0:rows, 0:n], ALU.mult),
         reads=[ps, sinb], writes=[(tbuf, tkey)])
    P.op("pool", lambda e: e.tensor_tensor(dap, sap, cosb[0:rows, 0:n], ALU.mult),
         reads=[(sbuf, skey), cosb], writes=[(dbuf, dkey)])
    P.op("pool", lambda e: e.tensor_tensor(dap, dap, tap, ALU.add),
         reads=[(dbuf, dkey), (tbuf, tkey)], writes=[(dbuf, dkey)])


def stage_A_even(Kk, l, xT, S):
    P = Kk.P
    e_ = l // 2
    W = Kk.ins["ev_w_in"][e_]
    st = {}

    def extra(_):
        st["qs"] = P.sb("qs", [128, 2, 512])
        st["qn"] = P.sb("qn", [128, 2, 512])
        st["qo"] = P.sb("qo", [128, 2, 512])
        st["tmp"] = P.sb("tmpr", [128, 2, 512])
        st["cos"] = P.sb("cosb", [128, 512])
        st["sin"] = P.sb("sinb", [128, 512])
        st["gain"] = P.sb("qkgain", [128, 2])
        st["RT"] = P.sb("RT", [128, 128])
        st["onesH"] = P.sb("onesH", [128, 128])
        st["rs2"] = P.sb("rs2", [128, 2, 512])
        st["dtb"] = P.sb("dtb", [32, 2])
        P.dma(st["gain"][:, :], Kk.ins["qk_gain"][e_], writes=[st["gain"]])
        P.dma(st["RT"][:, :], Kk.ins["c_RT128"][:, :], writes=[st["RT"]])
        P.dma(st["onesH"][:, :], Kk.ins["c_onesH"][:, :], writes=[st["onesH"]])
        P.dma(st["dtb"][:, :], Kk.ins["dt_ba"][e_], writes=[st["dtb"]])
        st["i"] = 0

    def tile_pre(t0, n, isctx):
        if not isctx:
            P.dma(st["cos"][:, 0:n], Kk.ins["c_cos128"][:, t0 - NCTX:t0 - NCTX + n], writes=[st["cos"]])
            P.dma(st["sin"][:, 0:n], Kk.ins["c_sin128"][:, t0 - NCTX:t0 - NCTX + n], writes=[st["sin"]])

    def mk_qk(t0, n, isctx):
        def evac(i, ps, cw):
            s = st["i"] % 2
            st["i"] += 1
            qs, qn, qo, tmp, rs2 = st["qs"], st["qn"], st["qo"], st["tmp"], st["rs2"]
            g = 0 if i < 8 else 1
            ps2 = Kk.psb[4 + s]
            P.op("act", lambda e: e.activation(qs[:, s, 0:n], ps[:, 0:n], AF.Identity), reads=[ps], writes=[(qs, s)])
            P.op("act", lambda e: e.activation(qn[:, s, 0:n], ps[:, 0:n], AF.Square), reads=[ps], writes=[(qn, s)])
            P.op("pe", lambda e: e.matmul(ps2[:, 0:n], st["onesH"][:, :], qn[:, s, 0:n], start=True, stop=True),
                 reads=[(qn, s), st["onesH"]], writes=[ps2])
            P.op("act", lambda e: e.activation(rs2[:, s, 0:n], ps2[:, 0:n], AF.Sqrt, bias=Kk.epsb[:, 0:1], scale=1.0),
                 reads=[ps2, Kk.epsb], writes=[(rs2, s)])
            P.op("dve", lambda e: e.reciprocal(rs2[:, s, 0:n], rs2[:, s, 0:n]), reads=[(rs2, s)], writes=[(rs2, s)])
            P.op("dve", lambda e: e.scalar_tensor_tensor(qn[:, s, 0:n], qs[:, s, 0:n], st["gain"][:, g:g + 1], rs2[:, s, 0:n],
                                                          ALU.mult, ALU.mult),
                 reads=[(qs, s), st["gain"], (rs2, s)], writes=[(qn, s)])
            dst = S["qT"][i] if i < 8 else S["kT"][i - 8]
            if isctx:
                P.dma(dst[:, t0:t0 + n], qn[:, s, 0:n], reads=[(qn, s)], qn="pool")
            else:
                rope_apply(Kk, (qn, s, qn[:, s, 0:n]), 128, n, st["cos"], st["sin"], st["RT"], ps2,
                           (qo, s, qo[:, s, 0:n]), (tmp, s, tmp[:, s, 0:n]))
                P.dma(dst[:, t0:t0 + n], qo[:, s, 0:n], reads=[(qo, s)], qn="pool")
        return evac

    def mk_v(t0, n, isctx):
        def evac(tb, ps, tn, cb0, cbw):
            ev_store(Kk, ps, tn, cbw, S["vM"][t0 + tb * 128:t0 + tb * 128 + tn, cb0:cb0 + cbw])
        return evac

    def mk_z(t0, n, isctx):
        def evac(i, ps, cw):
            ev_store(Kk, ps, cw, n, S["zT"][i * 128:i * 128 + cw, t0:t0 + n], func=AF.Silu)
        return evac

    def mk_xbc(t0, n, isctx):
        def evac(i, ps, cw):
            ev_store(Kk, ps, cw, n, S["xbcT"][i * 128:i * 128 + cw, t0:t0 + n])
        return evac

    def mk_dt(t0, n, isctx):
        def evac(i, ps, cw):
            s = Kk.evi % 4
            Kk.evi += 1
            ev = Kk.ev
            P.op("act", lambda e: e.activation(ev[0:32, s, 0:n], ps[0:32, 0:n], AF.Exp, bias=st["dtb"][:, 0:1], scale=1.0),
                 reads=[ps, st["dtb"]], writes=[(ev, s)])
            P.op("act", lambda e: e.activation(ev[0:32, s, 0:n], ev[0:32, s, 0:n], AF.Ln, bias=Kk.epsb[0:32, 3:4], scale=1.0),
                 reads=[(ev, s), Kk.epsb], writes=[(ev, s)])
            P.dma(S["dtT"][0:32, t0:t0 + n], ev[0:32, s, 0:n], reads=[(ev, s)], qn="pool")
        return evac

    qk_chunks = [(i * 128, 128) for i in range(10)]
    z_chunks = [(1536 + i * 128, 128) for i in range(8)]
    xbc_chunks = [(2560 + i * 128, 128) for i in range(12)]
    dt_chunks = [(4096, 32)]
    plan = [("post", tile_pre), ("fm", qk_chunks, mk_qk), ("tm", 1280, 256, mk_v),
            ("fm", z_chunks, mk_z), ("fm", xbc_chunks, mk_xbc), ("fm", dt_chunks, mk_dt)]
    stage_A(Kk, l, xT, W, plan, nsteps_extra=extra)


def attention(Kk, nheads, kv_of, load_k, load_v, load_q, kparts, scale, yT, row0, do_ctx):
    P = Kk.P
    pt = Kk.att["pt"]
    ot = Kk.att["ot"]
    rd = Kk.att["rd"]
    cur_kv = None
    it = 0
    pti = 0
    for h in range(nheads):
        kvh = kv_of(h)
        if kvh != cur_kv:
            kbufs = load_k(kvh)
            vbuf = load_v(kvh)
            cur_kv = kvh
        for (t0, n, isctx) in tiles_main():
            if isctx and not do_ctx:
                continue
            pti = _attn_tile(Kk, h, t0, n, isctx, it, pti, kbufs, vbuf, load_q, kparts, scale, yT, row0)
            it += 1


def _attn_tile(Kk, h, t0, n, isctx, it, pti, kbufs, vbuf, load_q, kparts, scale, yT, row0):
    P = Kk.P
    pt = Kk.att["pt"]
    ot = Kk.att["ot"]
    rd = Kk.att["rd"]
    qaps, qreads = load_q(h, t0, n, it % 2)
    ps_o = Kk.psb[2 + it % 2]
    ps_d = Kk.psb[4 + it % 2]
    jt = list(range(2)) if isctx else list(range(NT // 128))
    for ji, j in enumerate(jt):
        ps_s = Kk.psb[pti % 2]
        sl = pti % 3
        pti += 1
        for pi_, rows in enumerate(kparts):
            kb = kbufs[pi_]
            P.op("pe", lambda e, ps_s=ps_s, kb=kb, rows=rows, j=j, qa=qaps[pi_], pi_=pi_:
                 e.matmul(ps_s[:, 0:n], kb[0:rows, j * 128:(j + 1) * 128], qa,
                          start=(pi_ == 0), stop=(pi_ == len(kparts) - 1)),
                 reads=[kb] + qreads, writes=[ps_s])
        P.op("act", lambda e, ps_s=ps_s, sl=sl: e.activation(pt[:, sl, 0:n], ps_s[:, 0:n], AF.Exp, scale=scale),
             reads=[ps_s], writes=[(pt, sl)])
        P.op("pe", lambda e, sl=sl, j=j, ji=ji: e.matmul(ps_o[:, 0:n], vbuf[:, j, :], pt[:, sl, 0:n],
                                                         start=(ji == 0), stop=(ji == len(jt) - 1)),
             reads=[vbuf, (pt, sl)], writes=[ps_o])
        P.op("pe", lambda e, sl=sl, ji=ji: e.matmul(ps_d[:, 0:n], Kk.att["onesb"][:, :], pt[:, sl, 0:n],
                                                    start=(ji == 0), stop=(ji == len(jt) - 1)),
             reads=[Kk.att["onesb"], (pt, sl)], writes=[ps_d])
    s = it % 2
    P.op("dve", lambda e: e.reciprocal(rd[:, s, 0:n], ps_d[:, 0:n]), reads=[ps_d], writes=[(rd, s)])
    P.op("dve", lambda e: e.tensor_tensor(ot[:, s, 0:n], ps_o[:, 0:n], rd[:, s, 0:n], ALU.mult),
         reads=[ps_o, (rd, s)], writes=[(ot, s)])
    P.dma(yT[row0 + h * 128:row0 + (h + 1) * 128, t0:t0 + n], ot[:, s, 0:n], reads=[(ot, s)], qn="pool")
    return pti


def stage_attn_even(Kk, S, yT, do_ctx):
    P = Kk.P
    with ExitStack() as ses:
        P.stage_es = ses
        kt = P.sb("kt", [128, NT])
        vt = P.sb("vt", [128, NT // 128, 128])
        qt = P.sb("qt", [128, 2, 512])
        Kk.att = dict(pt=P.sb("pt", [128, 3, 512], BF16), ot=P.sb("ot", [128, 2, 512]), rd=P.sb("rd", [128, 2, 512]),
                      onesb=P.sb("onesb", [128, 128], BF16))
        P.op("pool", lambda e: e.tensor_copy(Kk.att["onesb"][:, :], Kk.ones[:, :]), reads=[Kk.ones], writes=[Kk.att["onesb"]])
        vtb = P.sb("vtb", [128, NT // 128, 128], BF16)

        def load_k(kvh):
            P.dma(kt[:, :], S["kT"][kvh], writes=[kt])
            return [kt]

        def load_v(kvh):
            for j0 in range(0, NT // 128, 4):
                j1 = min(NT // 128, j0 + 4)
                P.dma(vt[:, j0:j1, :], S["vM"][j0 * 128:j1 * 128, kvh * 128:(kvh + 1) * 128].rearrange("(j p) d -> p j d", p=128),
                      writes=[vt])
            P.op("pool", lambda e: e.tensor_copy(vtb[:, :, :], vt[:, :, :]), reads=[vt], writes=[vtb])
            return vtb

        def load_q(h, t0, n, slot):
            P.dma(qt[:, slot, 0:n], S["qT"][h][:, t0:t0 + n], writes=[(qt, slot)])
            return [qt[:, slot, 0:n]], [(qt, slot)]

        attention(Kk, 8, lambda h: h // 4, load_k, load_v, load_q, [128], A_SCALE, yT, 0, do_ctx)
        P.barrier()
        P.stage_es = None


def stage_ssd_conv(Kk, l, S):
    P = Kk.P
    e_ = l // 2
    with ExitStack() as ses:
        P.stage_es = ses
        cw = P.sb("scw", [128, 12, 5])
        cb = P.sb("scb", [128, 12])
        ub = P.sb("ub", [128, 3, 516])
        ac = P.sb("sac", [128, 3, 512])
        P.dma(cw[:, :, :], Kk.ins["ssm_conv_w"][e_], writes=[cw])
        P.dma(cb[:, :], Kk.ins["ssm_conv_b"][e_], writes=[cb])
        it = 0
        for (t0, n, isctx) in tiles_main():
            seg0, seg1 = (0, NCTX) if isctx else (NCTX, NT)
            lh = min(2, t0 - seg0)
            rh = min(2, seg1 - (t0 + n))
            for c in range(12):
                s = it % 3
                it += 1
                _ssd_conv_tile(Kk, S, cw, cb, ub, ac, t0, n, lh, rh, c, s)
        P.barrier()
        P.stage_es = None


def _ssd_conv_tile(Kk, S, cw, cb, ub, ac, t0, n, lh, rh, c, s):
    P = Kk.P
    k = (ub, s)
    if lh < 2 or rh < 2:
        P.op("pool", lambda e: e.memset(ub[:, s, :], 0.0), writes=[k])
    P.dma(ub[:, s, 2 - lh:2 + n + rh], S["xbcT"][c * 128:(c + 1) * 128, t0 - lh:t0 + n + rh], writes=[k])
    ka = (ac, s)
    P.op("act", lambda e: e.activation(ac[:, s, 0:n], ub[:, s, 2:2 + n], AF.Identity, bias=cb[:, c:c + 1], scale=cw[:, c, 2:3]),
         reads=[k, cw, cb], writes=[ka])
    for kk in (0, 1, 3, 4):
        P.op("dve", lambda e, kk=kk: e.scalar_tensor_tensor(ac[:, s, 0:n], ub[:, s, kk:kk + n], cw[:, c, kk:kk + 1], ac[:, s, 0:n],
                                                             ALU.mult, ALU.add),
             reads=[k, cw, ka], writes=[ka])
    P.op("act", lambda e: e.activation(ac[:, s, 0:n], ac[:, s, 0:n], AF.Silu), reads=[ka], writes=[ka])
    P.dma(S["xcT"][c * 128:(c + 1) * 128, t0:t0 + n], ac[:, s, 0:n], reads=[ka], qn="pool")


def stage_ssd(Kk, l, S, yT, do_ctx):
    for d in range(2):
        _ssd_dir(Kk, l, S, yT, do_ctx, d)


def _ssd_dir(Kk, l, S, yT, do_ctx, d):
    P = Kk.P
    e_ = l // 2
    NCH = NT // 128
    if True:
        with ExitStack() as ses:
            P.stage_es = ses
            R = {}
            R["tri"] = P.sb("tri", [128, 4, 128])
            P.dma(R["tri"][:, :, :], Kk.ins["c_tri"][:, :, :], writes=[R["tri"]])
            R["alog"] = P.sb("alog", [32, 2])
            P.dma(R["alog"][:, :], Kk.ins["dt_ba"][e_], writes=[R["alog"]])
            P.op("act", lambda e: e.activation(R["alog"][:, 1:2], R["alog"][:, 1:2], AF.Exp), reads=[R["alog"]], writes=[R["alog"]])
            P.op("dve", lambda e: e.tensor_scalar(R["alog"][:, 1:2], R["alog"][:, 1:2], -1.0, None, ALU.mult),
                 reads=[R["alog"]], writes=[R["alog"]])
            R["fm"] = P.sb("sfm", [128, 2, 12, 128])
            R["dtf"] = P.sb("dtf", [128, 2, 2, 128])
            P.op("pool", lambda e: e.memset(R["dtf"][:, :, :, :], 0.0), writes=[R["dtf"]])
            R["xs"] = P.sb("xstm", [128, 2, 1024])
            R["btm"] = P.sb("btm", [128, 2, 256])
            R["dtm"] = P.sb("dtm", [128, 2, 64])
            R["bc"] = P.sb("bc", [128, 16, 128])
            R["xdtp"] = P.sb("xdtp", [128, 16, 128])
            R["xdtw"] = P.sb("xdtw", [128, 1024])
            R["H"] = P.sb("Hs", [128, 16, 128])
            R["gm"] = P.sb("gm", [128, 2, 2, 128])
            R["acs"] = P.sb("acs", [128, 2, 48])
            R["rot"] = P.sb("rot", [128, 5, 4, 128])
            R["yacc"] = P.sb("yacc", [128, 2, 8, 128])
            R["fin"] = P.sb("fin", [128, 2, 8, 128])
            R["zt"] = P.sb("zt", [128, 2, 8, 128])
            R["sq"] = P.sb("ssq", [128, 4, 128])
            R["rs"] = P.sb("srs", [128, 2, 128])
            R["dsk"] = P.sb("dsk", [128, 8])
            R["gn"] = P.sb("gn", [128, 8])
            R["ones512"] = P.sb("ones512", [128, 128])
            P.dma(R["dsk"][:, :], Kk.ins["ssm_d"][e_], writes=[R["dsk"]])
            P.dma(R["gn"][:, :], Kk.ins["ssm_norm"][e_], writes=[R["gn"]])
            P.dma(R["ones512"][:, :], Kk.ins["c_ones512"][:, :], writes=[R["ones512"]])
            P.op("pool", lambda e: e.memset(R["H"][:, :, :], 0.0), writes=[R["H"]])
            P.op("pool", lambda e: e.memset(R["xdtp"][:, :, :], 0.0), writes=[R["xdtp"]])
            order = [0, 1] + list(range(2, NCH)) if d == 0 else [1, 0] + list(range(NCH - 1, 1, -1))
            R["hi"] = 0
            import os
            nlim = int(os.environ.get("SSD_NCH", "999"))
            for ci, ch in enumerate(order[:nlim]):
                _ssd_chunk(Kk, S, yT, R, d, ch, ci, do_ctx)
            P.barrier()
            P.stage_es = None


def _ssd_chunk(Kk, S, yT, R, d, ch, ci, do_ctx):
    P = Kk.P
    s = ci % 2
    t0 = ch * 128
    tri = R["tri"]
    TD = tri[:, 0 if d == 0 else 2, :]
    TDx = tri[:, 1 if d == 0 else 3, :]
    last = 127 if d == 0 else 0
    fm, dtf, xs, btm, dtm = R["fm"], R["dtf"], R["xs"], R["btm"], R["dtm"]
    import os
    LV = int(os.environ.get("SSD_LEVEL", "99"))
    if LV <= 0:
        return
    kfm = (fm, s)
    P.dma(fm[:, s, :, :], S["xcT"][:, t0:t0 + 128].rearrange("(c p) t -> p c t", p=128), writes=[kfm])
    kdf = (dtf, s)
    P.dma(dtf[0:32, s, 0, :], S["dtT"][:, t0:t0 + 128], writes=[kdf])
    P.op("dve", lambda e: e.tensor_scalar(dtf[0:32, s, 1, :], dtf[0:32, s, 0, :], R["alog"][:, 1:2], None, ALU.mult),
         reads=[kdf, R["alog"]], writes=[kdf])
    if LV <= 1:
        return
    pst = Kk.psb[4]
    for g4 in range(2):
        for j in range(4):
            c = g4 * 4 + j
            P.op("pe", lambda e, c=c, j=j: e.matmul(pst[:, j * 128:(j + 1) * 128], fm[:, s, c, :], Kk.ident[:, :], start=True, stop=True),
                 reads=[kfm, Kk.ident], writes=[pst])
        if not os.environ.get("SKIP_EV"):
            P.op("dve", lambda e, g4=g4: e.tensor_copy(xs[:, s, g4 * 512:(g4 + 1) * 512], pst[:, :]),
                 reads=[pst], writes=[(xs, (s, c)) for c in range(g4 * 4, g4 * 4 + 4)])
    for j in range(2):
        P.op("pe", lambda e, j=j: e.matmul(pst[:, j * 128:(j + 1) * 128], fm[:, s, 8 + j, :], Kk.ident[:, :], start=True, stop=True),
             reads=[kfm, Kk.ident], writes=[pst])
    for q in range(0 if os.environ.get("SKIP_DT") else 2):
        P.op("pe", lambda e, q=q: e.matmul(pst[:, 256 + q * 32:256 + (q + 1) * 32], dtf[:, s, q, :], Kk.ident[:, 0:32], start=True, stop=True),
             reads=[kdf, Kk.ident], writes=[pst])
    if not os.environ.get("SKIP_EV"):
        P.op("dve", lambda e: e.tensor_copy(btm[:, s, :], pst[:, 0:256]), reads=[pst], writes=[(btm, s)])
    if not os.environ.get("SKIP_DT") and not os.environ.get("SKIP_DTEV"):
        P.op("dve", lambda e: e.tensor_copy(dtm[:, s, :], pst[:, 256:320]), reads=[pst], writes=[(dtm, s)])
    LV = int(os.environ.get("SSD_LEVEL", "99"))
    if LV <= 2:
        return
    dt_d = dtm[:, s, d * 16:(d + 1) * 16]
    dta_d = dtm[:, s, 32 + d * 16:32 + (d + 1) * 16]
    psG = Kk.psb[5]
    gm = R["gm"]
    acs = R["acs"]
    ka = (acs, s)
    for g in range(2):
        P.op("pe", lambda e, g=g: e.matmul(psG[:, g * 128:(g + 1) * 128], fm[:, s, 8 + g, :], fm[:, s, 10 + g, :],
                                           start=True, stop=True),
             reads=[kfm], writes=[psG])
    P.op("pe", lambda e: e.matmul(psG[:, 256:272], TD, dta_d, start=True, stop=True), reads=[tri, (dtm, s)], writes=[psG])
    P.op("pe", lambda e: e.matmul(psG[:, 272:288], TDx, dta_d, start=True, stop=True), reads=[tri, (dtm, s)], writes=[psG])
    for g in range(2):
        P.op("dve", lambda e, g=g: e.tensor_tensor(gm[:, s, g, :], psG[:, g * 128:(g + 1) * 128], TD, ALU.mult),
             reads=[psG, tri], writes=[(gm, (s, g))])
    P.op("dve", lambda e: e.tensor_copy(acs[:, s, 0:32], psG[:, 256:288]), reads=[psG], writes=[ka])
    P.op("act", lambda e: e.activation(acs[:, s, 16:32], acs[:, s, 16:32], AF.Exp), reads=[ka], writes=[ka])
    P.op("dve", lambda e: e.tensor_tensor(acs[:, s, 32:48], acs[:, s, 16:32], dt_d, ALU.mult), reads=[ka, (dtm, s)], writes=[ka])
    if LV <= 4:
        return
    bc, xdtp, xdtw = R["bc"], R["xdtp"], R["xdtw"]
    P.op("pool", lambda e: e.tensor_copy(bc[:, :, :], dta_d.rearrange("p (h o) -> p h o", o=1).broadcast_to([128, 16, 128])),
         reads=[(dtm, s)], writes=[bc])
    xs3 = xs[:, s, :].rearrange("p (h q) -> p h q", q=64)
    for par in range(2):
        P.op("dve", lambda e, par=par: e.tensor_tensor(
            xdtp[:, par::2, par * 64:(par + 1) * 64], xs3[:, par::2, :],
            dtm[:, s, d * 16 + par:(d + 1) * 16:2].rearrange("p (h o) -> p h o", o=1).broadcast_to([128, 8, 64]), ALU.mult),
            reads=[(xs, (s, c)) for c in range(8)] + [(dtm, s)], writes=[xdtp])
    P.op("pool", lambda e: e.tensor_tensor(xdtw[:, :].rearrange("p (h q) -> p h q", q=64), xs3,
                                           acs[:, s, 32:48].rearrange("p (h o) -> p h o", o=1).broadcast_to([128, 16, 64]), ALU.mult),
         reads=[(xs, (s, c)) for c in range(8)] + [ka], writes=[xdtw])
    if LV <= 5:
        return
    pss = [Kk.psb[6], Kk.psb[7]]
    for g in range(2):
        P.op("pe", lambda e, g=g: e.matmul(pss[g][:, :], btm[:, s, g * 128:(g + 1) * 128], xdtw[:, g * 512:(g + 1) * 512],
                                           start=True, stop=True),
             reads=[(btm, s), xdtw], writes=[pss[g]])
    if LV <= 6:
        return
    rot = R["rot"]
    H = R["H"]
    yacc = R["yacc"]
    for h in range(16):
        g = h // 8
        pair = h // 2
        par = h % 2
        hi = R["hi"]
        R["hi"] += 1
        r = hi % 4
        psa = Kk.psb[hi % 2]
        psy = Kk.psb[2 + (hi // 2) % 2]
        kr = lambda i, r=r: (rot, (i, r))
        P.op("pe", lambda e, h=h, psa=psa: e.matmul(psa[:, 0:128], bc[:, h, :], TD, start=True, stop=True),
             reads=[bc, tri], writes=[psa])
        P.op("dve", lambda e, h=h, psa=psa, r=r: e.tensor_scalar(rot[:, 0, r, :], psa[:, 0:128], acs[:, s, h:h + 1], 0.0,
                                                                 ALU.subtract, ALU.min),
             reads=[psa, ka], writes=[kr(0)])
        P.op("act", lambda e, r=r: e.activation(rot[:, 1, r, :], rot[:, 0, r, :], AF.Exp), reads=[kr(0)], writes=[kr(1)])
        P.op("pool", lambda e, r=r, g=g: e.tensor_tensor(rot[:, 2, r, :], rot[:, 1, r, :], gm[:, s, g, :], ALU.mult),
             reads=[kr(1), (gm, (s, g))], writes=[kr(2)])
        P.op("act", lambda e, psa=psa, r=r: e.activation(rot[:, 3, r, :], psa[:, 0:128], AF.Exp), reads=[psa], writes=[kr(3)])
        P.op("dve", lambda e, r=r, g=g: e.tensor_tensor(rot[:, 4, r, :], rot[:, 3, r, :], fm[:, s, 10 + g, :], ALU.mult),
             reads=[kr(3), kfm], writes=[kr(4)])
        P.op("pe", lambda e, h=h, psy=psy, r=r, par=par: e.matmul(psy[:, 0:128], xdtp[:, h, :], rot[:, 2, r, :],
                                                                  start=(par == 0), stop=False),
             reads=[xdtp, kr(2)], writes=[psy])
        P.op("pe", lambda e, h=h, psy=psy, r=r, par=par: e.matmul(psy[:, 0:128], H[:, h, :], rot[:, 4, r, :],
                                                                  start=False, stop=(par == 1)),
             reads=[(H, h), kr(4)], writes=[psy])
        P.op("dve", lambda e, h=h, r=r, g=g, par=par: e.scalar_tensor_tensor(
            H[:, h, par * 64:(par + 1) * 64], H[:, h, par * 64:(par + 1) * 64], rot[:, 3, r, last:last + 1],
            pss[g][:, (h % 8) * 64:(h % 8 + 1) * 64], ALU.mult, ALU.add),
            reads=[(H, h), kr(3), pss[g]], writes=[(H, h)])
        if par == 1:
            ky = (yacc, (s, pair))
            P.op("act", lambda e, psy=psy, pair=pair: e.activation(yacc[:, s, pair, :], psy[:, 0:128], AF.Identity),
                 reads=[psy], writes=[ky])
    if LV <= 7:
        return
    isctx = ch < 2
    allk = [(yacc, (s, p)) for p in range(8)]
    if d == 0:
        P.dma(S["ysT"][:, t0:t0 + 128].rearrange("(c p) t -> p c t", p=128), yacc[:, s, :, :], reads=allk, qn="pool")
        return
    if isctx and not do_ctx:
        return
    fin, zt, sq, rs = R["fin"], R["zt"], R["sq"], R["rs"]
    kf = (fin, s)
    P.dma(fin[:, s, :, :], S["ysT"][:, t0:t0 + 128].rearrange("(c p) t -> p c t", p=128), writes=[kf])
    kz = (zt, s)
    P.dma(zt[:, s, :, :], S["zT"][:, t0:t0 + 128].rearrange("(c p) t -> p c t", p=128), writes=[kz])
    P.op("dve", lambda e: e.tensor_tensor(fin[:, s, :, :], fin[:, s, :, :], yacc[:, s, :, :], ALU.add), reads=[kf] + allk, writes=[kf])
    for c in range(8):
        P.op("dve", lambda e, c=c: e.scalar_tensor_tensor(fin[:, s, c, :], fm[:, s, c, :], R["dsk"][:, c:c + 1], fin[:, s, c, :],
                                                           ALU.mult, ALU.add),
             reads=[kf, kfm, R["dsk"]], writes=[kf])
    P.op("pool", lambda e: e.tensor_tensor(fin[:, s, :, :], fin[:, s, :, :], zt[:, s, :, :], ALU.mult), reads=[kf, kz], writes=[kf])
    psn = Kk.psb[4]
    for g in range(2):
        for c4 in range(4):
            c = g * 4 + c4
            q = c % 4
            P.op("act", lambda e, c=c, q=q: e.activation(sq[:, q, :], fin[:, s, c, :], AF.Square), reads=[kf], writes=[(sq, q)])
            P.op("pe", lambda e, c4=c4, q=q, g=g: e.matmul(psn[:, g * 128:(g + 1) * 128], R["ones512"][:, :], sq[:, q, :],
                                                           start=(c4 == 0), stop=(c4 == 3)),
                 reads=[(sq, q), R["ones512"]], writes=[psn])
        P.op("dve", lambda e, g=g: e.tensor_copy(rs[:, g, :], psn[:, g * 128:(g + 1) * 128]), reads=[psn], writes=[(rs, g)])
        P.op("act", lambda e, g=g: e.activation(rs[:, g, :], rs[:, g, :], AF.Sqrt, bias=Kk.epsb[:, 0:1], scale=1.0),
             reads=[(rs, g), Kk.epsb], writes=[(rs, g)])
        P.op("dve", lambda e, g=g: e.reciprocal(rs[:, g, :], rs[:, g, :]), reads=[(rs, g)], writes=[(rs, g)])
        for c4 in range(4):
            c = g * 4 + c4
            P.op("dve", lambda e, c=c, g=g: e.scalar_tensor_tensor(fin[:, s, c, :], fin[:, s, c, :], R["gn"][:, c:c + 1], rs[:, g, :],
                                                                    ALU.mult, ALU.mult),
                 reads=[kf, R["gn"], (rs, g)], writes=[kf])
    P.dma(yT[1024:2048, t0:t0 + 128].rearrange("(c p) t -> p c t", p=128), fin[:, s, :, :], reads=[kf], qn="pool")


C_SCALE = 192 ** -0.5
RW0 = 832


def stage_A_odd(Kk, l, xT, S):
    P = Kk.P
    o_ = l // 2
    W = Kk.ins["od_w_in"][o_]
    Wqb = Kk.ins["mla_q_b"][o_]
    Wkvb = Kk.ins["mla_kv_b"][o_]
    st = {}

    def extra(_):
        st["qa"] = P.sb("qa", [128, 6, 512])
        st["qn"] = P.sb("qan", [128, 6, 512])
        st["rs"] = P.sb("rsq", [128, 2, 512])
        st["gain"] = P.sb("mlagain", [128, 6])
        st["cos"] = P.sb("cosb", [64, 512])
        st["sin"] = P.sb("sinb", [64, 512])
        st["RT"] = P.sb("RT64", [64, 64])
        st["pe"] = P.sb("pe", [64, 3, 512])
        st["po"] = P.sb("po", [64, 3, 512])
        st["tmp"] = P.sb("ptmp", [64, 3, 512])
        st["o512"] = P.sb("o512", [128, 128])
        st["o256"] = P.sb("o256", [128, 128])
        P.dma(st["gain"][:, :], Kk.ins["mla_gain"][o_], writes=[st["gain"]])
        P.dma(st["RT"][:, :], Kk.ins["c_RT64"][:, :], writes=[st["RT"]])
        P.dma(st["o512"][:, :], Kk.ins["c_ones512"][:, :], writes=[st["o512"]])
        P.dma(st["o256"][:, :], Kk.ins["c_ones256"][:, :], writes=[st["o256"]])
        st["i"] = 0

    def tile_pre(t0, n, isctx):
        if not isctx:
            P.dma(st["cos"][:, 0:n], Kk.ins["c_cos64"][:, t0 - NCTX:t0 - NCTX + n], writes=[st["cos"]])
            P.dma(st["sin"][:, 0:n], Kk.ins["c_sin64"][:, t0 - NCTX:t0 - NCTX + n], writes=[st["sin"]])

    def rope64(ps, n, t0, isctx, dst):
        s = st["i"] % 3
        st["i"] += 1
        pe, po, tmp = st["pe"], st["po"], st["tmp"]
        P.op("act", lambda e: e.activation(pe[:, s, 0:n], ps[0:64, 0:n], AF.Identity), reads=[ps], writes=[(pe, s)])
        if isctx:
            P.dma(dst, pe[:, s, 0:n], reads=[(pe, s)], qn="pool")
        else:
            rope_apply(Kk, (pe, s, pe[:, s, 0:n]), 64, n, st["cos"], st["sin"], st["RT"], Kk.psb[6],
                       (po, s, po[:, s, 0:n]), (tmp, s, tmp[:, s, 0:n]))
            P.dma(dst, po[:, s, 0:n], reads=[(po, s)], qn="pool")

    def mk_a(t0, n, isctx):
        def evac(i, ps, cw):
            P.op("act", lambda e: e.activation(st["qa"][:, i, 0:n], ps[:, 0:n], AF.Identity), reads=[ps], writes=[(st["qa"], i)])
        return evac

    def mk_kpe(t0, n, isctx):
        def evac(i, ps, cw):
            rope64(ps, n, t0, isctx, S["kpT"][:, t0:t0 + n])
        return evac

    def post_a(t0, n, isctx):
        qa, qn, rs = st["qa"], st["qn"], st["rs"]
        for part, (c0, nch, ones_) in enumerate([(0, 4, st["o512"]), (4, 2, st["o256"])]):
            psr = Kk.psb[6]
            colsum_bcast(Kk, lambda c, kp, c0=c0: qa[:, c0 + c, 0:n], nch, n, psr, Kk.stA["sq"], ones_,
                         [(qa, c0 + c) for c in range(nch)])
            P.op("act", lambda e, part=part, psr=psr: e.activation(rs[:, part, 0:n], psr[:, 0:n], AF.Sqrt, bias=Kk.epsb[:, 0:1], scale=1.0),
                 reads=[psr, Kk.epsb], writes=[(rs, part)])
            P.op("dve", lambda e, part=part: e.reciprocal(rs[:, part, 0:n], rs[:, part, 0:n]), reads=[(rs, part)], writes=[(rs, part)])
            for c in range(c0, c0 + nch):
                P.op("dve", lambda e, c=c, part=part: e.scalar_tensor_tensor(qn[:, c, 0:n], qa[:, c, 0:n], st["gain"][:, c:c + 1],
                                                                              rs[:, part, 0:n], ALU.mult, ALU.mult),
                     reads=[(qa, c), st["gain"], (rs, part)], writes=[(qn, c)])
        chunks = []
        for h in range(8):
            chunks.append((h * 192, 128))
            chunks.append((h * 192 + 128, 64))

        def evq(i, ps, cw):
            h = i // 2
            if i % 2 == 0:
                ev_store(Kk, ps, 128, n, S["qnT"][h][:, t0:t0 + n])
            else:
                rope64(ps, n, t0, isctx, S["qpT"][h][:, t0:t0 + n])

        gemm_fm(Kk, Wqb, 512, chunks, lambda kc, kp: qn[0:kp, kc, 0:n], n, evq, ps_ids=(2, 3),
                rhs_reads=[(qn, c) for c in range(4)])
        kchunks = [(h * 256, 128) for h in range(8)]

        def evk(i, ps, cw):
            ev_store(Kk, ps, 128, n, S["knT"][i][:, t0:t0 + n])

        gemm_fm(Kk, Wkvb, 256, kchunks, lambda kc, kp: qn[0:kp, 4 + kc, 0:n], n, evk, ps_ids=(2, 3),
                rhs_reads=[(qn, 4), (qn, 5)])
        for h in range(8):
            def evv(tb, ps, tn, cb0, cbw, h=h):
                ev_store(Kk, ps, tn, cbw, S["vM"][t0 + tb * 128:t0 + tb * 128 + tn, h * 128:h * 128 + cbw])
            gemm_tm(Kk, Wkvb, 256, h * 256 + 128, 128, lambda kc, kp, a, tn: qn[0:kp, 4 + kc, a:a + tn], n, evv, ps_ids=(2, 3),
                    lhs_reads=[(qn, 4), (qn, 5)])

    def mk_ud(t0, n, isctx):
        def evac(i, ps, cw):
            c0 = ud_chunks[i][0] - RW0
            ev_store(Kk, ps, cw, n, S["udT"][c0:c0 + cw, t0:t0 + n])
        return evac

    a_chunks = [(i * 128, 128) for i in range(6)]
    kpe_chunks = [(768, 64)]
    ud_chunks = [(RW0 + i * 128, 128) for i in range(24)] + [(3904 + i * 96, 96) for i in range(4)] + [(4288, 128), (4416, 128)]
    plan = [("post", tile_pre), ("fm", a_chunks, mk_a), ("fm", kpe_chunks, mk_kpe), ("post", post_a), ("fm", ud_chunks, mk_ud)]
    stage_A(Kk, l, xT, W, plan, nsteps_extra=extra)


def stage_attn_odd(Kk, S, yT, do_ctx):
    P = Kk.P
    with ExitStack() as ses:
        P.stage_es = ses
        kt = P.sb("kt", [128, NT])
        kp = P.sb("kp", [64, NT])
        vt = P.sb("vt", [128, NT // 128, 128])
        qt = P.sb("qt", [128, 2, 512])
        qp = P.sb("qp", [64, 2, 512])
        Kk.att = dict(pt=P.sb("pt", [128, 3, 512], BF16), ot=P.sb("ot", [128, 2, 512]), rd=P.sb("rd", [128, 2, 512]),
                      onesb=P.sb("onesb", [128, 128], BF16))
        P.op("pool", lambda e: e.tensor_copy(Kk.att["onesb"][:, :], Kk.ones[:, :]), reads=[Kk.ones], writes=[Kk.att["onesb"]])
        vtb = P.sb("vtb", [128, NT // 128, 128], BF16)
        P.dma(kp[:, :], S["kpT"][:, :], writes=[kp])

        def load_k(h):
            P.dma(kt[:, :], S["knT"][h], writes=[kt])
            return [kt, kp]

        def load_v(h):
            for j0 in range(0, NT // 128, 4):
                j1 = min(NT // 128, j0 + 4)
                P.dma(vt[:, j0:j1, :], S["vM"][j0 * 128:j1 * 128, h * 128:(h + 1) * 128].rearrange("(j p) d -> p j d", p=128),
                      writes=[vt])
            P.op("pool", lambda e: e.tensor_copy(vtb[:, :, :], vt[:, :, :]), reads=[vt], writes=[vtb])
            return vtb

        def load_q(h, t0, n, slot):
            P.dma(qt[:, slot, 0:n], S["qnT"][h][:, t0:t0 + n], writes=[(qt, slot)])
            P.dma(qp[:, slot, 0:n], S["qpT"][h][:, t0:t0 + n], writes=[(qp, slot)])
            return [qt[:, slot, 0:n], qp[:, slot, 0:n]], [(qt, slot), (qp, slot)]

        attention(Kk, 8, lambda h: h, load_k, load_v, load_q, [128, 64], C_SCALE, yT, 0, do_ctx)
        P.barrier()
        P.stage_es = None

import math

NEG_EH = -math.exp(-0.5)
NCH64 = NT // 64


def stage_rwkv_prep(Kk, l, S):
    P = Kk.P
    o_ = l // 2
    with ExitStack() as ses:
        P.stage_es = ses
        R = {}
        R["udp"] = P.sb("udp", [128, 30, 512])
        R["ub"] = P.sb("rub", [128, 2, 516])
        R["T"] = P.sb("rT", [128, 12, 512])
        R["PK"] = P.sb("rPK", [128, 4, 512])
        R["stg"] = P.sb("rstg", [128, 6, 512])
        R["tm"] = P.sb("rtm", [128, 3, 512])
        R["mu"] = P.sb("rmu", [128, 30])
        R["vec"] = P.sb("rvec", [128, 7, 8])
        R["g2w"] = P.sb("g2w", [128, 2, 1024])
        R["w2w"] = P.sb("w2w", [128, 2, 1024])
        R["a2w"] = P.sb("a2w", [128, 2, 1024])
        R["blk"] = P.sb("blk", [128, 128])
        R["on64"] = P.sb("on64", [128, 64])
        R["wc"] = P.sb("wcst", [128, 4, 8])
        P.dma(R["mu"][:, :], Kk.ins["rw_mu"][o_], writes=[R["mu"]])
        P.dma(R["vec"][:, :, :], Kk.ins["rw_vec"][o_], writes=[R["vec"]])
        P.dma(R["g2w"][:, :, :], Kk.ins["rwkv_g2"][o_].rearrange("(kc p) c -> p kc c", p=128), writes=[R["g2w"]])
        for d in range(2):
            P.dma(R["w2w"][0:96, d, :], Kk.ins["rwkv_w2"][o_][d], writes=[R["w2w"]])
            P.dma(R["a2w"][0:96, d, :], Kk.ins["rwkv_a2"][o_][d], writes=[R["a2w"]])
        P.dma(R["blk"][:, :], Kk.ins["c_blk64"][:, :], writes=[R["blk"]])
        P.op("pool", lambda e: e.memset(R["on64"][:, :], 1.0), writes=[R["on64"]])
        R["si"] = 0
        R["ti"] = 0
        R["tmi"] = 0
        R["wi"] = 0
        for (t0, n, isctx) in tiles_main():
            _rw_prep_tile(Kk, S, R, t0, n, isctx)
        P.barrier()
        P.stage_es = None


def _rw_prep_tile(Kk, S, R, t0, n, isctx):
    P = Kk.P
    udp, ub, T, stg, tm, mu, vec = R["udp"], R["ub"], R["T"], R["stg"], R["tm"], R["mu"], R["vec"]
    seg0, seg1 = (0, NCTX) if isctx else (NCTX, NT)
    lh = 1 if t0 > seg0 else 0
    rh = 1 if t0 + n < seg1 else 0
    nch = n // 64
    ch0 = t0 // 64
    rows = [128] * 24 + [96] * 4 + [128] * 2
    roff = [i * 128 for i in range(24)] + [3072 + i * 96 for i in range(4)] + [3456, 3584]

    def tslot():
        s = R["ti"] % 12
        R["ti"] += 1
        return s

    def store_fm(src_ap, skey, dst):
        P.dma(dst, src_ap, reads=[skey], qn="pool")

    for c in range(30):
        rw = rows[c]
        s = c % 2
        kb = (ub, s)
        if not (lh and rh):
            P.op("pool", lambda e, s=s: e.memset(ub[:, s, :], 0.0), writes=[kb])
        P.dma(ub[0:rw, s, 1 - lh:1 + n + rh], S["udT"][roff[c]:roff[c] + rw, t0 - lh:t0 + n + rh], writes=[kb])
        ts = tslot()
        kt = (T, ts)
        P.op("dve", lambda e, s=s, ts=ts, rw=rw: e.tensor_tensor(T[0:rw, ts, 0:n], ub[0:rw, s, 0:n], ub[0:rw, s, 2:2 + n], ALU.add),
             reads=[kb], writes=[kt])
        P.op("dve", lambda e, s=s, ts=ts, rw=rw: e.scalar_tensor_tensor(T[0:rw, ts, 0:n], T[0:rw, ts, 0:n], 0.5, ub[0:rw, s, 1:1 + n],
                                                                         ALU.mult, ALU.subtract),
             reads=[kb, kt], writes=[kt])
        P.op("dve", lambda e, s=s, ts=ts, rw=rw, c=c: e.scalar_tensor_tensor(udp[0:rw, c, 0:n], T[0:rw, ts, 0:n], mu[0:rw, c:c + 1],
                                                                              ub[0:rw, s, 1:1 + n], ALU.mult, ALU.add),
             reads=[kb, kt, mu], writes=[(udp, c)])
    for c in (28, 29):
        P.op("act", lambda e, c=c: e.activation(udp[:, c, 0:n], udp[:, c, 0:n], AF.Sigmoid), reads=[(udp, c)], writes=[(udp, c)])
    for c in range(8):
        ps = Kk.psb[c % 2]
        for kc in range(2):
            P.op("pe", lambda e, c=c, kc=kc, ps=ps: e.matmul(ps[:, 0:n], R["g2w"][:, kc, c * 128:(c + 1) * 128], udp[:, 28 + kc, 0:n],
                                                            start=(kc == 0), stop=(kc == 1)),
                 reads=[R["g2w"], (udp, 28 + kc)], writes=[ps])
        s = R["si"] % 6
        R["si"] += 1
        P.op("act", lambda e, ps=ps, s=s: e.activation(stg[:, s, 0:n], ps[:, 0:n], AF.Identity), reads=[ps], writes=[(stg, s)])
        store_fm(stg[:, s, 0:n], (stg, s), S["gateT"][c * 128:(c + 1) * 128, t0:t0 + n])
    for d in range(2):
        P.op("act", lambda e, d=d: e.activation(udp[0:96, 24 + d, 0:n], udp[0:96, 24 + d, 0:n], AF.Tanh),
             reads=[(udp, 24 + d)], writes=[(udp, 24 + d)])
    for c in range(8):
        _rw_prep_chunk(Kk, S, R, t0, n, c, nch, ch0)


def _rw_prep_chunk(Kk, S, R, t0, n, c, nch, ch0):
    P = Kk.P
    udp, T, stg, tm, vec = R["udp"], R["T"], R["stg"], R["tm"], R["vec"]
    rC, kC, vC = (udp, c), (udp, 8 + c), (udp, 16 + c)
    r_ = udp[:, c, 0:n]
    k_ = udp[:, 8 + c, 0:n]
    v_ = udp[:, 16 + c, 0:n]

    def tslot():
        s = R["ti"] % 12
        R["ti"] += 1
        return s, (T, s)

    def sslot():
        s = R["si"] % 6
        R["si"] += 1
        return s, (stg, s)

    def transpose_store(src_ap, skey, dst_tm):
        pst = Kk.psb[4 + R["tmi"] % 2]
        s = R["tmi"] % 3
        R["tmi"] += 1
        nb = n // 128
        for b in range(nb):
            P.op("pe", lambda e, b=b: e.matmul(pst[:, b * 128:(b + 1) * 128], src_ap[:, b * 128:(b + 1) * 128], Kk.ident[:, :],
                                               start=True, stop=True),
                 reads=[skey, Kk.ident], writes=[pst])
        P.op("act", lambda e: e.activation(tm[:, s, 0:n], pst[:, 0:n], AF.Identity), reads=[pst], writes=[(tm, s)])
        P.dma(dst_tm[t0:t0 + n, c * 128:(c + 1) * 128].rearrange("(b p) f -> p b f", p=128),
              tm[:, s, 0:n].rearrange("p (b f) -> p b f", f=128), reads=[(tm, s)], qn="pool")

    def headsum(src_ap, skey, ps):
        P.op("pe", lambda e: e.matmul(ps[:, 0:n], R["blk"][:, :], src_ap, start=True, stop=True), reads=[skey, R["blk"]], writes=[ps])

    transpose_store(v_, vC, S["vTM"])
    PK = R["PK"]
    k_kkr, k_sq, k_kk, k_ks = (PK, 0), (PK, 1), (PK, 2), (PK, 3)
    P.op("dve", lambda e: e.tensor_scalar(PK[:, 0, 0:n], k_, vec[:, 0, c:c + 1], None, ALU.mult), reads=[kC, vec], writes=[k_kkr])
    P.op("act", lambda e: e.activation(PK[:, 1, 0:n], PK[:, 0, 0:n], AF.Square), reads=[k_kkr], writes=[k_sq])
    ps = Kk.psb[2]
    headsum(PK[:, 1, 0:n], k_sq, ps)
    P.op("act", lambda e: e.activation(PK[:, 1, 0:n], ps[:, 0:n], AF.Sqrt, bias=Kk.epsb[:, 2:3], scale=1.0),
         reads=[ps, Kk.epsb], writes=[k_sq])
    P.op("dve", lambda e: e.reciprocal(PK[:, 1, 0:n], PK[:, 1, 0:n]), reads=[k_sq], writes=[k_sq])
    P.op("dve", lambda e: e.tensor_tensor(PK[:, 2, 0:n], PK[:, 0, 0:n], PK[:, 1, 0:n], ALU.mult), reads=[k_kkr, k_sq], writes=[k_kk])
    kk = PK[:, 2, 0:n]
    for d in range(2):
        ps1 = Kk.psb[d]
        P.op("pe", lambda e, d=d, ps1=ps1: e.matmul(ps1[:, 0:n], R["w2w"][0:96, d, c * 128:(c + 1) * 128], udp[0:96, 24 + d, 0:n],
                                                    start=True, stop=True),
             reads=[R["w2w"], (udp, 24 + d)], writes=[ps1])
        s_lw, k_lw = tslot()
        P.op("act", lambda e, d=d, ps1=ps1, s_lw=s_lw: e.activation(T[:, s_lw, 0:n], ps1[:, 0:n], AF.Sigmoid, bias=vec[:, 3 + d, c:c + 1], scale=1.0),
             reads=[ps1, vec], writes=[k_lw])
        P.op("dve", lambda e, s_lw=s_lw: e.tensor_scalar(T[:, s_lw, 0:n], T[:, s_lw, 0:n], NEG_EH, None, ALU.mult), reads=[k_lw], writes=[k_lw])
        ps2 = Kk.psb[3]
        P.op("pe", lambda e, d=d: e.matmul(ps2[:, 0:n], R["a2w"][0:96, d, c * 128:(c + 1) * 128], udp[0:96, 26 + d, 0:n],
                                           start=True, stop=True),
             reads=[R["a2w"], (udp, 26 + d)], writes=[ps2])
        s_a, k_a = tslot()
        P.op("act", lambda e, d=d, s_a=s_a: e.activation(T[:, s_a, 0:n], ps2[:, 0:n], AF.Sigmoid, bias=vec[:, 5 + d, c:c + 1], scale=1.0),
             reads=[ps2, vec], writes=[k_a])
        s_kd, k_kd = tslot()
        P.op("dve", lambda e, s_a=s_a, s_kd=s_kd: e.tensor_scalar(T[:, s_kd, 0:n], T[:, s_a, 0:n], -1.0, vec[:, 1, c:c + 1], ALU.add, ALU.mult),
             reads=[k_a, vec], writes=[k_kd])
        P.op("dve", lambda e, s_kd=s_kd: e.scalar_tensor_tensor(T[:, s_kd, 0:n], T[:, s_kd, 0:n], 1.0, k_, ALU.add, ALU.mult),
             reads=[k_kd, kC], writes=[k_kd])
        if d == 0:
            P.op("pool", lambda e, s_kd=s_kd: e.tensor_copy(PK[:, 3, 0:n], T[:, s_kd, 0:n]), reads=[k_kd], writes=[k_ks])
        else:
            P.op("pool", lambda e, s_kd=s_kd: e.tensor_tensor(PK[:, 3, 0:n], PK[:, 3, 0:n], T[:, s_kd, 0:n], ALU.add),
                 reads=[k_kd, k_ks], writes=[k_ks])
        s_cl, k_cl = tslot()
        for q in range(nch):
            P.op("dve", lambda e, q=q, s_cl=s_cl, s_lw=s_lw: e.tensor_tensor_scan(T[:, s_cl, q * 64:(q + 1) * 64], R["on64"][:, :],
                                                                                  T[:, s_lw, q * 64:(q + 1) * 64], 0.0, ALU.mult, ALU.add),
                 reads=[k_lw, R["on64"]], writes=[k_cl])
        if d == 1:
            s_t2, k_t2 = tslot()
            P.op("dve", lambda e, s_t2=s_t2, s_cl=s_cl, s_lw=s_lw: e.tensor_tensor(T[:, s_t2, 0:n], T[:, s_lw, 0:n], T[:, s_cl, 0:n], ALU.subtract),
                 reads=[k_lw, k_cl], writes=[k_t2])
            cl3 = T[:, s_cl, 0:n].rearrange("p (q t) -> p q t", t=64)
            P.op("dve", lambda e, s_t2=s_t2, cl3=cl3: e.tensor_tensor(T[:, s_t2, 0:n].rearrange("p (q t) -> p q t", t=64),
                                                                      T[:, s_t2, 0:n].rearrange("p (q t) -> p q t", t=64),
                                                                      cl3[:, :, 63:64].broadcast_to([128, nch, 64]), ALU.add),
                 reads=[k_t2, k_cl], writes=[k_t2])
            s_cl, k_cl = s_t2, k_t2
        cl = T[:, s_cl, 0:n]
        s_ep, k_ep = tslot()
        P.op("act", lambda e, s_ep=s_ep, cl=cl: e.activation(T[:, s_ep, 0:n], cl, AF.Exp), reads=[k_cl], writes=[k_ep])
        s_em, k_em = tslot()
        P.op("act", lambda e, s_em=s_em, cl=cl: e.activation(T[:, s_em, 0:n], cl, AF.Exp, scale=-1.0), reads=[k_cl], writes=[k_em])
        P.op("dve", lambda e, s_lw=s_lw, cl=cl: e.tensor_tensor(T[:, s_lw, 0:n], cl, T[:, s_lw, 0:n], ALU.subtract), reads=[k_cl, k_lw], writes=[k_lw])
        P.op("act", lambda e, s_lw=s_lw: e.activation(T[:, s_lw, 0:n], T[:, s_lw, 0:n], AF.Exp), reads=[k_lw], writes=[k_lw])
        last = 63 if d == 0 else 0
        wi = R["wi"] % 4
        R["wi"] += 1
        P.op("pool", lambda e, s_ep=s_ep, wi=wi, last=last: e.tensor_copy(
            R["wc"][:, wi, 0:nch].rearrange("p (q o) -> p q o", o=1),
            T[:, s_ep, 0:n].rearrange("p (q t) -> p q t", t=64)[:, :, last:last + 1]), reads=[k_ep], writes=[(R["wc"], wi)])
        P.dma(S["wC"][d][c * 128:(c + 1) * 128, ch0:ch0 + nch], R["wc"][:, wi, 0:nch], reads=[(R["wc"], wi)], qn="pool")
        s1, ks1 = sslot()
        P.op("dve", lambda e, s1=s1, s_ep=s_ep: e.tensor_tensor(stg[:, s1, 0:n], r_, T[:, s_ep, 0:n], ALU.mult), reads=[rC, k_ep], writes=[ks1])
        P.dma(S["RW"][d][3][c * 128:(c + 1) * 128, t0:t0 + n], stg[:, s1, 0:n], reads=[ks1], qn="pool")
        s2, ks2 = sslot()
        P.op("pool", lambda e, s2=s2, s_lw=s_lw: e.tensor_tensor(stg[:, s2, 0:n], kk, T[:, s_lw, 0:n], ALU.mult), reads=[k_kk, k_lw], writes=[ks2])
        P.dma(S["RW"][d][2][c * 128:(c + 1) * 128, t0:t0 + n], stg[:, s2, 0:n], reads=[ks2], qn="pool")
        s3, ks3 = sslot()
        P.op("dve", lambda e, s3=s3, s_kd=s_kd, s_em=s_em: e.tensor_tensor(stg[:, s3, 0:n], T[:, s_kd, 0:n], T[:, s_em, 0:n], ALU.mult),
             reads=[k_kd, k_em], writes=[ks3])
        P.dma(S["RW"][d][0][c * 128:(c + 1) * 128, t0:t0 + n], stg[:, s3, 0:n], reads=[ks3], qn="pool")
        transpose_store(stg[:, s3, 0:n], ks3, S["kbTM"][d])
        s4, ks4 = sslot()
        P.op("pool", lambda e, s4=s4, s_a=s_a: e.tensor_tensor(stg[:, s4, 0:n], T[:, s_a, 0:n], kk, ALU.mult), reads=[k_a, k_kk], writes=[ks4])
        P.op("dve", lambda e, s4=s4, s_em=s_em: e.tensor_tensor(stg[:, s4, 0:n], stg[:, s4, 0:n], T[:, s_em, 0:n], ALU.mult),
             reads=[ks4, k_em], writes=[ks4])
        P.dma(S["RW"][d][1][c * 128:(c + 1) * 128, t0:t0 + n], stg[:, s4, 0:n], reads=[ks4], qn="pool")
        transpose_store(stg[:, s4, 0:n], ks4, S["bbTM"][d])
    P.op("dve", lambda e: e.scalar_tensor_tensor(PK[:, 3, 0:n], PK[:, 3, 0:n], vec[:, 2, c:c + 1], r_, ALU.mult, ALU.mult),
         reads=[k_ks, vec, rC], writes=[k_ks])
    ps3 = Kk.psb[2]
    headsum(PK[:, 3, 0:n], k_ks, ps3)
    s5, ks5 = sslot()
    P.op("dve", lambda e: e.tensor_tensor(stg[:, s5, 0:n], ps3[:, 0:n], v_, ALU.mult), reads=[ps3, vC], writes=[ks5])
    P.dma(S["bonT"][c * 128:(c + 1) * 128, t0:t0 + n], stg[:, s5, 0:n], reads=[ks5], qn="pool")


def stage_rwkv(Kk, l, S, yT, do_ctx):
    for d in range(2):
        _rwkv_dir(Kk, l, S, yT, do_ctx, d)


def _rwkv_dir(Kk, l, S, yT, do_ctx, d):
    P = Kk.P
    o_ = l // 2
    with ExitStack() as ses:
        P.stage_es = ses
        R = {}
        R["m64"] = P.sb("m64", [64, 4, 64])
        P.dma(R["m64"][:, :, :], Kk.ins["c_m64"][:, :, :], writes=[R["m64"]])
        R["wCt"] = P.sb("wCt", [64, 16, NCH64])
        P.dma(R["wCt"][:, :, :], S["wC"][d].rearrange("(h j) q -> j h q", j=64), writes=[R["wCt"]])
        R["ST"] = P.sb("ST", [64, 16, 64])
        P.op("pool", lambda e: e.memset(R["ST"][:, :, :], 0.0), writes=[R["ST"]])
        R["Xc"] = P.sb("Xc", [64, 2, 16, 4, 64])
        R["TMc"] = P.sb("TMc", [64, 1, 3, 1024])
        R["AM"] = P.sb("AM", [64, 16, 4, 64])
        R["Lm"] = P.sb("Lm", [64, 16, 64])
        R["Pb"] = P.sb("Pb", [64, 2, 8, 64])
        R["Qb"] = P.sb("Qb", [64, 2, 8, 64])
        R["Xb"] = P.sb("Xb", [64, 2, 8, 64])
        R["Qi"] = P.sb("Qi", [64, 8, 64])
        R["TT"] = P.sb("TT", [64, 16, 64])
        R["Zs"] = P.sb("Zs", [64, 16, 64])
        R["Us"] = P.sb("Us", [64, 16, 64])
        R["Yc"] = P.sb("Yc", [64, 2, 1024])
        R["tmpS"] = P.sb("tmpS", [64, 16, 64])
        if d == 1:
            R["Yf"] = P.sb("Yf", [64, 1, 1024])
            R["ln"] = P.sb("lnrow", [64, 2, 1024])
            P.dma(R["ln"][:, :, :], Kk.ins["rw_ln"][o_].rearrange("(o a) f -> o a f", o=1).broadcast_to([64, 2, 1024]), writes=[R["ln"]])
            R["st"] = P.sb("lnst", [64, 4, 16])
            R["cen"] = P.sb("cen", [64, 1024])
            R["sq"] = P.sb("lsq", [64, 1024])
            R["bg"] = P.sb("bg", [128, 1, 2, 8, 64])
            R["yo"] = P.sb("yo", [128, 2, 8, 64])
        order = list(range(4)) + list(range(4, NCH64)) if d == 0 else [3, 2, 1, 0] + list(range(NCH64 - 1, 3, -1))
        import os
        nlim = int(os.environ.get("RW_NCH", "999"))
        for ci, ch in enumerate(order[:nlim]):
            _rwkv_chunk(Kk, S, yT, R, d, ch, ci, do_ctx)
        P.barrier()
        P.stage_es = None


def _rwkv_chunk(Kk, S, yT, R, d, ch, ci, do_ctx):
    P = Kk.P
    s = ci % 2
    t0 = ch * 64
    Xc, TMc, AM, Lm, TT, ST, Zs, Us, Yc, m64 = R["Xc"], R["TMc"], R["AM"], R["Lm"], R["TT"], R["ST"], R["Zs"], R["Us"], R["Yc"], R["m64"]
    id64 = Kk.ident[0:64, 0:64]
    kx = (Xc, s)
    ktm = (TMc, 0)
    for q in range(4):
        P.dma(Xc[:, s, :, q, :], S["RW"][d][q][:, t0:t0 + 64].rearrange("(h j) t -> j h t", j=64), writes=[kx])
    P.dma(TMc[:, 0, 0, :], S["kbTM"][d][t0:t0 + 64, :], writes=[ktm])
    P.dma(TMc[:, 0, 1, :], S["bbTM"][d][t0:t0 + 64, :], writes=[ktm])
    P.dma(TMc[:, 0, 2, :], S["vTM"][t0:t0 + 64, :], writes=[ktm])
    import os
    LV = int(os.environ.get("RW_LEVEL", "99"))
    if LV <= 1:
        return
    mi = 0 if d == 0 else 2
    mL = m64[:, 2 if d == 0 else 0, :]
    psb = Kk.psb
    for hf in range(2):
        h0 = hf * 8
        for hh in range(8):
            h = h0 + hh
            bk = hh // 4
            col = (hh % 4) * 128
            P.op("pe", lambda e, h=h, bk=bk, col=col: e.matmul(psb[bk][0:64, col:col + 128], Xc[:, s, h, 0, :], Xc[:, s, h, 2:4, :],
                                                              start=True, stop=True), reads=[kx], writes=[psb[bk]])
            P.op("pe", lambda e, h=h, bk=bk, col=col: e.matmul(psb[2 + bk][0:64, col:col + 128], Xc[:, s, h, 1, :], Xc[:, s, h, 2:4, :],
                                                              start=True, stop=True), reads=[kx], writes=[psb[2 + bk]])
            P.op("pe", lambda e, h=h, hh=hh: e.matmul(psb[4][0:64, hh * 64:(hh + 1) * 64], Xc[:, s, h, 2, :], Xc[:, s, h, 1, :],
                                                      start=True, stop=True), reads=[kx], writes=[psb[4]])
        mask2 = m64[:, mi:mi + 2, :].rearrange("p (o a) t -> p o a t", o=1).broadcast_to([64, 4, 2, 64])
        for bk in range(2):
            hs = slice(h0 + bk * 4, h0 + bk * 4 + 4)
            P.op("dve", lambda e, bk=bk, hs=hs: e.tensor_tensor(AM[:, hs, 0:2, :], psb[bk][0:64, :].rearrange("p (h a t) -> p h a t", a=2, t=64),
                                                               mask2, ALU.mult), reads=[psb[bk], m64], writes=[(AM, (hf, bk, 0))])
            P.op("dve", lambda e, bk=bk, hs=hs: e.tensor_tensor(AM[:, hs, 2:4, :], psb[2 + bk][0:64, :].rearrange("p (h a t) -> p h a t", a=2, t=64),
                                                               mask2, ALU.mult), reads=[psb[2 + bk], m64], writes=[(AM, (hf, bk, 1))])
        kAM = [(AM, (hf, bk, a)) for bk in range(2) for a in range(2)]
        hs8 = slice(h0, h0 + 8)
        kL = (Lm, hf)
        P.op("dve", lambda e, hs8=hs8: e.tensor_tensor(Lm[:, hs8, :], psb[4][0:64, :].rearrange("p (h t) -> p h t", t=64),
                                              mL.rearrange("p (o t) -> p o t", o=1).broadcast_to([64, 8, 64]), ALU.mult),
             reads=[psb[4], m64], writes=[kL])
        if LV <= 2:
            continue
        Pb, Qb, Xb, Qi = R["Pb"], R["Qb"], R["Xb"], R["Qi"]
        idb = id64.rearrange("p (o t) -> p o t", o=1).broadcast_to([64, 8, 64])
        P.op("dve", lambda e, hs8=hs8, idb=idb: e.tensor_tensor(Xb[:, 0, :, :], idb, AM[:, hs8, 2, :], ALU.subtract), reads=kAM + [Kk.ident], writes=[(Xb, 0)])
        for lev in range(5):
            pi_, po_ = lev % 2, (lev + 1) % 2
            lastlev = (lev == 4)
            for hh in range(8):
                h = h0 + hh
                Pk = AM[:, h, 2, :] if lev == 0 else Pb[:, pi_, hh, :]
                Qk = Lm[:, h, :] if lev == 0 else Qb[:, pi_, hh, :]
                rd = kAM + [kL] if lev == 0 else [(Pb, pi_), (Qb, pi_)]
                if not lastlev:
                    P.op("pe", lambda e, hh=hh, Pk=Pk, Qk=Qk: e.matmul(psb[5][0:64, hh * 64:(hh + 1) * 64], Qk, Pk, start=True, stop=True),
                         reads=rd, writes=[psb[5]])
                P.op("pe", lambda e, hh=hh, Pk=Pk, Qk=Qk: e.matmul(psb[6][0:64, hh * 64:(hh + 1) * 64], Pk, Qk, start=True, stop=True),
                     reads=rd, writes=[psb[6]])
            if not lastlev:
                P.op("act", lambda e, po_=po_: e.activation(Pb[:, po_, :, :], psb[5][0:64, :].rearrange("p (h t) -> p h t", t=64), AF.Identity),
                     reads=[psb[5]], writes=[(Pb, po_)])
                P.op("dve", lambda e, po_=po_: e.tensor_copy(Qb[:, po_, :, :], psb[6][0:64, :].rearrange("p (h t) -> p h t", t=64)),
                     reads=[psb[6]], writes=[(Qb, po_)])
            P.op("dve", lambda e, idb=idb: e.tensor_tensor(Qi[:, :, :], psb[6][0:64, :].rearrange("p (h t) -> p h t", t=64), idb, ALU.add),
                 reads=[psb[6], Kk.ident], writes=[Qi])
            for hh in range(8):
                P.op("pe", lambda e, hh=hh, pi_=pi_: e.matmul(psb[7][0:64, hh * 64:(hh + 1) * 64], Qi[:, hh, :], Xb[:, pi_, hh, :],
                                                              start=True, stop=True), reads=[Qi, (Xb, pi_)], writes=[psb[7]])
            if lastlev:
                P.op("act", lambda e, hs8=hs8: e.activation(TT[:, hs8, :], psb[7][0:64, :].rearrange("p (h t) -> p h t", t=64), AF.Identity),
                     reads=[psb[7]], writes=[(TT, hf)])
            else:
                P.op("act", lambda e, po_=po_: e.activation(Xb[:, po_, :, :], psb[7][0:64, :].rearrange("p (h t) -> p h t", t=64), AF.Identity),
                     reads=[psb[7]], writes=[(Xb, po_)])
    allAM = [(AM, (hf, bk, a)) for hf in range(2) for bk in range(2) for a in range(2)]
    kTT = [(TT, 0), (TT, 1)]
    if LV <= 3:
        return
    V = lambda h: TMc[:, 0, 2, h * 64:(h + 1) * 64]
    for h in range(16):
        bk, col = h // 8, (h % 8) * 64
        P.op("pe", lambda e, h=h, bk=bk, col=col: e.matmul(psb[bk][0:64, col:col + 64], Xc[:, s, h, 2, :], ST[:, h, :], start=True, stop=False),
             reads=[kx, ST], writes=[psb[bk]])
        P.op("pe", lambda e, h=h, bk=bk, col=col: e.matmul(psb[bk][0:64, col:col + 64], AM[:, h, 0, :], V(h), start=False, stop=True),
             reads=allAM + [ktm], writes=[psb[bk]])
    for bk in range(2):
        P.op("act", lambda e, bk=bk: e.activation(Zs[:, bk * 8:(bk + 1) * 8, :], psb[bk][0:64, :].rearrange("p (h t) -> p h t", t=64),
                                                  AF.Identity, scale=-1.0), reads=[psb[bk]], writes=[(Zs, bk)])
    for h in range(16):
        bk, col = h // 8, (h % 8) * 64
        P.op("pe", lambda e, h=h, bk=bk, col=col: e.matmul(psb[2 + bk][0:64, col:col + 64], TT[:, h, :], Zs[:, h, :], start=True, stop=True),
             reads=kTT + [(Zs, bk)], writes=[psb[2 + bk]])
    for bk in range(2):
        P.op("dve", lambda e, bk=bk: e.tensor_copy(Us[:, bk * 8:(bk + 1) * 8, :], psb[2 + bk][0:64, :].rearrange("p (h t) -> p h t", t=64)),
             reads=[psb[2 + bk]], writes=[(Us, bk)])
    for h in range(16):
        bk, col = h // 8, (h % 8) * 64
        P.op("pe", lambda e, h=h, bk=bk, col=col: e.matmul(psb[4 + bk][0:64, col:col + 64], Xc[:, s, h, 3, :], ST[:, h, :], start=True, stop=False),
             reads=[kx, ST], writes=[psb[4 + bk]])
        P.op("pe", lambda e, h=h, bk=bk, col=col: e.matmul(psb[4 + bk][0:64, col:col + 64], AM[:, h, 1, :], V(h), start=False, stop=False),
             reads=allAM + [ktm], writes=[psb[4 + bk]])
        P.op("pe", lambda e, h=h, bk=bk, col=col: e.matmul(psb[4 + bk][0:64, col:col + 64], AM[:, h, 3, :], Us[:, h, :], start=False, stop=True),
             reads=allAM + [(Us, bk)], writes=[psb[4 + bk]])
        P.op("pe", lambda e, h=h, bk=bk, col=col: e.matmul(psb[6 + bk][0:64, col:col + 64], TMc[:, 0, 0, h * 64:(h + 1) * 64], V(h), start=True, stop=False),
             reads=[ktm], writes=[psb[6 + bk]])
        P.op("pe", lambda e, h=h, bk=bk, col=col: e.matmul(psb[6 + bk][0:64, col:col + 64], TMc[:, 0, 1, h * 64:(h + 1) * 64], Us[:, h, :], start=False, stop=True),
             reads=[ktm, (Us, bk)], writes=[psb[6 + bk]])
    ky = (Yc, s)
    tmpS = R["tmpS"]
    for bk in range(2):
        P.op("act", lambda e, bk=bk: e.activation(Yc[:, s, bk * 512:(bk + 1) * 512], psb[4 + bk][0:64, :], AF.Identity),
             reads=[psb[4 + bk]], writes=[(Yc, (s, bk))])
        hsb = slice(bk * 8, (bk + 1) * 8)
        P.op("dve", lambda e, bk=bk, hsb=hsb: e.tensor_tensor(tmpS[:, hsb, :], psb[6 + bk][0:64, :].rearrange("p (h t) -> p h t", t=64),
                                                             ST[:, hsb, :], ALU.add), reads=[psb[6 + bk], ST], writes=[(tmpS, bk)])
        P.op("pool", lambda e, bk=bk, hsb=hsb: e.tensor_tensor(ST[:, hsb, :], tmpS[:, hsb, :],
                                                              R["wCt"][:, hsb, ch:ch + 1].broadcast_to([64, 8, 64]), ALU.mult),
             reads=[(tmpS, bk), R["wCt"]], writes=[ST])
    kyall = [(Yc, (s, 0)), (Yc, (s, 1))]
    if os.environ.get("RW_DBG") and d == 0 and ci == 0:
        D = S["dbg"]
        P.dma(D["AM"], AM[:, :, :, :], reads=allAM, qn="pool")
        P.dma(D["Lm"], Lm[:, :, :], reads=[(Lm, 0), (Lm, 1)], qn="pool")
        P.dma(D["TT"], TT[:, :, :], reads=kTT, qn="pool")
        P.dma(D["Zs"], Zs[:, :, :], reads=[(Zs, 0), (Zs, 1)], qn="pool")
        P.dma(D["Us"], Us[:, :, :], reads=[(Us, 0), (Us, 1)], qn="pool")
        P.dma(D["Yc"], Yc[:, s, :], reads=kyall, qn="pool")
        P.dma(D["ST"], ST[:, :, :], reads=[ST], qn="pool")
    if LV <= 4:
        return
    if d == 0:
        P.dma(S["YfTM"][t0:t0 + 64, :], Yc[:, s, :], reads=kyall, qn="pool")
        return
    if ch < 4 and not do_ctx:
        return
    Yf, ln, stt, cen, sq, bg, yo = R["Yf"], R["ln"], R["st"], R["cen"], R["sq"], R["bg"], R["yo"]
    kf = (Yf, 0)
    P.dma(Yf[:, 0, :], S["YfTM"][t0:t0 + 64, :], writes=[kf])
    kb_ = (bg, 0)
    P.dma(bg[:, 0, 0, :, :], S["bonT"][:, t0:t0 + 64].rearrange("(c p) t -> p c t", p=128), writes=[kb_])
    P.dma(bg[:, 0, 1, :, :], S["gateT"][:, t0:t0 + 64].rearrange("(c p) t -> p c t", p=128), writes=[kb_])
    P.op("dve", lambda e: e.tensor_tensor(Yf[:, 0, :], Yf[:, 0, :], Yc[:, s, :], ALU.add), reads=[kf] + kyall, writes=[kf])
    y3 = Yf[:, 0, :].rearrange("p (h t) -> p h t", t=64)
    c3 = cen[:, :].rearrange("p (h t) -> p h t", t=64)
    q3 = sq[:, :].rearrange("p (h t) -> p h t", t=64)
    P.op("dve", lambda e: e.tensor_reduce(stt[:, 0, :], y3, AX.X, ALU.add), reads=[kf], writes=[(stt, 0)])
    P.op("dve", lambda e: e.tensor_scalar(stt[:, 0, :], stt[:, 0, :], 1.0 / 64, None, ALU.mult), reads=[(stt, 0)], writes=[(stt, 0)])
    P.op("dve", lambda e: e.tensor_tensor(c3, y3, stt[:, 0, :].rearrange("p (h o) -> p h o", o=1).broadcast_to([64, 16, 64]), ALU.subtract),
         reads=[kf, (stt, 0)], writes=[cen])
    P.op("act", lambda e: e.activation(sq[:, :], cen[:, :], AF.Square), reads=[cen], writes=[sq])
    P.op("dve", lambda e: e.tensor_reduce(stt[:, 1, :], q3, AX.X, ALU.add), reads=[sq], writes=[(stt, 1)])
    P.op("act", lambda e: e.activation(stt[:, 1, :], stt[:, 1, :], AF.Sqrt, bias=Kk.epsb[0:64, 1:2], scale=1.0 / 64),
         reads=[(stt, 1), Kk.epsb], writes=[(stt, 1)])
    P.op("dve", lambda e: e.reciprocal(stt[:, 1, :], stt[:, 1, :]), reads=[(stt, 1)], writes=[(stt, 1)])
    P.op("dve", lambda e: e.tensor_tensor(c3, c3, stt[:, 1, :].rearrange("p (h o) -> p h o", o=1).broadcast_to([64, 16, 64]), ALU.mult),
         reads=[cen, (stt, 1)], writes=[cen])
    P.op("pool", lambda e: e.tensor_tensor(cen[:, :], cen[:, :], ln[:, 0, :], ALU.mult), reads=[cen, ln], writes=[cen])
    P.op("pool", lambda e: e.tensor_tensor(cen[:, :], cen[:, :], ln[:, 1, :], ALU.add), reads=[cen, ln], writes=[cen])
    pst = psb[0]
    for c in range(8):
        P.op("pe", lambda e, c=c: e.matmul(pst[:, c * 64:(c + 1) * 64], cen[:, c * 128:(c + 1) * 128], id64, start=True, stop=True),
             reads=[cen, Kk.ident], writes=[pst])
    ko = (yo, s)
    P.op("dve", lambda e: e.tensor_tensor(yo[:, s, :, :], pst[:, :].rearrange("p (c t) -> p c t", t=64), bg[:, 0, 0, :, :], ALU.add),
         reads=[pst, kb_], writes=[ko])
    P.op("pool", lambda e: e.tensor_tensor(yo[:, s, :, :], yo[:, s, :, :], bg[:, 0, 1, :, :], ALU.mult), reads=[ko, kb_], writes=[ko])
    P.dma(yT[1024:2048, t0:t0 + 64].rearrange("(c p) t -> p c t", p=128), yo[:, s, :, :], reads=[ko], qn="pool")


def fm(v, n=None):
    v = np.asarray(v, np.float32)
    return np.ascontiguousarray(v.reshape(-1, 128).T)


def host_prep(inp, ncores=8):
    f32 = np.float32
    L = 4
    sh = {}
    sh["c_ones"] = np.ones((128, 128), f32)
    sh["c_ident"] = np.eye(128, dtype=f32)
    sh["c_onesD"] = np.full((128, 128), 1.0 / D, f32)
    eps = np.zeros((128, 4), f32)
    eps[:, 0] = EPS
    eps[:, 1] = 64e-5
    eps[:, 2] = 1e-12
    sh["c_eps"] = eps
    sh["ada_w"] = np.asarray(inp["ada_w"], f32)
    sh["ada_b"] = np.stack([fm(inp["ada_b"][l]) for l in range(L)])
    sh["norm_mix"] = np.stack([np.repeat(fm(inp["norm_mix"][l])[:, :, None], 2, 2) for l in range(L)])
    sh["norm_ffn"] = np.stack([np.repeat(fm(inp["norm_ffn"][l])[:, :, None], 2, 2) for l in range(L)])
    sh["ffn_w_up"] = np.asarray(inp["ffn_w_up"], f32)
    sh["ffn_w_down"] = np.asarray(inp["ffn_w_down"], f32)
    sh["ffn_conv_w"] = np.stack([np.stack([fm(inp["ffn_conv_w"][l][k]) for k in range(3)], 1) for l in range(L)])
    sh["ffn_conv_b"] = np.stack([fm(inp["ffn_conv_b"][l]) for l in range(L)])
    sh["ev_w_in"] = np.asarray(inp["ev_w_in"], f32)
    sh["ev_w_out"] = np.asarray(inp["ev_w_out"], f32)
    sh["od_w_in"] = np.asarray(inp["od_w_in"], f32)
    sh["od_w_out"] = np.asarray(inp["od_w_out"], f32)
    sh["final_norm"] = fm(inp["final_norm"])
    eps[:, 3] = 1.0
    host_prep_even(inp, sh)
    host_prep_odd(inp, sh)
    per = []
    x = np.asarray(inp["x"], f32)
    ctx = np.asarray(inp["ctx"], f32)
    c = np.asarray(inp["c"], f32)
    cc = np.asarray(inp["c_ctx"], f32)
    for i in range(ncores):
        b = i % 4
        d = {}
        d["xT0"] = np.ascontiguousarray(np.concatenate([ctx[b], x[b]], 0).T)
        d["cT"] = np.ascontiguousarray(np.stack([fm(c[b]), fm(cc)], 2))
        per.append(d)
    return sh, per


def rope_tables(hd):
    d = hd // 2
    half = d // 2
    t = np.arange(NLAT)
    row = (t // 64).astype(np.float32)
    col = (t % 64).astype(np.float32)
    inv = (10000.0 ** (-np.arange(half, dtype=np.float32) / half)).astype(np.float32)
    cos = np.zeros((hd, NLAT), np.float32)
    sin = np.zeros((hd, NLAT), np.float32)
    R = np.zeros((hd, hd), np.float32)
    for p in range(hd):
        pos = row if p < d else col
        i = (p % d) % half
        ang = pos * inv[i]
        cos[p] = np.cos(ang)
        sin[p] = np.sin(ang)
        if (p % d) < half:
            R[p, p + half] = -1.0
        else:
            R[p, p - half] = 1.0
    return cos, sin, np.ascontiguousarray(R.T)


def host_prep_even(inp, sh):
    f32 = np.float32
    sh["qk_gain"] = np.stack([np.stack([inp["attn_q_norm"][e], inp["attn_k_norm"][e]], 1) for e in range(2)]).astype(f32)
    cos, sin, RT = rope_tables(128)
    sh["c_cos128"], sh["c_sin128"], sh["c_RT128"] = cos, sin, RT
    sh["c_onesH"] = np.full((128, 128), 1.0 / 128, f32)
    sh["c_ones512"] = np.full((128, 128), 1.0 / 512, f32)
    sh["dt_ba"] = np.stack([np.stack([inp["ssm_dt_bias"][e].reshape(32), inp["ssm_a_log"][e].reshape(32)], 1) for e in range(2)]).astype(f32)
    sh["ssm_conv_w"] = np.stack([np.stack([fm(inp["ssm_conv_w"][e][k]) for k in range(5)], 2) for e in range(2)]).astype(f32)
    sh["ssm_conv_b"] = np.stack([fm(inp["ssm_conv_b"][e]) for e in range(2)]).astype(f32)
    s_ = np.arange(128)[:, None]
    l_ = np.arange(128)[None, :]
    sh["c_tri"] = np.ascontiguousarray(np.stack([s_ <= l_, s_ > l_, s_ >= l_, s_ < l_], 1).astype(f32))
    sh["ssm_d"] = np.stack([fm(np.repeat(inp["ssm_d"][e], 64)) for e in range(2)]).astype(f32)
    sh["ssm_norm"] = np.stack([fm(inp["ssm_norm"][e]) for e in range(2)]).astype(f32)


def host_prep_odd(inp, sh):
    f32 = np.float32
    sh["mla_q_b"] = np.asarray(inp["mla_q_b"], f32)
    sh["mla_kv_b"] = np.asarray(inp["mla_kv_b"], f32)
    sh["mla_gain"] = np.stack([np.concatenate([fm(inp["mla_q_a_norm"][o]), fm(inp["mla_kv_a_norm"][o])], 1) for o in range(2)]).astype(f32)
    cos, sin, RT = rope_tables(64)
    sh["c_cos64"], sh["c_sin64"], sh["c_RT64"] = cos, sin, RT
    sh["c_ones256"] = np.full((128, 128), 1.0 / 256, f32)
    def fm_ud(v):
        out = np.zeros((128, 30), f32)
        v = np.asarray(v, f32)
        for i in range(24):
            out[:, i] = v[i * 128:(i + 1) * 128]
        for i in range(4):
            out[:96, 24 + i] = v[3072 + i * 96:3072 + (i + 1) * 96]
        out[:, 28] = v[3456:3584]
        out[:, 29] = v[3584:3712]
        return out
    sh["rw_mu"] = np.stack([fm_ud(inp["rwkv_mu"][o]) for o in range(2)])
    sh["rw_vec"] = np.stack([np.stack([fm(inp["rwkv_k_k"][o]), fm(inp["rwkv_k_a"][o]), fm(inp["rwkv_r_k"][o].reshape(-1)),
                                       fm(inp["rwkv_w0"][o][0]), fm(inp["rwkv_w0"][o][1]), fm(inp["rwkv_a0"][o][0]), fm(inp["rwkv_a0"][o][1])], 1)
                             for o in range(2)]).astype(f32)
    sh["rwkv_g2"] = np.asarray(inp["rwkv_g2"], f32)
    sh["rwkv_w2"] = np.asarray(inp["rwkv_w2"], f32)
    sh["rwkv_a2"] = np.asarray(inp["rwkv_a2"], f32)
    sh["rw_ln"] = np.stack([np.stack([inp["rwkv_ln_w"][o], inp["rwkv_ln_b"][o]]) for o in range(2)]).astype(f32)
    blk = np.zeros((128, 128), f32); blk[:64, :64] = 1; blk[64:, 64:] = 1
    sh["c_blk64"] = blk
    s_ = np.arange(64)[:, None]; t_ = np.arange(64)[None, :]
    sh["c_m64"] = np.ascontiguousarray(np.stack([s_ < t_, s_ <= t_, s_ > t_, s_ >= t_], 1).astype(f32))


def declare_inputs(Kk, sh, per0):
    for k, v in list(sh.items()) + list(per0.items()):
        Kk.din(k, v.shape)


def run(nc, sh, per, trace=False):
    in_maps = []
    for d in per:
        m = dict(sh)
        m.update(d)
        in_maps.append(m)
    return run_bass_kernel_spmd(nc, in_maps, core_ids=list(range(len(per))), trace=trace)

def build_program(sh, per0):
    nc = bass.Bass("TRN2", target_bir_lowering=False)
    with ExitStack() as es:
        Kk = K(nc, es)
        declare_inputs(Kk, sh, per0)
        P = Kk.P
        Se = dict(qT=Kk.dscr("qT", [8, 128, NT]), kT=Kk.dscr("kT", [2, 128, NT]), vM=Kk.dscr("vMe", [NT, 256]),
                  zT=Kk.dscr("zT", [1024, NT]), xbcT=Kk.dscr("xbcT", [1536, NT]), dtT=Kk.dscr("dtT", [32, NT]),
                  xcT=Kk.dscr("xcT", [1536, NT]), ysT=Kk.dscr("ysT", [1024, NT]))
        So = dict(qnT=Kk.dscr("qnT", [8, 128, NT]), qpT=Kk.dscr("qpT", [8, 64, NT]), knT=Kk.dscr("knT", [8, 128, NT]),
                  kpT=Kk.dscr("kpT", [64, NT]), vM=Kk.dscr("vMo", [NT, 1024]), udT=Kk.dscr("udT", [3712, NT]),
                  RW=Kk.dscr("RW", [2, 4, 1024, NT]), kbTM=Kk.dscr("kbTM", [2, NT, 1024]), bbTM=Kk.dscr("bbTM", [2, NT, 1024]),
                  vTM=Kk.dscr("vTM", [NT, 1024]), wC=Kk.dscr("wC", [2, 1024, NCH64]), gateT=Kk.dscr("gateT", [1024, NT]),
                  bonT=Kk.dscr("bonT", [1024, NT]), YfTM=Kk.dscr("YfTM", [NT, 1024]))
        yT = Kk.dscr("yT", [2048, NT])
        xA = Kk.dscr("xA", [2048, NT])
        xB = Kk.dscr("xB", [2048, NT])
        outT = Kk.dscr("outT", [2048, NLAT], out=True)
        load_consts(Kk)
        with ExitStack() as ses:
            P.stage_es = ses
            cp = P.sb("cp", [128, 2, 16, 512])
            for i, (t0_, n, isctx) in enumerate(tiles_main()):
                P.dma(cp[:, i % 2, :, 0:n], Kk.ins["xT0"][:, t0_:t0_ + n].rearrange("(c p) t -> p c t", p=128), writes=[(cp, i % 2)])
                P.dma(xA[:, t0_:t0_ + n].rearrange("(c p) t -> p c t", p=128), cp[:, i % 2, :, 0:n], reads=[(cp, i % 2)], qn="pool")
            P.barrier()
            P.stage_es = None
        cur, oth = xA, xB
        for l in range(4):
            last = (l == 3)
            stage_adaln(Kk, l)
            if l % 2 == 0:
                stage_A_even(Kk, l, cur, Se)
                stage_attn_even(Kk, Se, yT, not last)
                stage_ssd_conv(Kk, l, Se)
                stage_ssd(Kk, l, Se, yT, not last)
                stage_C1(Kk, l, cur, yT, Kk.ins["ev_w_out"][l // 2], not last)
            else:
                stage_A_odd(Kk, l, cur, So)
                stage_attn_odd(Kk, So, yT, not last)
                stage_rwkv_prep(Kk, l, So)
                stage_rwkv(Kk, l, So, yT, not last)
                stage_C1(Kk, l, cur, yT, Kk.ins["od_w_out"][l // 2], not last)
            if last:
                stage_C2(Kk, l, cur, oth, False, final_norm=Kk.ins["final_norm"], outT=outT)
            else:
                stage_C2(Kk, l, cur, oth, True)
            cur, oth = oth, cur
        P.emit_all()
    return nc


def kernel(**inputs):
    sh, per = host_prep(inputs, ncores=8)
    nc = build_program(sh, per[0])
    res = run(nc, sh, per, trace=False)
    out = np.stack([np.ascontiguousarray(res.results[b]["outT"].T) for b in range(4)], 0)
    return out.astype(np.float32)
```

```python
from concourse.bass_utils import run_bass_kernel_spmd
import numpy as np
import concourse.bass as bass
import concourse.mybir as mybir
from contextlib import ExitStack

F32 = mybir.dt.float32
BF16 = mybir.dt.bfloat16
AF = mybir.ActivationFunctionType
ALU = mybir.AluOpType
AX = mybir.AxisListType

ENGS = ("pe", "act", "dve", "pool", "sp")


class Buf:
    _n = 0

    def __init__(self, t, name):
        self.t = t
        self.name = name
        self.st = {}
        Buf._n += 1

    def __getitem__(self, idx):
        return self.t[idx]


class Prog:
    def __init__(self, nc, es):
        self.nc = nc
        self.es = es
        self.q = {e: [] for e in ENGS}
        self.cnt = {e: 0 for e in ENGS if e != "sp"}
        self.esem = {e: es.enter_context(nc.semaphore("s_" + e)) for e in ENGS if e != "sp"}
        self.NS = 12
        self.dsem = {qn: [es.enter_context(nc.semaphore("d_%s%d" % (qn, i))) for i in range(self.NS)]
                     for qn in ("sp", "pool")}
        self.dcnt = {"sp": 0, "pool": 0}
        self.dval = {qn: [0] * self.NS for qn in ("sp", "pool")}
        self.bar = es.enter_context(nc.semaphore("bar"))
        self.nbar = 0
        self.known = {e: {} for e in ENGS}
        self.semobj = {}
        for e in self.esem:
            self.semobj[("c", e)] = self.esem[e]
        for qn in self.dsem:
            for i, s in enumerate(self.dsem[qn]):
                self.semobj[("d", qn, i)] = s
        self.bufs = []
        self.n_ins = 0
        self.stage_es = None

    def sb(self, name, shape, dt=F32):
        self.n_sb = getattr(self, "n_sb", 0) + 1
        name = "%s_%d" % (name, self.n_sb)
        t = (self.stage_es or self.es).enter_context(self.nc.sbuf_tensor(name, list(shape), dt))
        b = Buf(t, name)
        self.bufs.append(b)
        return b

    def ps(self, name, shape, dt=F32):
        t = self.es.enter_context(self.nc.psum_tensor(name, list(shape), dt))
        b = Buf(t, name)
        self.bufs.append(b)
        return b

    def _need(self, eng, tok):
        sk, val = tok
        if self.known[eng].get(sk, 0) >= val:
            return None
        self.known[eng][sk] = val
        return (sk, val)

    def _deps(self, eng, reads, writes):
        waits = {}

        def add(tok):
            r = self._need(eng, tok)
            if r is not None:
                waits[r[0]] = max(waits.get(r[0], 0), r[1])

        def keys(b, k):
            if k is None:
                return list(b.st.keys())
            return [k, None]

        for (b, k) in reads:
            for kk in keys(b, k):
                st = b.st.get(kk)
                if st:
                    for tok in st[0]:
                        add(tok)
        for (b, k) in writes:
            for kk in keys(b, k):
                st = b.st.get(kk)
                if st:
                    for tok in st[0]:
                        add(tok)
                    for tok in st[1]:
                        add(tok)
        return list(waits.items())

    def _commit(self, tok, reads, writes):
        for (b, k) in writes:
            if k is None:
                b.st = {None: ([tok], [])}
            else:
                b.st[k] = ([tok], [])
        for (b, k) in reads:
            st = b.st.setdefault(k, ([], []))
            rl = st[1]
            rl[:] = [t for t in rl if t[0] != tok[0]]
            rl.append(tok)

    @staticmethod
    def _norm(lst):
        out = []
        for x in lst or []:
            if isinstance(x, Buf):
                out.append((x, None))
            else:
                out.append(x)
        return out

    def op(self, eng, fn, reads=None, writes=None):
        reads = self._norm(reads)
        writes = self._norm(writes)
        waits = self._deps(eng, reads, writes)
        self.cnt[eng] += 1
        tok = (("c", eng), self.cnt[eng])
        sem = self.esem[eng]
        semobj = self.semobj

        def emit(e, waits=waits, fn=fn, sem=sem):
            for sk, v in waits:
                e.wait_ge(semobj[sk], v)
            fn(e).then_inc(sem, 1)

        self.q[eng].append(emit)
        self._commit(tok, reads, writes)
        self.n_ins += 1
        return tok

    def dma(self, out_ap, in_ap, reads=None, writes=None, qn="sp"):
        reads = self._norm(reads)
        writes = self._norm(writes)
        waits = self._deps(qn, reads, writes)
        i = self.dcnt[qn] % self.NS
        self.dcnt[qn] += 1
        prev = self.dval[qn][i]
        sk = ("d", qn, i)
        r = self._need(qn, (sk, prev)) if prev > 0 else None
        if r is not None:
            waits.append(r)
        self.dval[qn][i] = prev + 16
        tok = (sk, prev + 16)
        sem = self.semobj[sk]
        semobj = self.semobj

        def emit(e, waits=waits, sem=sem, out_ap=out_ap, in_ap=in_ap):
            for k, v in waits:
                e.wait_ge(semobj[k], v)
            e.dma_start(out=out_ap, in_=in_ap).then_inc(sem, 16)

        self.q[qn].append(emit)
        self._commit(tok, reads, writes)
        self.n_ins += 1
        return tok

    def barrier(self):
        self.nbar += 1
        nb = self.nbar
        semobj = self.semobj
        bar = self.bar
        for e in ENGS:
            waits = []
            if e in self.cnt:
                if self.cnt[e] > 0:
                    waits.append((("c", e), self.cnt[e]))
            if e in self.dsem:
                for i in range(self.NS):
                    if self.dval[e][i] > 0:
                        waits.append((("d", e, i), self.dval[e][i]))

            def emit(en, waits=waits, nb=nb):
                for k, v in waits:
                    en.wait_ge(semobj[k], v)
                en.sem_inc(bar, 1)
                en.wait_ge(bar, len(ENGS) * nb)

            self.q[e].append(emit)
        allk = {}
        for e in self.cnt:
            allk[("c", e)] = self.cnt[e]
        for qn in self.dsem:
            for i in range(self.NS):
                allk[("d", qn, i)] = self.dval[qn][i]
        for e in ENGS:
            self.known[e] = dict(allk)
        for b in self.bufs:
            b.st = {}

    def emit_all(self):
        nc = self.nc
        q = self.q
        with nc.Block() as block:
            @block.sync
            def _(e):
                for f in q["sp"]:
                    f(e)

            @block.tensor
            def _(e):
                for f in q["pe"]:
                    f(e)

            @block.scalar
            def _(e):
                for f in q["act"]:
                    f(e)

            @block.vector
            def _(e):
                for f in q["dve"]:
                    f(e)

            @block.gpsimd
            def _(e):
                for f in q["pool"]:
                    f(e)

import numpy as np

D = 2048
NKD = 16
NCTX = 256
NLAT = 4096
NT = NCTX + NLAT
DFF = 5632
EVEN_IN = 4128
ODD_IN = 4544
EPS = 1e-6


def tiles_main():
    t = [(0, NCTX, True)]
    for i in range(NLAT // 512):
        t.append((NCTX + 512 * i, 512, False))
    return t


def tiles_ffn():
    t = [(0, NCTX, 0, 0)]
    nt = 9
    base, rem = divmod(NLAT, nt)
    o = NCTX
    for i in range(nt):
        n = base + (1 if i < rem else 0)
        t.append((o, n, 0 if i == 0 else 1, 0 if i == nt - 1 else 1))
        o += n
    return t


class K:
    def __init__(self, nc, es):
        self.nc = nc
        self.P = Prog(nc, es)
        P = self.P
        self.ins = {}
        self.psb = [P.ps("ps%d" % i, [128, 512]) for i in range(8)]
        self.wb = []
        self.wbi = 0
        self.wbh = []
        self.wbhi = 0
        self.ones = P.sb("ones", [128, 128])
        self.ident = P.sb("ident", [128, 128])
        self.consts_loaded = False

    def din(self, name, shape):
        t = self.nc.dram_tensor(name, list(shape), F32, kind="ExternalInput").ap()
        self.ins[name] = t
        return t

    def dscr(self, name, shape, out=False):
        return self.nc.dram_tensor(name, list(shape), F32,
                                   kind="ExternalOutput" if out else "Internal").ap()


def alloc_wb(Kk, n_stage, n_bf=3):
    P = Kk.P
    Kk.wb = [P.sb("wb%d" % i, [128, 16, 256]) for i in range(n_stage)]
    Kk.wbh = [P.sb("wbh%d" % i, [128, 16, 256], BF16) for i in range(n_bf)] if n_bf else []


def _cast_w(Kk, wb, nfull, bw):
    P = Kk.P
    wh = Kk.wbh[Kk.wbhi % len(Kk.wbh)]
    eng = "act"
    Kk.wbhi += 1
    if eng == "pool":
        P.op("pool", lambda e: e.tensor_copy(wh[:, 0:nfull, 0:bw], wb[:, 0:nfull, 0:bw]), reads=[wb], writes=[wh])
    else:
        P.op("act", lambda e: e.activation(wh[:, 0:nfull, 0:bw], wb[:, 0:nfull, 0:bw], AF.Identity), reads=[wb], writes=[wh])
    return wh


def gemm_fm(Kk, W, krows, chunks, rhs_fn, ntok, evac, ps_ids=(0, 1), rhs_reads=(), lowp=False):
    P = Kk.P
    nk = (krows + 127) // 128
    nfull = krows // 128
    blocks = []
    cur = []
    for i, (c0, cw) in enumerate(chunks):
        if cur and (cur[0][1] + sum(c[2] for c in cur) == c0) and (sum(c[2] for c in cur) + cw <= 256):
            cur.append((i, c0, cw))
        else:
            if cur:
                blocks.append(cur)
            cur = [(i, c0, cw)]
    if cur:
        blocks.append(cur)
    pi = 0
    for blk in blocks:
        b0 = blk[0][1]
        bw = sum(c[2] for c in blk)
        wb = Kk.wb[Kk.wbi % len(Kk.wb)]
        Kk.wbi += 1
        if nfull > 0:
            src = W[0:nfull * 128, b0:b0 + bw].rearrange("(kc p) c -> p kc c", p=128)
            P.dma(wb[:, 0:nfull, 0:bw], src, writes=[wb])
        if nfull < nk:
            kp = krows - nfull * 128
            P.dma(wb[0:kp, nfull, 0:bw], W[nfull * 128:krows, b0:b0 + bw], writes=[wb])
        if lowp:
            assert nfull == nk
            wb = _cast_w(Kk, wb, nfull, bw)
        for (i, c0, cw) in blk:
            ps = Kk.psb[ps_ids[pi % len(ps_ids)]]
            pi += 1
            off = c0 - b0
            for kc in range(nk):
                kp = min(128, krows - kc * 128)
                rhs = rhs_fn(kc, kp)
                P.op("pe", lambda e, ps=ps, wb=wb, kc=kc, kp=kp, off=off, cw=cw, rhs=rhs:
                     e.matmul(ps[0:cw, 0:ntok], wb[0:kp, kc, off:off + cw], rhs,
                              start=(kc == 0), stop=(kc == nk - 1)),
                     reads=[wb] + list(rhs_reads), writes=[ps])
            evac(i, ps, cw)


def gemm_tm(Kk, W, krows, c0, ncols, lhs_fn, ntok, evac, ps_ids=(0, 1), lhs_reads=(), lowp=False):
    P = Kk.P
    nk = (krows + 127) // 128
    nfull = krows // 128
    pi = 0
    for cb0 in range(0, ncols, 256):
        cbw = min(256, ncols - cb0)
        wb = Kk.wb[Kk.wbi % len(Kk.wb)]
        Kk.wbi += 1
        if nfull > 0:
            src = W[0:nfull * 128, c0 + cb0:c0 + cb0 + cbw].rearrange("(kc p) c -> p kc c", p=128)
            P.dma(wb[:, 0:nfull, 0:cbw], src, writes=[wb])
        if nfull < nk:
            kp = krows - nfull * 128
            P.dma(wb[0:kp, nfull, 0:cbw], W[nfull * 128:krows, c0 + cb0:c0 + cb0 + cbw], writes=[wb])
        if lowp:
            assert nfull == nk
            wb = _cast_w(Kk, wb, nfull, cbw)
        for tb in range((ntok + 127) // 128):
            t0 = tb * 128
            tn = min(128, ntok - t0)
            ps = Kk.psb[ps_ids[pi % len(ps_ids)]]
            pi += 1
            for kc in range(nk):
                kp = min(128, krows - kc * 128)
                lhs = lhs_fn(kc, kp, t0, tn)
                P.op("pe", lambda e, ps=ps, wb=wb, kc=kc, kp=kp, t0=t0, tn=tn, cbw=cbw, lhs=lhs:
                     e.matmul(ps[0:tn, 0:cbw], lhs, wb[0:kp, kc, 0:cbw],
                              start=(kc == 0), stop=(kc == nk - 1)),
                     reads=[wb] + list(lhs_reads), writes=[ps])
            evac(tb, ps, tn, cb0, cbw)


def colsum_bcast(Kk, src_fn, nchunks, ntok, ps, sq, scale_ones, src_reads, kp_fn=None):
    P = Kk.P
    for c in range(nchunks):
        kp = 128 if kp_fn is None else kp_fn(c)
        s = c % 4
        src = src_fn(c, kp)
        P.op("act", lambda e, c=c, s=s, kp=kp, src=src: e.activation(sq[0:kp, s, 0:ntok], src, AF.Square),
             reads=list(src_reads), writes=[(sq, s)])
        P.op("pe", lambda e, c=c, s=s, kp=kp: e.matmul(ps[:, 0:ntok], scale_ones[0:kp, :], sq[0:kp, s, 0:ntok],
                                                        start=(c == 0), stop=(c == nchunks - 1)),
             reads=[(sq, s), scale_ones], writes=[ps])


def rstd_from(Kk, ps, ntok, rstd, eps):
    P = Kk.P
    P.op("act", lambda e: e.activation(rstd[:, 0:ntok], ps[:, 0:ntok], AF.Sqrt, bias=Kk.epsb[:, 0:1] if eps == EPS else Kk.eps2b[:, 0:1], scale=1.0),
         reads=[ps, Kk.epsb], writes=[rstd])
    P.op("dve", lambda e: e.reciprocal(rstd[:, 0:ntok], rstd[:, 0:ntok]), reads=[rstd], writes=[rstd])


def load_consts(Kk):
    P = Kk.P
    P.dma(Kk.ones[:, :], Kk.ins["c_ones"][:, :], writes=[Kk.ones])
    P.dma(Kk.ident[:, :], Kk.ins["c_ident"][:, :], writes=[Kk.ident])
    Kk.onesD = P.sb("onesD", [128, 128])
    P.dma(Kk.onesD[:, :], Kk.ins["c_onesD"][:, :], writes=[Kk.onesD])
    Kk.epsb = P.sb("epsb", [128, 4])
    P.dma(Kk.epsb[:, :], Kk.ins["c_eps"][:, :], writes=[Kk.epsb])
    Kk.modT = P.sb("modT", [128, 96, 2])
    Kk.Amix = P.sb("Amix", [128, 16, 2])
    Kk.Affn = P.sb("Affn", [128, 16, 2])
    Kk.actT = P.sb("actT", [128, 16, 2])
    P.dma(Kk.actT[:, :, :], Kk.ins["cT"][:, :, :], writes=[Kk.actT])
    P.op("act", lambda e: e.activation(Kk.actT[:, :, :], Kk.actT[:, :, :], AF.Silu), reads=[Kk.actT], writes=[Kk.actT])


def stage_adaln(Kk, l):
    P = Kk.P
    with ExitStack() as ses:
        P.stage_es = ses
        alloc_wb(Kk, 4, 0)
        adab = P.sb("adab", [128, 96])
        nrm = P.sb("nrm", [128, 2, 16, 2])
        P.dma(adab[:, :], Kk.ins["ada_b"][l], writes=[adab])
        P.dma(nrm[:, 0], Kk.ins["norm_mix"][l], writes=[nrm])
        P.dma(nrm[:, 1], Kk.ins["norm_ffn"][l], writes=[nrm])
        W = Kk.ins["ada_w"][l]
        chunks = [(i * 128, 128) for i in range(96)]

        def evac(i, ps, cw):
            P.op("act", lambda e, i=i, ps=ps: e.activation(Kk.modT[:, i, :], ps[:, 0:2], AF.Identity,
                                                            bias=adab[:, i:i + 1], scale=1.0),
                 reads=[ps, adab], writes=[(Kk.modT, i)])

        gemm_fm(Kk, W, D, chunks, lambda kc, kp: Kk.actT[0:kp, kc, :], 2, evac, rhs_reads=[Kk.actT])
        allm = [(Kk.modT, i) for i in range(96)]
        P.op("dve", lambda e: e.scalar_tensor_tensor(Kk.Amix[:, :, :], Kk.modT[:, 16:32, :], 1.0, nrm[:, 0], ALU.add, ALU.mult),
             reads=allm + [nrm], writes=[Kk.Amix])
        P.op("dve", lambda e: e.scalar_tensor_tensor(Kk.Affn[:, :, :], Kk.modT[:, 64:80, :], 1.0, nrm[:, 1], ALU.add, ALU.mult),
             reads=allm + [nrm], writes=[Kk.Affn])
        P.barrier()
        P.stage_es = None


def modulate(Kk, xt, ht, ntok, A, Bidx, m, sq, rstd, ps, tmpf=None):
    P = Kk.P
    colsum_bcast(Kk, lambda c, kp: xt[:, c, 0:ntok], 16, ntok, ps, sq, Kk.onesD, [xt])
    rstd_from(Kk, ps, ntok, rstd, EPS)
    allm = [(Kk.modT, i) for i in range(96)]
    for c in range(16):
        if tmpf is None:
            P.op("dve", lambda e, c=c: e.scalar_tensor_tensor(ht[:, c, 0:ntok], xt[:, c, 0:ntok], A[:, c, m:m + 1],
                                                               rstd[:, 0:ntok], ALU.mult, ALU.mult),
                 reads=[xt, A, rstd], writes=[(ht, c)])
            P.op("pool", lambda e, c=c: e.tensor_scalar(ht[:, c, 0:ntok], ht[:, c, 0:ntok], Kk.modT[:, Bidx + c, m:m + 1], None, ALU.add),
                 reads=[(ht, c), (Kk.modT, Bidx + c)], writes=[(ht, c)])
        else:
            s_ = c % 2
            P.op("dve", lambda e, c=c, s_=s_: e.scalar_tensor_tensor(tmpf[:, s_, 0:ntok], xt[:, c, 0:ntok], A[:, c, m:m + 1],
                                                                      rstd[:, 0:ntok], ALU.mult, ALU.mult),
                 reads=[xt, A, rstd], writes=[(tmpf, s_)])
            P.op("pool", lambda e, c=c, s_=s_: e.tensor_scalar(ht[:, c, 0:ntok], tmpf[:, s_, 0:ntok], Kk.modT[:, Bidx + c, m:m + 1], None, ALU.add),
                 reads=[(tmpf, s_), (Kk.modT, Bidx + c)], writes=[(ht, c)])


def stage_A(Kk, l, xT, W, plan, nsteps_extra=None):
    P = Kk.P
    with ExitStack() as ses:
        P.stage_es = ses
        alloc_wb(Kk, 3, 3)
        xt = P.sb("xt", [128, 16, 512])
        ht = P.sb("ht", [128, 16, 512], BF16)
        htf = P.sb("htf", [128, 2, 512])
        sq = P.sb("sq", [128, 4, 512])
        rstd = P.sb("rstd", [128, 512])
        Kk.ev = P.sb("ev", [128, 4, 512])
        Kk.evi = 0
        Kk.stA = dict(xt=xt, ht=ht, sq=sq, rstd=rstd)
        if nsteps_extra:
            nsteps_extra("alloc")
        for (t0, n, isctx) in tiles_main():
            m = 1 if isctx else 0
            P.dma(xt[:, :, 0:n], xT[:, t0:t0 + n].rearrange("(c p) t -> p c t", p=128), writes=[xt])
            modulate(Kk, xt, ht, n, Kk.Amix, 0, m, sq, rstd, Kk.psb[7], tmpf=htf)
            hreads = [(ht, c) for c in range(16)]
            for g in plan:
                if g[0] == "fm":
                    gemm_fm(Kk, W, D, g[1], lambda kc, kp: ht[0:kp, kc, 0:n], n, g[2](t0, n, isctx), rhs_reads=hreads, lowp=True)
                elif g[0] == "tm":
                    gemm_tm(Kk, W, D, g[1], g[2], lambda kc, kp, a, tn: ht[0:kp, kc, a:a + tn], n, g[3](t0, n, isctx),
                            lhs_reads=hreads, lowp=True)
                else:
                    g[1](t0, n, isctx)
        P.barrier()
        P.stage_es = None


def ev_store(Kk, ps, rows, n, dst_ap, func=None, eng="act"):
    P = Kk.P
    s = Kk.evi % 4
    Kk.evi += 1
    ev = Kk.ev
    if eng == "act":
        P.op("act", lambda e: e.activation(ev[0:rows, s, 0:n], ps[0:rows, 0:n], func or AF.Identity),
             reads=[ps], writes=[(ev, s)])
    else:
        P.op("dve", lambda e: e.tensor_copy(ev[0:rows, s, 0:n], ps[0:rows, 0:n]), reads=[ps], writes=[(ev, s)])
    P.dma(dst_ap, ev[0:rows, s, 0:n], reads=[(ev, s)], qn="pool")


def stage_C1(Kk, l, xT, yT, Wout, do_ctx):
    P = Kk.P
    with ExitStack() as ses:
        P.stage_es = ses
        alloc_wb(Kk, 4, 3)
        xt = P.sb("xt", [128, 16, 512])
        yt = P.sb("yt", [128, 16, 512])
        ytb = P.sb("ytb", [128, 16, 512], BF16)
        for (t0, n, isctx) in tiles_main():
            if isctx and not do_ctx:
                continue
            m = 1 if isctx else 0
            P.dma(xt[:, :, 0:n], xT[:, t0:t0 + n].rearrange("(c p) t -> p c t", p=128), writes=[(xt, c) for c in range(16)])
            P.dma(yt[:, :, 0:n], yT[:, t0:t0 + n].rearrange("(c p) t -> p c t", p=128), writes=[yt])
            for c4 in range(4):
                if c4 % 2 == 0:
                    P.op("pool", lambda e, c4=c4, n=n: e.tensor_copy(ytb[:, c4 * 4:(c4 + 1) * 4, 0:n], yt[:, c4 * 4:(c4 + 1) * 4, 0:n]),
                         reads=[yt], writes=[(ytb, c4)])
                else:
                    P.op("act", lambda e, c4=c4, n=n: e.activation(ytb[:, c4 * 4:(c4 + 1) * 4, 0:n], yt[:, c4 * 4:(c4 + 1) * 4, 0:n], AF.Identity),
                         reads=[yt], writes=[(ytb, c4)])

            def evac(i, ps, cw, n=n, m=m):
                P.op("dve", lambda e: e.scalar_tensor_tensor(xt[:, i, 0:n], ps[:, 0:n], Kk.modT[:, 32 + i, m:m + 1],
                                                              xt[:, i, 0:n], ALU.mult, ALU.add),
                     reads=[ps, (xt, i), (Kk.modT, 32 + i)], writes=[(xt, i)])

            gemm_fm(Kk, Wout, D, [(i * 128, 128) for i in range(16)], lambda kc, kp: ytb[0:kp, kc, 0:n], n, evac,
                    rhs_reads=[(ytb, c4) for c4 in range(4)], lowp=True)
            P.dma(xT[:, t0:t0 + n].rearrange("(c p) t -> p c t", p=128), xt[:, :, 0:n],
                  reads=[(xt, c) for c in range(16)], qn="pool")
        P.barrier()
        P.stage_es = None


def stage_C2(Kk, l, xTa, xTb, do_ctx, final_norm=None, outT=None):
    P = Kk.P
    Wup = Kk.ins["ffn_w_up"][l]
    Wdn = Kk.ins["ffn_w_down"][l]
    with ExitStack() as ses:
        P.stage_es = ses
        alloc_wb(Kk, 3 if final_norm is not None else 5, 3)
        xt = P.sb("xt", [128, 16, 512])
        ht = P.sb("ht", [128, 16, 512], BF16)
        htf = P.sb("htf", [128, 2, 512])
        gt = P.sb("gt", [128, 11, 512], BF16)
        sq = P.sb("sq", [128, 4, 512])
        rstd = P.sb("rstd", [128, 512])
        acc = P.sb("acc", [128, 2, 2, 512])
        cw_ = P.sb("convw", [128, 3, 88])
        cb_ = P.sb("convb", [128, 88])
        P.dma(cw_[:, :, :], Kk.ins["ffn_conv_w"][l], writes=[cw_])
        P.dma(cb_[:, :], Kk.ins["ffn_conv_b"][l], writes=[cb_])
        if final_norm is not None:
            ot_ = P.sb("otf", [128, 16, 512])
            fnw = P.sb("fnw", [128, 16])
            P.dma(fnw[:, :], final_norm, writes=[fnw])
        for (o0, n, lh, rh) in tiles_ffn():
            isctx = o0 < NCTX
            if isctx and not do_ctx:
                continue
            m = 1 if isctx else 0
            nn = n + lh + rh
            a0 = o0 - lh
            xk = [(xt, c) for c in range(16)]
            P.dma(xt[:, :, 0:nn], xTa[:, a0:a0 + nn].rearrange("(c p) t -> p c t", p=128), writes=xk)
            modulate(Kk, xt, ht, nn, Kk.Affn, 48, m, sq, rstd, Kk.psb[7], tmpf=htf)
            hreads = [(ht, c) for c in range(16)]
            for grp in range(4):
                chunks = []
                for j in range(11):
                    chunks.append(((grp * 11 + j) * 128, 128))
                    chunks.append((DFF + (grp * 11 + j) * 128, 128))

                def evac(i, ps, cw, grp=grp, n=n, lh=lh, rh=rh):
                    j = i // 2
                    isval = i % 2
                    fc = (grp * 11 + j) + (44 if isval else 0)
                    s = j % 2
                    a = acc[:, isval, s, :]
                    key = (acc, (isval, s))
                    P.op("act", lambda e: e.activation(a[:, 0:n], ps[:, lh:lh + n], AF.Identity,
                                                        bias=cb_[:, fc:fc + 1], scale=cw_[:, 1, fc:fc + 1]),
                         reads=[ps, cw_, cb_], writes=[key])
                    lo = 0 if lh else 1
                    P.op("dve", lambda e: e.scalar_tensor_tensor(a[:, lo:n], ps[:, lh - 1 + lo:lh - 1 + n], cw_[:, 0, fc:fc + 1],
                                                                  a[:, lo:n], ALU.mult, ALU.add),
                         reads=[ps, cw_, key], writes=[key])
                    hi = 0 if rh else 1
                    P.op("dve", lambda e: e.scalar_tensor_tensor(a[:, 0:n - hi], ps[:, lh + 1:lh + 1 + n - hi], cw_[:, 2, fc:fc + 1],
                                                                  a[:, 0:n - hi], ALU.mult, ALU.add),
                         reads=[ps, cw_, key], writes=[key])
                    if isval:
                        kg = (acc, (0, s))
                        P.op("act", lambda e: e.activation(acc[:, 0, s, 0:n], acc[:, 0, s, 0:n], AF.Silu),
                             reads=[kg], writes=[kg])
                        P.op("pool", lambda e: e.tensor_tensor(gt[:, j, 0:n], acc[:, 0, s, 0:n], acc[:, 1, s, 0:n], ALU.mult),
                             reads=[kg, key], writes=[(gt, j)])

                gemm_fm(Kk, Wup, D, chunks, lambda kc, kp: ht[0:kp, kc, 0:nn], nn, evac, ps_ids=(0, 1, 2, 3), rhs_reads=hreads, lowp=True)
                Wd = Wdn[grp * 11 * 128:(grp + 1) * 11 * 128, :]

                def evac2(i, ps, cw, n=n, lh=lh, m=m):
                    P.op("dve", lambda e: e.scalar_tensor_tensor(xt[:, i, lh:lh + n], ps[:, 0:n], Kk.modT[:, 80 + i, m:m + 1],
                                                                  xt[:, i, lh:lh + n], ALU.mult, ALU.add),
                         reads=[ps, (xt, i), (Kk.modT, 80 + i)], writes=[(xt, i)])

                gemm_fm(Kk, Wd, 11 * 128, [(i * 128, 128) for i in range(16)], lambda kc, kp: gt[0:kp, kc, 0:n], n, evac2,
                        ps_ids=(4, 5), rhs_reads=[(gt, j) for j in range(11)], lowp=True)
            if final_norm is None:
                P.dma(xTb[:, o0:o0 + n].rearrange("(c p) t -> p c t", p=128), xt[:, :, lh:lh + n], reads=xk, qn="pool")
            else:
                if not isctx:
                    colsum_bcast(Kk, lambda c, kp: xt[:, c, lh:lh + n], 16, n, Kk.psb[7], sq, Kk.onesD, xk)
                    rstd_from(Kk, Kk.psb[7], n, rstd, EPS)
                    for c in range(16):
                        P.op("dve", lambda e, c=c, n=n, lh=lh: e.scalar_tensor_tensor(ot_[:, c, 0:n], xt[:, c, lh:lh + n], fnw[:, c:c + 1],
                                                                           rstd[:, 0:n], ALU.mult, ALU.mult),
                             reads=[(xt, c), fnw, rstd], writes=[(ot_, c)])
                    P.dma(outT[:, o0 - NCTX:o0 - NCTX + n].rearrange("(c p) t -> p c t", p=128), ot_[:, :, 0:n],
                          reads=[(ot_, c) for c in range(16)], qn="pool")
        P.barrier()
        P.stage_es = None


A_SCALE = 128 ** -0.5


def rope_apply(Kk, src, rows, n, cosb, sinb, RT, ps, dst, tmp):
    P = Kk.P
    sbuf, skey, sap = src
    dbuf, dkey, dap = dst
    tbuf, tkey, tap = tmp
    P.op("pe", lambda e: e.matmul(ps[0:rows, 0:n], RT[0:rows, 0:rows], sap, start=True, stop=True),
         reads=[(sbuf, skey), RT], writes=[ps])
    P.op("dve", lambda e: e.tensor_tensor(tap, ps[0:rows, 0:n], sinb[0:rows, 0:n], ALU.mult),
         reads=[ps, sinb], writes=[(tbuf, tkey)])
    P.op("pool", lambda e: e.tensor_tensor(dap, sap, cosb[0:rows, 0:n], ALU.mult),
         reads=[(sbuf, skey), cosb], writes=[(dbuf, dkey)])
    P.op("pool", lambda e: e.tensor_tensor(dap, dap, tap, ALU.add),
         reads=[(dbuf, dkey), (tbuf, tkey)], writes=[(dbuf, dkey)])


def stage_A_even(Kk, l, xT, S):
    P = Kk.P
    e_ = l // 2
    W = Kk.ins["ev_w_in"][e_]
    st = {}

    def extra(_):
        st["qs"] = P.sb("qs", [128, 2, 512])
        st["qn"] = P.sb("qn", [128, 2, 512])
        st["qo"] = P.sb("qo", [128, 2, 512])
        st["tmp"] = P.sb("tmpr", [128, 2, 512])
        st["cos"] = P.sb("cosb", [128, 512])
        st["sin"] = P.sb("sinb", [128, 512])
        st["gain"] = P.sb("qkgain", [128, 2])
        st["RT"] = P.sb("RT", [128, 128])
        st["onesH"] = P.sb("onesH", [128, 128])
        st["rs2"] = P.sb("rs2", [128, 2, 512])
        st["dtb"] = P.sb("dtb", [32, 2])
        P.dma(st["gain"][:, :], Kk.ins["qk_gain"][e_], writes=[st["gain"]])
        P.dma(st["RT"][:, :], Kk.ins["c_RT128"][:, :], writes=[st["RT"]])
        P.dma(st["onesH"][:, :], Kk.ins["c_onesH"][:, :], writes=[st["onesH"]])
        P.dma(st["dtb"][:, :], Kk.ins["dt_ba"][e_], writes=[st["dtb"]])
        st["i"] = 0

    def tile_pre(t0, n, isctx):
        if not isctx:
            P.dma(st["cos"][:, 0:n], Kk.ins["c_cos128"][:, t0 - NCTX:t0 - NCTX + n], writes=[st["cos"]])
            P.dma(st["sin"][:, 0:n], Kk.ins["c_sin128"][:, t0 - NCTX:t0 - NCTX + n], writes=[st["sin"]])

    def mk_qk(t0, n, isctx):
        def evac(i, ps, cw):
            s = st["i"] % 2
            st["i"] += 1
            qs, qn, qo, tmp, rs2 = st["qs"], st["qn"], st["qo"], st["tmp"], st["rs2"]
            g = 0 if i < 8 else 1
            ps2 = Kk.psb[4 + s]
            P.op("act", lambda e: e.activation(qs[:, s, 0:n], ps[:, 0:n], AF.Identity), reads=[ps], writes=[(qs, s)])
            P.op("act", lambda e: e.activation(qn[:, s, 0:n], ps[:, 0:n], AF.Square), reads=[ps], writes=[(qn, s)])
            P.op("pe", lambda e: e.matmul(ps2[:, 0:n], st["onesH"][:, :], qn[:, s, 0:n], start=True, stop=True),
                 reads=[(qn, s), st["onesH"]], writes=[ps2])
            P.op("act", lambda e: e.activation(rs2[:, s, 0:n], ps2[:, 0:n], AF.Sqrt, bias=Kk.epsb[:, 0:1], scale=1.0),
                 reads=[ps2, Kk.epsb], writes=[(rs2, s)])
            P.op("dve", lambda e: e.reciprocal(rs2[:, s, 0:n], rs2[:, s, 0:n]), reads=[(rs2, s)], writes=[(rs2, s)])
            P.op("dve", lambda e: e.scalar_tensor_tensor(qn[:, s, 0:n], qs[:, s, 0:n], st["gain"][:, g:g + 1], rs2[:, s, 0:n],
                                                          ALU.mult, ALU.mult),
                 reads=[(qs, s), st["gain"], (rs2, s)], writes=[(qn, s)])
            dst = S["qT"][i] if i < 8 else S["kT"][i - 8]
            if isctx:
                P.dma(dst[:, t0:t0 + n], qn[:, s, 0:n], reads=[(qn, s)], qn="pool")
            else:
                rope_apply(Kk, (qn, s, qn[:, s, 0:n]), 128, n, st["cos"], st["sin"], st["RT"], ps2,
                           (qo, s, qo[:, s, 0:n]), (tmp, s, tmp[:, s, 0:n]))
                P.dma(dst[:, t0:t0 + n], qo[:, s, 0:n], reads=[(qo, s)], qn="pool")
        return evac

    def mk_v(t0, n, isctx):
        def evac(tb, ps, tn, cb0, cbw):
            ev_store(Kk, ps, tn, cbw, S["vM"][t0 + tb * 128:t0 + tb * 128 + tn, cb0:cb0 + cbw])
        return evac

    def mk_z(t0, n, isctx):
        def evac(i, ps, cw):
            ev_store(Kk, ps, cw, n, S["zT"][i * 128:i * 128 + cw, t0:t0 + n], func=AF.Silu)
        return evac

    def mk_xbc(t0, n, isctx):
        def evac(i, ps, cw):
            ev_store(Kk, ps, cw, n, S["xbcT"][i * 128:i * 128 + cw, t0:t0 + n])
        return evac

    def mk_dt(t0, n, isctx):
        def evac(i, ps, cw):
            s = Kk.evi % 4
            Kk.evi += 1
            ev = Kk.ev
            P.op("act", lambda e: e.activation(ev[0:32, s, 0:n], ps[0:32, 0:n], AF.Exp, bias=st["dtb"][:, 0:1], scale=1.0),
                 reads=[ps, st["dtb"]], writes=[(ev, s)])
            P.op("act", lambda e: e.activation(ev[0:32, s, 0:n], ev[0:32, s, 0:n], AF.Ln, bias=Kk.epsb[0:32, 3:4], scale=1.0),
                 reads=[(ev, s), Kk.epsb], writes=[(ev, s)])
            P.dma(S["dtT"][0:32, t0:t0 + n], ev[0:32, s, 0:n], reads=[(ev, s)], qn="pool")
        return evac

    qk_chunks = [(i * 128, 128) for i in range(10)]
    z_chunks = [(1536 + i * 128, 128) for i in range(8)]
    xbc_chunks = [(2560 + i * 128, 128) for i in range(12)]
    dt_chunks = [(4096, 32)]
    plan = [("post", tile_pre), ("fm", qk_chunks, mk_qk), ("tm", 1280, 256, mk_v),
            ("fm", z_chunks, mk_z), ("fm", xbc_chunks, mk_xbc), ("fm", dt_chunks, mk_dt)]
    stage_A(Kk, l, xT, W, plan, nsteps_extra=extra)


def attention(Kk, nheads, kv_of, load_k, load_v, load_q, kparts, scale, yT, row0, do_ctx):
    P = Kk.P
    pt = Kk.att["pt"]
    ot = Kk.att["ot"]
    rd = Kk.att["rd"]
    cur_kv = None
    it = 0
    pti = 0
    for h in range(nheads):
        kvh = kv_of(h)
        if kvh != cur_kv:
            kbufs = load_k(kvh)
            vbuf = load_v(kvh)
            cur_kv = kvh
        for (t0, n, isctx) in tiles_main():
            if isctx and not do_ctx:
                continue
            pti = _attn_tile(Kk, h, t0, n, isctx, it, pti, kbufs, vbuf, load_q, kparts, scale, yT, row0)
            it += 1


def _attn_tile(Kk, h, t0, n, isctx, it, pti, kbufs, vbuf, load_q, kparts, scale, yT, row0):
    P = Kk.P
    pt = Kk.att["pt"]
    ot = Kk.att["ot"]
    rd = Kk.att["rd"]
    qaps, qreads = load_q(h, t0, n, it % 2)
    ps_o = Kk.psb[2 + it % 2]
    ps_d = Kk.psb[4 + it % 2]
    jt = list(range(2)) if isctx else list(range(NT // 128))
    for ji, j in enumerate(jt):
        ps_s = Kk.psb[pti % 2]
        sl = pti % 3
        pti += 1
        for pi_, rows in enumerate(kparts):
            kb = kbufs[pi_]
            P.op("pe", lambda e, ps_s=ps_s, kb=kb, rows=rows, j=j, qa=qaps[pi_], pi_=pi_:
                 e.matmul(ps_s[:, 0:n], kb[0:rows, j * 128:(j + 1) * 128], qa,
                          start=(pi_ == 0), stop=(pi_ == len(kparts) - 1)),
                 reads=[kb] + qreads, writes=[ps_s])
        P.op("act", lambda e, ps_s=ps_s, sl=sl: e.activation(pt[:, sl, 0:n], ps_s[:, 0:n], AF.Exp, scale=scale),
             reads=[ps_s], writes=[(pt, sl)])
        P.op("pe", lambda e, sl=sl, j=j, ji=ji: e.matmul(ps_o[:, 0:n], vbuf[:, j, :], pt[:, sl, 0:n],
                                                         start=(ji == 0), stop=(ji == len(jt) - 1)),
             reads=[vbuf, (pt, sl)], writes=[ps_o])
        P.op("pe", lambda e, sl=sl, ji=ji: e.matmul(ps_d[:, 0:n], Kk.att["onesb"][:, :], pt[:, sl, 0:n],
                                                    start=(ji == 0), stop=(ji == len(jt) - 1)),
             reads=[Kk.att["onesb"], (pt, sl)], writes=[ps_d])
    s = it % 2
    P.op("dve", lambda e: e.reciprocal(rd[:, s, 0:n], ps_d[:, 0:n]), reads=[ps_d], writes=[(rd, s)])
    P.op("dve", lambda e: e.tensor_tensor(ot[:, s, 0:n], ps_o[:, 0:n], rd[:, s, 0:n], ALU.mult),
         reads=[ps_o, (rd, s)], writes=[(ot, s)])
    P.dma(yT[row0 + h * 128:row0 + (h + 1) * 128, t0:t0 + n], ot[:, s, 0:n], reads=[(ot, s)], qn="pool")
    return pti


def stage_attn_even(Kk, S, yT, do_ctx):
    P = Kk.P
    with ExitStack() as ses:
        P.stage_es = ses
        kt = P.sb("kt", [128, NT])
        vt = P.sb("vt", [128, NT // 128, 128])
        qt = P.sb("qt", [128, 2, 512])
        Kk.att = dict(pt=P.sb("pt", [128, 3, 512], BF16), ot=P.sb("ot", [128, 2, 512]), rd=P.sb("rd", [128, 2, 512]),
                      onesb=P.sb("onesb", [128, 128], BF16))
        P.op("pool", lambda e: e.tensor_copy(Kk.att["onesb"][:, :], Kk.ones[:, :]), reads=[Kk.ones], writes=[Kk.att["onesb"]])
        vtb = P.sb("vtb", [128, NT // 128, 128], BF16)

        ktb = P.sb("ktb", [128, NT], BF16)
        qtb = P.sb("qtb", [128, 2, 512], BF16)

        def load_k(kvh):
            P.dma(kt[:, :], S["kT"][kvh], writes=[kt])
            P.op("pool", lambda e: e.tensor_copy(ktb[:, :], kt[:, :]), reads=[kt], writes=[ktb])
            return [ktb]

        def load_v(kvh):
            for j0 in range(0, NT // 128, 4):
                j1 = min(NT // 128, j0 + 4)
                P.dma(vt[:, j0:j1, :], S["vM"][j0 * 128:j1 * 128, kvh * 128:(kvh + 1) * 128].rearrange("(j p) d -> p j d", p=128),
                      writes=[vt])
            P.op("pool", lambda e: e.tensor_copy(vtb[:, :, :], vt[:, :, :]), reads=[vt], writes=[vtb])
            return vtb

        def load_q(h, t0, n, slot):
            P.dma(qt[:, slot, 0:n], S["qT"][h][:, t0:t0 + n], writes=[(qt, slot)])
            P.op("dve", lambda e: e.tensor_copy(qtb[:, slot, 0:n], qt[:, slot, 0:n]), reads=[(qt, slot)], writes=[(qtb, slot)])
            return [qtb[:, slot, 0:n]], [(qtb, slot)]

        attention(Kk, 8, lambda h: h // 4, load_k, load_v, load_q, [128], A_SCALE, yT, 0, do_ctx)
        P.barrier()
        P.stage_es = None


def stage_ssd_conv(Kk, l, S):
    P = Kk.P
    e_ = l // 2
    with ExitStack() as ses:
        P.stage_es = ses
        cw = P.sb("scw", [128, 12, 5])
        cb = P.sb("scb", [128, 12])
        ub = P.sb("ub", [128, 3, 516])
        ac = P.sb("sac", [128, 3, 512])
        P.dma(cw[:, :, :], Kk.ins["ssm_conv_w"][e_], writes=[cw])
        P.dma(cb[:, :], Kk.ins["ssm_conv_b"][e_], writes=[cb])
        it = 0
        for (t0, n, isctx) in tiles_main():
            seg0, seg1 = (0, NCTX) if isctx else (NCTX, NT)
            lh = min(2, t0 - seg0)
            rh = min(2, seg1 - (t0 + n))
            for c in range(12):
                s = it % 3
                it += 1
                _ssd_conv_tile(Kk, S, cw, cb, ub, ac, t0, n, lh, rh, c, s)
        P.barrier()
        P.stage_es = None


def _ssd_conv_tile(Kk, S, cw, cb, ub, ac, t0, n, lh, rh, c, s):
    P = Kk.P
    k = (ub, s)
    if lh < 2 or rh < 2:
        P.op("pool", lambda e: e.memset(ub[:, s, :], 0.0), writes=[k])
    P.dma(ub[:, s, 2 - lh:2 + n + rh], S["xbcT"][c * 128:(c + 1) * 128, t0 - lh:t0 + n + rh], writes=[k])
    ka = (ac, s)
    P.op("act", lambda e: e.activation(ac[:, s, 0:n], ub[:, s, 2:2 + n], AF.Identity, bias=cb[:, c:c + 1], scale=cw[:, c, 2:3]),
         reads=[k, cw, cb], writes=[ka])
    for kk in (0, 1, 3, 4):
        P.op("dve", lambda e, kk=kk: e.scalar_tensor_tensor(ac[:, s, 0:n], ub[:, s, kk:kk + n], cw[:, c, kk:kk + 1], ac[:, s, 0:n],
                                                             ALU.mult, ALU.add),
             reads=[k, cw, ka], writes=[ka])
    P.op("act", lambda e: e.activation(ac[:, s, 0:n], ac[:, s, 0:n], AF.Silu), reads=[ka], writes=[ka])
    P.dma(S["xcT"][c * 128:(c + 1) * 128, t0:t0 + n], ac[:, s, 0:n], reads=[ka], qn="pool")


def stage_ssd(Kk, l, S, yT, do_ctx):
    for d in range(2):
        _ssd_dir(Kk, l, S, yT, do_ctx, d)


def _ssd_dir(Kk, l, S, yT, do_ctx, d):
    P = Kk.P
    e_ = l // 2
    NCH = NT // 128
    if True:
        with ExitStack() as ses:
            P.stage_es = ses
            R = {}
            R["tri"] = P.sb("tri", [128, 4, 128])
            P.dma(R["tri"][:, :, :], Kk.ins["c_tri"][:, :, :], writes=[R["tri"]])
            R["alog"] = P.sb("alog", [32, 2])
            P.dma(R["alog"][:, :], Kk.ins["dt_ba"][e_], writes=[R["alog"]])
            P.op("act", lambda e: e.activation(R["alog"][:, 1:2], R["alog"][:, 1:2], AF.Exp), reads=[R["alog"]], writes=[R["alog"]])
            P.op("dve", lambda e: e.tensor_scalar(R["alog"][:, 1:2], R["alog"][:, 1:2], -1.0, None, ALU.mult),
                 reads=[R["alog"]], writes=[R["alog"]])
            R["fm"] = P.sb("sfm", [128, 2, 12, 128])
            R["dtf"] = P.sb("dtf", [128, 2, 2, 128])
            P.op("pool", lambda e: e.memset(R["dtf"][:, :, :, :], 0.0), writes=[R["dtf"]])
            R["xs"] = P.sb("xstm", [128, 2, 1024])
            R["btm"] = P.sb("btm", [128, 2, 256])
            R["dtm"] = P.sb("dtm", [128, 2, 64])
            R["bc"] = P.sb("bc", [128, 16, 128])
            R["xdtp"] = P.sb("xdtp", [128, 16, 128])
            R["xdtw"] = P.sb("xdtw", [128, 1024])
            R["H"] = P.sb("Hs", [128, 16, 128])
            R["gm"] = P.sb("gm", [128, 2, 2, 128])
            R["acs"] = P.sb("acs", [128, 2, 48])
            R["rot"] = P.sb("rot", [128, 5, 4, 128])
            R["yacc"] = P.sb("yacc", [128, 2, 8, 128])
            R["fin"] = P.sb("fin", [128, 2, 8, 128])
            R["zt"] = P.sb("zt", [128, 2, 8, 128])
            R["sq"] = P.sb("ssq", [128, 4, 128])
            R["rs"] = P.sb("srs", [128, 2, 128])
            R["dsk"] = P.sb("dsk", [128, 8])
            R["gn"] = P.sb("gn", [128, 8])
            R["ones512"] = P.sb("ones512", [128, 128])
            P.dma(R["dsk"][:, :], Kk.ins["ssm_d"][e_], writes=[R["dsk"]])
            P.dma(R["gn"][:, :], Kk.ins["ssm_norm"][e_], writes=[R["gn"]])
            P.dma(R["ones512"][:, :], Kk.ins["c_ones512"][:, :], writes=[R["ones512"]])
            P.op("pool", lambda e: e.memset(R["H"][:, :, :], 0.0), writes=[R["H"]])
            P.op("pool", lambda e: e.memset(R["xdtp"][:, :, :], 0.0), writes=[R["xdtp"]])
            order = [0, 1] + list(range(2, NCH)) if d == 0 else [1, 0] + list(range(NCH - 1, 1, -1))
            R["hi"] = 0
            import os
            nlim = int(os.environ.get("SSD_NCH", "999"))
            for ci, ch in enumerate(order[:nlim]):
                _ssd_chunk(Kk, S, yT, R, d, ch, ci, do_ctx)
            P.barrier()
            P.stage_es = None


def _ssd_chunk(Kk, S, yT, R, d, ch, ci, do_ctx):
    P = Kk.P
    s = ci % 2
    t0 = ch * 128
    tri = R["tri"]
    TD = tri[:, 0 if d == 0 else 2, :]
    TDx = tri[:, 1 if d == 0 else 3, :]
    last = 127 if d == 0 else 0
    fm, dtf, xs, btm, dtm = R["fm"], R["dtf"], R["xs"], R["btm"], R["dtm"]
    import os
    LV = int(os.environ.get("SSD_LEVEL", "99"))
    if LV <= 0:
        return
    kfm = (fm, s)
    P.dma(fm[:, s, :, :], S["xcT"][:, t0:t0 + 128].rearrange("(c p) t -> p c t", p=128), writes=[kfm])
    kdf = (dtf, s)
    P.dma(dtf[0:32, s, 0, :], S["dtT"][:, t0:t0 + 128], writes=[kdf])
    P.op("dve", lambda e: e.tensor_scalar(dtf[0:32, s, 1, :], dtf[0:32, s, 0, :], R["alog"][:, 1:2], None, ALU.mult),
         reads=[kdf, R["alog"]], writes=[kdf])
    if LV <= 1:
        return
    pst = Kk.psb[4]
    for g4 in range(2):
        for j in range(4):
            c = g4 * 4 + j
            P.op("pe", lambda e, c=c, j=j: e.matmul(pst[:, j * 128:(j + 1) * 128], fm[:, s, c, :], Kk.ident[:, :], start=True, stop=True),
                 reads=[kfm, Kk.ident], writes=[pst])
        if not os.environ.get("SKIP_EV"):
            P.op("dve", lambda e, g4=g4: e.tensor_copy(xs[:, s, g4 * 512:(g4 + 1) * 512], pst[:, :]),
                 reads=[pst], writes=[(xs, (s, c)) for c in range(g4 * 4, g4 * 4 + 4)])
    for j in range(2):
        P.op("pe", lambda e, j=j: e.matmul(pst[:, j * 128:(j + 1) * 128], fm[:, s, 8 + j, :], Kk.ident[:, :], start=True, stop=True),
             reads=[kfm, Kk.ident], writes=[pst])
    for q in range(0 if os.environ.get("SKIP_DT") else 2):
        P.op("pe", lambda e, q=q: e.matmul(pst[:, 256 + q * 32:256 + (q + 1) * 32], dtf[:, s, q, :], Kk.ident[:, 0:32], start=True, stop=True),
             reads=[kdf, Kk.ident], writes=[pst])
    if not os.environ.get("SKIP_EV"):
        P.op("dve", lambda e: e.tensor_copy(btm[:, s, :], pst[:, 0:256]), reads=[pst], writes=[(btm, s)])
    if not os.environ.get("SKIP_DT") and not os.environ.get("SKIP_DTEV"):
        P.op("dve", lambda e: e.tensor_copy(dtm[:, s, :], pst[:, 256:320]), reads=[pst], writes=[(dtm, s)])
    LV = int(os.environ.get("SSD_LEVEL", "99"))
    if LV <= 2:
        return
    dt_d = dtm[:, s, d * 16:(d + 1) * 16]
    dta_d = dtm[:, s, 32 + d * 16:32 + (d + 1) * 16]
    psG = Kk.psb[5]
    gm = R["gm"]
    acs = R["acs"]
    ka = (acs, s)
    for g in range(2):
        P.op("pe", lambda e, g=g: e.matmul(psG[:, g * 128:(g + 1) * 128], fm[:, s, 8 + g, :], fm[:, s, 10 + g, :],
                                           start=True, stop=True),
             reads=[kfm], writes=[psG])
    P.op("pe", lambda e: e.matmul(psG[:, 256:272], TD, dta_d, start=True, stop=True), reads=[tri, (dtm, s)], writes=[psG])
    P.op("pe", lambda e: e.matmul(psG[:, 272:288], TDx, dta_d, start=True, stop=True), reads=[tri, (dtm, s)], writes=[psG])
    for g in range(2):
        P.op("dve", lambda e, g=g: e.tensor_tensor(gm[:, s, g, :], psG[:, g * 128:(g + 1) * 128], TD, ALU.mult),
             reads=[psG, tri], writes=[(gm, (s, g))])
    P.op("dve", lambda e: e.tensor_copy(acs[:, s, 0:32], psG[:, 256:288]), reads=[psG], writes=[ka])
    P.op("act", lambda e: e.activation(acs[:, s, 16:32], acs[:, s, 16:32], AF.Exp), reads=[ka], writes=[ka])
    P.op("dve", lambda e: e.tensor_tensor(acs[:, s, 32:48], acs[:, s, 16:32], dt_d, ALU.mult), reads=[ka, (dtm, s)], writes=[ka])
    if LV <= 4:
        return
    bc, xdtp, xdtw = R["bc"], R["xdtp"], R["xdtw"]
    P.op("pool", lambda e: e.tensor_copy(bc[:, :, :], dta_d.rearrange("p (h o) -> p h o", o=1).broadcast_to([128, 16, 128])),
         reads=[(dtm, s)], writes=[bc])
    xs3 = xs[:, s, :].rearrange("p (h q) -> p h q", q=64)
    for par in range(2):
        P.op("dve", lambda e, par=par: e.tensor_tensor(
            xdtp[:, par::2, par * 64:(par + 1) * 64], xs3[:, par::2, :],
            dtm[:, s, d * 16 + par:(d + 1) * 16:2].rearrange("p (h o) -> p h o", o=1).broadcast_to([128, 8, 64]), ALU.mult),
            reads=[(xs, (s, c)) for c in range(8)] + [(dtm, s)], writes=[xdtp])
    P.op("pool", lambda e: e.tensor_tensor(xdtw[:, :].rearrange("p (h q) -> p h q", q=64), xs3,
                                           acs[:, s, 32:48].rearrange("p (h o) -> p h o", o=1).broadcast_to([128, 16, 64]), ALU.mult),
         reads=[(xs, (s, c)) for c in range(8)] + [ka], writes=[xdtw])
    if LV <= 5:
        return
    pss = [Kk.psb[6], Kk.psb[7]]
    for g in range(2):
        P.op("pe", lambda e, g=g: e.matmul(pss[g][:, :], btm[:, s, g * 128:(g + 1) * 128], xdtw[:, g * 512:(g + 1) * 512],
                                           start=True, stop=True),
             reads=[(btm, s), xdtw], writes=[pss[g]])
    if LV <= 6:
        return
    rot = R["rot"]
    H = R["H"]
    yacc = R["yacc"]
    for h in range(16):
        g = h // 8
        pair = h // 2
        par = h % 2
        hi = R["hi"]
        R["hi"] += 1
        r = hi % 4
        psa = Kk.psb[hi % 2]
        psy = Kk.psb[2 + (hi // 2) % 2]
        kr = lambda i, r=r: (rot, (i, r))
        P.op("pe", lambda e, h=h, psa=psa: e.matmul(psa[:, 0:128], bc[:, h, :], TD, start=True, stop=True),
             reads=[bc, tri], writes=[psa])
        P.op("dve", lambda e, h=h, psa=psa, r=r: e.tensor_scalar(rot[:, 0, r, :], psa[:, 0:128], acs[:, s, h:h + 1], 0.0,
                                                                 ALU.subtract, ALU.min),
             reads=[psa, ka], writes=[kr(0)])
        P.op("act", lambda e, r=r: e.activation(rot[:, 1, r, :], rot[:, 0, r, :], AF.Exp), reads=[kr(0)], writes=[kr(1)])
        P.op("pool", lambda e, r=r, g=g: e.tensor_tensor(rot[:, 2, r, :], rot[:, 1, r, :], gm[:, s, g, :], ALU.mult),
             reads=[kr(1), (gm, (s, g))], writes=[kr(2)])
        P.op("act", lambda e, psa=psa, r=r: e.activation(rot[:, 3, r, :], psa[:, 0:128], AF.Exp), reads=[psa], writes=[kr(3)])
        P.op("dve", lambda e, r=r, g=g: e.tensor_tensor(rot[:, 4, r, :], rot[:, 3, r, :], fm[:, s, 10 + g, :], ALU.mult),
             reads=[kr(3), kfm], writes=[kr(4)])
        P.op("pe", lambda e, h=h, psy=psy, r=r, par=par: e.matmul(psy[:, 0:128], xdtp[:, h, :], rot[:, 2, r, :],
                                                                  start=(par == 0), stop=False),
             reads=[xdtp, kr(2)], writes=[psy])
        P.op("pe", lambda e, h=h, psy=psy, r=r, par=par: e.matmul(psy[:, 0:128], H[:, h, :], rot[:, 4, r, :],
                                                                  start=False, stop=(par == 1)),
             reads=[(H, h), kr(4)], writes=[psy])
        P.op("dve", lambda e, h=h, r=r, g=g, par=par: e.scalar_tensor_tensor(
            H[:, h, par * 64:(par + 1) * 64], H[:, h, par * 64:(par + 1) * 64], rot[:, 3, r, last:last + 1],
            pss[g][:, (h % 8) * 64:(h % 8 + 1) * 64], ALU.mult, ALU.add),
            reads=[(H, h), kr(3), pss[g]], writes=[(H, h)])
        if par == 1:
            ky = (yacc, (s, pair))
            P.op("act", lambda e, psy=psy, pair=pair: e.activation(yacc[:, s, pair, :], psy[:, 0:128], AF.Identity),
                 reads=[psy], writes=[ky])
    if LV <= 7:
        return
    isctx = ch < 2
    allk = [(yacc, (s, p)) for p in range(8)]
    if d == 0:
        P.dma(S["ysT"][:, t0:t0 + 128].rearrange("(c p) t -> p c t", p=128), yacc[:, s, :, :], reads=allk, qn="pool")
        return
    if isctx and not do_ctx:
        return
    fin, zt, sq, rs = R["fin"], R["zt"], R["sq"], R["rs"]
    kf = (fin, s)
    P.dma(fin[:, s, :, :], S["ysT"][:, t0:t0 + 128].rearrange("(c p) t -> p c t", p=128), writes=[kf])
    kz = (zt, s)
    P.dma(zt[:, s, :, :], S["zT"][:, t0:t0 + 128].rearrange("(c p) t -> p c t", p=128), writes=[kz])
    P.op("dve", lambda e: e.tensor_tensor(fin[:, s, :, :], fin[:, s, :, :], yacc[:, s, :, :], ALU.add), reads=[kf] + allk, writes=[kf])
    for c in range(8):
        P.op("dve", lambda e, c=c: e.scalar_tensor_tensor(fin[:, s, c, :], fm[:, s, c, :], R["dsk"][:, c:c + 1], fin[:, s, c, :],
                                                           ALU.mult, ALU.add),
             reads=[kf, kfm, R["dsk"]], writes=[kf])
    P.op("pool", lambda e: e.tensor_tensor(fin[:, s, :, :], fin[:, s, :, :], zt[:, s, :, :], ALU.mult), reads=[kf, kz], writes=[kf])
    psn = Kk.psb[4]
    for g in range(2):
        for c4 in range(4):
            c = g * 4 + c4
            q = c % 4
            P.op("act", lambda e, c=c, q=q: e.activation(sq[:, q, :], fin[:, s, c, :], AF.Square), reads=[kf], writes=[(sq, q)])
            P.op("pe", lambda e, c4=c4, q=q, g=g: e.matmul(psn[:, g * 128:(g + 1) * 128], R["ones512"][:, :], sq[:, q, :],
                                                           start=(c4 == 0), stop=(c4 == 3)),
                 reads=[(sq, q), R["ones512"]], writes=[psn])
        P.op("dve", lambda e, g=g: e.tensor_copy(rs[:, g, :], psn[:, g * 128:(g + 1) * 128]), reads=[psn], writes=[(rs, g)])
        P.op("act", lambda e, g=g: e.activation(rs[:, g, :], rs[:, g, :], AF.Sqrt, bias=Kk.epsb[:, 0:1], scale=1.0),
             reads=[(rs, g), Kk.epsb], writes=[(rs, g)])
        P.op("dve", lambda e, g=g: e.reciprocal(rs[:, g, :], rs[:, g, :]), reads=[(rs, g)], writes=[(rs, g)])
        for c4 in range(4):
            c = g * 4 + c4
            P.op("dve", lambda e, c=c, g=g: e.scalar_tensor_tensor(fin[:, s, c, :], fin[:, s, c, :], R["gn"][:, c:c + 1], rs[:, g, :],
                                                                    ALU.mult, ALU.mult),
                 reads=[kf, R["gn"], (rs, g)], writes=[kf])
    P.dma(yT[1024:2048, t0:t0 + 128].rearrange("(c p) t -> p c t", p=128), fin[:, s, :, :], reads=[kf], qn="pool")


C_SCALE = 192 ** -0.5
RW0 = 832


def stage_A_odd(Kk, l, xT, S):
    P = Kk.P
    o_ = l // 2
    W = Kk.ins["od_w_in"][o_]
    Wqb = Kk.ins["mla_q_b"][o_]
    Wkvb = Kk.ins["mla_kv_b"][o_]
    st = {}

    def extra(_):
        st["qa"] = P.sb("qa", [128, 6, 512])
        st["qn"] = P.sb("qan", [128, 6, 512])
        st["rs"] = P.sb("rsq", [128, 2, 512])
        st["gain"] = P.sb("mlagain", [128, 6])
        st["cos"] = P.sb("cosb", [64, 512])
        st["sin"] = P.sb("sinb", [64, 512])
        st["RT"] = P.sb("RT64", [64, 64])
        st["pe"] = P.sb("pe", [64, 3, 512])
        st["po"] = P.sb("po", [64, 3, 512])
        st["tmp"] = P.sb("ptmp", [64, 3, 512])
        st["o512"] = P.sb("o512", [128, 128])
        st["o256"] = P.sb("o256", [128, 128])
        P.dma(st["gain"][:, :], Kk.ins["mla_gain"][o_], writes=[st["gain"]])
        P.dma(st["RT"][:, :], Kk.ins["c_RT64"][:, :], writes=[st["RT"]])
        P.dma(st["o512"][:, :], Kk.ins["c_ones512"][:, :], writes=[st["o512"]])
        P.dma(st["o256"][:, :], Kk.ins["c_ones256"][:, :], writes=[st["o256"]])
        st["i"] = 0

    def tile_pre(t0, n, isctx):
        if not isctx:
            P.dma(st["cos"][:, 0:n], Kk.ins["c_cos64"][:, t0 - NCTX:t0 - NCTX + n], writes=[st["cos"]])
            P.dma(st["sin"][:, 0:n], Kk.ins["c_sin64"][:, t0 - NCTX:t0 - NCTX + n], writes=[st["sin"]])

    def rope64(ps, n, t0, isctx, dst):
        s = st["i"] % 3
        st["i"] += 1
        pe, po, tmp = st["pe"], st["po"], st["tmp"]
        P.op("act", lambda e: e.activation(pe[:, s, 0:n], ps[0:64, 0:n], AF.Identity), reads=[ps], writes=[(pe, s)])
        if isctx:
            P.dma(dst, pe[:, s, 0:n], reads=[(pe, s)], qn="pool")
        else:
            rope_apply(Kk, (pe, s, pe[:, s, 0:n]), 64, n, st["cos"], st["sin"], st["RT"], Kk.psb[6],
                       (po, s, po[:, s, 0:n]), (tmp, s, tmp[:, s, 0:n]))
            P.dma(dst, po[:, s, 0:n], reads=[(po, s)], qn="pool")

    def mk_a(t0, n, isctx):
        def evac(i, ps, cw):
            P.op("act", lambda e: e.activation(st["qa"][:, i, 0:n], ps[:, 0:n], AF.Identity), reads=[ps], writes=[(st["qa"], i)])
        return evac

    def mk_kpe(t0, n, isctx):
        def evac(i, ps, cw):
            rope64(ps, n, t0, isctx, S["kpT"][:, t0:t0 + n])
        return evac

    def post_a(t0, n, isctx):
        qa, qn, rs = st["qa"], st["qn"], st["rs"]
        for part, (c0, nch, ones_) in enumerate([(0, 4, st["o512"]), (4, 2, st["o256"])]):
            psr = Kk.psb[6]
            colsum_bcast(Kk, lambda c, kp, c0=c0: qa[:, c0 + c, 0:n], nch, n, psr, Kk.stA["sq"], ones_,
                         [(qa, c0 + c) for c in range(nch)])
            P.op("act", lambda e, part=part, psr=psr: e.activation(rs[:, part, 0:n], psr[:, 0:n], AF.Sqrt, bias=Kk.epsb[:, 0:1], scale=1.0),
                 reads=[psr, Kk.epsb], writes=[(rs, part)])
            P.op("dve", lambda e, part=part: e.reciprocal(rs[:, part, 0:n], rs[:, part, 0:n]), reads=[(rs, part)], writes=[(rs, part)])
            for c in range(c0, c0 + nch):
                P.op("dve", lambda e, c=c, part=part: e.scalar_tensor_tensor(qn[:, c, 0:n], qa[:, c, 0:n], st["gain"][:, c:c + 1],
                                                                              rs[:, part, 0:n], ALU.mult, ALU.mult),
                     reads=[(qa, c), st["gain"], (rs, part)], writes=[(qn, c)])
        chunks = []
        for h in range(8):
            chunks.append((h * 192, 128))
            chunks.append((h * 192 + 128, 64))

        def evq(i, ps, cw):
            h = i // 2
            if i % 2 == 0:
                ev_store(Kk, ps, 128, n, S["qnT"][h][:, t0:t0 + n])
            else:
                rope64(ps, n, t0, isctx, S["qpT"][h][:, t0:t0 + n])

        gemm_fm(Kk, Wqb, 512, chunks, lambda kc, kp: qn[0:kp, kc, 0:n], n, evq, ps_ids=(2, 3),
                rhs_reads=[(qn, c) for c in range(4)])
        kchunks = [(h * 256, 128) for h in range(8)]

        def evk(i, ps, cw):
            ev_store(Kk, ps, 128, n, S["knT"][i][:, t0:t0 + n])

        gemm_fm(Kk, Wkvb, 256, kchunks, lambda kc, kp: qn[0:kp, 4 + kc, 0:n], n, evk, ps_ids=(2, 3),
                rhs_reads=[(qn, 4), (qn, 5)])
        for h in range(8):
            def evv(tb, ps, tn, cb0, cbw, h=h):
                ev_store(Kk, ps, tn, cbw, S["vM"][t0 + tb * 128:t0 + tb * 128 + tn, h * 128:h * 128 + cbw])
            gemm_tm(Kk, Wkvb, 256, h * 256 + 128, 128, lambda kc, kp, a, tn: qn[0:kp, 4 + kc, a:a + tn], n, evv, ps_ids=(2, 3),
                    lhs_reads=[(qn, 4), (qn, 5)])

    def mk_ud(t0, n, isctx):
        def evac(i, ps, cw):
            c0 = ud_chunks[i][0] - RW0
            ev_store(Kk, ps, cw, n, S["udT"][c0:c0 + cw, t0:t0 + n])
        return evac

    a_chunks = [(i * 128, 128) for i in range(6)]
    kpe_chunks = [(768, 64)]
    ud_chunks = [(RW0 + i * 128, 128) for i in range(24)] + [(3904 + i * 96, 96) for i in range(4)] + [(4288, 128), (4416, 128)]
    plan = [("post", tile_pre), ("fm", a_chunks, mk_a), ("fm", kpe_chunks, mk_kpe), ("post", post_a), ("fm", ud_chunks, mk_ud)]
    stage_A(Kk, l, xT, W, plan, nsteps_extra=extra)


def stage_attn_odd(Kk, S, yT, do_ctx):
    P = Kk.P
    with ExitStack() as ses:
        P.stage_es = ses
        kt = P.sb("kt", [128, NT])
        kp = P.sb("kp", [64, NT])
        vt = P.sb("vt", [128, NT // 128, 128])
        qt = P.sb("qt", [128, 2, 512])
        qp = P.sb("qp", [64, 2, 512])
        Kk.att = dict(pt=P.sb("pt", [128, 3, 512], BF16), ot=P.sb("ot", [128, 2, 512]), rd=P.sb("rd", [128, 2, 512]),
                      onesb=P.sb("onesb", [128, 128], BF16))
        P.op("pool", lambda e: e.tensor_copy(Kk.att["onesb"][:, :], Kk.ones[:, :]), reads=[Kk.ones], writes=[Kk.att["onesb"]])
        vtb = P.sb("vtb", [128, NT // 128, 128], BF16)
        P.dma(kp[:, :], S["kpT"][:, :], writes=[kp])

        ktb = P.sb("ktb", [128, NT], BF16)
        kpb = P.sb("kpb", [64, NT], BF16)
        qtb = P.sb("qtb", [128, 2, 512], BF16)
        qpb = P.sb("qpb", [64, 2, 512], BF16)
        P.op("pool", lambda e: e.tensor_copy(kpb[:, :], kp[:, :]), reads=[kp], writes=[kpb])

        def load_k(h):
            P.dma(kt[:, :], S["knT"][h], writes=[kt])
            P.op("pool", lambda e: e.tensor_copy(ktb[:, :], kt[:, :]), reads=[kt], writes=[ktb])
            return [ktb, kpb]

        def load_v(h):
            for j0 in range(0, NT // 128, 4):
                j1 = min(NT // 128, j0 + 4)
                P.dma(vt[:, j0:j1, :], S["vM"][j0 * 128:j1 * 128, h * 128:(h + 1) * 128].rearrange("(j p) d -> p j d", p=128),
                      writes=[vt])
            P.op("pool", lambda e: e.tensor_copy(vtb[:, :, :], vt[:, :, :]), reads=[vt], writes=[vtb])
            return vtb

        def load_q(h, t0, n, slot):
            P.dma(qt[:, slot, 0:n], S["qnT"][h][:, t0:t0 + n], writes=[(qt, slot)])
            P.dma(qp[:, slot, 0:n], S["qpT"][h][:, t0:t0 + n], writes=[(qp, slot)])
            P.op("dve", lambda e: e.tensor_copy(qtb[:, slot, 0:n], qt[:, slot, 0:n]), reads=[(qt, slot)], writes=[(qtb, slot)])
            P.op("dve", lambda e: e.tensor_copy(qpb[:, slot, 0:n], qp[:, slot, 0:n]), reads=[(qp, slot)], writes=[(qpb, slot)])
            return [qtb[:, slot, 0:n], qpb[:, slot, 0:n]], [(qtb, slot), (qpb, slot)]

        attention(Kk, 8, lambda h: h, load_k, load_v, load_q, [128, 64], C_SCALE, yT, 0, do_ctx)
        P.barrier()
        P.stage_es = None

import math

NEG_EH = -math.exp(-0.5)
NCH64 = NT // 64


def stage_rwkv_prep(Kk, l, S):
    P = Kk.P
    o_ = l // 2
    with ExitStack() as ses:
        P.stage_es = ses
        R = {}
        R["udp"] = P.sb("udp", [128, 30, 512])
        R["ub"] = P.sb("rub", [128, 2, 516])
        R["T"] = P.sb("rT", [128, 12, 512])
        R["PK"] = P.sb("rPK", [128, 4, 512])
        R["stg"] = P.sb("rstg", [128, 6, 512])
        R["tm"] = P.sb("rtm", [128, 3, 512])
        R["mu"] = P.sb("rmu", [128, 30])
        R["vec"] = P.sb("rvec", [128, 7, 8])
        R["g2w"] = P.sb("g2w", [128, 2, 1024])
        R["w2w"] = P.sb("w2w", [128, 2, 1024])
        R["a2w"] = P.sb("a2w", [128, 2, 1024])
        R["blk"] = P.sb("blk", [128, 128])
        R["on64"] = P.sb("on64", [128, 64])
        R["wc"] = P.sb("wcst", [128, 4, 8])
        P.dma(R["mu"][:, :], Kk.ins["rw_mu"][o_], writes=[R["mu"]])
        P.dma(R["vec"][:, :, :], Kk.ins["rw_vec"][o_], writes=[R["vec"]])
        P.dma(R["g2w"][:, :, :], Kk.ins["rwkv_g2"][o_].rearrange("(kc p) c -> p kc c", p=128), writes=[R["g2w"]])
        for d in range(2):
            P.dma(R["w2w"][0:96, d, :], Kk.ins["rwkv_w2"][o_][d], writes=[R["w2w"]])
            P.dma(R["a2w"][0:96, d, :], Kk.ins["rwkv_a2"][o_][d], writes=[R["a2w"]])
        P.dma(R["blk"][:, :], Kk.ins["c_blk64"][:, :], writes=[R["blk"]])
        P.op("pool", lambda e: e.memset(R["on64"][:, :], 1.0), writes=[R["on64"]])
        R["si"] = 0
        R["ti"] = 0
        R["tmi"] = 0
        R["wi"] = 0
        for (t0, n, isctx) in tiles_main():
            _rw_prep_tile(Kk, S, R, t0, n, isctx)
        P.barrier()
        P.stage_es = None


def _rw_prep_tile(Kk, S, R, t0, n, isctx):
    P = Kk.P
    udp, ub, T, stg, tm, mu, vec = R["udp"], R["ub"], R["T"], R["stg"], R["tm"], R["mu"], R["vec"]
    seg0, seg1 = (0, NCTX) if isctx else (NCTX, NT)
    lh = 1 if t0 > seg0 else 0
    rh = 1 if t0 + n < seg1 else 0
    nch = n // 64
    ch0 = t0 // 64
    rows = [128] * 24 + [96] * 4 + [128] * 2
    roff = [i * 128 for i in range(24)] + [3072 + i * 96 for i in range(4)] + [3456, 3584]

    def tslot():
        s = R["ti"] % 12
        R["ti"] += 1
        return s

    def store_fm(src_ap, skey, dst):
        P.dma(dst, src_ap, reads=[skey], qn="pool")

    for c in range(30):
        rw = rows[c]
        s = c % 2
        kb = (ub, s)
        if not (lh and rh):
            P.op("pool", lambda e, s=s: e.memset(ub[:, s, :], 0.0), writes=[kb])
        P.dma(ub[0:rw, s, 1 - lh:1 + n + rh], S["udT"][roff[c]:roff[c] + rw, t0 - lh:t0 + n + rh], writes=[kb])
        ts = tslot()
        kt = (T, ts)
        P.op("dve", lambda e, s=s, ts=ts, rw=rw: e.tensor_tensor(T[0:rw, ts, 0:n], ub[0:rw, s, 0:n], ub[0:rw, s, 2:2 + n], ALU.add),
             reads=[kb], writes=[kt])
        P.op("dve", lambda e, s=s, ts=ts, rw=rw: e.scalar_tensor_tensor(T[0:rw, ts, 0:n], T[0:rw, ts, 0:n], 0.5, ub[0:rw, s, 1:1 + n],
                                                                         ALU.mult, ALU.subtract),
             reads=[kb, kt], writes=[kt])
        P.op("dve", lambda e, s=s, ts=ts, rw=rw, c=c: e.scalar_tensor_tensor(udp[0:rw, c, 0:n], T[0:rw, ts, 0:n], mu[0:rw, c:c + 1],
                                                                              ub[0:rw, s, 1:1 + n], ALU.mult, ALU.add),
             reads=[kb, kt, mu], writes=[(udp, c)])
    for c in (28, 29):
        P.op("act", lambda e, c=c: e.activation(udp[:, c, 0:n], udp[:, c, 0:n], AF.Sigmoid), reads=[(udp, c)], writes=[(udp, c)])
    for c in range(8):
        ps = Kk.psb[c % 2]
        for kc in range(2):
            P.op("pe", lambda e, c=c, kc=kc, ps=ps: e.matmul(ps[:, 0:n], R["g2w"][:, kc, c * 128:(c + 1) * 128], udp[:, 28 + kc, 0:n],
                                                            start=(kc == 0), stop=(kc == 1)),
                 reads=[R["g2w"], (udp, 28 + kc)], writes=[ps])
        s = R["si"] % 6
        R["si"] += 1
        P.op("act", lambda e, ps=ps, s=s: e.activation(stg[:, s, 0:n], ps[:, 0:n], AF.Identity), reads=[ps], writes=[(stg, s)])
        store_fm(stg[:, s, 0:n], (stg, s), S["gateT"][c * 128:(c + 1) * 128, t0:t0 + n])
    for d in range(2):
        P.op("act", lambda e, d=d: e.activation(udp[0:96, 24 + d, 0:n], udp[0:96, 24 + d, 0:n], AF.Tanh),
             reads=[(udp, 24 + d)], writes=[(udp, 24 + d)])
    for c in range(8):
        _rw_prep_chunk(Kk, S, R, t0, n, c, nch, ch0)


def _rw_prep_chunk(Kk, S, R, t0, n, c, nch, ch0):
    P = Kk.P
    udp, T, stg, tm, vec = R["udp"], R["T"], R["stg"], R["tm"], R["vec"]
    rC, kC, vC = (udp, c), (udp, 8 + c), (udp, 16 + c)
    r_ = udp[:, c, 0:n]
    k_ = udp[:, 8 + c, 0:n]
    v_ = udp[:, 16 + c, 0:n]

    def tslot():
        s = R["ti"] % 12
        R["ti"] += 1
        return s, (T, s)

    def sslot():
        s = R["si"] % 6
        R["si"] += 1
        return s, (stg, s)

    def transpose_store(src_ap, skey, dst_tm):
        pst = Kk.psb[4 + R["tmi"] % 2]
        s = R["tmi"] % 3
        R["tmi"] += 1
        nb = n // 128
        for b in range(nb):
            P.op("pe", lambda e, b=b: e.matmul(pst[:, b * 128:(b + 1) * 128], src_ap[:, b * 128:(b + 1) * 128], Kk.ident[:, :],
                                               start=True, stop=True),
                 reads=[skey, Kk.ident], writes=[pst])
        P.op("act", lambda e: e.activation(tm[:, s, 0:n], pst[:, 0:n], AF.Identity), reads=[pst], writes=[(tm, s)])
        P.dma(dst_tm[t0:t0 + n, c * 128:(c + 1) * 128].rearrange("(b p) f -> p b f", p=128),
              tm[:, s, 0:n].rearrange("p (b f) -> p b f", f=128), reads=[(tm, s)], qn="pool")

    def headsum(src_ap, skey, ps):
        P.op("pe", lambda e: e.matmul(ps[:, 0:n], R["blk"][:, :], src_ap, start=True, stop=True), reads=[skey, R["blk"]], writes=[ps])

    transpose_store(v_, vC, S["vTM"])
    PK = R["PK"]
    k_kkr, k_sq, k_kk, k_ks = (PK, 0), (PK, 1), (PK, 2), (PK, 3)
    P.op("dve", lambda e: e.tensor_scalar(PK[:, 0, 0:n], k_, vec[:, 0, c:c + 1], None, ALU.mult), reads=[kC, vec], writes=[k_kkr])
    P.op("act", lambda e: e.activation(PK[:, 1, 0:n], PK[:, 0, 0:n], AF.Square), reads=[k_kkr], writes=[k_sq])
    ps = Kk.psb[2]
    headsum(PK[:, 1, 0:n], k_sq, ps)
    P.op("act", lambda e: e.activation(PK[:, 1, 0:n], ps[:, 0:n], AF.Sqrt, bias=Kk.epsb[:, 2:3], scale=1.0),
         reads=[ps, Kk.epsb], writes=[k_sq])
    P.op("dve", lambda e: e.reciprocal(PK[:, 1, 0:n], PK[:, 1, 0:n]), reads=[k_sq], writes=[k_sq])
    P.op("dve", lambda e: e.tensor_tensor(PK[:, 2, 0:n], PK[:, 0, 0:n], PK[:, 1, 0:n], ALU.mult), reads=[k_kkr, k_sq], writes=[k_kk])
    kk = PK[:, 2, 0:n]
    for d in range(2):
        ps1 = Kk.psb[d]
        P.op("pe", lambda e, d=d, ps1=ps1: e.matmul(ps1[:, 0:n], R["w2w"][0:96, d, c * 128:(c + 1) * 128], udp[0:96, 24 + d, 0:n],
                                                    start=True, stop=True),
             reads=[R["w2w"], (udp, 24 + d)], writes=[ps1])
        s_lw, k_lw = tslot()
        P.op("act", lambda e, d=d, ps1=ps1, s_lw=s_lw: e.activation(T[:, s_lw, 0:n], ps1[:, 0:n], AF.Sigmoid, bias=vec[:, 3 + d, c:c + 1], scale=1.0),
             reads=[ps1, vec], writes=[k_lw])
        P.op("dve", lambda e, s_lw=s_lw: e.tensor_scalar(T[:, s_lw, 0:n], T[:, s_lw, 0:n], NEG_EH, None, ALU.mult), reads=[k_lw], writes=[k_lw])
        ps2 = Kk.psb[3]
        P.op("pe", lambda e, d=d: e.matmul(ps2[:, 0:n], R["a2w"][0:96, d, c * 128:(c + 1) * 128], udp[0:96, 26 + d, 0:n],
                                           start=True, stop=True),
             reads=[R["a2w"], (udp, 26 + d)], writes=[ps2])
        s_a, k_a = tslot()
        P.op("act", lambda e, d=d, s_a=s_a: e.activation(T[:, s_a, 0:n], ps2[:, 0:n], AF.Sigmoid, bias=vec[:, 5 + d, c:c + 1], scale=1.0),
             reads=[ps2, vec], writes=[k_a])
        s_kd, k_kd = tslot()
        P.op("dve", lambda e, s_a=s_a, s_kd=s_kd: e.tensor_scalar(T[:, s_kd, 0:n], T[:, s_a, 0:n], -1.0, vec[:, 1, c:c + 1], ALU.add, ALU.mult),
             reads=[k_a, vec], writes=[k_kd])
        P.op("dve", lambda e, s_kd=s_kd: e.scalar_tensor_tensor(T[:, s_kd, 0:n], T[:, s_kd, 0:n], 1.0, k_, ALU.add, ALU.mult),
             reads=[k_kd, kC], writes=[k_kd])
        if d == 0:
            P.op("pool", lambda e, s_kd=s_kd: e.tensor_copy(PK[:, 3, 0:n], T[:, s_kd, 0:n]), reads=[k_kd], writes=[k_ks])
        else:
            P.op("pool", lambda e, s_kd=s_kd: e.tensor_tensor(PK[:, 3, 0:n], PK[:, 3, 0:n], T[:, s_kd, 0:n], ALU.add),
                 reads=[k_kd, k_ks], writes=[k_ks])
        s_cl, k_cl = tslot()
        for q in range(nch):
            P.op("dve", lambda e, q=q, s_cl=s_cl, s_lw=s_lw: e.tensor_tensor_scan(T[:, s_cl, q * 64:(q + 1) * 64], R["on64"][:, :],
                                                                                  T[:, s_lw, q * 64:(q + 1) * 64], 0.0, ALU.mult, ALU.add),
                 reads=[k_lw, R["on64"]], writes=[k_cl])
        if d == 1:
            s_t2, k_t2 = tslot()
            P.op("dve", lambda e, s_t2=s_t2, s_cl=s_cl, s_lw=s_lw: e.tensor_tensor(T[:, s_t2, 0:n], T[:, s_lw, 0:n], T[:, s_cl, 0:n], ALU.subtract),
                 reads=[k_lw, k_cl], writes=[k_t2])
            cl3 = T[:, s_cl, 0:n].rearrange("p (q t) -> p q t", t=64)
            P.op("dve", lambda e, s_t2=s_t2, cl3=cl3: e.tensor_tensor(T[:, s_t2, 0:n].rearrange("p (q t) -> p q t", t=64),
                                                                      T[:, s_t2, 0:n].rearrange("p (q t) -> p q t", t=64),
                                                                      cl3[:, :, 63:64].broadcast_to([128, nch, 64]), ALU.add),
                 reads=[k_t2, k_cl], writes=[k_t2])
            s_cl, k_cl = s_t2, k_t2
        cl = T[:, s_cl, 0:n]
        s_ep, k_ep = tslot()
        P.op("act", lambda e, s_ep=s_ep, cl=cl: e.activation(T[:, s_ep, 0:n], cl, AF.Exp), reads=[k_cl], writes=[k_ep])
        s_em, k_em = tslot()
        P.op("act", lambda e, s_em=s_em, cl=cl: e.activation(T[:, s_em, 0:n], cl, AF.Exp, scale=-1.0), reads=[k_cl], writes=[k_em])
        P.op("dve", lambda e, s_lw=s_lw, cl=cl: e.tensor_tensor(T[:, s_lw, 0:n], cl, T[:, s_lw, 0:n], ALU.subtract), reads=[k_cl, k_lw], writes=[k_lw])
        P.op("act", lambda e, s_lw=s_lw: e.activation(T[:, s_lw, 0:n], T[:, s_lw, 0:n], AF.Exp), reads=[k_lw], writes=[k_lw])
        last = 63 if d == 0 else 0
        wi = R["wi"] % 4
        R["wi"] += 1
        P.op("pool", lambda e, s_ep=s_ep, wi=wi, last=last: e.tensor_copy(
            R["wc"][:, wi, 0:nch].rearrange("p (q o) -> p q o", o=1),
            T[:, s_ep, 0:n].rearrange("p (q t) -> p q t", t=64)[:, :, last:last + 1]), reads=[k_ep], writes=[(R["wc"], wi)])
        P.dma(S["wC"][d][c * 128:(c + 1) * 128, ch0:ch0 + nch], R["wc"][:, wi, 0:nch], reads=[(R["wc"], wi)], qn="pool")
        s1, ks1 = sslot()
        P.op("dve", lambda e, s1=s1, s_ep=s_ep: e.tensor_tensor(stg[:, s1, 0:n], r_, T[:, s_ep, 0:n], ALU.mult), reads=[rC, k_ep], writes=[ks1])
        P.dma(S["RW"][d][3][c * 128:(c + 1) * 128, t0:t0 + n], stg[:, s1, 0:n], reads=[ks1], qn="pool")
        s2, ks2 = sslot()
        P.op("pool", lambda e, s2=s2, s_lw=s_lw: e.tensor_tensor(stg[:, s2, 0:n], kk, T[:, s_lw, 0:n], ALU.mult), reads=[k_kk, k_lw], writes=[ks2])
        P.dma(S["RW"][d][2][c * 128:(c + 1) * 128, t0:t0 + n], stg[:, s2, 0:n], reads=[ks2], qn="pool")
        s3, ks3 = sslot()
        P.op("dve", lambda e, s3=s3, s_kd=s_kd, s_em=s_em: e.tensor_tensor(stg[:, s3, 0:n], T[:, s_kd, 0:n], T[:, s_em, 0:n], ALU.mult),
             reads=[k_kd, k_em], writes=[ks3])
        P.dma(S["RW"][d][0][c * 128:(c + 1) * 128, t0:t0 + n], stg[:, s3, 0:n], reads=[ks3], qn="pool")
        transpose_store(stg[:, s3, 0:n], ks3, S["kbTM"][d])
        s4, ks4 = sslot()
        P.op("pool", lambda e, s4=s4, s_a=s_a: e.tensor_tensor(stg[:, s4, 0:n], T[:, s_a, 0:n], kk, ALU.mult), reads=[k_a, k_kk], writes=[ks4])
        P.op("dve", lambda e, s4=s4, s_em=s_em: e.tensor_tensor(stg[:, s4, 0:n], stg[:, s4, 0:n], T[:, s_em, 0:n], ALU.mult),
             reads=[ks4, k_em], writes=[ks4])
        P.dma(S["RW"][d][1][c * 128:(c + 1) * 128, t0:t0 + n], stg[:, s4, 0:n], reads=[ks4], qn="pool")
        transpose_store(stg[:, s4, 0:n], ks4, S["bbTM"][d])
    P.op("dve", lambda e: e.scalar_tensor_tensor(PK[:, 3, 0:n], PK[:, 3, 0:n], vec[:, 2, c:c + 1], r_, ALU.mult, ALU.mult),
         reads=[k_ks, vec, rC], writes=[k_ks])
    ps3 = Kk.psb[2]
    headsum(PK[:, 3, 0:n], k_ks, ps3)
    s5, ks5 = sslot()
    P.op("dve", lambda e: e.tensor_tensor(stg[:, s5, 0:n], ps3[:, 0:n], v_, ALU.mult), reads=[ps3, vC], writes=[ks5])
    P.dma(S["bonT"][c * 128:(c + 1) * 128, t0:t0 + n], stg[:, s5, 0:n], reads=[ks5], qn="pool")


def stage_rwkv(Kk, l, S, yT, do_ctx):
    for d in range(2):
        _rwkv_dir(Kk, l, S, yT, do_ctx, d)


def _rwkv_dir(Kk, l, S, yT, do_ctx, d):
    P = Kk.P
    o_ = l // 2
    with ExitStack() as ses:
        P.stage_es = ses
        R = {}
        R["m64"] = P.sb("m64", [64, 4, 64])
        P.dma(R["m64"][:, :, :], Kk.ins["c_m64"][:, :, :], writes=[R["m64"]])
        R["wCt"] = P.sb("wCt", [64, 16, NCH64])
        P.dma(R["wCt"][:, :, :], S["wC"][d].rearrange("(h j) q -> j h q", j=64), writes=[R["wCt"]])
        R["ST"] = P.sb("ST", [64, 16, 64])
        P.op("pool", lambda e: e.memset(R["ST"][:, :, :], 0.0), writes=[R["ST"]])
        R["Xc"] = P.sb("Xc", [64, 2, 16, 4, 64])
        R["TMc"] = P.sb("TMc", [64, 1, 3, 1024])
        R["AM"] = P.sb("AM", [64, 16, 4, 64])
        R["Lm"] = P.sb("Lm", [64, 16, 64])
        R["Pb"] = P.sb("Pb", [64, 2, 8, 64])
        R["Qb"] = P.sb("Qb", [64, 2, 8, 64])
        R["Xb"] = P.sb("Xb", [64, 2, 8, 64])
        R["Qi"] = P.sb("Qi", [64, 8, 64])
        R["TT"] = P.sb("TT", [64, 16, 64])
        R["Zs"] = P.sb("Zs", [64, 16, 64])
        R["Us"] = P.sb("Us", [64, 16, 64])
        R["Yc"] = P.sb("Yc", [64, 2, 1024])
        R["tmpS"] = P.sb("tmpS", [64, 16, 64])
        if d == 1:
            R["Yf"] = P.sb("Yf", [64, 1, 1024])
            R["ln"] = P.sb("lnrow", [64, 2, 1024])
            P.dma(R["ln"][:, :, :], Kk.ins["rw_ln"][o_].rearrange("(o a) f -> o a f", o=1).broadcast_to([64, 2, 1024]), writes=[R["ln"]])
            R["st"] = P.sb("lnst", [64, 4, 16])
            R["cen"] = P.sb("cen", [64, 1024])
            R["sq"] = P.sb("lsq", [64, 1024])
            R["bg"] = P.sb("bg", [128, 1, 2, 8, 64])
            R["yo"] = P.sb("yo", [128, 2, 8, 64])
        order = list(range(4)) + list(range(4, NCH64)) if d == 0 else [3, 2, 1, 0] + list(range(NCH64 - 1, 3, -1))
        import os
        nlim = int(os.environ.get("RW_NCH", "999"))
        for ci, ch in enumerate(order[:nlim]):
            _rwkv_chunk(Kk, S, yT, R, d, ch, ci, do_ctx)
        P.barrier()
        P.stage_es = None


def _rwkv_chunk(Kk, S, yT, R, d, ch, ci, do_ctx):
    P = Kk.P
    s = ci % 2
    t0 = ch * 64
    Xc, TMc, AM, Lm, TT, ST, Zs, Us, Yc, m64 = R["Xc"], R["TMc"], R["AM"], R["Lm"], R["TT"], R["ST"], R["Zs"], R["Us"], R["Yc"], R["m64"]
    id64 = Kk.ident[0:64, 0:64]
    kx = (Xc, s)
    ktm = (TMc, 0)
    for q in range(4):
        P.dma(Xc[:, s, :, q, :], S["RW"][d][q][:, t0:t0 + 64].rearrange("(h j) t -> j h t", j=64), writes=[kx])
    P.dma(TMc[:, 0, 0, :], S["kbTM"][d][t0:t0 + 64, :], writes=[ktm])
    P.dma(TMc[:, 0, 1, :], S["bbTM"][d][t0:t0 + 64, :], writes=[ktm])
    P.dma(TMc[:, 0, 2, :], S["vTM"][t0:t0 + 64, :], writes=[ktm])
    import os
    LV = int(os.environ.get("RW_LEVEL", "99"))
    if LV <= 1:
        return
    mi = 0 if d == 0 else 2
    mL = m64[:, 2 if d == 0 else 0, :]
    psb = Kk.psb
    for hf in range(2):
        h0 = hf * 8
        for hh in range(8):
            h = h0 + hh
            bk = hh // 4
            col = (hh % 4) * 128
            P.op("pe", lambda e, h=h, bk=bk, col=col: e.matmul(psb[bk][0:64, col:col + 128], Xc[:, s, h, 0, :], Xc[:, s, h, 2:4, :],
                                                              start=True, stop=True), reads=[kx], writes=[psb[bk]])
            P.op("pe", lambda e, h=h, bk=bk, col=col: e.matmul(psb[2 + bk][0:64, col:col + 128], Xc[:, s, h, 1, :], Xc[:, s, h, 2:4, :],
                                                              start=True, stop=True), reads=[kx], writes=[psb[2 + bk]])
            P.op("pe", lambda e, h=h, hh=hh: e.matmul(psb[4][0:64, hh * 64:(hh + 1) * 64], Xc[:, s, h, 2, :], Xc[:, s, h, 1, :],
                                                      start=True, stop=True), reads=[kx], writes=[psb[4]])
        mask2 = m64[:, mi:mi + 2, :].rearrange("p (o a) t -> p o a t", o=1).broadcast_to([64, 4, 2, 64])
        for bk in range(2):
            hs = slice(h0 + bk * 4, h0 + bk * 4 + 4)
            P.op("dve", lambda e, bk=bk, hs=hs: e.tensor_tensor(AM[:, hs, 0:2, :], psb[bk][0:64, :].rearrange("p (h a t) -> p h a t", a=2, t=64),
                                                               mask2, ALU.mult), reads=[psb[bk], m64], writes=[(AM, (hf, bk, 0))])
            P.op("dve", lambda e, bk=bk, hs=hs: e.tensor_tensor(AM[:, hs, 2:4, :], psb[2 + bk][0:64, :].rearrange("p (h a t) -> p h a t", a=2, t=64),
                                                               mask2, ALU.mult), reads=[psb[2 + bk], m64], writes=[(AM, (hf, bk, 1))])
        kAM = [(AM, (hf, bk, a)) for bk in range(2) for a in range(2)]
        hs8 = slice(h0, h0 + 8)
        kL = (Lm, hf)
        P.op("dve", lambda e, hs8=hs8: e.tensor_tensor(Lm[:, hs8, :], psb[4][0:64, :].rearrange("p (h t) -> p h t", t=64),
                                              mL.rearrange("p (o t) -> p o t", o=1).broadcast_to([64, 8, 64]), ALU.mult),
             reads=[psb[4], m64], writes=[kL])
        if LV <= 2:
            continue
        Pb, Qb, Xb, Qi = R["Pb"], R["Qb"], R["Xb"], R["Qi"]
        idb = id64.rearrange("p (o t) -> p o t", o=1).broadcast_to([64, 8, 64])
        P.op("dve", lambda e, hs8=hs8, idb=idb: e.tensor_tensor(Xb[:, 0, :, :], idb, AM[:, hs8, 2, :], ALU.subtract), reads=kAM + [Kk.ident], writes=[(Xb, 0)])
        for lev in range(5):
            pi_, po_ = lev % 2, (lev + 1) % 2
            lastlev = (lev == 4)
            for hh in range(8):
                h = h0 + hh
                Pk = AM[:, h, 2, :] if lev == 0 else Pb[:, pi_, hh, :]
                Qk = Lm[:, h, :] if lev == 0 else Qb[:, pi_, hh, :]
                rd = kAM + [kL] if lev == 0 else [(Pb, pi_), (Qb, pi_)]
                if not lastlev:
                    P.op("pe", lambda e, hh=hh, Pk=Pk, Qk=Qk: e.matmul(psb[5][0:64, hh * 64:(hh + 1) * 64], Qk, Pk, start=True, stop=True),
                         reads=rd, writes=[psb[5]])
                P.op("pe", lambda e, hh=hh, Pk=Pk, Qk=Qk: e.matmul(psb[6][0:64, hh * 64:(hh + 1) * 64], Pk, Qk, start=True, stop=True),
                     reads=rd, writes=[psb[6]])
            if not lastlev:
                P.op("act", lambda e, po_=po_: e.activation(Pb[:, po_, :, :], psb[5][0:64, :].rearrange("p (h t) -> p h t", t=64), AF.Identity),
                     reads=[psb[5]], writes=[(Pb, po_)])
                P.op("dve", lambda e, po_=po_: e.tensor_copy(Qb[:, po_, :, :], psb[6][0:64, :].rearrange("p (h t) -> p h t", t=64)),
                     reads=[psb[6]], writes=[(Qb, po_)])
            P.op("dve", lambda e, idb=idb: e.tensor_tensor(Qi[:, :, :], psb[6][0:64, :].rearrange("p (h t) -> p h t", t=64), idb, ALU.add),
                 reads=[psb[6], Kk.ident], writes=[Qi])
            for hh in range(8):
                P.op("pe", lambda e, hh=hh, pi_=pi_: e.matmul(psb[7][0:64, hh * 64:(hh + 1) * 64], Qi[:, hh, :], Xb[:, pi_, hh, :],
                                                              start=True, stop=True), reads=[Qi, (Xb, pi_)], writes=[psb[7]])
            if lastlev:
                P.op("act", lambda e, hs8=hs8: e.activation(TT[:, hs8, :], psb[7][0:64, :].rearrange("p (h t) -> p h t", t=64), AF.Identity),
                     reads=[psb[7]], writes=[(TT, hf)])
            else:
                P.op("act", lambda e, po_=po_: e.activation(Xb[:, po_, :, :], psb[7][0:64, :].rearrange("p (h t) -> p h t", t=64), AF.Identity),
                     reads=[psb[7]], writes=[(Xb, po_)])
    allAM = [(AM, (hf, bk, a)) for hf in range(2) for bk in range(2) for a in range(2)]
    kTT = [(TT, 0), (TT, 1)]
    if LV <= 3:
        return
    V = lambda h: TMc[:, 0, 2, h * 64:(h + 1) * 64]
    for h in range(16):
        bk, col = h // 8, (h % 8) * 64
        P.op("pe", lambda e, h=h, bk=bk, col=col: e.matmul(psb[bk][0:64, col:col + 64], Xc[:, s, h, 2, :], ST[:, h, :], start=True, stop=False),
             reads=[kx, ST], writes=[psb[bk]])
        P.op("pe", lambda e, h=h, bk=bk, col=col: e.matmul(psb[bk][0:64, col:col + 64], AM[:, h, 0, :], V(h), start=False, stop=True),
             reads=allAM + [ktm], writes=[psb[bk]])
    for bk in range(2):
        P.op("act", lambda e, bk=bk: e.activation(Zs[:, bk * 8:(bk + 1) * 8, :], psb[bk][0:64, :].rearrange("p (h t) -> p h t", t=64),
                                                  AF.Identity, scale=-1.0), reads=[psb[bk]], writes=[(Zs, bk)])
    for h in range(16):
        bk, col = h // 8, (h % 8) * 64
        P.op("pe", lambda e, h=h, bk=bk, col=col: e.matmul(psb[2 + bk][0:64, col:col + 64], TT[:, h, :], Zs[:, h, :], start=True, stop=True),
             reads=kTT + [(Zs, bk)], writes=[psb[2 + bk]])
    for bk in range(2):
        P.op("dve", lambda e, bk=bk: e.tensor_copy(Us[:, bk * 8:(bk + 1) * 8, :], psb[2 + bk][0:64, :].rearrange("p (h t) -> p h t", t=64)),
             reads=[psb[2 + bk]], writes=[(Us, bk)])
    for h in range(16):
        bk, col = h // 8, (h % 8) * 64
        P.op("pe", lambda e, h=h, bk=bk, col=col: e.matmul(psb[4 + bk][0:64, col:col + 64], Xc[:, s, h, 3, :], ST[:, h, :], start=True, stop=False),
             reads=[kx, ST], writes=[psb[4 + bk]])
        P.op("pe", lambda e, h=h, bk=bk, col=col: e.matmul(psb[4 + bk][0:64, col:col + 64], AM[:, h, 1, :], V(h), start=False, stop=False),
             reads=allAM + [ktm], writes=[psb[4 + bk]])
        P.op("pe", lambda e, h=h, bk=bk, col=col: e.matmul(psb[4 + bk][0:64, col:col + 64], AM[:, h, 3, :], Us[:, h, :], start=False, stop=True),
             reads=allAM + [(Us, bk)], writes=[psb[4 + bk]])
        P.op("pe", lambda e, h=h, bk=bk, col=col: e.matmul(psb[6 + bk][0:64, col:col + 64], TMc[:, 0, 0, h * 64:(h + 1) * 64], V(h), start=True, stop=False),
             reads=[ktm], writes=[psb[6 + bk]])
        P.op("pe", lambda e, h=h, bk=bk, col=col: e.matmul(psb[6 + bk][0:64, col:col + 64], TMc[:, 0, 1, h * 64:(h + 1) * 64], Us[:, h, :], start=False, stop=True),
             reads=[ktm, (Us, bk)], writes=[psb[6 + bk]])
    ky = (Yc, s)
    tmpS = R["tmpS"]
    for bk in range(2):
        P.op("act", lambda e, bk=bk: e.activation(Yc[:, s, bk * 512:(bk + 1) * 512], psb[4 + bk][0:64, :], AF.Identity),
             reads=[psb[4 + bk]], writes=[(Yc, (s, bk))])
        hsb = slice(bk * 8, (bk + 1) * 8)
        P.op("dve", lambda e, bk=bk, hsb=hsb: e.tensor_tensor(tmpS[:, hsb, :], psb[6 + bk][0:64, :].rearrange("p (h t) -> p h t", t=64),
                                                             ST[:, hsb, :], ALU.add), reads=[psb[6 + bk], ST], writes=[(tmpS, bk)])
        P.op("pool", lambda e, bk=bk, hsb=hsb: e.tensor_tensor(ST[:, hsb, :], tmpS[:, hsb, :],
                                                              R["wCt"][:, hsb, ch:ch + 1].broadcast_to([64, 8, 64]), ALU.mult),
             reads=[(tmpS, bk), R["wCt"]], writes=[ST])
    kyall = [(Yc, (s, 0)), (Yc, (s, 1))]
    if os.environ.get("RW_DBG") and d == 0 and ci == 0:
        D = S["dbg"]
        P.dma(D["AM"], AM[:, :, :, :], reads=allAM, qn="pool")
        P.dma(D["Lm"], Lm[:, :, :], reads=[(Lm, 0), (Lm, 1)], qn="pool")
        P.dma(D["TT"], TT[:, :, :], reads=kTT, qn="pool")
        P.dma(D["Zs"], Zs[:, :, :], reads=[(Zs, 0), (Zs, 1)], qn="pool")
        P.dma(D["Us"], Us[:, :, :], reads=[(Us, 0), (Us, 1)], qn="pool")
        P.dma(D["Yc"], Yc[:, s, :], reads=kyall, qn="pool")
        P.dma(D["ST"], ST[:, :, :], reads=[ST], qn="pool")
    if LV <= 4:
        return
    if d == 0:
        P.dma(S["YfTM"][t0:t0 + 64, :], Yc[:, s, :], reads=kyall, qn="pool")
        return
    if ch < 4 and not do_ctx:
        return
    Yf, ln, stt, cen, sq, bg, yo = R["Yf"], R["ln"], R["st"], R["cen"], R["sq"], R["bg"], R["yo"]
    kf = (Yf, 0)
    P.dma(Yf[:, 0, :], S["YfTM"][t0:t0 + 64, :], writes=[kf])
    kb_ = (bg, 0)
    P.dma(bg[:, 0, 0, :, :], S["bonT"][:, t0:t0 + 64].rearrange("(c p) t -> p c t", p=128), writes=[kb_])
    P.dma(bg[:, 0, 1, :, :], S["gateT"][:, t0:t0 + 64].rearrange("(c p) t -> p c t", p=128), writes=[kb_])
    P.op("dve", lambda e: e.tensor_tensor(Yf[:, 0, :], Yf[:, 0, :], Yc[:, s, :], ALU.add), reads=[kf] + kyall, writes=[kf])
    y3 = Yf[:, 0, :].rearrange("p (h t) -> p h t", t=64)
    c3 = cen[:, :].rearrange("p (h t) -> p h t", t=64)
    q3 = sq[:, :].rearrange("p (h t) -> p h t", t=64)
    P.op("dve", lambda e: e.tensor_reduce(stt[:, 0, :], y3, AX.X, ALU.add), reads=[kf], writes=[(stt, 0)])
    P.op("dve", lambda e: e.tensor_scalar(stt[:, 0, :], stt[:, 0, :], 1.0 / 64, None, ALU.mult), reads=[(stt, 0)], writes=[(stt, 0)])
    P.op("dve", lambda e: e.tensor_tensor(c3, y3, stt[:, 0, :].rearrange("p (h o) -> p h o", o=1).broadcast_to([64, 16, 64]), ALU.subtract),
         reads=[kf, (stt, 0)], writes=[cen])
    P.op("act", lambda e: e.activation(sq[:, :], cen[:, :], AF.Square), reads=[cen], writes=[sq])
    P.op("dve", lambda e: e.tensor_reduce(stt[:, 1, :], q3, AX.X, ALU.add), reads=[sq], writes=[(stt, 1)])
    P.op("act", lambda e: e.activation(stt[:, 1, :], stt[:, 1, :], AF.Sqrt, bias=Kk.epsb[0:64, 1:2], scale=1.0 / 64),
         reads=[(stt, 1), Kk.epsb], writes=[(stt, 1)])
    P.op("dve", lambda e: e.reciprocal(stt[:, 1, :], stt[:, 1, :]), reads=[(stt, 1)], writes=[(stt, 1)])
    P.op("dve", lambda e: e.tensor_tensor(c3, c3, stt[:, 1, :].rearrange("p (h o) -> p h o", o=1).broadcast_to([64, 16, 64]), ALU.mult),
         reads=[cen, (stt, 1)], writes=[cen])
    P.op("pool", lambda e: e.tensor_tensor(cen[:, :], cen[:, :], ln[:, 0, :], ALU.mult), reads=[cen, ln], writes=[cen])
    P.op("pool", lambda e: e.tensor_tensor(cen[:, :], cen[:, :], ln[:, 1, :], ALU.add), reads=[cen, ln], writes=[cen])
    pst = psb[0]
    for c in range(8):
        P.op("pe", lambda e, c=c: e.matmul(pst[:, c * 64:(c + 1) * 64], cen[:, c * 128:(c + 1) * 128], id64, start=True, stop=True),
             reads=[cen, Kk.ident], writes=[pst])
    ko = (yo, s)
    P.op("dve", lambda e: e.tensor_tensor(yo[:, s, :, :], pst[:, :].rearrange("p (c t) -> p c t", t=64), bg[:, 0, 0, :, :], ALU.add),
         reads=[pst, kb_], writes=[ko])
    P.op("pool", lambda e: e.tensor_tensor(yo[:, s, :, :], yo[:, s, :, :], bg[:, 0, 1, :, :], ALU.mult), reads=[ko, kb_], writes=[ko])
    P.dma(yT[1024:2048, t0:t0 + 64].rearrange("(c p) t -> p c t", p=128), yo[:, s, :, :], reads=[ko], qn="pool")


def fm(v, n=None):
    v = np.asarray(v, np.float32)
    return np.ascontiguousarray(v.reshape(-1, 128).T)


def host_prep(inp, ncores=8):
    f32 = np.float32
    L = 4
    sh = {}
    sh["c_ones"] = np.ones((128, 128), f32)
    sh["c_ident"] = np.eye(128, dtype=f32)
    sh["c_onesD"] = np.full((128, 128), 1.0 / D, f32)
    eps = np.zeros((128, 4), f32)
    eps[:, 0] = EPS
    eps[:, 1] = 64e-5
    eps[:, 2] = 1e-12
    sh["c_eps"] = eps
    sh["ada_w"] = np.asarray(inp["ada_w"], f32)
    sh["ada_b"] = np.stack([fm(inp["ada_b"][l]) for l in range(L)])
    sh["norm_mix"] = np.stack([np.repeat(fm(inp["norm_mix"][l])[:, :, None], 2, 2) for l in range(L)])
    sh["norm_ffn"] = np.stack([np.repeat(fm(inp["norm_ffn"][l])[:, :, None], 2, 2) for l in range(L)])
    sh["ffn_w_up"] = np.asarray(inp["ffn_w_up"], f32)
    sh["ffn_w_down"] = np.asarray(inp["ffn_w_down"], f32)
    sh["ffn_conv_w"] = np.stack([np.stack([fm(inp["ffn_conv_w"][l][k]) for k in range(3)], 1) for l in range(L)])
    sh["ffn_conv_b"] = np.stack([fm(inp["ffn_conv_b"][l]) for l in range(L)])
    sh["ev_w_in"] = np.asarray(inp["ev_w_in"], f32)
    sh["ev_w_out"] = np.asarray(inp["ev_w_out"], f32)
    sh["od_w_in"] = np.asarray(inp["od_w_in"], f32)
    sh["od_w_out"] = np.asarray(inp["od_w_out"], f32)
    sh["final_norm"] = fm(inp["final_norm"])
    eps[:, 3] = 1.0
    host_prep_even(inp, sh)
    host_prep_odd(inp, sh)
    per = []
    x = np.asarray(inp["x"], f32)
    ctx = np.asarray(inp["ctx"], f32)
    c = np.asarray(inp["c"], f32)
    cc = np.asarray(inp["c_ctx"], f32)
    for i in range(ncores):
        b = i % 4
        d = {}
        d["xT0"] = np.ascontiguousarray(np.concatenate([ctx[b], x[b]], 0).T)
        d["cT"] = np.ascontiguousarray(np.stack([fm(c[b]), fm(cc)], 2))
        per.append(d)
    return sh, per


def rope_tables(hd):
    d = hd // 2
    half = d // 2
    t = np.arange(NLAT)
    row = (t // 64).astype(np.float32)
    col = (t % 64).astype(np.float32)
    inv = (10000.0 ** (-np.arange(half, dtype=np.float32) / half)).astype(np.float32)
    cos = np.zeros((hd, NLAT), np.float32)
    sin = np.zeros((hd, NLAT), np.float32)
    R = np.zeros((hd, hd), np.float32)
    for p in range(hd):
        pos = row if p < d else col
        i = (p % d) % half
        ang = pos * inv[i]
        cos[p] = np.cos(ang)
        sin[p] = np.sin(ang)
        if (p % d) < half:
            R[p, p + half] = -1.0
        else:
            R[p, p - half] = 1.0
    return cos, sin, np.ascontiguousarray(R.T)


def host_prep_even(inp, sh):
    f32 = np.float32
    sh["qk_gain"] = np.stack([np.stack([inp["attn_q_norm"][e], inp["attn_k_norm"][e]], 1) for e in range(2)]).astype(f32)
    cos, sin, RT = rope_tables(128)
    sh["c_cos128"], sh["c_sin128"], sh["c_RT128"] = cos, sin, RT
    sh["c_onesH"] = np.full((128, 128), 1.0 / 128, f32)
    sh["c_ones512"] = np.full((128, 128), 1.0 / 512, f32)
    sh["dt_ba"] = np.stack([np.stack([inp["ssm_dt_bias"][e].reshape(32), inp["ssm_a_log"][e].reshape(32)], 1) for e in range(2)]).astype(f32)
    sh["ssm_conv_w"] = np.stack([np.stack([fm(inp["ssm_conv_w"][e][k]) for k in range(5)], 2) for e in range(2)]).astype(f32)
    sh["ssm_conv_b"] = np.stack([fm(inp["ssm_conv_b"][e]) for e in range(2)]).astype(f32)
    s_ = np.arange(128)[:, None]
    l_ = np.arange(128)[None, :]
    sh["c_tri"] = np.ascontiguousarray(np.stack([s_ <= l_, s_ > l_, s_ >= l_, s_ < l_], 1).astype(f32))
    sh["ssm_d"] = np.stack([fm(np.repeat(inp["ssm_d"][e], 64)) for e in range(2)]).astype(f32)
    sh["ssm_norm"] = np.stack([fm(inp["ssm_norm"][e]) for e in range(2)]).astype(f32)


def host_prep_odd(inp, sh):
    f32 = np.float32
    sh["mla_q_b"] = np.asarray(inp["mla_q_b"], f32)
    sh["mla_kv_b"] = np.asarray(inp["mla_kv_b"], f32)
    sh["mla_gain"] = np.stack([np.concatenate([fm(inp["mla_q_a_norm"][o]), fm(inp["mla_kv_a_norm"][o])], 1) for o in range(2)]).astype(f32)
    cos, sin, RT = rope_tables(64)
    sh["c_cos64"], sh["c_sin64"], sh["c_RT64"] = cos, sin, RT
    sh["c_ones256"] = np.full((128, 128), 1.0 / 256, f32)
    def fm_ud(v):
        out = np.zeros((128, 30), f32)
        v = np.asarray(v, f32)
        for i in range(24):
            out[:, i] = v[i * 128:(i + 1) * 128]
        for i in range(4):
            out[:96, 24 + i] = v[3072 + i * 96:3072 + (i + 1) * 96]
        out[:, 28] = v[3456:3584]
        out[:, 29] = v[3584:3712]
        return out
    sh["rw_mu"] = np.stack([fm_ud(inp["rwkv_mu"][o]) for o in range(2)])
    sh["rw_vec"] = np.stack([np.stack([fm(inp["rwkv_k_k"][o]), fm(inp["rwkv_k_a"][o]), fm(inp["rwkv_r_k"][o].reshape(-1)),
                                       fm(inp["rwkv_w0"][o][0]), fm(inp["rwkv_w0"][o][1]), fm(inp["rwkv_a0"][o][0]), fm(inp["rwkv_a0"][o][1])], 1)
                             for o in range(2)]).astype(f32)
    sh["rwkv_g2"] = np.asarray(inp["rwkv_g2"], f32)
    sh["rwkv_w2"] = np.asarray(inp["rwkv_w2"], f32)
    sh["rwkv_a2"] = np.asarray(inp["rwkv_a2"], f32)
    sh["rw_ln"] = np.stack([np.stack([inp["rwkv_ln_w"][o], inp["rwkv_ln_b"][o]]) for o in range(2)]).astype(f32)
    blk = np.zeros((128, 128), f32); blk[:64, :64] = 1; blk[64:, 64:] = 1
    sh["c_blk64"] = blk
    s_ = np.arange(64)[:, None]; t_ = np.arange(64)[None, :]
    sh["c_m64"] = np.ascontiguousarray(np.stack([s_ < t_, s_ <= t_, s_ > t_, s_ >= t_], 1).astype(f32))


def declare_inputs(Kk, sh, per0):
    for k, v in list(sh.items()) + list(per0.items()):
        Kk.din(k, v.shape)


def run(nc, sh, per, trace=False):
    in_maps = []
    for d in per:
        m = dict(sh)
        m.update(d)
        in_maps.append(m)
    return run_bass_kernel_spmd(nc, in_maps, core_ids=list(range(len(per))), trace=trace)

def build_program(sh, per0):
    nc = bass.Bass("TRN2", target_bir_lowering=False)
    with ExitStack() as es:
        Kk = K(nc, es)
        declare_inputs(Kk, sh, per0)
        P = Kk.P
        Se = dict(qT=Kk.dscr("qT", [8, 128, NT]), kT=Kk.dscr("kT", [2, 128, NT]), vM=Kk.dscr("vMe", [NT, 256]),
                  zT=Kk.dscr("zT", [1024, NT]), xbcT=Kk.dscr("xbcT", [1536, NT]), dtT=Kk.dscr("dtT", [32, NT]),
                  xcT=Kk.dscr("xcT", [1536, NT]), ysT=Kk.dscr("ysT", [1024, NT]))
        So = dict(qnT=Kk.dscr("qnT", [8, 128, NT]), qpT=Kk.dscr("qpT", [8, 64, NT]), knT=Kk.dscr("knT", [8, 128, NT]),
                  kpT=Kk.dscr("kpT", [64, NT]), vM=Kk.dscr("vMo", [NT, 1024]), udT=Kk.dscr("udT", [3712, NT]),
                  RW=Kk.dscr("RW", [2, 4, 1024, NT]), kbTM=Kk.dscr("kbTM", [2, NT, 1024]), bbTM=Kk.dscr("bbTM", [2, NT, 1024]),
                  vTM=Kk.dscr("vTM", [NT, 1024]), wC=Kk.dscr("wC", [2, 1024, NCH64]), gateT=Kk.dscr("gateT", [1024, NT]),
                  bonT=Kk.dscr("bonT", [1024, NT]), YfTM=Kk.dscr("YfTM", [NT, 1024]))
        yT = Kk.dscr("yT", [2048, NT])
        xA = Kk.dscr("xA", [2048, NT])
        xB = Kk.dscr("xB", [2048, NT])
        outT = Kk.dscr("outT", [2048, NLAT], out=True)
        load_consts(Kk)
        with ExitStack() as ses:
            P.stage_es = ses
            cp = P.sb("cp", [128, 2, 16, 512])
            for i, (t0_, n, isctx) in enumerate(tiles_main()):
                P.dma(cp[:, i % 2, :, 0:n], Kk.ins["xT0"][:, t0_:t0_ + n].rearrange("(c p) t -> p c t", p=128), writes=[(cp, i % 2)])
                P.dma(xA[:, t0_:t0_ + n].rearrange("(c p) t -> p c t", p=128), cp[:, i % 2, :, 0:n], reads=[(cp, i % 2)], qn="pool")
            P.barrier()
            P.stage_es = None
        cur, oth = xA, xB
        for l in range(4):
            last = (l == 3)
            stage_adaln(Kk, l)
            if l % 2 == 0:
                stage_A_even(Kk, l, cur, Se)
                stage_attn_even(Kk, Se, yT, not last)
                stage_ssd_conv(Kk, l, Se)
                stage_ssd(Kk, l, Se, yT, not last)
                stage_C1(Kk, l, cur, yT, Kk.ins["ev_w_out"][l // 2], not last)
            else:
                stage_A_odd(Kk, l, cur, So)
                stage_attn_odd(Kk, So, yT, not last)
                stage_rwkv_prep(Kk, l, So)
                stage_rwkv(Kk, l, So, yT, not last)
                stage_C1(Kk, l, cur, yT, Kk.ins["od_w_out"][l // 2], not last)
            if last:
                stage_C2(Kk, l, cur, oth, False, final_norm=Kk.ins["final_norm"], outT=outT)
            else:
                stage_C2(Kk, l, cur, oth, True)
            cur, oth = oth, cur
        P.emit_all()
    return nc


def kernel(**inputs):
    sh, per = host_prep(inputs, ncores=8)
    nc = build_program(sh, per[0])
    res = run(nc, sh, per, trace=False)
    out = np.stack([np.ascontiguousarray(res.results[b]["outT"].T) for b in range(4)], 0)
    return out.astype(np.float32)
```

```python
from concourse.bass_utils import run_bass_kernel_spmd
import numpy as np
import concourse.bass as bass
import concourse.mybir as mybir
from contextlib import ExitStack

F32 = mybir.dt.float32
BF16 = mybir.dt.bfloat16
AF = mybir.ActivationFunctionType
ALU = mybir.AluOpType
AX = mybir.AxisListType

ENGS = ("pe", "act", "dve", "pool", "sp")


class Buf:
    _n = 0

    def __init__(self, t, name):
        self.t = t
        self.name = name
        self.st = {}
        Buf._n += 1

    def __getitem__(self, idx):
        return self.t[idx]


class Prog:
    def __init__(self, nc, es):
        self.nc = nc
        self.es = es
        self.q = {e: [] for e in ENGS}
        self.cnt = {e: 0 for e in ENGS if e != "sp"}
        self.esem = {e: es.enter_context(nc.semaphore("s_" + e)) for e in ENGS if e != "sp"}
        self.NS = 12
        self.dsem = {qn: [es.enter_context(nc.semaphore("d_%s%d" % (qn, i))) for i in range(self.NS)]
                     for qn in ("sp", "pool")}
        self.dcnt = {"sp": 0, "pool": 0}
        self.dval = {qn: [0] * self.NS for qn in ("sp", "pool")}
        self.bar = es.enter_context(nc.semaphore("bar"))
        self.nbar = 0
        self.known = {e: {} for e in ENGS}
        self.semobj = {}
        for e in self.esem:
            self.semobj[("c", e)] = self.esem[e]
        for qn in self.dsem:
            for i, s in enumerate(self.dsem[qn]):
                self.semobj[("d", qn, i)] = s
        self.bufs = []
        self.n_ins = 0
        self.stage_es = None

    def sb(self, name, shape, dt=F32):
        self.n_sb = getattr(self, "n_sb", 0) + 1
        name = "%s_%d" % (name, self.n_sb)
        t = (self.stage_es or self.es).enter_context(self.nc.sbuf_tensor(name, list(shape), dt))
        b = Buf(t, name)
        self.bufs.append(b)
        return b

    def ps(self, name, shape, dt=F32):
        t = self.es.enter_context(self.nc.psum_tensor(name, list(shape), dt))
        b = Buf(t, name)
        self.bufs.append(b)
        return b

    def _need(self, eng, tok):
        sk, val = tok
        if self.known[eng].get(sk, 0) >= val:
            return None
        self.known[eng][sk] = val
        return (sk, val)

    def _deps(self, eng, reads, writes):
        waits = {}

        def add(tok):
            if eng == "pe" and tok[0] == ("c", "pe"):
                return
            r = self._need(eng, tok)
            if r is not None:
                waits[r[0]] = max(waits.get(r[0], 0), r[1])

        def keys(b, k):
            if k is None:
                return list(b.st.keys())
            return [k, None]

        for (b, k) in reads:
            for kk in keys(b, k):
                st = b.st.get(kk)
                if st:
                    for tok in st[0]:
                        add(tok)
        for (b, k) in writes:
            for kk in keys(b, k):
                st = b.st.get(kk)
                if st:
                    for tok in st[0]:
                        add(tok)
                    for tok in st[1]:
                        add(tok)
        return list(waits.items())

    def _commit(self, tok, reads, writes):
        for (b, k) in writes:
            if k is None:
                b.st = {None: ([tok], [])}
            else:
                b.st[k] = ([tok], [])
        for (b, k) in reads:
            st = b.st.setdefault(k, ([], []))
            rl = st[1]
            rl[:] = [t for t in rl if t[0] != tok[0]]
            rl.append(tok)

    @staticmethod
    def _norm(lst):
        out = []
        for x in lst or []:
            if isinstance(x, Buf):
                out.append((x, None))
            else:
                out.append(x)
        return out

    def op(self, eng, fn, reads=None, writes=None):
        reads = self._norm(reads)
        writes = self._norm(writes)
        waits = self._deps(eng, reads, writes)
        self.cnt[eng] += 1
        tok = (("c", eng), self.cnt[eng])
        sem = self.esem[eng]
        semobj = self.semobj

        def emit(e, waits=waits, fn=fn, sem=sem):
            for sk, v in waits:
                e.wait_ge(semobj[sk], v)
            fn(e).then_inc(sem, 1)

        self.q[eng].append(emit)
        self._commit(tok, reads, writes)
        self.n_ins += 1
        return tok

    def dma(self, out_ap, in_ap, reads=None, writes=None, qn="sp"):
        reads = self._norm(reads)
        writes = self._norm(writes)
        waits = self._deps(qn, reads, writes)
        i = self.dcnt[qn] % self.NS
        self.dcnt[qn] += 1
        prev = self.dval[qn][i]
        sk = ("d", qn, i)
        r = self._need(qn, (sk, prev)) if prev > 0 else None
        if r is not None:
            waits.append(r)
        self.dval[qn][i] = prev + 16
        tok = (sk, prev + 16)
        sem = self.semobj[sk]
        semobj = self.semobj

        def emit(e, waits=waits, sem=sem, out_ap=out_ap, in_ap=in_ap):
            for k, v in waits:
                e.wait_ge(semobj[k], v)
            e.dma_start(out=out_ap, in_=in_ap).then_inc(sem, 16)

        self.q[qn].append(emit)
        self._commit(tok, reads, writes)
        self.n_ins += 1
        return tok

    def barrier(self):
        self.nbar += 1
        nb = self.nbar
        semobj = self.semobj
        bar = self.bar
        for e in ENGS:
            waits = []
            if e in self.cnt:
                if self.cnt[e] > 0:
                    waits.append((("c", e), self.cnt[e]))
            if e in self.dsem:
                for i in range(self.NS):
                    if self.dval[e][i] > 0:
                        waits.append((("d", e, i), self.dval[e][i]))

            def emit(en, waits=waits, nb=nb):
                for k, v in waits:
                    en.wait_ge(semobj[k], v)
                en.sem_inc(bar, 1)
                en.wait_ge(bar, len(ENGS) * nb)

            self.q[e].append(emit)
        allk = {}
        for e in self.cnt:
            allk[("c", e)] = self.cnt[e]
        for qn in self.dsem:
            for i in range(self.NS):
                allk[("d", qn, i)] = self.dval[qn][i]
        for e in ENGS:
            self.known[e] = dict(allk)
        for b in self.bufs:
            b.st = {}

    def emit_all(self):
        nc = self.nc
        q = self.q
        with nc.Block() as block:
            @block.sync
            def _(e):
                for f in q["sp"]:
                    f(e)

            @block.tensor
            def _(e):
                for f in q["pe"]:
                    f(e)

            @block.scalar
            def _(e):
                for f in q["act"]:
                    f(e)

            @block.vector
            def _(e):
                for f in q["dve"]:
                    f(e)

            @block.gpsimd
            def _(e):
                for f in q["pool"]:
                    f(e)

import numpy as np

D = 2048
NKD = 16
NCTX = 256
NLAT = 4096
NT = NCTX + NLAT
DFF = 5632
EVEN_IN = 4128
ODD_IN = 4544
EPS = 1e-6


def tiles_main():
    t = [(0, NCTX, True)]
    for i in range(NLAT // 512):
        t.append((NCTX + 512 * i, 512, False))
    return t


def tiles_ffn():
    t = [(0, NCTX, 0, 0)]
    nt = 9
    base, rem = divmod(NLAT, nt)
    o = NCTX
    for i in range(nt):
        n = base + (1 if i < rem else 0)
        t.append((o, n, 0 if i == 0 else 1, 0 if i == nt - 1 else 1))
        o += n
    return t


class K:
    def __init__(self, nc, es):
        self.nc = nc
        self.P = Prog(nc, es)
        P = self.P
        self.ins = {}
        self.psb = [P.ps("ps%d" % i, [128, 512]) for i in range(8)]
        self.wb = []
        self.wbi = 0
        self.wbh = []
        self.wbhi = 0
        self.ones = P.sb("ones", [128, 128])
        self.ident = P.sb("ident", [128, 128])
        self.consts_loaded = False

    def din(self, name, shape):
        t = self.nc.dram_tensor(name, list(shape), F32, kind="ExternalInput").ap()
        self.ins[name] = t
        return t

    def dscr(self, name, shape, out=False):
        return self.nc.dram_tensor(name, list(shape), F32,
                                   kind="ExternalOutput" if out else "Internal").ap()


def alloc_wb(Kk, n_stage, n_bf=3):
    P = Kk.P
    Kk.wb = [P.sb("wb%d" % i, [128, 16, 256]) for i in range(n_stage)]
    Kk.wbh = [P.sb("wbh%d" % i, [128, 16, 256], BF16) for i in range(n_bf)] if n_bf else []


def _cast_w(Kk, wb, nfull, bw):
    P = Kk.P
    wh = Kk.wbh[Kk.wbhi % len(Kk.wbh)]
    eng = "act"
    Kk.wbhi += 1
    if eng == "pool":
        P.op("pool", lambda e: e.tensor_copy(wh[:, 0:nfull, 0:bw], wb[:, 0:nfull, 0:bw]), reads=[wb], writes=[wh])
    else:
        P.op("act", lambda e: e.activation(wh[:, 0:nfull, 0:bw], wb[:, 0:nfull, 0:bw], AF.Identity), reads=[wb], writes=[wh])
    return wh


def gemm_fm(Kk, W, krows, chunks, rhs_fn, ntok, evac, ps_ids=(0, 1), rhs_reads=(), lowp=False):
    P = Kk.P
    nk = (krows + 127) // 128
    nfull = krows // 128
    blocks = []
    cur = []
    for i, (c0, cw) in enumerate(chunks):
        if cur and (cur[0][1] + sum(c[2] for c in cur) == c0) and (sum(c[2] for c in cur) + cw <= 256):
            cur.append((i, c0, cw))
        else:
            if cur:
                blocks.append(cur)
            cur = [(i, c0, cw)]
    if cur:
        blocks.append(cur)
    pi = 0
    for blk in blocks:
        b0 = blk[0][1]
        bw = sum(c[2] for c in blk)
        wb = Kk.wb[Kk.wbi % len(Kk.wb)]
        Kk.wbi += 1
        if nfull > 0:
            src = W[0:nfull * 128, b0:b0 + bw].rearrange("(kc p) c -> p kc c", p=128)
            P.dma(wb[:, 0:nfull, 0:bw], src, writes=[wb])
        if nfull < nk:
            kp = krows - nfull * 128
            P.dma(wb[0:kp, nfull, 0:bw], W[nfull * 128:krows, b0:b0 + bw], writes=[wb])
        if lowp:
            assert nfull == nk
            wb = _cast_w(Kk, wb, nfull, bw)
        for (i, c0, cw) in blk:
            ps = Kk.psb[ps_ids[pi % len(ps_ids)]]
            pi += 1
            off = c0 - b0
            for kc in range(nk):
                kp = min(128, krows - kc * 128)
                rhs = rhs_fn(kc, kp)
                P.op("pe", lambda e, ps=ps, wb=wb, kc=kc, kp=kp, off=off, cw=cw, rhs=rhs:
                     e.matmul(ps[0:cw, 0:ntok], wb[0:kp, kc, off:off + cw], rhs,
                              start=(kc == 0), stop=(kc == nk - 1)),
                     reads=[wb] + list(rhs_reads), writes=[ps])
            evac(i, ps, cw)


def gemm_tm(Kk, W, krows, c0, ncols, lhs_fn, ntok, evac, ps_ids=(0, 1), lhs_reads=(), lowp=False):
    P = Kk.P
    nk = (krows + 127) // 128
    nfull = krows // 128
    pi = 0
    for cb0 in range(0, ncols, 256):
        cbw = min(256, ncols - cb0)
        wb = Kk.wb[Kk.wbi % len(Kk.wb)]
        Kk.wbi += 1
        if nfull > 0:
            src = W[0:nfull * 128, c0 + cb0:c0 + cb0 + cbw].rearrange("(kc p) c -> p kc c", p=128)
            P.dma(wb[:, 0:nfull, 0:cbw], src, writes=[wb])
        if nfull < nk:
            kp = krows - nfull * 128
            P.dma(wb[0:kp, nfull, 0:cbw], W[nfull * 128:krows, c0 + cb0:c0 + cb0 + cbw], writes=[wb])
        if lowp:
            assert nfull == nk
            wb = _cast_w(Kk, wb, nfull, cbw)
        for tb in range((ntok + 127) // 128):
            t0 = tb * 128
            tn = min(128, ntok - t0)
            ps = Kk.psb[ps_ids[pi % len(ps_ids)]]
            pi += 1
            for kc in range(nk):
                kp = min(128, krows - kc * 128)
                lhs = lhs_fn(kc, kp, t0, tn)
                P.op("pe", lambda e, ps=ps, wb=wb, kc=kc, kp=kp, t0=t0, tn=tn, cbw=cbw, lhs=lhs:
                     e.matmul(ps[0:tn, 0:cbw], lhs, wb[0:kp, kc, 0:cbw],
                              start=(kc == 0), stop=(kc == nk - 1)),
                     reads=[wb] + list(lhs_reads), writes=[ps])
            evac(tb, ps, tn, cb0, cbw)


def colsum_bcast(Kk, src_fn, nchunks, ntok, ps, sq, scale_ones, src_reads, kp_fn=None):
    P = Kk.P
    for c in range(nchunks):
        kp = 128 if kp_fn is None else kp_fn(c)
        s = c % 4
        src = src_fn(c, kp)
        P.op("act", lambda e, c=c, s=s, kp=kp, src=src: e.activation(sq[0:kp, s, 0:ntok], src, AF.Square),
             reads=list(src_reads), writes=[(sq, s)])
        P.op("pe", lambda e, c=c, s=s, kp=kp: e.matmul(ps[:, 0:ntok], scale_ones[0:kp, :], sq[0:kp, s, 0:ntok],
                                                        start=(c == 0), stop=(c == nchunks - 1)),
             reads=[(sq, s), scale_ones], writes=[ps])


def rstd_from(Kk, ps, ntok, rstd, eps):
    P = Kk.P
    P.op("act", lambda e: e.activation(rstd[:, 0:ntok], ps[:, 0:ntok], AF.Sqrt, bias=Kk.epsb[:, 0:1] if eps == EPS else Kk.eps2b[:, 0:1], scale=1.0),
         reads=[ps, Kk.epsb], writes=[rstd])
    P.op("dve", lambda e: e.reciprocal(rstd[:, 0:ntok], rstd[:, 0:ntok]), reads=[rstd], writes=[rstd])


def load_consts(Kk):
    P = Kk.P
    P.dma(Kk.ones[:, :], Kk.ins["c_ones"][:, :], writes=[Kk.ones])
    P.dma(Kk.ident[:, :], Kk.ins["c_ident"][:, :], writes=[Kk.ident])
    Kk.onesD = P.sb("onesD", [128, 128])
    P.dma(Kk.onesD[:, :], Kk.ins["c_onesD"][:, :], writes=[Kk.onesD])
    Kk.epsb = P.sb("epsb", [128, 4])
    P.dma(Kk.epsb[:, :], Kk.ins["c_eps"][:, :], writes=[Kk.epsb])
    Kk.modT = P.sb("modT", [128, 96, 2])
    Kk.Amix = P.sb("Amix", [128, 16, 2])
    Kk.Affn = P.sb("Affn", [128, 16, 2])
    Kk.actT = P.sb("actT", [128, 16, 2])
    P.dma(Kk.actT[:, :, :], Kk.ins["cT"][:, :, :], writes=[Kk.actT])
    P.op("act", lambda e: e.activation(Kk.actT[:, :, :], Kk.actT[:, :, :], AF.Silu), reads=[Kk.actT], writes=[Kk.actT])


def stage_adaln(Kk, l):
    P = Kk.P
    with ExitStack() as ses:
        P.stage_es = ses
        alloc_wb(Kk, 4, 0)
        adab = P.sb("adab", [128, 96])
        nrm = P.sb("nrm", [128, 2, 16, 2])
        P.dma(adab[:, :], Kk.ins["ada_b"][l], writes=[adab])
        P.dma(nrm[:, 0], Kk.ins["norm_mix"][l], writes=[nrm])
        P.dma(nrm[:, 1], Kk.ins["norm_ffn"][l], writes=[nrm])
        W = Kk.ins["ada_w"][l]
        chunks = [(i * 128, 128) for i in range(96)]

        def evac(i, ps, cw):
            P.op("act", lambda e, i=i, ps=ps: e.activation(Kk.modT[:, i, :], ps[:, 0:2], AF.Identity,
                                                            bias=adab[:, i:i + 1], scale=1.0),
                 reads=[ps, adab], writes=[(Kk.modT, i)])

        gemm_fm(Kk, W, D, chunks, lambda kc, kp: Kk.actT[0:kp, kc, :], 2, evac, rhs_reads=[Kk.actT])
        allm = [(Kk.modT, i) for i in range(96)]
        P.op("dve", lambda e: e.scalar_tensor_tensor(Kk.Amix[:, :, :], Kk.modT[:, 16:32, :], 1.0, nrm[:, 0], ALU.add, ALU.mult),
             reads=allm + [nrm], writes=[Kk.Amix])
        P.op("dve", lambda e: e.scalar_tensor_tensor(Kk.Affn[:, :, :], Kk.modT[:, 64:80, :], 1.0, nrm[:, 1], ALU.add, ALU.mult),
             reads=allm + [nrm], writes=[Kk.Affn])
        P.barrier()
        P.stage_es = None


def modulate(Kk, xt, ht, ntok, A, Bidx, m, sq, rstd, ps, tmpf=None):
    P = Kk.P
    colsum_bcast(Kk, lambda c, kp: xt[:, c, 0:ntok], 16, ntok, ps, sq, Kk.onesD, [xt])
    rstd_from(Kk, ps, ntok, rstd, EPS)
    allm = [(Kk.modT, i) for i in range(96)]
    for c in range(16):
        if tmpf is None:
            P.op("dve", lambda e, c=c: e.scalar_tensor_tensor(ht[:, c, 0:ntok], xt[:, c, 0:ntok], A[:, c, m:m + 1],
                                                               rstd[:, 0:ntok], ALU.mult, ALU.mult),
                 reads=[xt, A, rstd], writes=[(ht, c)])
            P.op("pool", lambda e, c=c: e.tensor_scalar(ht[:, c, 0:ntok], ht[:, c, 0:ntok], Kk.modT[:, Bidx + c, m:m + 1], None, ALU.add),
                 reads=[(ht, c), (Kk.modT, Bidx + c)], writes=[(ht, c)])
        else:
            s_ = c % 2
            P.op("dve", lambda e, c=c, s_=s_: e.scalar_tensor_tensor(tmpf[:, s_, 0:ntok], xt[:, c, 0:ntok], A[:, c, m:m + 1],
                                                                      rstd[:, 0:ntok], ALU.mult, ALU.mult),
                 reads=[xt, A, rstd], writes=[(tmpf, s_)])
            P.op("pool", lambda e, c=c, s_=s_: e.tensor_scalar(ht[:, c, 0:ntok], tmpf[:, s_, 0:ntok], Kk.modT[:, Bidx + c, m:m + 1], None, ALU.add),
                 reads=[(tmpf, s_), (Kk.modT, Bidx + c)], writes=[(ht, c)])


def stage_A(Kk, l, xT, W, plan, nsteps_extra=None):
    P = Kk.P
    with ExitStack() as ses:
        P.stage_es = ses
        alloc_wb(Kk, 3, 3)
        xt = P.sb("xt", [128, 16, 512])
        ht = P.sb("ht", [128, 16, 512], BF16)
        htf = P.sb("htf", [128, 2, 512])
        sq = P.sb("sq", [128, 4, 512])
        rstd = P.sb("rstd", [128, 512])
        Kk.ev = P.sb("ev", [128, 4, 512])
        Kk.evi = 0
        Kk.stA = dict(xt=xt, ht=ht, sq=sq, rstd=rstd)
        if nsteps_extra:
            nsteps_extra("alloc")
        for (t0, n, isctx) in tiles_main():
            m = 1 if isctx else 0
            P.dma(xt[:, :, 0:n], xT[:, t0:t0 + n].rearrange("(c p) t -> p c t", p=128), writes=[xt])
            modulate(Kk, xt, ht, n, Kk.Amix, 0, m, sq, rstd, Kk.psb[7], tmpf=htf)
            hreads = [(ht, c) for c in range(16)]
            for g in plan:
                if g[0] == "fm":
                    gemm_fm(Kk, W, D, g[1], lambda kc, kp: ht[0:kp, kc, 0:n], n, g[2](t0, n, isctx), rhs_reads=hreads, lowp=True)
                elif g[0] == "tm":
                    gemm_tm(Kk, W, D, g[1], g[2], lambda kc, kp, a, tn: ht[0:kp, kc, a:a + tn], n, g[3](t0, n, isctx),
                            lhs_reads=hreads, lowp=True)
                else:
                    g[1](t0, n, isctx)
        P.barrier()
        P.stage_es = None


def ev_store(Kk, ps, rows, n, dst_ap, func=None, eng="act"):
    P = Kk.P
    s = Kk.evi % 4
    Kk.evi += 1
    ev = Kk.ev
    if eng == "act":
        P.op("act", lambda e: e.activation(ev[0:rows, s, 0:n], ps[0:rows, 0:n], func or AF.Identity),
             reads=[ps], writes=[(ev, s)])
    else:
        P.op("dve", lambda e: e.tensor_copy(ev[0:rows, s, 0:n], ps[0:rows, 0:n]), reads=[ps], writes=[(ev, s)])
    P.dma(dst_ap, ev[0:rows, s, 0:n], reads=[(ev, s)], qn="pool")


def stage_C1(Kk, l, xT, yT, Wout, do_ctx):
    P = Kk.P
    with ExitStack() as ses:
        P.stage_es = ses
        alloc_wb(Kk, 4, 3)
        xt = P.sb("xt", [128, 16, 512])
        yt = P.sb("yt", [128, 16, 512])
        ytb = P.sb("ytb", [128, 16, 512], BF16)
        for (t0, n, isctx) in tiles_main():
            if isctx and not do_ctx:
                continue
            m = 1 if isctx else 0
            P.dma(xt[:, :, 0:n], xT[:, t0:t0 + n].rearrange("(c p) t -> p c t", p=128), writes=[(xt, c) for c in range(16)])
            P.dma(yt[:, :, 0:n], yT[:, t0:t0 + n].rearrange("(c p) t -> p c t", p=128), writes=[yt])
            for c4 in range(4):
                if c4 % 2 == 0:
                    P.op("pool", lambda e, c4=c4, n=n: e.tensor_copy(ytb[:, c4 * 4:(c4 + 1) * 4, 0:n], yt[:, c4 * 4:(c4 + 1) * 4, 0:n]),
                         reads=[yt], writes=[(ytb, c4)])
                else:
                    P.op("act", lambda e, c4=c4, n=n: e.activation(ytb[:, c4 * 4:(c4 + 1) * 4, 0:n], yt[:, c4 * 4:(c4 + 1) * 4, 0:n], AF.Identity),
                         reads=[yt], writes=[(ytb, c4)])

            def evac(i, ps, cw, n=n, m=m):
                P.op("dve", lambda e: e.scalar_tensor_tensor(xt[:, i, 0:n], ps[:, 0:n], Kk.modT[:, 32 + i, m:m + 1],
                                                              xt[:, i, 0:n], ALU.mult, ALU.add),
                     reads=[ps, (xt, i), (Kk.modT, 32 + i)], writes=[(xt, i)])

            gemm_fm(Kk, Wout, D, [(i * 128, 128) for i in range(16)], lambda kc, kp: ytb[0:kp, kc, 0:n], n, evac,
                    rhs_reads=[(ytb, c4) for c4 in range(4)], lowp=True)
            P.dma(xT[:, t0:t0 + n].rearrange("(c p) t -> p c t", p=128), xt[:, :, 0:n],
                  reads=[(xt, c) for c in range(16)], qn="pool")
        P.barrier()
        P.stage_es = None


def stage_C2(Kk, l, xTa, xTb, do_ctx, final_norm=None, outT=None):
    P = Kk.P
    Wup = Kk.ins["ffn_w_up"][l]
    Wdn = Kk.ins["ffn_w_down"][l]
    with ExitStack() as ses:
        P.stage_es = ses
        alloc_wb(Kk, 3 if final_norm is not None else 5, 3)
        xt = P.sb("xt", [128, 16, 512])
        ht = P.sb("ht", [128, 16, 512], BF16)
        htf = P.sb("htf", [128, 2, 512])
        gt = P.sb("gt", [128, 11, 512], BF16)
        sq = P.sb("sq", [128, 4, 512])
        rstd = P.sb("rstd", [128, 512])
        acc = P.sb("acc", [128, 2, 2, 512])
        cw_ = P.sb("convw", [128, 3, 88])
        cb_ = P.sb("convb", [128, 88])
        P.dma(cw_[:, :, :], Kk.ins["ffn_conv_w"][l], writes=[cw_])
        P.dma(cb_[:, :], Kk.ins["ffn_conv_b"][l], writes=[cb_])
        if final_norm is not None:
            ot_ = P.sb("otf", [128, 16, 512])
            fnw = P.sb("fnw", [128, 16])
            P.dma(fnw[:, :], final_norm, writes=[fnw])
        for (o0, n, lh, rh) in tiles_ffn():
            isctx = o0 < NCTX
            if isctx and not do_ctx:
                continue
            m = 1 if isctx else 0
            nn = n + lh + rh
            a0 = o0 - lh
            xk = [(xt, c) for c in range(16)]
            P.dma(xt[:, :, 0:nn], xTa[:, a0:a0 + nn].rearrange("(c p) t -> p c t", p=128), writes=xk)
            modulate(Kk, xt, ht, nn, Kk.Affn, 48, m, sq, rstd, Kk.psb[7], tmpf=htf)
            hreads = [(ht, c) for c in range(16)]
            for grp in range(4):
                chunks = []
                for j in range(11):
                    chunks.append(((grp * 11 + j) * 128, 128))
                    chunks.append((DFF + (grp * 11 + j) * 128, 128))

                def evac(i, ps, cw, grp=grp, n=n, lh=lh, rh=rh):
                    j = i // 2
                    isval = i % 2
                    fc = (grp * 11 + j) + (44 if isval else 0)
                    s = j % 2
                    a = acc[:, isval, s, :]
                    key = (acc, (isval, s))
                    P.op("act", lambda e: e.activation(a[:, 0:n], ps[:, lh:lh + n], AF.Identity,
                                                        bias=cb_[:, fc:fc + 1], scale=cw_[:, 1, fc:fc + 1]),
                         reads=[ps, cw_, cb_], writes=[key])
                    lo = 0 if lh else 1
                    P.op("dve", lambda e: e.scalar_tensor_tensor(a[:, lo:n], ps[:, lh - 1 + lo:lh - 1 + n], cw_[:, 0, fc:fc + 1],
                                                                  a[:, lo:n], ALU.mult, ALU.add),
                         reads=[ps, cw_, key], writes=[key])
                    hi = 0 if rh else 1
                    P.op("dve", lambda e: e.scalar_tensor_tensor(a[:, 0:n - hi], ps[:, lh + 1:lh + 1 + n - hi], cw_[:, 2, fc:fc + 1],
                                                                  a[:, 0:n - hi], ALU.mult, ALU.add),
                         reads=[ps, cw_, key], writes=[key])
                    if isval:
                        kg = (acc, (0, s))
                        P.op("act", lambda e: e.activation(acc[:, 0, s, 0:n], acc[:, 0, s, 0:n], AF.Silu),
                             reads=[kg], writes=[kg])
                        P.op("pool", lambda e: e.tensor_tensor(gt[:, j, 0:n], acc[:, 0, s, 0:n], acc[:, 1, s, 0:n], ALU.mult),
                             reads=[kg, key], writes=[(gt, j)])

                gemm_fm(Kk, Wup, D, chunks, lambda kc, kp: ht[0:kp, kc, 0:nn], nn, evac, ps_ids=(0, 1, 2, 3), rhs_reads=hreads, lowp=True)
                Wd = Wdn[grp * 11 * 128:(grp + 1) * 11 * 128, :]

                def evac2(i, ps, cw, n=n, lh=lh, m=m):
                    P.op("dve", lambda e: e.scalar_tensor_tensor(xt[:, i, lh:lh + n], ps[:, 0:n], Kk.modT[:, 80 + i, m:m + 1],
                                                                  xt[:, i, lh:lh + n], ALU.mult, ALU.add),
                         reads=[ps, (xt, i), (Kk.modT, 80 + i)], writes=[(xt, i)])

                gemm_fm(Kk, Wd, 11 * 128, [(i * 128, 128) for i in range(16)], lambda kc, kp: gt[0:kp, kc, 0:n], n, evac2,
                        ps_ids=(4, 5), rhs_reads=[(gt, j) for j in range(11)], lowp=True)
            if final_norm is None:
                P.dma(xTb[:, o0:o0 + n].rearrange("(c p) t -> p c t", p=128), xt[:, :, lh:lh + n], reads=xk, qn="pool")
            else:
                if not isctx:
                    colsum_bcast(Kk, lambda c, kp: xt[:, c, lh:lh + n], 16, n, Kk.psb[7], sq, Kk.onesD, xk)
                    rstd_from(Kk, Kk.psb[7], n, rstd, EPS)
                    for c in range(16):
                        P.op("dve", lambda e, c=c, n=n, lh=lh: e.scalar_tensor_tensor(ot_[:, c, 0:n], xt[:, c, lh:lh + n], fnw[:, c:c + 1],
                                                                           rstd[:, 0:n], ALU.mult, ALU.mult),
                             reads=[(xt, c), fnw, rstd], writes=[(ot_, c)])
                    P.dma(outT[:, o0 - NCTX:o0 - NCTX + n].rearrange("(c p) t -> p c t", p=128), ot_[:, :, 0:n],
                          reads=[(ot_, c) for c in range(16)], qn="pool")
        P.barrier()
        P.stage_es = None


A_SCALE = 128 ** -0.5


def rope_apply(Kk, src, rows, n, cosb, sinb, RT, ps, dst, tmp):
    P = Kk.P
    sbuf, skey, sap = src
    dbuf, dkey, dap = dst
    tbuf, tkey, tap = tmp
    P.op("pe", lambda e: e.matmul(ps[0:rows, 0:n], RT[0:rows, 0:rows], sap, start=True, stop=True),
         reads=[(sbuf, skey), RT], writes=[ps])
    P.op("dve", lambda e: e.tensor_tensor(tap, ps[0:rows, 0:n], sinb[0:rows, 0:n], ALU.mult),
         reads=[ps, sinb], writes=[(tbuf, tkey)])
    P.op("pool", lambda e: e.tensor_tensor(dap, sap, cosb[0:rows, 0:n], ALU.mult),
         reads=[(sbuf, skey), cosb], writes=[(dbuf, dkey)])
    P.op("pool", lambda e: e.tensor_tensor(dap, dap, tap, ALU.add),
         reads=[(dbuf, dkey), (tbuf, tkey)], writes=[(dbuf, dkey)])


def stage_A_even(Kk, l, xT, S):
    P = Kk.P
    e_ = l // 2
    W = Kk.ins["ev_w_in"][e_]
    st = {}

    def extra(_):
        st["qs"] = P.sb("qs", [128, 2, 512])
        st["qn"] = P.sb("qn", [128, 2, 512])
        st["qo"] = P.sb("qo", [128, 2, 512])
        st["tmp"] = P.sb("tmpr", [128, 2, 512])
        st["cos"] = P.sb("cosb", [128, 512])
        st["sin"] = P.sb("sinb", [128, 512])
        st["gain"] = P.sb("qkgain", [128, 2])
        st["RT"] = P.sb("RT", [128, 128])
        st["onesH"] = P.sb("onesH", [128, 128])
        st["rs2"] = P.sb("rs2", [128, 2, 512])
        st["dtb"] = P.sb("dtb", [32, 2])
        P.dma(st["gain"][:, :], Kk.ins["qk_gain"][e_], writes=[st["gain"]])
        P.dma(st["RT"][:, :], Kk.ins["c_RT128"][:, :], writes=[st["RT"]])
        P.dma(st["onesH"][:, :], Kk.ins["c_onesH"][:, :], writes=[st["onesH"]])
        P.dma(st["dtb"][:, :], Kk.ins["dt_ba"][e_], writes=[st["dtb"]])
        st["i"] = 0

    def tile_pre(t0, n, isctx):
        if not isctx:
            P.dma(st["cos"][:, 0:n], Kk.ins["c_cos128"][:, t0 - NCTX:t0 - NCTX + n], writes=[st["cos"]])
            P.dma(st["sin"][:, 0:n], Kk.ins["c_sin128"][:, t0 - NCTX:t0 - NCTX + n], writes=[st["sin"]])

    def mk_qk(t0, n, isctx):
        def evac(i, ps, cw):
            s = st["i"] % 2
            st["i"] += 1
            qs, qn, qo, tmp, rs2 = st["qs"], st["qn"], st["qo"], st["tmp"], st["rs2"]
            g = 0 if i < 8 else 1
            ps2 = Kk.psb[4 + s]
            P.op("act", lambda e: e.activation(qs[:, s, 0:n], ps[:, 0:n], AF.Identity), reads=[ps], writes=[(qs, s)])
            P.op("act", lambda e: e.activation(qn[:, s, 0:n], ps[:, 0:n], AF.Square), reads=[ps], writes=[(qn, s)])
            P.op("pe", lambda e: e.matmul(ps2[:, 0:n], st["onesH"][:, :], qn[:, s, 0:n], start=True, stop=True),
                 reads=[(qn, s), st["onesH"]], writes=[ps2])
            P.op("act", lambda e: e.activation(rs2[:, s, 0:n], ps2[:, 0:n], AF.Sqrt, bias=Kk.epsb[:, 0:1], scale=1.0),
                 reads=[ps2, Kk.epsb], writes=[(rs2, s)])
            P.op("dve", lambda e: e.reciprocal(rs2[:, s, 0:n], rs2[:, s, 0:n]), reads=[(rs2, s)], writes=[(rs2, s)])
            P.op("dve", lambda e: e.scalar_tensor_tensor(qn[:, s, 0:n], qs[:, s, 0:n], st["gain"][:, g:g + 1], rs2[:, s, 0:n],
                                                          ALU.mult, ALU.mult),
                 reads=[(qs, s), st["gain"], (rs2, s)], writes=[(qn, s)])
            dst = S["qT"][i] if i < 8 else S["kT"][i - 8]
            if isctx:
                P.dma(dst[:, t0:t0 + n], qn[:, s, 0:n], reads=[(qn, s)], qn="pool")
            else:
                rope_apply(Kk, (qn, s, qn[:, s, 0:n]), 128, n, st["cos"], st["sin"], st["RT"], ps2,
                           (qo, s, qo[:, s, 0:n]), (tmp, s, tmp[:, s, 0:n]))
                P.dma(dst[:, t0:t0 + n], qo[:, s, 0:n], reads=[(qo, s)], qn="pool")
        return evac

    def mk_v(t0, n, isctx):
        def evac(tb, ps, tn, cb0, cbw):
            ev_store(Kk, ps, tn, cbw, S["vM"][t0 + tb * 128:t0 + tb * 128 + tn, cb0:cb0 + cbw])
        return evac

    def mk_z(t0, n, isctx):
        def evac(i, ps, cw):
            ev_store(Kk, ps, cw, n, S["zT"][i * 128:i * 128 + cw, t0:t0 + n], func=AF.Silu)
        return evac

    def mk_xbc(t0, n, isctx):
        def evac(i, ps, cw):
            ev_store(Kk, ps, cw, n, S["xbcT"][i * 128:i * 128 + cw, t0:t0 + n])
        return evac

    def mk_dt(t0, n, isctx):
        def evac(i, ps, cw):
            s = Kk.evi % 4
            Kk.evi += 1
            ev = Kk.ev
            P.op("act", lambda e: e.activation(ev[0:32, s, 0:n], ps[0:32, 0:n], AF.Exp, bias=st["dtb"][:, 0:1], scale=1.0),
                 reads=[ps, st["dtb"]], writes=[(ev, s)])
            P.op("act", lambda e: e.activation(ev[0:32, s, 0:n], ev[0:32, s, 0:n], AF.Ln, bias=Kk.epsb[0:32, 3:4], scale=1.0),
                 reads=[(ev, s), Kk.epsb], writes=[(ev, s)])
            P.dma(S["dtT"][0:32, t0:t0 + n], ev[0:32, s, 0:n], reads=[(ev, s)], qn="pool")
        return evac

    qk_chunks = [(i * 128, 128) for i in range(10)]
    z_chunks = [(1536 + i * 128, 128) for i in range(8)]
    xbc_chunks = [(2560 + i * 128, 128) for i in range(12)]
    dt_chunks = [(4096, 32)]
    plan = [("post", tile_pre), ("fm", qk_chunks, mk_qk), ("tm", 1280, 256, mk_v),
            ("fm", z_chunks, mk_z), ("fm", xbc_chunks, mk_xbc), ("fm", dt_chunks, mk_dt)]
    stage_A(Kk, l, xT, W, plan, nsteps_extra=extra)


def attention(Kk, nheads, kv_of, load_k, load_v, load_q, kparts, scale, yT, row0, do_ctx):
    P = Kk.P
    pt = Kk.att["pt"]
    ot = Kk.att["ot"]
    rd = Kk.att["rd"]
    cur_kv = None
    it = 0
    pti = 0
    for h in range(nheads):
        kvh = kv_of(h)
        if kvh != cur_kv:
            kbufs = load_k(kvh)
            vbuf = load_v(kvh)
            cur_kv = kvh
        for (t0, n, isctx) in tiles_main():
            if isctx and not do_ctx:
                continue
            pti = _attn_tile(Kk, h, t0, n, isctx, it, pti, kbufs, vbuf, load_q, kparts, scale, yT, row0)
            it += 1


def _attn_tile(Kk, h, t0, n, isctx, it, pti, kbufs, vbuf, load_q, kparts, scale, yT, row0):
    P = Kk.P
    pt = Kk.att["pt"]
    ot = Kk.att["ot"]
    rd = Kk.att["rd"]
    qaps, qreads = load_q(h, t0, n, it % 2)
    ps_o = Kk.psb[2 + it % 2]
    ps_d = Kk.psb[4 + it % 2]
    jt = list(range(2)) if isctx else list(range(NT // 128))
    for ji, j in enumerate(jt):
        ps_s = Kk.psb[pti % 2]
        sl = pti % 3
        pti += 1
        for pi_, rows in enumerate(kparts):
            kb = kbufs[pi_]
            P.op("pe", lambda e, ps_s=ps_s, kb=kb, rows=rows, j=j, qa=qaps[pi_], pi_=pi_:
                 e.matmul(ps_s[:, 0:n], kb[0:rows, j * 128:(j + 1) * 128], qa,
                          start=(pi_ == 0), stop=(pi_ == len(kparts) - 1)),
                 reads=[kb] + qreads, writes=[ps_s])
        P.op("act", lambda e, ps_s=ps_s, sl=sl: e.activation(pt[:, sl, 0:n], ps_s[:, 0:n], AF.Exp, scale=scale),
             reads=[ps_s], writes=[(pt, sl)])
        P.op("pe", lambda e, sl=sl, j=j, ji=ji: e.matmul(ps_o[:, 0:n], vbuf[:, j, :], pt[:, sl, 0:n],
                                                         start=(ji == 0), stop=(ji == len(jt) - 1)),
             reads=[vbuf, (pt, sl)], writes=[ps_o])
        P.op("pe", lambda e, sl=sl, ji=ji: e.matmul(ps_d[:, 0:n], Kk.att["onesb"][:, :], pt[:, sl, 0:n],
                                                    start=(ji == 0), stop=(ji == len(jt) - 1)),
             reads=[Kk.att["onesb"], (pt, sl)], writes=[ps_d])
    s = it % 2
    P.op("dve", lambda e: e.reciprocal(rd[:, s, 0:n], ps_d[:, 0:n]), reads=[ps_d], writes=[(rd, s)])
    P.op("dve", lambda e: e.tensor_tensor(ot[:, s, 0:n], ps_o[:, 0:n], rd[:, s, 0:n], ALU.mult),
         reads=[ps_o, (rd, s)], writes=[(ot, s)])
    P.dma(yT[row0 + h * 128:row0 + (h + 1) * 128, t0:t0 + n], ot[:, s, 0:n], reads=[(ot, s)], qn="pool")
    return pti


def stage_attn_even(Kk, S, yT, do_ctx):
    P = Kk.P
    with ExitStack() as ses:
        P.stage_es = ses
        kt = P.sb("kt", [128, NT])
        vt = P.sb("vt", [128, NT // 128, 128])
        qt = P.sb("qt", [128, 2, 512])
        Kk.att = dict(pt=P.sb("pt", [128, 3, 512], BF16), ot=P.sb("ot", [128, 2, 512]), rd=P.sb("rd", [128, 2, 512]),
                      onesb=P.sb("onesb", [128, 128], BF16))
        P.op("pool", lambda e: e.tensor_copy(Kk.att["onesb"][:, :], Kk.ones[:, :]), reads=[Kk.ones], writes=[Kk.att["onesb"]])
        vtb = P.sb("vtb", [128, NT // 128, 128], BF16)

        ktb = P.sb("ktb", [128, NT], BF16)
        qtb = P.sb("qtb", [128, 2, 512], BF16)

        def load_k(kvh):
            P.dma(kt[:, :], S["kT"][kvh], writes=[kt])
            P.op("pool", lambda e: e.tensor_copy(ktb[:, :], kt[:, :]), reads=[kt], writes=[ktb])
            return [ktb]

        def load_v(kvh):
            for j0 in range(0, NT // 128, 4):
                j1 = min(NT // 128, j0 + 4)
                P.dma(vt[:, j0:j1, :], S["vM"][j0 * 128:j1 * 128, kvh * 128:(kvh + 1) * 128].rearrange("(j p) d -> p j d", p=128),
                      writes=[vt])
            P.op("pool", lambda e: e.tensor_copy(vtb[:, :, :], vt[:, :, :]), reads=[vt], writes=[vtb])
            return vtb

        def load_q(h, t0, n, slot):
            P.dma(qt[:, slot, 0:n], S["qT"][h][:, t0:t0 + n], writes=[(qt, slot)])
            P.op("dve", lambda e: e.tensor_copy(qtb[:, slot, 0:n], qt[:, slot, 0:n]), reads=[(qt, slot)], writes=[(qtb, slot)])
            return [qtb[:, slot, 0:n]], [(qtb, slot)]

        attention(Kk, 8, lambda h: h // 4, load_k, load_v, load_q, [128], A_SCALE, yT, 0, do_ctx)
        P.barrier()
        P.stage_es = None


def stage_ssd_conv(Kk, l, S):
    P = Kk.P
    e_ = l // 2
    with ExitStack() as ses:
        P.stage_es = ses
        cw = P.sb("scw", [128, 12, 5])
        cb = P.sb("scb", [128, 12])
        ub = P.sb("ub", [128, 3, 516])
        ac = P.sb("sac", [128, 3, 512])
        P.dma(cw[:, :, :], Kk.ins["ssm_conv_w"][e_], writes=[cw])
        P.dma(cb[:, :], Kk.ins["ssm_conv_b"][e_], writes=[cb])
        it = 0
        for (t0, n, isctx) in tiles_main():
            seg0, seg1 = (0, NCTX) if isctx else (NCTX, NT)
            lh = min(2, t0 - seg0)
            rh = min(2, seg1 - (t0 + n))
            for c in range(12):
                s = it % 3
                it += 1
                _ssd_conv_tile(Kk, S, cw, cb, ub, ac, t0, n, lh, rh, c, s)
        P.barrier()
        P.stage_es = None


def _ssd_conv_tile(Kk, S, cw, cb, ub, ac, t0, n, lh, rh, c, s):
    P = Kk.P
    k = (ub, s)
    if lh < 2 or rh < 2:
        P.op("pool", lambda e: e.memset(ub[:, s, :], 0.0), writes=[k])
    P.dma(ub[:, s, 2 - lh:2 + n + rh], S["xbcT"][c * 128:(c + 1) * 128, t0 - lh:t0 + n + rh], writes=[k])
    ka = (ac, s)
    P.op("act", lambda e: e.activation(ac[:, s, 0:n], ub[:, s, 2:2 + n], AF.Identity, bias=cb[:, c:c + 1], scale=cw[:, c, 2:3]),
         reads=[k, cw, cb], writes=[ka])
    for kk in (0, 1, 3, 4):
        P.op("dve", lambda e, kk=kk: e.scalar_tensor_tensor(ac[:, s, 0:n], ub[:, s, kk:kk + n], cw[:, c, kk:kk + 1], ac[:, s, 0:n],
                                                             ALU.mult, ALU.add),
             reads=[k, cw, ka], writes=[ka])
    P.op("act", lambda e: e.activation(ac[:, s, 0:n], ac[:, s, 0:n], AF.Silu), reads=[ka], writes=[ka])
    P.dma(S["xcT"][c * 128:(c + 1) * 128, t0:t0 + n], ac[:, s, 0:n], reads=[ka], qn="pool")


def stage_ssd(Kk, l, S, yT, do_ctx):
    for d in range(2):
        _ssd_dir(Kk, l, S, yT, do_ctx, d)


def _ssd_dir(Kk, l, S, yT, do_ctx, d):
    P = Kk.P
    e_ = l // 2
    NCH = NT // 128
    if True:
        with ExitStack() as ses:
            P.stage_es = ses
            R = {}
            R["tri"] = P.sb("tri", [128, 4, 128])
            P.dma(R["tri"][:, :, :], Kk.ins["c_tri"][:, :, :], writes=[R["tri"]])
            R["alog"] = P.sb("alog", [32, 2])
            P.dma(R["alog"][:, :], Kk.ins["dt_ba"][e_], writes=[R["alog"]])
            P.op("act", lambda e: e.activation(R["alog"][:, 1:2], R["alog"][:, 1:2], AF.Exp), reads=[R["alog"]], writes=[R["alog"]])
            P.op("dve", lambda e: e.tensor_scalar(R["alog"][:, 1:2], R["alog"][:, 1:2], -1.0, None, ALU.mult),
                 reads=[R["alog"]], writes=[R["alog"]])
            R["fm"] = P.sb("sfm", [128, 2, 12, 128])
            R["dtf"] = P.sb("dtf", [128, 2, 2, 128])
            P.op("pool", lambda e: e.memset(R["dtf"][:, :, :, :], 0.0), writes=[R["dtf"]])
            R["xs"] = P.sb("xstm", [128, 2, 1024])
            R["btm"] = P.sb("btm", [128, 2, 256])
            R["dtm"] = P.sb("dtm", [128, 2, 64])
            R["bc"] = P.sb("bc", [128, 16, 128])
            R["xdtp"] = P.sb("xdtp", [128, 16, 128])
            R["xdtw"] = P.sb("xdtw", [128, 1024])
            R["H"] = P.sb("Hs", [128, 16, 128])
            R["gm"] = P.sb("gm", [128, 2, 2, 128])
            R["acs"] = P.sb("acs", [128, 2, 48])
            R["rot"] = P.sb("rot", [128, 5, 4, 128])
            R["yacc"] = P.sb("yacc", [128, 2, 8, 128])
            R["fin"] = P.sb("fin", [128, 2, 8, 128])
            R["zt"] = P.sb("zt", [128, 2, 8, 128])
            R["sq"] = P.sb("ssq", [128, 4, 128])
            R["rs"] = P.sb("srs", [128, 2, 128])
            R["dsk"] = P.sb("dsk", [128, 8])
            R["gn"] = P.sb("gn", [128, 8])
            R["ones512"] = P.sb("ones512", [128, 128])
            P.dma(R["dsk"][:, :], Kk.ins["ssm_d"][e_], writes=[R["dsk"]])
            P.dma(R["gn"][:, :], Kk.ins["ssm_norm"][e_], writes=[R["gn"]])
            P.dma(R["ones512"][:, :], Kk.ins["c_ones512"][:, :], writes=[R["ones512"]])
            P.op("pool", lambda e: e.memset(R["H"][:, :, :], 0.0), writes=[R["H"]])
            P.op("pool", lambda e: e.memset(R["xdtp"][:, :, :], 0.0), writes=[R["xdtp"]])
            order = [0, 1] + list(range(2, NCH)) if d == 0 else [1, 0] + list(range(NCH - 1, 1, -1))
            R["hi"] = 0
            import os
            nlim = int(os.environ.get("SSD_NCH", "999"))
            for ci, ch in enumerate(order[:nlim]):
                _ssd_chunk(Kk, S, yT, R, d, ch, ci, do_ctx)
            P.barrier()
            P.stage_es = None


def _ssd_chunk(Kk, S, yT, R, d, ch, ci, do_ctx):
    P = Kk.P
    s = ci % 2
    t0 = ch * 128
    tri = R["tri"]
    TD = tri[:, 0 if d == 0 else 2, :]
    TDx = tri[:, 1 if d == 0 else 3, :]
    last = 127 if d == 0 else 0
    fm, dtf, xs, btm, dtm = R["fm"], R["dtf"], R["xs"], R["btm"], R["dtm"]
    import os
    LV = int(os.environ.get("SSD_LEVEL", "99"))
    if LV <= 0:
        return
    kfm = (fm, s)
    P.dma(fm[:, s, :, :], S["xcT"][:, t0:t0 + 128].rearrange("(c p) t -> p c t", p=128), writes=[kfm])
    kdf = (dtf, s)
    P.dma(dtf[0:32, s, 0, :], S["dtT"][:, t0:t0 + 128], writes=[kdf])
    P.op("dve", lambda e: e.tensor_scalar(dtf[0:32, s, 1, :], dtf[0:32, s, 0, :], R["alog"][:, 1:2], None, ALU.mult),
         reads=[kdf, R["alog"]], writes=[kdf])
    if LV <= 1:
        return
    pst = Kk.psb[4]
    for g4 in range(2):
        for j in range(4):
            c = g4 * 4 + j
            P.op("pe", lambda e, c=c, j=j: e.matmul(pst[:, j * 128:(j + 1) * 128], fm[:, s, c, :], Kk.ident[:, :], start=True, stop=True),
                 reads=[kfm, Kk.ident], writes=[pst])
        if not os.environ.get("SKIP_EV"):
            P.op("dve", lambda e, g4=g4: e.tensor_copy(xs[:, s, g4 * 512:(g4 + 1) * 512], pst[:, :]),
                 reads=[pst], writes=[(xs, (s, c)) for c in range(g4 * 4, g4 * 4 + 4)])
    for j in range(2):
        P.op("pe", lambda e, j=j: e.matmul(pst[:, j * 128:(j + 1) * 128], fm[:, s, 8 + j, :], Kk.ident[:, :], start=True, stop=True),
             reads=[kfm, Kk.ident], writes=[pst])
    for q in range(0 if os.environ.get("SKIP_DT") else 2):
        P.op("pe", lambda e, q=q: e.matmul(pst[:, 256 + q * 32:256 + (q + 1) * 32], dtf[:, s, q, :], Kk.ident[:, 0:32], start=True, stop=True),
             reads=[kdf, Kk.ident], writes=[pst])
    if not os.environ.get("SKIP_EV"):
        P.op("dve", lambda e: e.tensor_copy(btm[:, s, :], pst[:, 0:256]), reads=[pst], writes=[(btm, s)])
    if not os.environ.get("SKIP_DT") and not os.environ.get("SKIP_DTEV"):
        P.op("dve", lambda e: e.tensor_copy(dtm[:, s, :], pst[:, 256:320]), reads=[pst], writes=[(dtm, s)])
    LV = int(os.environ.get("SSD_LEVEL", "99"))
    if LV <= 2:
        return
    dt_d = dtm[:, s, d * 16:(d + 1) * 16]
    dta_d = dtm[:, s, 32 + d * 16:32 + (d + 1) * 16]
    psG = Kk.psb[5]
    gm = R["gm"]
    acs = R["acs"]
    ka = (acs, s)
    for g in range(2):
        P.op("pe", lambda e, g=g: e.matmul(psG[:, g * 128:(g + 1) * 128], fm[:, s, 8 + g, :], fm[:, s, 10 + g, :],
                                           start=True, stop=True),
             reads=[kfm], writes=[psG])
    P.op("pe", lambda e: e.matmul(psG[:, 256:272], TD, dta_d, start=True, stop=True), reads=[tri, (dtm, s)], writes=[psG])
    P.op("pe", lambda e: e.matmul(psG[:, 272:288], TDx, dta_d, start=True, stop=True), reads=[tri, (dtm, s)], writes=[psG])
    for g in range(2):
        P.op("dve", lambda e, g=g: e.tensor_tensor(gm[:, s, g, :], psG[:, g * 128:(g + 1) * 128], TD, ALU.mult),
             reads=[psG, tri], writes=[(gm, (s, g))])
    P.op("dve", lambda e: e.tensor_copy(acs[:, s, 0:32], psG[:, 256:288]), reads=[psG], writes=[ka])
    P.op("act", lambda e: e.activation(acs[:, s, 16:32], acs[:, s, 16:32], AF.Exp), reads=[ka], writes=[ka])
    P.op("dve", lambda e: e.tensor_tensor(acs[:, s, 32:48], acs[:, s, 16:32], dt_d, ALU.mult), reads=[ka, (dtm, s)], writes=[ka])
    if LV <= 4:
        return
    bc, xdtp, xdtw = R["bc"], R["xdtp"], R["xdtw"]
    P.op("pool", lambda e: e.tensor_copy(bc[:, :, :], dta_d.rearrange("p (h o) -> p h o", o=1).broadcast_to([128, 16, 128])),
         reads=[(dtm, s)], writes=[bc])
    xs3 = xs[:, s, :].rearrange("p (h q) -> p h q", q=64)
    for par in range(2):
        P.op("dve", lambda e, par=par: e.tensor_tensor(
            xdtp[:, par::2, par * 64:(par + 1) * 64], xs3[:, par::2, :],
            dtm[:, s, d * 16 + par:(d + 1) * 16:2].rearrange("p (h o) -> p h o", o=1).broadcast_to([128, 8, 64]), ALU.mult),
            reads=[(xs, (s, c)) for c in range(8)] + [(dtm, s)], writes=[xdtp])
    P.op("pool", lambda e: e.tensor_tensor(xdtw[:, :].rearrange("p (h q) -> p h q", q=64), xs3,
                                           acs[:, s, 32:48].rearrange("p (h o) -> p h o", o=1).broadcast_to([128, 16, 64]), ALU.mult),
         reads=[(xs, (s, c)) for c in range(8)] + [ka], writes=[xdtw])
    if LV <= 5:
        return
    pss = [Kk.psb[6], Kk.psb[7]]
    for g in range(2):
        P.op("pe", lambda e, g=g: e.matmul(pss[g][:, :], btm[:, s, g * 128:(g + 1) * 128], xdtw[:, g * 512:(g + 1) * 512],
                                           start=True, stop=True),
             reads=[(btm, s), xdtw], writes=[pss[g]])
    if LV <= 6:
        return
    rot = R["rot"]
    H = R["H"]
    yacc = R["yacc"]
    for h in range(16):
        g = h // 8
        pair = h // 2
        par = h % 2
        hi = R["hi"]
        R["hi"] += 1
        r = hi % 4
        psa = Kk.psb[hi % 2]
        psy = Kk.psb[2 + (hi // 2) % 2]
        kr = lambda i, r=r: (rot, (i, r))
        P.op("pe", lambda e, h=h, psa=psa: e.matmul(psa[:, 0:128], bc[:, h, :], TD, start=True, stop=True),
             reads=[bc, tri], writes=[psa])
        P.op("dve", lambda e, h=h, psa=psa, r=r: e.tensor_scalar(rot[:, 0, r, :], psa[:, 0:128], acs[:, s, h:h + 1], 0.0,
                                                                 ALU.subtract, ALU.min),
             reads=[psa, ka], writes=[kr(0)])
        P.op("act", lambda e, r=r: e.activation(rot[:, 1, r, :], rot[:, 0, r, :], AF.Exp), reads=[kr(0)], writes=[kr(1)])
        P.op("pool", lambda e, r=r, g=g: e.tensor_tensor(rot[:, 2, r, :], rot[:, 1, r, :], gm[:, s, g, :], ALU.mult),
             reads=[kr(1), (gm, (s, g))], writes=[kr(2)])
        P.op("act", lambda e, psa=psa, r=r: e.activation(rot[:, 3, r, :], psa[:, 0:128], AF.Exp), reads=[psa], writes=[kr(3)])
        P.op("dve", lambda e, r=r, g=g: e.tensor_tensor(rot[:, 4, r, :], rot[:, 3, r, :], fm[:, s, 10 + g, :], ALU.mult),
             reads=[kr(3), kfm], writes=[kr(4)])
        P.op("pe", lambda e, h=h, psy=psy, r=r, par=par: e.matmul(psy[:, 0:128], xdtp[:, h, :], rot[:, 2, r, :],
                                                                  start=(par == 0), stop=False),
             reads=[xdtp, kr(2)], writes=[psy])
        P.op("pe", lambda e, h=h, psy=psy, r=r, par=par: e.matmul(psy[:, 0:128], H[:, h, :], rot[:, 4, r, :],
                                                                  start=False, stop=(par == 1)),
             reads=[(H, h), kr(4)], writes=[psy])
        P.op("dve", lambda e, h=h, r=r, g=g, par=par: e.scalar_tensor_tensor(
            H[:, h, par * 64:(par + 1) * 64], H[:, h, par * 64:(par + 1) * 64], rot[:, 3, r, last:last + 1],
            pss[g][:, (h % 8) * 64:(h % 8 + 1) * 64], ALU.mult, ALU.add),
            reads=[(H, h), kr(3), pss[g]], writes=[(H, h)])
        if par == 1:
            ky = (yacc, (s, pair))
            P.op("act", lambda e, psy=psy, pair=pair: e.activation(yacc[:, s, pair, :], psy[:, 0:128], AF.Identity),
                 reads=[psy], writes=[ky])
    if LV <= 7:
        return
    isctx = ch < 2
    allk = [(yacc, (s, p)) for p in range(8)]
    if d == 0:
        P.dma(S["ysT"][:, t0:t0 + 128].rearrange("(c p) t -> p c t", p=128), yacc[:, s, :, :], reads=allk, qn="pool")
        return
    if isctx and not do_ctx:
        return
    fin, zt, sq, rs = R["fin"], R["zt"], R["sq"], R["rs"]
    kf = (fin, s)
    P.dma(fin[:, s, :, :], S["ysT"][:, t0:t0 + 128].rearrange("(c p) t -> p c t", p=128), writes=[kf])
    kz = (zt, s)
    P.dma(zt[:, s, :, :], S["zT"][:, t0:t0 + 128].rearrange("(c p) t -> p c t", p=128), writes=[kz])
    P.op("dve", lambda e: e.tensor_tensor(fin[:, s, :, :], fin[:, s, :, :], yacc[:, s, :, :], ALU.add), reads=[kf] + allk, writes=[kf])
    for c in range(8):
        P.op("dve", lambda e, c=c: e.scalar_tensor_tensor(fin[:, s, c, :], fm[:, s, c, :], R["dsk"][:, c:c + 1], fin[:, s, c, :],
                                                           ALU.mult, ALU.add),
             reads=[kf, kfm, R["dsk"]], writes=[kf])
    P.op("pool", lambda e: e.tensor_tensor(fin[:, s, :, :], fin[:, s, :, :], zt[:, s, :, :], ALU.mult), reads=[kf, kz], writes=[kf])
    psn = Kk.psb[4]
    for g in range(2):
        for c4 in range(4):
            c = g * 4 + c4
            q = c % 4
            P.op("act", lambda e, c=c, q=q: e.activation(sq[:, q, :], fin[:, s, c, :], AF.Square), reads=[kf], writes=[(sq, q)])
            P.op("pe", lambda e, c4=c4, q=q, g=g: e.matmul(psn[:, g * 128:(g + 1) * 128], R["ones512"][:, :], sq[:, q, :],
                                                           start=(c4 == 0), stop=(c4 == 3)),
                 reads=[(sq, q), R["ones512"]], writes=[psn])
        P.op("dve", lambda e, g=g: e.tensor_copy(rs[:, g, :], psn[:, g * 128:(g + 1) * 128]), reads=[psn], writes=[(rs, g)])
        P.op("act", lambda e, g=g: e.activation(rs[:, g, :], rs[:, g, :], AF.Sqrt, bias=Kk.epsb[:, 0:1], scale=1.0),
             reads=[(rs, g), Kk.epsb], writes=[(rs, g)])
        P.op("dve", lambda e, g=g: e.reciprocal(rs[:, g, :], rs[:, g, :]), reads=[(rs, g)], writes=[(rs, g)])
        for c4 in range(4):
            c = g * 4 + c4
            P.op("dve", lambda e, c=c, g=g: e.scalar_tensor_tensor(fin[:, s, c, :], fin[:, s, c, :], R["gn"][:, c:c + 1], rs[:, g, :],
                                                                    ALU.mult, ALU.mult),
                 reads=[kf, R["gn"], (rs, g)], writes=[kf])
    P.dma(yT[1024:2048, t0:t0 + 128].rearrange("(c p) t -> p c t", p=128), fin[:, s, :, :], reads=[kf], qn="pool")


C_SCALE = 192 ** -0.5
RW0 = 832


def stage_A_odd(Kk, l, xT, S):
    P = Kk.P
    o_ = l // 2
    W = Kk.ins["od_w_in"][o_]
    Wqb = Kk.ins["mla_q_b"][o_]
    Wkvb = Kk.ins["mla_kv_b"][o_]
    st = {}

    def extra(_):
        st["qa"] = P.sb("qa", [128, 6, 512])
        st["qn"] = P.sb("qan", [128, 6, 512])
        st["rs"] = P.sb("rsq", [128, 2, 512])
        st["gain"] = P.sb("mlagain", [128, 6])
        st["cos"] = P.sb("cosb", [64, 512])
        st["sin"] = P.sb("sinb", [64, 512])
        st["RT"] = P.sb("RT64", [64, 64])
        st["pe"] = P.sb("pe", [64, 3, 512])
        st["po"] = P.sb("po", [64, 3, 512])
        st["tmp"] = P.sb("ptmp", [64, 3, 512])
        st["o512"] = P.sb("o512", [128, 128])
        st["o256"] = P.sb("o256", [128, 128])
        P.dma(st["gain"][:, :], Kk.ins["mla_gain"][o_], writes=[st["gain"]])
        P.dma(st["RT"][:, :], Kk.ins["c_RT64"][:, :], writes=[st["RT"]])
        P.dma(st["o512"][:, :], Kk.ins["c_ones512"][:, :], writes=[st["o512"]])
        P.dma(st["o256"][:, :], Kk.ins["c_ones256"][:, :], writes=[st["o256"]])
        st["i"] = 0

    def tile_pre(t0, n, isctx):
        if not isctx:
            P.dma(st["cos"][:, 0:n], Kk.ins["c_cos64"][:, t0 - NCTX:t0 - NCTX + n], writes=[st["cos"]])
            P.dma(st["sin"][:, 0:n], Kk.ins["c_sin64"][:, t0 - NCTX:t0 - NCTX + n], writes=[st["sin"]])

    def rope64(ps, n, t0, isctx, dst):
        s = st["i"] % 3
        st["i"] += 1
        pe, po, tmp = st["pe"], st["po"], st["tmp"]
        P.op("act", lambda e: e.activation(pe[:, s, 0:n], ps[0:64, 0:n], AF.Identity), reads=[ps], writes=[(pe, s)])
        if isctx:
            P.dma(dst, pe[:, s, 0:n], reads=[(pe, s)], qn="pool")
        else:
            rope_apply(Kk, (pe, s, pe[:, s, 0:n]), 64, n, st["cos"], st["sin"], st["RT"], Kk.psb[6],
                       (po, s, po[:, s, 0:n]), (tmp, s, tmp[:, s, 0:n]))
            P.dma(dst, po[:, s, 0:n], reads=[(po, s)], qn="pool")

    def mk_a(t0, n, isctx):
        def evac(i, ps, cw):
            P.op("act", lambda e: e.activation(st["qa"][:, i, 0:n], ps[:, 0:n], AF.Identity), reads=[ps], writes=[(st["qa"], i)])
        return evac

    def mk_kpe(t0, n, isctx):
        def evac(i, ps, cw):
            rope64(ps, n, t0, isctx, S["kpT"][:, t0:t0 + n])
        return evac

    def post_a(t0, n, isctx):
        qa, qn, rs = st["qa"], st["qn"], st["rs"]
        for part, (c0, nch, ones_) in enumerate([(0, 4, st["o512"]), (4, 2, st["o256"])]):
            psr = Kk.psb[6]
            colsum_bcast(Kk, lambda c, kp, c0=c0: qa[:, c0 + c, 0:n], nch, n, psr, Kk.stA["sq"], ones_,
                         [(qa, c0 + c) for c in range(nch)])
            P.op("act", lambda e, part=part, psr=psr: e.activation(rs[:, part, 0:n], psr[:, 0:n], AF.Sqrt, bias=Kk.epsb[:, 0:1], scale=1.0),
                 reads=[psr, Kk.epsb], writes=[(rs, part)])
            P.op("dve", lambda e, part=part: e.reciprocal(rs[:, part, 0:n], rs[:, part, 0:n]), reads=[(rs, part)], writes=[(rs, part)])
            for c in range(c0, c0 + nch):
                P.op("dve", lambda e, c=c, part=part: e.scalar_tensor_tensor(qn[:, c, 0:n], qa[:, c, 0:n], st["gain"][:, c:c + 1],
                                                                              rs[:, part, 0:n], ALU.mult, ALU.mult),
                     reads=[(qa, c), st["gain"], (rs, part)], writes=[(qn, c)])
        chunks = []
        for h in range(8):
            chunks.append((h * 192, 128))
            chunks.append((h * 192 + 128, 64))

        def evq(i, ps, cw):
            h = i // 2
            if i % 2 == 0:
                ev_store(Kk, ps, 128, n, S["qnT"][h][:, t0:t0 + n])
            else:
                rope64(ps, n, t0, isctx, S["qpT"][h][:, t0:t0 + n])

        gemm_fm(Kk, Wqb, 512, chunks, lambda kc, kp: qn[0:kp, kc, 0:n], n, evq, ps_ids=(2, 3),
                rhs_reads=[(qn, c) for c in range(4)])
        kchunks = [(h * 256, 128) for h in range(8)]

        def evk(i, ps, cw):
            ev_store(Kk, ps, 128, n, S["knT"][i][:, t0:t0 + n])

        gemm_fm(Kk, Wkvb, 256, kchunks, lambda kc, kp: qn[0:kp, 4 + kc, 0:n], n, evk, ps_ids=(2, 3),
                rhs_reads=[(qn, 4), (qn, 5)])
        for h in range(8):
            def evv(tb, ps, tn, cb0, cbw, h=h):
                ev_store(Kk, ps, tn, cbw, S["vM"][t0 + tb * 128:t0 + tb * 128 + tn, h * 128:h * 128 + cbw])
            gemm_tm(Kk, Wkvb, 256, h * 256 + 128, 128, lambda kc, kp, a, tn: qn[0:kp, 4 + kc, a:a + tn], n, evv, ps_ids=(2, 3),
                    lhs_reads=[(qn, 4), (qn, 5)])

    def mk_ud(t0, n, isctx):
        def evac(i, ps, cw):
            c0 = ud_chunks[i][0] - RW0
            ev_store(Kk, ps, cw, n, S["udT"][c0:c0 + cw, t0:t0 + n])
        return evac

    a_chunks = [(i * 128, 128) for i in range(6)]
    kpe_chunks = [(768, 64)]
    ud_chunks = [(RW0 + i * 128, 128) for i in range(24)] + [(3904 + i * 96, 96) for i in range(4)] + [(4288, 128), (4416, 128)]
    plan = [("post", tile_pre), ("fm", a_chunks, mk_a), ("fm", kpe_chunks, mk_kpe), ("post", post_a), ("fm", ud_chunks, mk_ud)]
    stage_A(Kk, l, xT, W, plan, nsteps_extra=extra)


def stage_attn_odd(Kk, S, yT, do_ctx):
    P = Kk.P
    with ExitStack() as ses:
        P.stage_es = ses
        kt = P.sb("kt", [128, NT])
        kp = P.sb("kp", [64, NT])
        vt = P.sb("vt", [128, NT // 128, 128])
        qt = P.sb("qt", [128, 2, 512])
        qp = P.sb("qp", [64, 2, 512])
        Kk.att = dict(pt=P.sb("pt", [128, 3, 512], BF16), ot=P.sb("ot", [128, 2, 512]), rd=P.sb("rd", [128, 2, 512]),
                      onesb=P.sb("onesb", [128, 128], BF16))
        P.op("pool", lambda e: e.tensor_copy(Kk.att["onesb"][:, :], Kk.ones[:, :]), reads=[Kk.ones], writes=[Kk.att["onesb"]])
        vtb = P.sb("vtb", [128, NT // 128, 128], BF16)
        P.dma(kp[:, :], S["kpT"][:, :], writes=[kp])

        ktb = P.sb("ktb", [128, NT], BF16)
        kpb = P.sb("kpb", [64, NT], BF16)
        qtb = P.sb("qtb", [128, 2, 512], BF16)
        qpb = P.sb("qpb", [64, 2, 512], BF16)
        P.op("pool", lambda e: e.tensor_copy(kpb[:, :], kp[:, :]), reads=[kp], writes=[kpb])

        def load_k(h):
            P.dma(kt[:, :], S["knT"][h], writes=[kt])
            P.op("pool", lambda e: e.tensor_copy(ktb[:, :], kt[:, :]), reads=[kt], writes=[ktb])
            return [ktb, kpb]

        def load_v(h):
            for j0 in range(0, NT // 128, 4):
                j1 = min(NT // 128, j0 + 4)
                P.dma(vt[:, j0:j1, :], S["vM"][j0 * 128:j1 * 128, h * 128:(h + 1) * 128].rearrange("(j p) d -> p j d", p=128),
                      writes=[vt])
            P.op("pool", lambda e: e.tensor_copy(vtb[:, :, :], vt[:, :, :]), reads=[vt], writes=[vtb])
            return vtb

        def load_q(h, t0, n, slot):
            P.dma(qt[:, slot, 0:n], S["qnT"][h][:, t0:t0 + n], writes=[(qt, slot)])
            P.dma(qp[:, slot, 0:n], S["qpT"][h][:, t0:t0 + n], writes=[(qp, slot)])
            P.op("dve", lambda e: e.tensor_copy(qtb[:, slot, 0:n], qt[:, slot, 0:n]), reads=[(qt, slot)], writes=[(qtb, slot)])
            P.op("dve", lambda e: e.tensor_copy(qpb[:, slot, 0:n], qp[:, slot, 0:n]), reads=[(qp, slot)], writes=[(qpb, slot)])
            return [qtb[:, slot, 0:n], qpb[:, slot, 0:n]], [(qtb, slot), (qpb, slot)]

        attention(Kk, 8, lambda h: h, load_k, load_v, load_q, [128, 64], C_SCALE, yT, 0, do_ctx)
        P.barrier()
        P.stage_es = None

import math

NEG_EH = -math.exp(-0.5)
NCH64 = NT // 64


def stage_rwkv_prep(Kk, l, S):
    P = Kk.P
    o_ = l // 2
    with ExitStack() as ses:
        P.stage_es = ses
        R = {}
        R["udp"] = P.sb("udp", [128, 30, 512])
        R["ub"] = P.sb("rub", [128, 2, 516])
        R["T"] = P.sb("rT", [128, 12, 512])
        R["PK"] = P.sb("rPK", [128, 4, 512])
        R["stg"] = P.sb("rstg", [128, 6, 512])
        R["tm"] = P.sb("rtm", [128, 3, 512])
        R["mu"] = P.sb("rmu", [128, 30])
        R["vec"] = P.sb("rvec", [128, 7, 8])
        R["g2w"] = P.sb("g2w", [128, 2, 1024])
        R["w2w"] = P.sb("w2w", [128, 2, 1024])
        R["a2w"] = P.sb("a2w", [128, 2, 1024])
        R["blk"] = P.sb("blk", [128, 128])
        R["on64"] = P.sb("on64", [128, 64])
        R["wc"] = P.sb("wcst", [128, 4, 8])
        P.dma(R["mu"][:, :], Kk.ins["rw_mu"][o_], writes=[R["mu"]])
        P.dma(R["vec"][:, :, :], Kk.ins["rw_vec"][o_], writes=[R["vec"]])
        P.dma(R["g2w"][:, :, :], Kk.ins["rwkv_g2"][o_].rearrange("(kc p) c -> p kc c", p=128), writes=[R["g2w"]])
        for d in range(2):
            P.dma(R["w2w"][0:96, d, :], Kk.ins["rwkv_w2"][o_][d], writes=[R["w2w"]])
            P.dma(R["a2w"][0:96, d, :], Kk.ins["rwkv_a2"][o_][d], writes=[R["a2w"]])
        P.dma(R["blk"][:, :], Kk.ins["c_blk64"][:, :], writes=[R["blk"]])
        P.op("pool", lambda e: e.memset(R["on64"][:, :], 1.0), writes=[R["on64"]])
        R["si"] = 0
        R["ti"] = 0
        R["tmi"] = 0
        R["wi"] = 0
        for (t0, n, isctx) in tiles_main():
            _rw_prep_tile(Kk, S, R, t0, n, isctx)
        P.barrier()
        P.stage_es = None


def _rw_prep_tile(Kk, S, R, t0, n, isctx):
    P = Kk.P
    udp, ub, T, stg, tm, mu, vec = R["udp"], R["ub"], R["T"], R["stg"], R["tm"], R["mu"], R["vec"]
    seg0, seg1 = (0, NCTX) if isctx else (NCTX, NT)
    lh = 1 if t0 > seg0 else 0
    rh = 1 if t0 + n < seg1 else 0
    nch = n // 64
    ch0 = t0 // 64
    rows = [128] * 24 + [96] * 4 + [128] * 2
    roff = [i * 128 for i in range(24)] + [3072 + i * 96 for i in range(4)] + [3456, 3584]

    def tslot():
        s = R["ti"] % 12
        R["ti"] += 1
        return s

    def store_fm(src_ap, skey, dst):
        P.dma(dst, src_ap, reads=[skey], qn="pool")

    for c in range(30):
        rw = rows[c]
        s = c % 2
        kb = (ub, s)
        if not (lh and rh):
            P.op("pool", lambda e, s=s: e.memset(ub[:, s, :], 0.0), writes=[kb])
        P.dma(ub[0:rw, s, 1 - lh:1 + n + rh], S["udT"][roff[c]:roff[c] + rw, t0 - lh:t0 + n + rh], writes=[kb])
        ts = tslot()
        kt = (T, ts)
        P.op("dve", lambda e, s=s, ts=ts, rw=rw: e.tensor_tensor(T[0:rw, ts, 0:n], ub[0:rw, s, 0:n], ub[0:rw, s, 2:2 + n], ALU.add),
             reads=[kb], writes=[kt])
        P.op("dve", lambda e, s=s, ts=ts, rw=rw: e.scalar_tensor_tensor(T[0:rw, ts, 0:n], T[0:rw, ts, 0:n], 0.5, ub[0:rw, s, 1:1 + n],
                                                                         ALU.mult, ALU.subtract),
             reads=[kb, kt], writes=[kt])
        P.op("dve", lambda e, s=s, ts=ts, rw=rw, c=c: e.scalar_tensor_tensor(udp[0:rw, c, 0:n], T[0:rw, ts, 0:n], mu[0:rw, c:c + 1],
                                                                              ub[0:rw, s, 1:1 + n], ALU.mult, ALU.add),
             reads=[kb, kt, mu], writes=[(udp, c)])
    for c in (28, 29):
        P.op("act", lambda e, c=c: e.activation(udp[:, c, 0:n], udp[:, c, 0:n], AF.Sigmoid), reads=[(udp, c)], writes=[(udp, c)])
    for c in range(8):
        ps = Kk.psb[c % 2]
        for kc in range(2):
            P.op("pe", lambda e, c=c, kc=kc, ps=ps: e.matmul(ps[:, 0:n], R["g2w"][:, kc, c * 128:(c + 1) * 128], udp[:, 28 + kc, 0:n],
                                                            start=(kc == 0), stop=(kc == 1)),
                 reads=[R["g2w"], (udp, 28 + kc)], writes=[ps])
        s = R["si"] % 6
        R["si"] += 1
        P.op("act", lambda e, ps=ps, s=s: e.activation(stg[:, s, 0:n], ps[:, 0:n], AF.Identity), reads=[ps], writes=[(stg, s)])
        store_fm(stg[:, s, 0:n], (stg, s), S["gateT"][c * 128:(c + 1) * 128, t0:t0 + n])
    for d in range(2):
        P.op("act", lambda e, d=d: e.activation(udp[0:96, 24 + d, 0:n], udp[0:96, 24 + d, 0:n], AF.Tanh),
             reads=[(udp, 24 + d)], writes=[(udp, 24 + d)])
    for c in range(8):
        _rw_prep_chunk(Kk, S, R, t0, n, c, nch, ch0)


def _rw_prep_chunk(Kk, S, R, t0, n, c, nch, ch0):
    P = Kk.P
    udp, T, stg, tm, vec = R["udp"], R["T"], R["stg"], R["tm"], R["vec"]
    rC, kC, vC = (udp, c), (udp, 8 + c), (udp, 16 + c)
    r_ = udp[:, c, 0:n]
    k_ = udp[:, 8 + c, 0:n]
    v_ = udp[:, 16 + c, 0:n]

    def tslot():
        s = R["ti"] % 12
        R["ti"] += 1
        return s, (T, s)

    def sslot():
        s = R["si"] % 6
        R["si"] += 1
        return s, (stg, s)

    def transpose_store(src_ap, skey, dst_tm):
        pst = Kk.psb[4 + R["tmi"] % 2]
        s = R["tmi"] % 3
        R["tmi"] += 1
        nb = n // 128
        for b in range(nb):
            P.op("pe", lambda e, b=b: e.matmul(pst[:, b * 128:(b + 1) * 128], src_ap[:, b * 128:(b + 1) * 128], Kk.ident[:, :],
                                               start=True, stop=True),
                 reads=[skey, Kk.ident], writes=[pst])
        P.op("act", lambda e: e.activation(tm[:, s, 0:n], pst[:, 0:n], AF.Identity), reads=[pst], writes=[(tm, s)])
        P.dma(dst_tm[t0:t0 + n, c * 128:(c + 1) * 128].rearrange("(b p) f -> p b f", p=128),
              tm[:, s, 0:n].rearrange("p (b f) -> p b f", f=128), reads=[(tm, s)], qn="pool")

    def headsum(src_ap, skey, ps):
        P.op("pe", lambda e: e.matmul(ps[:, 0:n], R["blk"][:, :], src_ap, start=True, stop=True), reads=[skey, R["blk"]], writes=[ps])

    transpose_store(v_, vC, S["vTM"])
    PK = R["PK"]
    k_kkr, k_sq, k_kk, k_ks = (PK, 0), (PK, 1), (PK, 2), (PK, 3)
    P.op("dve", lambda e: e.tensor_scalar(PK[:, 0, 0:n], k_, vec[:, 0, c:c + 1], None, ALU.mult), reads=[kC, vec], writes=[k_kkr])
    P.op("act", lambda e: e.activation(PK[:, 1, 0:n], PK[:, 0, 0:n], AF.Square), reads=[k_kkr], writes=[k_sq])
    ps = Kk.psb[2]
    headsum(PK[:, 1, 0:n], k_sq, ps)
    P.op("act", lambda e: e.activation(PK[:, 1, 0:n], ps[:, 0:n], AF.Sqrt, bias=Kk.epsb[:, 2:3], scale=1.0),
         reads=[ps, Kk.epsb], writes=[k_sq])
    P.op("dve", lambda e: e.reciprocal(PK[:, 1, 0:n], PK[:, 1, 0:n]), reads=[k_sq], writes=[k_sq])
    P.op("dve", lambda e: e.tensor_tensor(PK[:, 2, 0:n], PK[:, 0, 0:n], PK[:, 1, 0:n], ALU.mult), reads=[k_kkr, k_sq], writes=[k_kk])
    kk = PK[:, 2, 0:n]
    for d in range(2):
        ps1 = Kk.psb[d]
        P.op("pe", lambda e, d=d, ps1=ps1: e.matmul(ps1[:, 0:n], R["w2w"][0:96, d, c * 128:(c + 1) * 128], udp[0:96, 24 + d, 0:n],
                                                    start=True, stop=True),
             reads=[R["w2w"], (udp, 24 + d)], writes=[ps1])
        s_lw, k_lw = tslot()
        P.op("act", lambda e, d=d, ps1=ps1, s_lw=s_lw: e.activation(T[:, s_lw, 0:n], ps1[:, 0:n], AF.Sigmoid, bias=vec[:, 3 + d, c:c + 1], scale=1.0),
             reads=[ps1, vec], writes=[k_lw])
        P.op("dve", lambda e, s_lw=s_lw: e.tensor_scalar(T[:, s_lw, 0:n], T[:, s_lw, 0:n], NEG_EH, None, ALU.mult), reads=[k_lw], writes=[k_lw])
        ps2 = Kk.psb[3]
        P.op("pe", lambda e, d=d: e.matmul(ps2[:, 0:n], R["a2w"][0:96, d, c * 128:(c + 1) * 128], udp[0:96, 26 + d, 0:n],
                                           start=True, stop=True),
             reads=[R["a2w"], (udp, 26 + d)], writes=[ps2])
        s_a, k_a = tslot()
        P.op("act", lambda e, d=d, s_a=s_a: e.activation(T[:, s_a, 0:n], ps2[:, 0:n], AF.Sigmoid, bias=vec[:, 5 + d, c:c + 1], scale=1.0),
             reads=[ps2, vec], writes=[k_a])
        s_kd, k_kd = tslot()
        P.op("dve", lambda e, s_a=s_a, s_kd=s_kd: e.tensor_scalar(T[:, s_kd, 0:n], T[:, s_a, 0:n], -1.0, vec[:, 1, c:c + 1], ALU.add, ALU.mult),
             reads=[k_a, vec], writes=[k_kd])
        P.op("dve", lambda e, s_kd=s_kd: e.scalar_tensor_tensor(T[:, s_kd, 0:n], T[:, s_kd, 0:n], 1.0, k_, ALU.add, ALU.mult),
             reads=[k_kd, kC], writes=[k_kd])
        if d == 0:
            P.op("pool", lambda e, s_kd=s_kd: e.tensor_copy(PK[:, 3, 0:n], T[:, s_kd, 0:n]), reads=[k_kd], writes=[k_ks])
        else:
            P.op("pool", lambda e, s_kd=s_kd: e.tensor_tensor(PK[:, 3, 0:n], PK[:, 3, 0:n], T[:, s_kd, 0:n], ALU.add),
                 reads=[k_kd, k_ks], writes=[k_ks])
        s_cl, k_cl = tslot()
        for q in range(nch):
            P.op("dve", lambda e, q=q, s_cl=s_cl, s_lw=s_lw: e.tensor_tensor_scan(T[:, s_cl, q * 64:(q + 1) * 64], R["on64"][:, :],
                                                                                  T[:, s_lw, q * 64:(q + 1) * 64], 0.0, ALU.mult, ALU.add),
                 reads=[k_lw, R["on64"]], writes=[k_cl])
        if d == 1:
            s_t2, k_t2 = tslot()
            P.op("dve", lambda e, s_t2=s_t2, s_cl=s_cl, s_lw=s_lw: e.tensor_tensor(T[:, s_t2, 0:n], T[:, s_lw, 0:n], T[:, s_cl, 0:n], ALU.subtract),
                 reads=[k_lw, k_cl], writes=[k_t2])
            cl3 = T[:, s_cl, 0:n].rearrange("p (q t) -> p q t", t=64)
            P.op("dve", lambda e, s_t2=s_t2, cl3=cl3: e.tensor_tensor(T[:, s_t2, 0:n].rearrange("p (q t) -> p q t", t=64),
                                                                      T[:, s_t2, 0:n].rearrange("p (q t) -> p q t", t=64),
                                                                      cl3[:, :, 63:64].broadcast_to([128, nch, 64]), ALU.add),
                 reads=[k_t2, k_cl], writes=[k_t2])
            s_cl, k_cl = s_t2, k_t2
        cl = T[:, s_cl, 0:n]
        s_ep, k_ep = tslot()
        P.op("act", lambda e, s_ep=s_ep, cl=cl: e.activation(T[:, s_ep, 0:n], cl, AF.Exp), reads=[k_cl], writes=[k_ep])
        s_em, k_em = tslot()
        P.op("act", lambda e, s_em=s_em, cl=cl: e.activation(T[:, s_em, 0:n], cl, AF.Exp, scale=-1.0), reads=[k_cl], writes=[k_em])
        P.op("dve", lambda e, s_lw=s_lw, cl=cl: e.tensor_tensor(T[:, s_lw, 0:n], cl, T[:, s_lw, 0:n], ALU.subtract), reads=[k_cl, k_lw], writes=[k_lw])
        P.op("act", lambda e, s_lw=s_lw: e.activation(T[:, s_lw, 0:n], T[:, s_lw, 0:n], AF.Exp), reads=[k_lw], writes=[k_lw])
        last = 63 if d == 0 else 0
        wi = R["wi"] % 4
        R["wi"] += 1
        P.op("pool", lambda e, s_ep=s_ep, wi=wi, last=last: e.tensor_copy(
            R["wc"][:, wi, 0:nch].rearrange("p (q o) -> p q o", o=1),
            T[:, s_ep, 0:n].rearrange("p (q t) -> p q t", t=64)[:, :, last:last + 1]), reads=[k_ep], writes=[(R["wc"], wi)])
        P.dma(S["wC"][d][c * 128:(c + 1) * 128, ch0:ch0 + nch], R["wc"][:, wi, 0:nch], reads=[(R["wc"], wi)], qn="pool")
        s1, ks1 = sslot()
        P.op("dve", lambda e, s1=s1, s_ep=s_ep: e.tensor_tensor(stg[:, s1, 0:n], r_, T[:, s_ep, 0:n], ALU.mult), reads=[rC, k_ep], writes=[ks1])
        P.dma(S["RW"][d][3][c * 128:(c + 1) * 128, t0:t0 + n], stg[:, s1, 0:n], reads=[ks1], qn="pool")
        s2, ks2 = sslot()
        P.op("pool", lambda e, s2=s2, s_lw=s_lw: e.tensor_tensor(stg[:, s2, 0:n], kk, T[:, s_lw, 0:n], ALU.mult), reads=[k_kk, k_lw], writes=[ks2])
        P.dma(S["RW"][d][2][c * 128:(c + 1) * 128, t0:t0 + n], stg[:, s2, 0:n], reads=[ks2], qn="pool")
        s3, ks3 = sslot()
        P.op("dve", lambda e, s3=s3, s_kd=s_kd, s_em=s_em: e.tensor_tensor(stg[:, s3, 0:n], T[:, s_kd, 0:n], T[:, s_em, 0:n], ALU.mult),
             reads=[k_kd, k_em], writes=[ks3])
        P.dma(S["RW"][d][0][c * 128:(c + 1) * 128, t0:t0 + n], stg[:, s3, 0:n], reads=[ks3], qn="pool")
        transpose_store(stg[:, s3, 0:n], ks3, S["kbTM"][d])
        s4, ks4 = sslot()
        P.op("pool", lambda e, s4=s4, s_a=s_a: e.tensor_tensor(stg[:, s4, 0:n], T[:, s_a, 0:n], kk, ALU.mult), reads=[k_a, k_kk], writes=[ks4])
        P.op("dve", lambda e, s4=s4, s_em=s_em: e.tensor_tensor(stg[:, s4, 0:n], stg[:, s4, 0:n], T[:, s_em, 0:n], ALU.mult),
             reads=[ks4, k_em], writes=[ks4])
        P.dma(S["RW"][d][1][c * 128:(c + 1) * 128, t0:t0 + n], stg[:, s4, 0:n], reads=[ks4], qn="pool")
        transpose_store(stg[:, s4, 0:n], ks4, S["bbTM"][d])
    P.op("dve", lambda e: e.scalar_tensor_tensor(PK[:, 3, 0:n], PK[:, 3, 0:n], vec[:, 2, c:c + 1], r_, ALU.mult, ALU.mult),
         reads=[k_ks, vec, rC], writes=[k_ks])
    ps3 = Kk.psb[2]
    headsum(PK[:, 3, 0:n], k_ks, ps3)
    s5, ks5 = sslot()
    P.op("dve", lambda e: e.tensor_tensor(stg[:, s5, 0:n], ps3[:, 0:n], v_, ALU.mult), reads=[ps3, vC], writes=[ks5])
    P.dma(S["bonT"][c * 128:(c + 1) * 128, t0:t0 + n], stg[:, s5, 0:n], reads=[ks5], qn="pool")


def stage_rwkv(Kk, l, S, yT, do_ctx):
    for d in range(2):
        _rwkv_dir(Kk, l, S, yT, do_ctx, d)


def _rwkv_dir(Kk, l, S, yT, do_ctx, d):
    P = Kk.P
    o_ = l // 2
    with ExitStack() as ses:
        P.stage_es = ses
        R = {}
        R["m64"] = P.sb("m64", [64, 4, 64])
        P.dma(R["m64"][:, :, :], Kk.ins["c_m64"][:, :, :], writes=[R["m64"]])
        R["wCt"] = P.sb("wCt", [64, 16, NCH64])
        P.dma(R["wCt"][:, :, :], S["wC"][d].rearrange("(h j) q -> j h q", j=64), writes=[R["wCt"]])
        R["ST"] = P.sb("ST", [64, 16, 64])
        P.op("pool", lambda e: e.memset(R["ST"][:, :, :], 0.0), writes=[R["ST"]])
        R["Xc"] = P.sb("Xc", [64, 2, 16, 4, 64])
        R["TMc"] = P.sb("TMc", [64, 1, 3, 1024])
        R["AM"] = P.sb("AM", [64, 16, 4, 64])
        R["Lm"] = P.sb("Lm", [64, 16, 64])
        R["Pb"] = P.sb("Pb", [64, 2, 8, 64])
        R["Qb"] = P.sb("Qb", [64, 2, 8, 64])
        R["Xb"] = P.sb("Xb", [64, 2, 8, 64])
        R["Qi"] = P.sb("Qi", [64, 8, 64])
        R["TT"] = P.sb("TT", [64, 16, 64])
        R["Zs"] = P.sb("Zs", [64, 16, 64])
        R["Us"] = P.sb("Us", [64, 16, 64])
        R["Yc"] = P.sb("Yc", [64, 2, 1024])
        R["tmpS"] = P.sb("tmpS", [64, 16, 64])
        if d == 1:
            R["Yf"] = P.sb("Yf", [64, 1, 1024])
            R["ln"] = P.sb("lnrow", [64, 2, 1024])
            P.dma(R["ln"][:, :, :], Kk.ins["rw_ln"][o_].rearrange("(o a) f -> o a f", o=1).broadcast_to([64, 2, 1024]), writes=[R["ln"]])
            R["st"] = P.sb("lnst", [64, 4, 16])
            R["cen"] = P.sb("cen", [64, 1024])
            R["sq"] = P.sb("lsq", [64, 1024])
            R["bg"] = P.sb("bg", [128, 1, 2, 8, 64])
            R["yo"] = P.sb("yo", [128, 2, 8, 64])
        order = list(range(4)) + list(range(4, NCH64)) if d == 0 else [3, 2, 1, 0] + list(range(NCH64 - 1, 3, -1))
        import os
        nlim = int(os.environ.get("RW_NCH", "999"))
        for ci, ch in enumerate(order[:nlim]):
            _rwkv_chunk(Kk, S, yT, R, d, ch, ci, do_ctx)
        P.barrier()
        P.stage_es = None


def _rwkv_chunk(Kk, S, yT, R, d, ch, ci, do_ctx):
    P = Kk.P
    s = ci % 2
    t0 = ch * 64
    Xc, TMc, AM, Lm, TT, ST, Zs, Us, Yc, m64 = R["Xc"], R["TMc"], R["AM"], R["Lm"], R["TT"], R["ST"], R["Zs"], R["Us"], R["Yc"], R["m64"]
    id64 = Kk.ident[0:64, 0:64]
    kx = (Xc, s)
    ktm = (TMc, 0)
    for q in range(4):
        P.dma(Xc[:, s, :, q, :], S["RW"][d][q][:, t0:t0 + 64].rearrange("(h j) t -> j h t", j=64), writes=[kx])
    P.dma(TMc[:, 0, 0, :], S["kbTM"][d][t0:t0 + 64, :], writes=[ktm])
    P.dma(TMc[:, 0, 1, :], S["bbTM"][d][t0:t0 + 64, :], writes=[ktm])
    P.dma(TMc[:, 0, 2, :], S["vTM"][t0:t0 + 64, :], writes=[ktm])
    import os
    LV = int(os.environ.get("RW_LEVEL", "99"))
    if LV <= 1:
        return
    mi = 0 if d == 0 else 2
    mL = m64[:, 2 if d == 0 else 0, :]
    psb = Kk.psb
    for hf in range(2):
        h0 = hf * 8
        for hh in range(8):
            h = h0 + hh
            bk = hh // 4
            col = (hh % 4) * 128
            P.op("pe", lambda e, h=h, bk=bk, col=col: e.matmul(psb[bk][0:64, col:col + 128], Xc[:, s, h, 0, :], Xc[:, s, h, 2:4, :],
                                                              start=True, stop=True), reads=[kx], writes=[psb[bk]])
            P.op("pe", lambda e, h=h, bk=bk, col=col: e.matmul(psb[2 + bk][0:64, col:col + 128], Xc[:, s, h, 1, :], Xc[:, s, h, 2:4, :],
                                                              start=True, stop=True), reads=[kx], writes=[psb[2 + bk]])
            P.op("pe", lambda e, h=h, hh=hh: e.matmul(psb[4][0:64, hh * 64:(hh + 1) * 64], Xc[:, s, h, 2, :], Xc[:, s, h, 1, :],
                                                      start=True, stop=True), reads=[kx], writes=[psb[4]])
        mask2 = m64[:, mi:mi + 2, :].rearrange("p (o a) t -> p o a t", o=1).broadcast_to([64, 4, 2, 64])
        for bk in range(2):
            hs = slice(h0 + bk * 4, h0 + bk * 4 + 4)
            P.op("dve", lambda e, bk=bk, hs=hs: e.tensor_tensor(AM[:, hs, 0:2, :], psb[bk][0:64, :].rearrange("p (h a t) -> p h a t", a=2, t=64),
                                                               mask2, ALU.mult), reads=[psb[bk], m64], writes=[(AM, (hf, bk, 0))])
            P.op("dve", lambda e, bk=bk, hs=hs: e.tensor_tensor(AM[:, hs, 2:4, :], psb[2 + bk][0:64, :].rearrange("p (h a t) -> p h a t", a=2, t=64),
                                                               mask2, ALU.mult), reads=[psb[2 + bk], m64], writes=[(AM, (hf, bk, 1))])
        kAM = [(AM, (hf, bk, a)) for bk in range(2) for a in range(2)]
        hs8 = slice(h0, h0 + 8)
        kL = (Lm, hf)
        P.op("dve", lambda e, hs8=hs8: e.tensor_tensor(Lm[:, hs8, :], psb[4][0:64, :].rearrange("p (h t) -> p h t", t=64),
                                              mL.rearrange("p (o t) -> p o t", o=1).broadcast_to([64, 8, 64]), ALU.mult),
             reads=[psb[4], m64], writes=[kL])
        if LV <= 2:
            continue
        Pb, Qb, Xb, Qi = R["Pb"], R["Qb"], R["Xb"], R["Qi"]
        idb = id64.rearrange("p (o t) -> p o t", o=1).broadcast_to([64, 8, 64])
        P.op("dve", lambda e, hs8=hs8, idb=idb: e.tensor_tensor(Xb[:, 0, :, :], idb, AM[:, hs8, 2, :], ALU.subtract), reads=kAM + [Kk.ident], writes=[(Xb, 0)])
        for lev in range(5):
            pi_, po_ = lev % 2, (lev + 1) % 2
            lastlev = (lev == 4)
            for hh in range(8):
                h = h0 + hh
                Pk = AM[:, h, 2, :] if lev == 0 else Pb[:, pi_, hh, :]
                Qk = Lm[:, h, :] if lev == 0 else Qb[:, pi_, hh, :]
                rd = kAM + [kL] if lev == 0 else [(Pb, pi_), (Qb, pi_)]
                if not lastlev:
                    P.op("pe", lambda e, hh=hh, Pk=Pk, Qk=Qk: e.matmul(psb[5][0:64, hh * 64:(hh + 1) * 64], Qk, Pk, start=True, stop=True),
                         reads=rd, writes=[psb[5]])
                P.op("pe", lambda e, hh=hh, Pk=Pk, Qk=Qk: e.matmul(psb[6][0:64, hh * 64:(hh + 1) * 64], Pk, Qk, start=True, stop=True),
                     reads=rd, writes=[psb[6]])
            if not lastlev:
                P.op("act", lambda e, po_=po_: e.activation(Pb[:, po_, :, :], psb[5][0:64, :].rearrange("p (h t) -> p h t", t=64), AF.Identity),
                     reads=[psb[5]], writes=[(Pb, po_)])
                P.op("dve", lambda e, po_=po_: e.tensor_copy(Qb[:, po_, :, :], psb[6][0:64, :].rearrange("p (h t) -> p h t", t=64)),
                     reads=[psb[6]], writes=[(Qb, po_)])
            P.op("dve", lambda e, idb=idb: e.tensor_tensor(Qi[:, :, :], psb[6][0:64, :].rearrange("p (h t) -> p h t", t=64), idb, ALU.add),
                 reads=[psb[6], Kk.ident], writes=[Qi])
            for hh in range(8):
                P.op("pe", lambda e, hh=hh, pi_=pi_: e.matmul(psb[7][0:64, hh * 64:(hh + 1) * 64], Qi[:, hh, :], Xb[:, pi_, hh, :],
                                                              start=True, stop=True), reads=[Qi, (Xb, pi_)], writes=[psb[7]])
            if lastlev:
                P.op("act", lambda e, hs8=hs8: e.activation(TT[:, hs8, :], psb[7][0:64, :].rearrange("p (h t) -> p h t", t=64), AF.Identity),
                     reads=[psb[7]], writes=[(TT, hf)])
            else:
                P.op("act", lambda e, po_=po_: e.activation(Xb[:, po_, :, :], psb[7][0:64, :].rearrange("p (h t) -> p h t", t=64), AF.Identity),
                     reads=[psb[7]], writes=[(Xb, po_)])
    allAM = [(AM, (hf, bk, a)) for hf in range(2) for bk in range(2) for a in range(2)]
    kTT = [(TT, 0), (TT, 1)]
    if LV <= 3:
        return
    V = lambda h: TMc[:, 0, 2, h * 64:(h + 1) * 64]
    for h in range(16):
        bk, col = h // 8, (h % 8) * 64
        P.op("pe", lambda e, h=h, bk=bk, col=col: e.matmul(psb[bk][0:64, col:col + 64], Xc[:, s, h, 2, :], ST[:, h, :], start=True, stop=False),
             reads=[kx, ST], writes=[psb[bk]])
        P.op("pe", lambda e, h=h, bk=bk, col=col: e.matmul(psb[bk][0:64, col:col + 64], AM[:, h, 0, :], V(h), start=False, stop=True),
             reads=allAM + [ktm], writes=[psb[bk]])
    for bk in range(2):
        P.op("act", lambda e, bk=bk: e.activation(Zs[:, bk * 8:(bk + 1) * 8, :], psb[bk][0:64, :].rearrange("p (h t) -> p h t", t=64),
                                                  AF.Identity, scale=-1.0), reads=[psb[bk]], writes=[(Zs, bk)])
    for h in range(16):
        bk, col = h // 8, (h % 8) * 64
        P.op("pe", lambda e, h=h, bk=bk, col=col: e.matmul(psb[2 + bk][0:64, col:col + 64], TT[:, h, :], Zs[:, h, :], start=True, stop=True),
             reads=kTT + [(Zs, bk)], writes=[psb[2 + bk]])
    for bk in range(2):
        P.op("dve", lambda e, bk=bk: e.tensor_copy(Us[:, bk * 8:(bk + 1) * 8, :], psb[2 + bk][0:64, :].rearrange("p (h t) -> p h t", t=64)),
             reads=[psb[2 + bk]], writes=[(Us, bk)])
    for h in range(16):
        bk, col = h // 8, (h % 8) * 64
        P.op("pe", lambda e, h=h, bk=bk, col=col: e.matmul(psb[4 + bk][0:64, col:col + 64], Xc[:, s, h, 3, :], ST[:, h, :], start=True, stop=False),
             reads=[kx, ST], writes=[psb[4 + bk]])
        P.op("pe", lambda e, h=h, bk=bk, col=col: e.matmul(psb[4 + bk][0:64, col:col + 64], AM[:, h, 1, :], V(h), start=False, stop=False),
             reads=allAM + [ktm], writes=[psb[4 + bk]])
        P.op("pe", lambda e, h=h, bk=bk, col=col: e.matmul(psb[4 + bk][0:64, col:col + 64], AM[:, h, 3, :], Us[:, h, :], start=False, stop=True),
             reads=allAM + [(Us, bk)], writes=[psb[4 + bk]])
        P.op("pe", lambda e, h=h, bk=bk, col=col: e.matmul(psb[6 + bk][0:64, col:col + 64], TMc[:, 0, 0, h * 64:(h + 1) * 64], V(h), start=True, stop=False),
             reads=[ktm], writes=[psb[6 + bk]])
        P.op("pe", lambda e, h=h, bk=bk, col=col: e.matmul(psb[6 + bk][0:64, col:col + 64], TMc[:, 0, 1, h * 64:(h + 1) * 64], Us[:, h, :], start=False, stop=True),
             reads=[ktm, (Us, bk)], writes=[psb[6 + bk]])
    ky = (Yc, s)
    tmpS = R["tmpS"]
    for bk in range(2):
        P.op("act", lambda e, bk=bk: e.activation(Yc[:, s, bk * 512:(bk + 1) * 512], psb[4 + bk][0:64, :], AF.Identity),
             reads=[psb[4 + bk]], writes=[(Yc, (s, bk))])
        hsb = slice(bk * 8, (bk + 1) * 8)
        P.op("dve", lambda e, bk=bk, hsb=hsb: e.tensor_tensor(tmpS[:, hsb, :], psb[6 + bk][0:64, :].rearrange("p (h t) -> p h t", t=64),
                                                             ST[:, hsb, :], ALU.add), reads=[psb[6 + bk], ST], writes=[(tmpS, bk)])
        P.op("pool", lambda e, bk=bk, hsb=hsb: e.tensor_tensor(ST[:, hsb, :], tmpS[:, hsb, :],
                                                              R["wCt"][:, hsb, ch:ch + 1].broadcast_to([64, 8, 64]), ALU.mult),
             reads=[(tmpS, bk), R["wCt"]], writes=[ST])
    kyall = [(Yc, (s, 0)), (Yc, (s, 1))]
    if os.environ.get("RW_DBG") and d == 0 and ci == 0:
        D = S["dbg"]
        P.dma(D["AM"], AM[:, :, :, :], reads=allAM, qn="pool")
        P.dma(D["Lm"], Lm[:, :, :], reads=[(Lm, 0), (Lm, 1)], qn="pool")
        P.dma(D["TT"], TT[:, :, :], reads=kTT, qn="pool")
        P.dma(D["Zs"], Zs[:, :, :], reads=[(Zs, 0), (Zs, 1)], qn="pool")
        P.dma(D["Us"], Us[:, :, :], reads=[(Us, 0), (Us, 1)], qn="pool")
        P.dma(D["Yc"], Yc[:, s, :], reads=kyall, qn="pool")
        P.dma(D["ST"], ST[:, :, :], reads=[ST], qn="pool")
    if LV <= 4:
        return
    if d == 0:
        P.dma(S["YfTM"][t0:t0 + 64, :], Yc[:, s, :], reads=kyall, qn="pool")
        return
    if ch < 4 and not do_ctx:
        return
    Yf, ln, stt, cen, sq, bg, yo = R["Yf"], R["ln"], R["st"], R["cen"], R["sq"], R["bg"], R["yo"]
    kf = (Yf, 0)
    P.dma(Yf[:, 0, :], S["YfTM"][t0:t0 + 64, :], writes=[kf])
    kb_ = (bg, 0)
    P.dma(bg[:, 0, 0, :, :], S["bonT"][:, t0:t0 + 64].rearrange("(c p) t -> p c t", p=128), writes=[kb_])
    P.dma(bg[:, 0, 1, :, :], S["gateT"][:, t0:t0 + 64].rearrange("(c p) t -> p c t", p=128), writes=[kb_])
    P.op("dve", lambda e: e.tensor_tensor(Yf[:, 0, :], Yf[:, 0, :], Yc[:, s, :], ALU.add), reads=[kf] + kyall, writes=[kf])
    y3 = Yf[:, 0, :].rearrange("p (h t) -> p h t", t=64)
    c3 = cen[:, :].rearrange("p (h t) -> p h t", t=64)
    q3 = sq[:, :].rearrange("p (h t) -> p h t", t=64)
    P.op("dve", lambda e: e.tensor_reduce(stt[:, 0, :], y3, AX.X, ALU.add), reads=[kf], writes=[(stt, 0)])
    P.op("dve", lambda e: e.tensor_scalar(stt[:, 0, :], stt[:, 0, :], 1.0 / 64, None, ALU.mult), reads=[(stt, 0)], writes=[(stt, 0)])
    P.op("dve", lambda e: e.tensor_tensor(c3, y3, stt[:, 0, :].rearrange("p (h o) -> p h o", o=1).broadcast_to([64, 16, 64]), ALU.subtract),
         reads=[kf, (stt, 0)], writes=[cen])
    P.op("act", lambda e: e.activation(sq[:, :], cen[:, :], AF.Square), reads=[cen], writes=[sq])
    P.op("dve", lambda e: e.tensor_reduce(stt[:, 1, :], q3, AX.X, ALU.add), reads=[sq], writes=[(stt, 1)])
    P.op("act", lambda e: e.activation(stt[:, 1, :], stt[:, 1, :], AF.Sqrt, bias=Kk.epsb[0:64, 1:2], scale=1.0 / 64),
         reads=[(stt, 1), Kk.epsb], writes=[(stt, 1)])
    P.op("dve", lambda e: e.reciprocal(stt[:, 1, :], stt[:, 1, :]), reads=[(stt, 1)], writes=[(stt, 1)])
    P.op("dve", lambda e: e.tensor_tensor(c3, c3, stt[:, 1, :].rearrange("p (h o) -> p h o", o=1).broadcast_to([64, 16, 64]), ALU.mult),
         reads=[cen, (stt, 1)], writes=[cen])
    P.op("pool", lambda e: e.tensor_tensor(cen[:, :], cen[:, :], ln[:, 0, :], ALU.mult), reads=[cen, ln], writes=[cen])
    P.op("pool", lambda e: e.tensor_tensor(cen[:, :], cen[:, :], ln[:, 1, :], ALU.add), reads=[cen, ln], writes=[cen])
    pst = psb[0]
    for c in range(8):
        P.op("pe", lambda e, c=c: e.matmul(pst[:, c * 64:(c + 1) * 64], cen[:, c * 128:(c + 1) * 128], id64, start=True, stop=True),
             reads=[cen, Kk.ident], writes=[pst])
    ko = (yo, s)
    P.op("dve", lambda e: e.tensor_tensor(yo[:, s, :, :], pst[:, :].rearrange("p (c t) -> p c t", t=64), bg[:, 0, 0, :, :], ALU.add),
         reads=[pst, kb_], writes=[ko])
    P.op("pool", lambda e: e.tensor_tensor(yo[:, s, :, :], yo[:, s, :, :], bg[:, 0, 1, :, :], ALU.mult), reads=[ko, kb_], writes=[ko])
    P.dma(yT[1024:2048, t0:t0 + 64].rearrange("(c p) t -> p c t", p=128), yo[:, s, :, :], reads=[ko], qn="pool")


def fm(v, n=None):
    v = np.asarray(v, np.float32)
    return np.ascontiguousarray(v.reshape(-1, 128).T)


def host_prep(inp, ncores=8):
    f32 = np.float32
    L = 4
    sh = {}
    sh["c_ones"] = np.ones((128, 128), f32)
    sh["c_ident"] = np.eye(128, dtype=f32)
    sh["c_onesD"] = np.full((128, 128), 1.0 / D, f32)
    eps = np.zeros((128, 4), f32)
    eps[:, 0] = EPS
    eps[:, 1] = 64e-5
    eps[:, 2] = 1e-12
    sh["c_eps"] = eps
    sh["ada_w"] = np.asarray(inp["ada_w"], f32)
    sh["ada_b"] = np.stack([fm(inp["ada_b"][l]) for l in range(L)])
    sh["norm_mix"] = np.stack([np.repeat(fm(inp["norm_mix"][l])[:, :, None], 2, 2) for l in range(L)])
    sh["norm_ffn"] = np.stack([np.repeat(fm(inp["norm_ffn"][l])[:, :, None], 2, 2) for l in range(L)])
    sh["ffn_w_up"] = np.asarray(inp["ffn_w_up"], f32)
    sh["ffn_w_down"] = np.asarray(inp["ffn_w_down"], f32)
    sh["ffn_conv_w"] = np.stack([np.stack([fm(inp["ffn_conv_w"][l][k]) for k in range(3)], 1) for l in range(L)])
    sh["ffn_conv_b"] = np.stack([fm(inp["ffn_conv_b"][l]) for l in range(L)])
    sh["ev_w_in"] = np.asarray(inp["ev_w_in"], f32)
    sh["ev_w_out"] = np.asarray(inp["ev_w_out"], f32)
    sh["od_w_in"] = np.asarray(inp["od_w_in"], f32)
    sh["od_w_out"] = np.asarray(inp["od_w_out"], f32)
    sh["final_norm"] = fm(inp["final_norm"])
    eps[:, 3] = 1.0
    host_prep_even(inp, sh)
    host_prep_odd(inp, sh)
    per = []
    x = np.asarray(inp["x"], f32)
    ctx = np.asarray(inp["ctx"], f32)
    c = np.asarray(inp["c"], f32)
    cc = np.asarray(inp["c_ctx"], f32)
    for i in range(ncores):
        b = i % 4
        d = {}
        d["xT0"] = np.ascontiguousarray(np.concatenate([ctx[b], x[b]], 0).T)
        d["cT"] = np.ascontiguousarray(np.stack([fm(c[b]), fm(cc)], 2))
        per.append(d)
    return sh, per


def rope_tables(hd):
    d = hd // 2
    half = d // 2
    t = np.arange(NLAT)
    row = (t // 64).astype(np.float32)
    col = (t % 64).astype(np.float32)
    inv = (10000.0 ** (-np.arange(half, dtype=np.float32) / half)).astype(np.float32)
    cos = np.zeros((hd, NLAT), np.float32)
    sin = np.zeros((hd, NLAT), np.float32)
    R = np.zeros((hd, hd), np.float32)
    for p in range(hd):
        pos = row if p < d else col
        i = (p % d) % half
        ang = pos * inv[i]
        cos[p] = np.cos(ang)
        sin[p] = np.sin(ang)
        if (p % d) < half:
            R[p, p + half] = -1.0
        else:
            R[p, p - half] = 1.0
    return cos, sin, np.ascontiguousarray(R.T)


def host_prep_even(inp, sh):
    f32 = np.float32
    sh["qk_gain"] = np.stack([np.stack([inp["attn_q_norm"][e], inp["attn_k_norm"][e]], 1) for e in range(2)]).astype(f32)
    cos, sin, RT = rope_tables(128)
    sh["c_cos128"], sh["c_sin128"], sh["c_RT128"] = cos, sin, RT
    sh["c_onesH"] = np.full((128, 128), 1.0 / 128, f32)
    sh["c_ones512"] = np.full((128, 128), 1.0 / 512, f32)
    sh["dt_ba"] = np.stack([np.stack([inp["ssm_dt_bias"][e].reshape(32), inp["ssm_a_log"][e].reshape(32)], 1) for e in range(2)]).astype(f32)
    sh["ssm_conv_w"] = np.stack([np.stack([fm(inp["ssm_conv_w"][e][k]) for k in range(5)], 2) for e in range(2)]).astype(f32)
    sh["ssm_conv_b"] = np.stack([fm(inp["ssm_conv_b"][e]) for e in range(2)]).astype(f32)
    s_ = np.arange(128)[:, None]
    l_ = np.arange(128)[None, :]
    sh["c_tri"] = np.ascontiguousarray(np.stack([s_ <= l_, s_ > l_, s_ >= l_, s_ < l_], 1).astype(f32))
    sh["ssm_d"] = np.stack([fm(np.repeat(inp["ssm_d"][e], 64)) for e in range(2)]).astype(f32)
    sh["ssm_norm"] = np.stack([fm(inp["ssm_norm"][e]) for e in range(2)]).astype(f32)


def host_prep_odd(inp, sh):
    f32 = np.float32
    sh["mla_q_b"] = np.asarray(inp["mla_q_b"], f32)
    sh["mla_kv_b"] = np.asarray(inp["mla_kv_b"], f32)
    sh["mla_gain"] = np.stack([np.concatenate([fm(inp["mla_q_a_norm"][o]), fm(inp["mla_kv_a_norm"][o])], 1) for o in range(2)]).astype(f32)
    cos, sin, RT = rope_tables(64)
    sh["c_cos64"], sh["c_sin64"], sh["c_RT64"] = cos, sin, RT
    sh["c_ones256"] = np.full((128, 128), 1.0 / 256, f32)
    def fm_ud(v):
        out = np.zeros((128, 30), f32)
        v = np.asarray(v, f32)
        for i in range(24):
            out[:, i] = v[i * 128:(i + 1) * 128]
        for i in range(4):
            out[:96, 24 + i] = v[3072 + i * 96:3072 + (i + 1) * 96]
        out[:, 28] = v[3456:3584]
        out[:, 29] = v[3584:3712]
        return out
    sh["rw_mu"] = np.stack([fm_ud(inp["rwkv_mu"][o]) for o in range(2)])
    sh["rw_vec"] = np.stack([np.stack([fm(inp["rwkv_k_k"][o]), fm(inp["rwkv_k_a"][o]), fm(inp["rwkv_r_k"][o].reshape(-1)),
                                       fm(inp["rwkv_w0"][o][0]), fm(inp["rwkv_w0"][o][1]), fm(inp["rwkv_a0"][o][0]), fm(inp["rwkv_a0"][o][1])], 1)
                             for o in range(2)]).astype(f32)
    sh["rwkv_g2"] = np.asarray(inp["rwkv_g2"], f32)
    sh["rwkv_w2"] = np.asarray(inp["rwkv_w2"], f32)
    sh["rwkv_a2"] = np.asarray(inp["rwkv_a2"], f32)
    sh["rw_ln"] = np.stack([np.stack([inp["rwkv_ln_w"][o], inp["rwkv_ln_b"][o]]) for o in range(2)]).astype(f32)
    blk = np.zeros((128, 128), f32); blk[:64, :64] = 1; blk[64:, 64:] = 1
    sh["c_blk64"] = blk
    s_ = np.arange(64)[:, None]; t_ = np.arange(64)[None, :]
    sh["c_m64"] = np.ascontiguousarray(np.stack([s_ < t_, s_ <= t_, s_ > t_, s_ >= t_], 1).astype(f32))


def declare_inputs(Kk, sh, per0):
    for k, v in list(sh.items()) + list(per0.items()):
        Kk.din(k, v.shape)


def run(nc, sh, per, trace=False):
    in_maps = []
    for d in per:
        m = dict(sh)
        m.update(d)
        in_maps.append(m)
    return run_bass_kernel_spmd(nc, in_maps, core_ids=list(range(len(per))), trace=trace)

def build_program(sh, per0):
    nc = bass.Bass("TRN2", target_bir_lowering=False)
    with ExitStack() as es:
        Kk = K(nc, es)
        declare_inputs(Kk, sh, per0)
        P = Kk.P
        Se = dict(qT=Kk.dscr("qT", [8, 128, NT]), kT=Kk.dscr("kT", [2, 128, NT]), vM=Kk.dscr("vMe", [NT, 256]),
                  zT=Kk.dscr("zT", [1024, NT]), xbcT=Kk.dscr("xbcT", [1536, NT]), dtT=Kk.dscr("dtT", [32, NT]),
                  xcT=Kk.dscr("xcT", [1536, NT]), ysT=Kk.dscr("ysT", [1024, NT]))
        So = dict(qnT=Kk.dscr("qnT", [8, 128, NT]), qpT=Kk.dscr("qpT", [8, 64, NT]), knT=Kk.dscr("knT", [8, 128, NT]),
                  kpT=Kk.dscr("kpT", [64, NT]), vM=Kk.dscr("vMo", [NT, 1024]), udT=Kk.dscr("udT", [3712, NT]),
                  RW=Kk.dscr("RW", [2, 4, 1024, NT]), kbTM=Kk.dscr("kbTM", [2, NT, 1024]), bbTM=Kk.dscr("bbTM", [2, NT, 1024]),
                  vTM=Kk.dscr("vTM", [NT, 1024]), wC=Kk.dscr("wC", [2, 1024, NCH64]), gateT=Kk.dscr("gateT", [1024, NT]),
                  bonT=Kk.dscr("bonT", [1024, NT]), YfTM=Kk.dscr("YfTM", [NT, 1024]))
        yT = Kk.dscr("yT", [2048, NT])
        xA = Kk.dscr("xA", [2048, NT])
        xB = Kk.dscr("xB", [2048, NT])
        outT = Kk.dscr("outT", [2048, NLAT], out=True)
        load_consts(Kk)
        with ExitStack() as ses:
            P.stage_es = ses
            cp = P.sb("cp", [128, 2, 16, 512])
            for i, (t0_, n, isctx) in enumerate(tiles_main()):
                P.dma(cp[:, i % 2, :, 0:n], Kk.ins["xT0"][:, t0_:t0_ + n].rearrange("(c p) t -> p c t", p=128), writes=[(cp, i % 2)])
                P.dma(xA[:, t0_:t0_ + n].rearrange("(c p) t -> p c t", p=128), cp[:, i % 2, :, 0:n], reads=[(cp, i % 2)], qn="pool")
            P.barrier()
            P.stage_es = None
        cur, oth = xA, xB
        for l in range(4):
            last = (l == 3)
            stage_adaln(Kk, l)
            if l % 2 == 0:
                stage_A_even(Kk, l, cur, Se)
                stage_attn_even(Kk, Se, yT, not last)
                stage_ssd_conv(Kk, l, Se)
                stage_ssd(Kk, l, Se, yT, not last)
                stage_C1(Kk, l, cur, yT, Kk.ins["ev_w_out"][l // 2], not last)
            else:
                stage_A_odd(Kk, l, cur, So)
                stage_attn_odd(Kk, So, yT, not last)
                stage_rwkv_prep(Kk, l, So)
                stage_rwkv(Kk, l, So, yT, not last)
                stage_C1(Kk, l, cur, yT, Kk.ins["od_w_out"][l // 2], not last)
            if last:
                stage_C2(Kk, l, cur, oth, False, final_norm=Kk.ins["final_norm"], outT=outT)
            else:
                stage_C2(Kk, l, cur, oth, True)
            cur, oth = oth, cur
        P.emit_all()
    return nc


def kernel(**inputs):
    sh, per = host_prep(inputs, ncores=8)
    nc = build_program(sh, per[0])
    res = run(nc, sh, per, trace=False)
    out = np.stack([np.ascontiguousarray(res.results[b]["outT"].T) for b in range(4)], 0)
    return out.astype(np.float32)
```
